# Optimizing a Trainium2 kernel written in Bass

```python
import math
import jax, jax.numpy as jnp
from jax import lax
import numpy as np

D_MODEL = 1024
BATCH = 4
SEQ = 4096
DEPTH = 1

NSA_HEADS = 16
NSA_KV_GROUPS = 2
NSA_HEADS_PER_GROUP = NSA_HEADS // NSA_KV_GROUPS
NSA_HEAD_DIM = 64
CMP_BLOCK = 32
CMP_STRIDE = 16
CMP_HIDDEN = 256
SLC_BLOCK = 64
SLC_TOPN = 16
WINDOW = 512
Q_BLOCK = 128
ROPE_THETA = 500000.0
ROPE_DIM = NSA_HEAD_DIM // 4
FORCED_SCORE = 1.0e4

RET_HEADS = 8
RET_KEY_DIM = 64
RET_VAL_DIM = 128
RET_CHUNK = 128
RET_ROPE_THETA = 10000.0

D_FF = 2816
EPS = 1e-6
GN_EPS = 1e-5
NEG_INF = -1e30

NSA_Q_W = NSA_HEADS * NSA_HEAD_DIM
NSA_KV_W = NSA_KV_GROUPS * NSA_HEAD_DIM
NSA_GATE_W = 3 * NSA_HEADS
RET_QK_W = RET_HEADS * RET_KEY_DIM
RET_V_W = RET_HEADS * RET_VAL_DIM
IN_WIDTHS = (NSA_Q_W, NSA_KV_W, NSA_KV_W, NSA_KV_W, NSA_KV_W, NSA_KV_W, NSA_KV_W,
             NSA_GATE_W, RET_QK_W, RET_QK_W, RET_V_W, RET_V_W, 2 * D_MODEL)
D_IN = sum(IN_WIDTHS)

kernel_name = "hybrid_nsa_retention_macaron"


def _rmsnorm(x, g):
    xf = x.astype(jnp.float32)
    y = xf * lax.rsqrt(jnp.mean(xf * xf, axis=-1, keepdims=True) + EPS)
    return (y * g.astype(jnp.float32)).astype(x.dtype)


def _swiglu(x, w_gate, w_up, w_down):
    return (jax.nn.silu(x @ w_gate) * (x @ w_up)) @ w_down


def _rope(x, pos, rot_dim, theta):
    half = rot_dim // 2
    freqs = theta ** (-(jnp.arange(half, dtype=jnp.float32) * 2.0 / rot_dim))
    ang = pos.astype(jnp.float32)[:, None] * freqs[None, :]
    cos = jnp.cos(ang).astype(x.dtype)
    sin = jnp.sin(ang).astype(x.dtype)
    x1 = x[..., :half]
    x2 = x[..., half:rot_dim]
    return jnp.concatenate([x1 * cos - x2 * sin, x2 * cos + x1 * sin, x[..., rot_dim:]], axis=-1)


def _masked_softmax(s, mask):
    s = jnp.where(mask, s.astype(jnp.float32), NEG_INF)
    m = jnp.max(s, axis=-1, keepdims=True)
    p = jnp.where(mask, jnp.exp(s - m), 0.0)
    return p / jnp.maximum(jnp.sum(p, axis=-1, keepdims=True), 1e-20)


def _nsa(q, k_cmp, v_cmp, k_slc, v_slc, k_win, v_win, gates, pos_emb, ck_w1, ck_w2, cv_w1, cv_w2):
    B, G, Hg, S, dh = q.shape
    scale = dh ** -0.5
    pos = jnp.arange(S)
    q = _rope(q, pos, ROPE_DIM, ROPE_THETA)
    k_slc = _rope(k_slc, pos, ROPE_DIM, ROPE_THETA)
    k_win = _rope(k_win, pos, ROPE_DIM, ROPE_THETA)

    n_cmp = (S - CMP_BLOCK) // CMP_STRIDE + 1
    cmp_start = jnp.arange(n_cmp) * CMP_STRIDE
    cmp_end = cmp_start + CMP_BLOCK - 1
    cmp_idx = cmp_start[:, None] + jnp.arange(CMP_BLOCK)[None, :]

    def compress(tok, w1, w2):
        blk = tok[:, :, cmp_idx] + pos_emb
        blk = blk.reshape(B, G, n_cmp, CMP_BLOCK * dh)
        return jax.nn.gelu(blk @ w1) @ w2

    kc = _rope(compress(k_cmp, ck_w1, ck_w2), cmp_end, ROPE_DIM, ROPE_THETA)
    vc = compress(v_cmp, cv_w1, cv_w2)

    n_slc = S // SLC_BLOCK
    s_start = jnp.arange(n_slc) * SLC_BLOCK
    overlap = jnp.clip(jnp.minimum(cmp_start[:, None] + CMP_BLOCK, s_start[None, :] + SLC_BLOCK)
                       - jnp.maximum(cmp_start[:, None], s_start[None, :]), 0, None)
    cmp_to_slc = overlap.astype(jnp.float32) / CMP_BLOCK
    top_n = min(SLC_TOPN, n_slc)

    k_blocks = k_slc.reshape(B, G, n_slc, SLC_BLOCK, dh)
    v_blocks = v_slc.reshape(B, G, n_slc, SLC_BLOCK, dh)
    k_pad = jnp.pad(k_win, ((0, 0), (0, 0), (WINDOW, 0), (0, 0)))
    v_pad = jnp.pad(v_win, ((0, 0), (0, 0), (WINDOW, 0), (0, 0)))
    b_ix = jnp.arange(B)[:, None, None, None]
    g_ix = jnp.arange(G)[None, :, None, None]
    sb = jnp.arange(n_slc)

    def one_block(c):
        q0 = c * Q_BLOCK
        qc = lax.dynamic_slice_in_dim(q, q0, Q_BLOCK, axis=3)
        gc = lax.dynamic_slice_in_dim(gates, q0, Q_BLOCK, axis=3)
        t = q0 + jnp.arange(Q_BLOCK)

        s_c = jnp.einsum('bghqd,bgnd->bghqn', qc, kc) * scale
        p_c = _masked_softmax(s_c, cmp_end[None, :] <= t[:, None])
        o_c = jnp.einsum('bghqn,bgnd->bghqd', p_c.astype(vc.dtype), vc)

        imp = jnp.einsum('bghqn,ns->bgqs', p_c, cmp_to_slc)
        cur = t // SLC_BLOCK
        forced = (sb[None, :] == 0) | (sb[None, :] == cur[:, None]) | (sb[None, :] == cur[:, None] - 1)
        future = sb[None, :] > cur[:, None]
        imp = jnp.where(forced, FORCED_SCORE, jnp.where(future, -FORCED_SCORE, imp))
        _, sel = lax.top_k(imp, top_n)
        k_sel = k_blocks[b_ix, g_ix, sel]
        v_sel = v_blocks[b_ix, g_ix, sel]
        kpos = sel[..., None] * SLC_BLOCK + jnp.arange(SLC_BLOCK)
        s_s = jnp.einsum('bghqd,bgqnkd->bghqnk', qc, k_sel) * scale
        s_s = s_s.reshape(B, G, Hg, Q_BLOCK, top_n * SLC_BLOCK)
        mask_s = (kpos <= t[:, None, None]).reshape(B, G, 1, Q_BLOCK, top_n * SLC_BLOCK)
        p_s = _masked_softmax(s_s, mask_s).reshape(B, G, Hg, Q_BLOCK, top_n, SLC_BLOCK)
        o_s = jnp.einsum('bghqnk,bgqnkd->bghqd', p_s.astype(v_sel.dtype), v_sel)

        kw = lax.dynamic_slice_in_dim(k_pad, q0, WINDOW + Q_BLOCK, axis=2)
        vw = lax.dynamic_slice_in_dim(v_pad, q0, WINDOW + Q_BLOCK, axis=2)
        wpos = q0 - WINDOW + jnp.arange(WINDOW + Q_BLOCK)
        dist = t[:, None] - wpos[None, :]
        mask_w = (wpos[None, :] >= 0) & (dist >= 0) & (dist < WINDOW)
        s_w = jnp.einsum('bghqd,bgkd->bghqk', qc, kw) * scale
        p_w = _masked_softmax(s_w, mask_w)
        o_w = jnp.einsum('bghqk,bgkd->bghqd', p_w.astype(vw.dtype), vw)

        return gc[..., 0:1] * o_c + gc[..., 1:2] * o_s + gc[..., 2:3] * o_w

    out = lax.map(one_block, jnp.arange(S // Q_BLOCK))
    return out.transpose(1, 0, 4, 2, 3, 5).reshape(B, S, G * Hg * dh)


def _retention(q, k, v, gn_gain):
    B, H, S, dk = q.shape
    dv = v.shape[-1]
    C = RET_CHUNK
    N = S // C
    dt = q.dtype
    pos = jnp.arange(S)
    q = _rope(q, pos, dk, RET_ROPE_THETA) * (dk ** -0.5)
    k = _rope(k, pos, dk, RET_ROPE_THETA)
    log_g = jnp.log(1.0 - 2.0 ** (-5.0 - jnp.arange(H, dtype=jnp.float32)))
    i = jnp.arange(C, dtype=jnp.float32)
    diff = i[:, None] - i[None, :]
    decay = jnp.where(diff >= 0, jnp.exp(jnp.maximum(diff, 0.0) * log_g[:, None, None]), 0.0)
    zeta = jnp.exp((C - 1.0 - i)[None, :] * log_g[:, None])
    xi = jnp.exp((i + 1.0)[None, :] * log_g[:, None])
    g_chunk = jnp.exp(C * log_g)

    qc = q.reshape(B, H, N, C, dk)
    kc = k.reshape(B, H, N, C, dk)
    vc = v.reshape(B, H, N, C, dv)
    inner = jnp.einsum('bhncd,bhnkd->bhnck', qc, kc) * decay[None, :, None].astype(dt)
    o_inner = jnp.einsum('bhnck,bhnke->bhnce', inner, vc)
    kv = jnp.einsum('bhncd,bhnce->nbhde', kc * zeta[None, :, None, :, None].astype(dt), vc)
    kv = kv.astype(jnp.float32)

    def step(R, kv_n):
        return g_chunk[None, :, None, None] * R + kv_n, R

    _, R_prev = lax.scan(step, jnp.zeros((B, H, dk, dv), jnp.float32), kv)
    o_cross = jnp.einsum('bhncd,nbhde->bhnce',
                         (qc * xi[None, :, None, :, None].astype(dt)).astype(jnp.float32), R_prev)
    o = (o_inner.astype(jnp.float32) + o_cross).reshape(B, H, S, dv)
    mu = jnp.mean(o, axis=-1, keepdims=True)
    var = jnp.mean(jnp.square(o - mu), axis=-1, keepdims=True)
    on = (o - mu) * lax.rsqrt(var + GN_EPS) * gn_gain.astype(jnp.float32)[None, :, None, :]
    return on.astype(dt).transpose(0, 2, 1, 3).reshape(B, S, H * dv)


def setup_inputs(seed: int = 0) -> dict:
    key = jax.random.key(seed)
    ks = jax.random.split(key, 24)
    f32 = jnp.float32

    def w(k, shape, fan_in):
        return jax.random.normal(k, shape, f32) * (fan_in ** -0.5)

    def gain(k, shape):
        return 1.0 + 0.02 * jax.random.normal(k, shape, f32)

    L = DEPTH
    return {
        "x": jax.random.normal(ks[0], (BATCH, SEQ, D_MODEL), f32),
        "ffn1_norm": gain(ks[1], (L, D_MODEL)),
        "ffn1_w_gate": w(ks[2], (L, D_MODEL, D_FF), D_MODEL),
        "ffn1_w_up": w(ks[3], (L, D_MODEL, D_FF), D_MODEL),
        "ffn1_w_down": w(ks[4], (L, D_FF, D_MODEL), D_FF),
        "mix_norm": gain(ks[5], (L, D_MODEL)),
        "w_in": w(ks[6], (L, D_MODEL, D_IN), D_MODEL),
        "cmp_pos_emb": 0.1 * jax.random.normal(ks[7], (L, CMP_BLOCK, NSA_HEAD_DIM), f32),
        "cmp_k_w1": w(ks[8], (L, CMP_BLOCK * NSA_HEAD_DIM, CMP_HIDDEN), CMP_BLOCK * NSA_HEAD_DIM),
        "cmp_k_w2": w(ks[9], (L, CMP_HIDDEN, NSA_HEAD_DIM), CMP_HIDDEN),
        "cmp_v_w1": w(ks[10], (L, CMP_BLOCK * NSA_HEAD_DIM, CMP_HIDDEN), CMP_BLOCK * NSA_HEAD_DIM),
        "cmp_v_w2": w(ks[11], (L, CMP_HIDDEN, NSA_HEAD_DIM), CMP_HIDDEN),
        "ret_gn_gain": gain(ks[12], (L, RET_HEADS, RET_VAL_DIM)),
        "w_branch_nsa": w(ks[13], (L, NSA_Q_W, D_MODEL), NSA_Q_W),
        "w_branch_ret": w(ks[14], (L, RET_V_W, D_MODEL), RET_V_W),
        "w_out": w(ks[15], (L, D_MODEL, D_MODEL), D_MODEL),
        "ffn2_norm": gain(ks[16], (L, D_MODEL)),
        "ffn2_w_gate": w(ks[17], (L, D_MODEL, D_FF), D_MODEL),
        "ffn2_w_up": w(ks[18], (L, D_MODEL, D_FF), D_MODEL),
        "ffn2_w_down": w(ks[19], (L, D_FF, D_MODEL), D_FF),
        "final_norm": gain(ks[20], (D_MODEL,)),
    }


def reference(x, ffn1_norm, ffn1_w_gate, ffn1_w_up, ffn1_w_down, mix_norm, w_in,
              cmp_pos_emb, cmp_k_w1, cmp_k_w2, cmp_v_w1, cmp_v_w2, ret_gn_gain,
              w_branch_nsa, w_branch_ret, w_out, ffn2_norm, ffn2_w_gate, ffn2_w_up,
              ffn2_w_down, final_norm):
    B, S, _ = x.shape
    G, Hg, dh = NSA_KV_GROUPS, NSA_HEADS_PER_GROUP, NSA_HEAD_DIM
    split_points = [int(p) for p in np.cumsum(IN_WIDTHS)[:-1]]

    def nsa_heads(t):
        return t.reshape(B, S, G, Hg, dh).transpose(0, 2, 3, 1, 4)

    def nsa_kv(t):
        return t.reshape(B, S, G, dh).transpose(0, 2, 1, 3)

    def ret_heads(t, d):
        return t.reshape(B, S, RET_HEADS, d).transpose(0, 2, 1, 3)

    for layer in range(DEPTH):
        h = _rmsnorm(x, ffn1_norm[layer])
        x = x + 0.5 * _swiglu(h, ffn1_w_gate[layer], ffn1_w_up[layer], ffn1_w_down[layer])

        h = _rmsnorm(x, mix_norm[layer])
        proj = h @ w_in[layer]
        (q_a, kc_a, vc_a, ks_a, vs_a, kw_a, vw_a, g_a,
         q_r, k_r, v_r, g_r, g_merge) = jnp.split(proj, split_points, axis=-1)

        nsa_gates = jax.nn.sigmoid(g_a.reshape(B, S, G, Hg, 3).transpose(0, 2, 3, 1, 4))
        a_out = _nsa(nsa_heads(q_a), nsa_kv(kc_a), nsa_kv(vc_a), nsa_kv(ks_a), nsa_kv(vs_a),
                     nsa_kv(kw_a), nsa_kv(vw_a), nsa_gates, cmp_pos_emb[layer],
                     cmp_k_w1[layer], cmp_k_w2[layer], cmp_v_w1[layer], cmp_v_w2[layer])
        r_out = _retention(ret_heads(q_r, RET_KEY_DIM), ret_heads(k_r, RET_KEY_DIM),
                           ret_heads(v_r, RET_VAL_DIM), ret_gn_gain[layer])
        r_out = jax.nn.silu(g_r) * r_out

        gate_a, gate_r = jnp.split(jax.nn.sigmoid(g_merge), 2, axis=-1)
        mixed = gate_a * (a_out @ w_branch_nsa[layer]) + gate_r * (r_out @ w_branch_ret[layer])
        x = x + mixed @ w_out[layer]

        h = _rmsnorm(x, ffn2_norm[layer])
        x = x + 0.5 * _swiglu(h, ffn2_w_gate[layer], ffn2_w_up[layer], ffn2_w_down[layer])

    return _rmsnorm(x, final_norm)
```

```python
import contextlib
import os as _os
import numpy as np
import concourse.bass as bass
import concourse.mybir as mybir
from concourse.bass_utils import run_bass_kernel_spmd

F32 = mybir.dt.float32
BF16 = mybir.dt.bfloat16
AF = mybir.ActivationFunctionType
ALU = mybir.AluOpType
AX = mybir.AxisListType

S = 4096
D = 1024
DFF = 2816
NT = S // 128
DIN = 6960


class T:
    __slots__ = ("ap", "name", "last_w", "readers", "excl")

    def __init__(self, ap, name="", excl=False):
        self.ap = ap
        self.name = name
        self.last_w = None
        self.readers = []
        self.excl = excl

    def __getitem__(self, k):
        return self.ap[k]


class Prog:
    ENGS = ("pe", "act", "dve", "pool", "sp")

    def __init__(self, nc, n_dma_sems=12, same_engine_sync=True):
        self.nc = nc
        self.ops = []
        self.es = contextlib.ExitStack()
        self.same_engine_sync = same_engine_sync
        self.n_dma_sems = n_dma_sems
        self.ncnt = 0

    def sb(self, shape, dt, name=None):
        self.ncnt += 1
        name = name or f"sb{self.ncnt}"
        t = self.es.enter_context(self.nc.sbuf_tensor(name, list(shape), dt))
        return T(t[:], name)

    def ps(self, shape, dt, name=None):
        self.ncnt += 1
        name = name or f"ps{self.ncnt}"
        t = self.es.enter_context(self.nc.psum_tensor(name, list(shape), dt))
        return T(t[:], name, excl=True)

    def dram(self, name, shape, dt, kind="Internal"):
        t = self.nc.dram_tensor(name, list(shape), dt, kind=kind)
        return T(t.ap(), name)

    def op(self, eng, fn, r=(), w=(), dma=False):
        idx = len(self.ops)
        deps = set()
        for t in r:
            if t.last_w is not None:
                deps.add(t.last_w)
            if t.excl:
                for rd in t.readers:
                    if self.ops[rd]["eng"] != eng:
                        deps.add(rd)
        for t in w:
            if t.last_w is not None:
                deps.add(t.last_w)
            deps.update(t.readers)
        for t in r:
            t.readers.append(idx)
        for t in w:
            t.last_w = idx
            t.readers = []
        deps.discard(idx)
        self.ops.append({"eng": eng, "fn": fn, "deps": deps, "dma": dma})
        return idx

    def dma(self, out_t, out_ap, in_t, in_ap, q="sp", **kw):
        return self.op(q, lambda e: e.dma_start(out=out_ap, in_=in_ap, **kw),
                       r=[in_t], w=[out_t], dma=True)

    def emit(self):
        nc = self.nc
        ops = self.ops
        nops = len(ops)
        dma_use = {}
        for i, o in enumerate(ops):
            if o["dma"]:
                dma_use.setdefault(o["eng"], []).append(i)
        dma_sem = {}
        for q, lst in dma_use.items():
            for k, i in enumerate(lst):
                slot = k % self.n_dma_sems
                val = 16 * (k // self.n_dma_sems + 1)
                dma_sem[i] = (q, slot, val)
                if k >= self.n_dma_sems:
                    ops[i]["deps"].add(lst[k - self.n_dma_sems])
        waited = {e: {} for e in self.ENGS}
        waited_dma = {e: set() for e in self.ENGS}
        plan = [None] * nops
        signaling = set()
        for i, o in enumerate(ops):
            e = o["eng"]
            need = {}
            need_dma = []
            for d in sorted(o["deps"]):
                po = ops[d]
                if po["dma"]:
                    if d not in waited_dma[e]:
                        waited_dma[e].add(d)
                        need_dma.append(d)
                    continue
                p = po["eng"]
                if p == e and (e == "pe" or not self.same_engine_sync):
                    continue
                if waited[e].get(p, -1) >= d:
                    continue
                need[p] = max(need.get(p, -1), d)
            for p, d in need.items():
                waited[e][p] = d
                signaling.add(d)
            plan[i] = (need, need_dma)
        semval = {}
        cnt = {e: 0 for e in self.ENGS}
        for i, o in enumerate(ops):
            if o["dma"]:
                continue
            if i in signaling:
                cnt[o["eng"]] += 1
                semval[i] = cnt[o["eng"]]
        self.stats = {e: sum(1 for o in ops if o["eng"] == e) for e in self.ENGS}
        self.stats["signals"] = dict(cnt)
        es = self.es
        sems = {e: es.enter_context(nc.semaphore(f"sem_{e}")) for e in self.ENGS}
        dsems = {q: [es.enter_context(nc.semaphore(f"dsem_{q}{k}")) for k in range(self.n_dma_sems)]
                 for q in dma_use}
        block = es.enter_context(nc.Block())

        def body(ename):
            def f(eng):
                for i, o in enumerate(ops):
                    if o["eng"] != ename:
                        continue
                    need, need_dma = plan[i]
                    for p, d in need.items():
                        eng.wait_ge(sems[p], semval[d])
                    for d in need_dma:
                        q, slot, val = dma_sem[d]
                        eng.wait_ge(dsems[q][slot], val)
                    ins = o["fn"](eng)
                    if o["dma"]:
                        q, slot, val = dma_sem[i]
                        ins.then_inc(dsems[q][slot], 16)
                    elif i in signaling:
                        ins.then_inc(sems[ename], 1)
                for q, lst in dma_use.items():
                    if q != ename:
                        continue
                    last = {}
                    for i in lst:
                        _, slot, val = dma_sem[i]
                        last[slot] = val
                    for slot, val in last.items():
                        eng.wait_ge(dsems[q][slot], val)
            return f

        block.tensor(body("pe"))
        block.scalar(body("act"))
        block.vector(body("dve"))
        block.gpsimd(body("pool"))
        block.sync(body("sp"))

    def close(self):
        self.es.close()


def _consts():
    c = {}
    pos = np.arange(S, dtype=np.float32)

    def rope_tabs(p, rot, theta, nrep):
        half = rot // 2
        fr = (np.float32(theta) ** (-(np.arange(half, dtype=np.float32) * np.float32(2.0) / np.float32(rot)))).astype(np.float32)
        ang = p.astype(np.float32)[:, None] * fr[None, :]
        cs, sn = np.cos(ang).astype(np.float32), np.sin(ang).astype(np.float32)
        C = np.ones((len(p), 64), np.float32)
        C[:, 0:half] = cs
        C[:, half:rot] = cs
        Sg = np.concatenate([-sn, sn], 1)
        return np.tile(C, (1, nrep)), np.tile(Sg, (1, nrep))

    c["ropeCa"], c["ropeSa"] = rope_tabs(pos, 16, 500000.0, 8)
    pc = np.arange(256, dtype=np.float32) * 16 + 31
    c["ropeCc"], c["ropeSc"] = rope_tabs(pc, 16, 500000.0, 1)
    c["ropeCr"], c["ropeSr"] = rope_tabs(pos, 64, 10000.0, 8)
    lg = np.log(1.0 - 2.0 ** (-5.0 - np.arange(8, dtype=np.float64)))
    i = np.arange(128, dtype=np.float64)
    diff = i[None, :] - i[:, None]
    dT = np.zeros((128, 8, 128), np.float64)
    for h in range(8):
        dT[:, h, :] = np.where(diff >= 0, np.exp(np.maximum(diff, 0) * lg[h]), 0.0) * 0.125
    c["decayT"] = dT.astype(np.float32).reshape(128, 1024)
    zeta = np.exp((127.0 - i)[None, :] * lg[:, None])
    c["ZT"] = np.repeat(zeta.T[:, :, None], 64, axis=2).reshape(128, 512).astype(np.float32)
    xi = np.exp((i + 1.0)[None, :] * lg[:, None]) * 0.125
    c["XI"] = np.ascontiguousarray(xi.T).astype(np.float32)
    n = np.arange(256)
    cs_ = n * 16
    ss_ = np.arange(64) * 64
    ov = np.clip(np.minimum(cs_[:, None] + 32, ss_[None, :] + 64) - np.maximum(cs_[:, None], ss_[None, :]), 0, None)
    c2s = ov.astype(np.float32) / 32.0
    c2s[255] = 0.0
    c["C2S"] = c2s
    key = np.arange(S)
    c["EM"] = (key[None, :] // 64 == np.arange(64)[:, None]).astype(np.float32)
    sp = np.arange(128) - 64
    curp = (np.arange(128) >= 64).astype(np.int64)[:, None]
    forced = (sp[None, :] == curp) | (sp[None, :] == curp - 1)
    future = sp[None, :] > curp
    c["TK"] = (~(forced | future)).astype(np.float32)
    c["TA"] = np.where(forced, 1.0e4, np.where(future, -1.0e4, 0.0)).astype(np.float32)
    return c


CONST_SHAPES = {"ropeCa": [S, 512], "ropeSa": [S, 128], "ropeCc": [256, 64], "ropeSc": [256, 16],
                "ropeCr": [S, 512], "ropeSr": [S, 512], "decayT": [128, 1024], "ZT": [128, 512],
                "XI": [128, 8], "C2S": [256, 64], "EM": [64, S],
                "TK": [128, 128], "TA": [128, 128]}

IN_SHAPES = {"x": [S, D], "ffn1_norm": [1, D], "ffn1_w_gate": [D, DFF], "ffn1_w_up": [D, DFF],
             "ffn1_w_down": [DFF, D], "mix_norm": [1, D], "w_in": [D, DIN], "cmp_pos_emb": [32, 64],
             "cmp_k_w1": [2048, 256], "cmp_k_w2": [256, 64], "cmp_v_w1": [2048, 256], "cmp_v_w2": [256, 64],
             "ret_gn_gain": [1, 1024], "w_branch_nsa": [D, D], "w_branch_ret": [D, D], "w_out": [D, D],
             "ffn2_norm": [1, D], "ffn2_w_gate": [D, DFF], "ffn2_w_up": [D, DFF], "ffn2_w_down": [DFF, D],
             "final_norm": [1, D]}

G_CHUNK = [float(np.exp(128.0 * np.log(1.0 - 2.0 ** (-5.0 - h)))) for h in range(8)]
NEGBIG = -2.0e4


def build(phases=("ffn1", "proj", "cmp", "nsa", "ret", "merge", "ffn2"), dbg=(), lim=None):
    nc = bass.Bass("TRN2", target_bir_lowering=False)
    P = Prog(nc)
    lim_c = [0, 5, 17] if lim else None
    I = {k: P.dram(k, v, F32, kind="ExternalInput") for k, v in IN_SHAPES.items()}
    C = {k: P.dram(k, v, F32, kind="ExternalInput") for k, v in CONST_SHAPES.items()}
    out_d = P.dram("out", [S, D], F32, kind="ExternalOutput")

    def scratch(name, shape, dt):
        return P.dram(name, shape, dt, kind=("ExternalOutput" if name in dbg else "Internal"))

    def tiles(t, n=NT, rows=128):
        return [T(t.ap[i * rows:(i + 1) * rows], f"{t.name}{i}") for i in range(n)]

    def ctiles(t, n=NT, cols=128):
        return [T(t.ap[..., i * cols:(i + 1) * cols], f"{t.name}{i}") for i in range(n)]

    x_t = tiles(I["x"])
    x1_d = scratch("x1_d", [S, D], F32); x1_t = tiles(x1_d)
    x2_d = scratch("x2_d", [S, D], F32); x2_t = tiles(x2_d)
    hT_d = scratch("hT_d", [8, 128, 8, 512], BF16)
    hT_t = [T(hT_d.ap[i], f"hT{i}") for i in range(8)]
    hmT_d = scratch("hmT_d", [NT, 128, 8, 128], BF16)
    hmT_t = [T(hmT_d.ap[i], f"hmT{i}") for i in range(NT)]
    QT_d = scratch("QT_d", [2, 512, S], BF16)
    QT_t = [ctiles(T(QT_d.ap[g], f"QT{g}_")) for g in range(2)]
    kcT_d = scratch("kcT_d", [128, S], BF16); kcT_t = ctiles(kcT_d)
    vcT_d = scratch("vcT_d", [128, S], BF16); vcT_t = ctiles(vcT_d)
    ksT_d = scratch("ksT_d", [128, S], BF16); ksT_t = ctiles(ksT_d)
    kwT_d = scratch("kwT_d", [128, S], BF16); kwT_t = ctiles(kwT_d)
    vs_d = scratch("vs_d", [S, 128], BF16); vs_t = tiles(vs_d)
    vw_d = scratch("vw_d", [S, 128], BF16); vw_t = tiles(vw_d)
    ga_d = scratch("ga_d", [S, 48], F32); ga_t = tiles(ga_d)
    QrT_d = scratch("QrT_d", [512, S], BF16); QrT_t = ctiles(QrT_d)
    KrT_d = scratch("KrT_d", [512, S], BF16); KrT_t = ctiles(KrT_d)
    kz_d = scratch("kz_d", [S, 512], BF16); kz_t = tiles(kz_d)
    vr_d = scratch("vr_d", [S, 1024], BF16); vr_t = tiles(vr_d)
    sgr_d = scratch("sgr_d", [S, 1024], F32); sgr_t = tiles(sgr_d)
    gm_d = scratch("gm_d", [S, 2048], F32); gm_t = tiles(gm_d)
    kc_d = scratch("kc_d", [2, 256, 64], BF16)
    vc_d = scratch("vc_d", [2, 256, 64], BF16)
    aT_d = scratch("aT_d", [1024, S], BF16); aT_t = ctiles(aT_d)
    rT_d = scratch("rT_d", [1024, S], BF16); rT_t = ctiles(rT_d)

    WSLOT = 33792
    wslot = [P.sb([128, WSLOT], BF16, f"wslot{i}") for i in range(2)]
    ident = P.sb([128, 128], BF16, "ident")
    gain = P.sb([128, 1024], F32, "gain")
    gain2 = P.sb([128, 1024], F32, "gain2")
    xin = [P.sb([128, 1024], F32, f"xin{i}") for i in range(2)]
    xres = xin
    hb = [P.sb([128, 1024], BF16, f"hb{i}") for i in range(2)]
    st1 = [P.sb([128, 4], F32, f"st1_{i}") for i in range(2)]
    big = P.sb([128, 5632], BF16, "big")
    hT = P.sb([128, 8, 512], BF16, "hT")
    wk32 = [P.sb([128, 1024], F32, f"wk32_{i}") for i in range(3)]
    wk16 = [P.sb([128, 1024], BF16, f"wk16_{i}") for i in range(3)]
    sm = [P.sb([128, 64], F32, f"sm{i}") for i in range(8)]
    tCa = P.sb([128, 512], F32, "tCa"); tSa = P.sb([128, 128], F32, "tSa")
    tCr = P.sb([128, 512], F32, "tCr"); tSr = P.sb([128, 512], F32, "tSr")
    tZ = P.sb([128, 512], F32, "tZ")
    pb = [P.ps([128, 512], F32, f"pb{i}") for i in range(7)]
    pbf = P.ps([128, 1024], BF16, "pbf")

    P.op("pool", lambda e: e.memset(ident[:], 0.0), w=[ident])
    P.op("pool", lambda e: e.affine_select(out=ident[:], in_=ident[:], pattern=[[-1, 128]], compare_op=ALU.not_equal,
                                           fill=1.0, base=0, channel_multiplier=1), r=[ident], w=[ident])

    cnt = {"rr": 0}

    def rr(lst):
        cnt["rr"] += 1
        return lst[cnt["rr"] % len(lst)]

    def mm(out_t, out_ap, a_t, a_ap, b_t, b_ap, start, stop, sgc=False):
        P.op("pe", lambda e: e.matmul(out_ap, lhsT=a_ap, rhs=b_ap, start=start, stop=stop, skip_group_check=sgc),
             r=[a_t, b_t], w=[out_t])

    def tr(out_t, out_ap, in_t, in_ap):
        P.op("pe", lambda e: e.transpose(out=out_ap, in_=in_ap, identity=ident[:]), r=[in_t, ident], w=[out_t])

    def load_w(slot_t, off, src_t, src_ap, shape):
        n = int(np.prod(shape[1:]))
        dst = slot_t[:, off:off + n]
        if len(shape) == 3:
            dst = dst.rearrange("p (a b) -> p a b", a=shape[1])
        P.dma(slot_t, dst, src_t, src_ap, q="pool")
        return dst

    def rmsnorm(x_tile, gain_t, hb_t, stt):
        sq = wk32[2]
        P.op("act", lambda e: e.activation(out=sq[:], in_=x_tile[:], func=AF.Square, accum_out=stt[:, 0:1]),
             r=[x_tile], w=[sq, stt])
        P.op("dve", lambda e: e.tensor_scalar(out=stt[:, 1:2], in0=stt[:, 0:1], scalar1=1.0 / D, scalar2=1e-6,
                                              op0=ALU.mult, op1=ALU.add), r=[stt], w=[stt])
        P.op("act", lambda e: e.sqrt(out=stt[:, 2:3], in_=stt[:, 1:2]), r=[stt], w=[stt])
        P.op("dve", lambda e: e.reciprocal(out=stt[:, 3:4], in_=stt[:, 2:3]), r=[stt], w=[stt])
        P.op("dve", lambda e: e.scalar_tensor_tensor(out=hb_t[:], in0=x_tile[:], scalar=stt[:, 3:4], in1=gain_t[:],
                                                     op0=ALU.mult, op1=ALU.mult), r=[x_tile, stt, gain_t], w=[hb_t])

    def transpose8(src_t, dst_t, dst_ap, nblk=8, eng="act"):
        for k in range(nblk):
            tr(pbf, pbf[:, k * 128:(k + 1) * 128], src_t, src_t[:, k * 128:(k + 1) * 128])
        src = pbf[:, 0:nblk * 128].rearrange("p (k t) -> p k t", k=nblk)
        if eng == "act":
            P.op("act", lambda e: e.copy(out=dst_ap, in_=src), r=[pbf], w=[dst_t])
        else:
            P.op("dve", lambda e: e.tensor_copy(out=dst_ap, in_=src), r=[pbf], w=[dst_t])

    def load_gain(gt, src):
        P.dma(gt, gt[:], src, src.ap.partition_broadcast(128))

    def ffn_pass(which, hf, slot, xsrc_t, xdst_t, norm_gain, final=False):
        pre = f"ffn{which}_"
        wg_d, wu_d, wd_d = I[pre + "w_gate"], I[pre + "w_up"], I[pre + "w_down"]
        c0 = hf * 1408
        wg = load_w(slot, 0, wg_d, wg_d.ap[:, c0:c0 + 1408].rearrange("(k p) f -> p k f", p=128), [128, 8, 1408])
        wu = load_w(slot, 11264, wu_d, wu_d.ap[:, c0:c0 + 1408].rearrange("(k p) f -> p k f", p=128), [128, 8, 1408])
        wd = load_w(slot, 22528, wd_d, wd_d.ap[c0:c0 + 1408, :].rearrange("(k p) d -> p k d", p=128), [128, 11, 1024])
        yield
        aT = big[:, 0:11 * 512].rearrange("p (f t) -> p f t", f=11)
        if hf == 0:
            load_gain(gain, norm_gain)
        if final:
            load_gain(gain2, I["final_norm"])
        for st in range(8 if lim is None else 1):
            if hf == 0:
                for tt in range(4):
                    ti = st * 4 + tt
                    xt = xin[ti % 2]
                    P.dma(xt, xt[:], xsrc_t[ti], xsrc_t[ti][:, :])
                    hbt = hb[ti % 2]
                    rmsnorm(xt, gain, hbt, st1[ti % 2])
                    transpose8(hbt, hT, hT[:, :, tt * 128:(tt + 1) * 128], eng="act")
                P.dma(hT_t[st], hT_t[st][:], hT, hT[:], q="pool")
            else:
                P.dma(hT, hT[:], hT_t[st], hT_t[st][:])
            for f in range(11):
                pg = pb[(2 * f) % 4]
                pu = pb[(2 * f + 1) % 4]
                for k in range(8):
                    mm(pg, pg[:], slot, wg[:, k, f * 128:(f + 1) * 128], hT, hT[:, k, :], k == 0, k == 7)
                for k in range(8):
                    mm(pu, pu[:], slot, wu[:, k, f * 128:(f + 1) * 128], hT, hT[:, k, :], k == 0, k == 7)
                sg = wk32[f % 2]
                P.op("act", lambda e, sg=sg, pg=pg: e.activation(out=sg[:, 0:512], in_=pg[:], func=AF.Silu),
                     r=[pg], w=[sg])
                P.op("dve", lambda e, sg=sg, pu=pu, f=f: e.tensor_tensor(out=aT[:, f, :], in0=pu[:], in1=sg[:, 0:512],
                                                                        op=ALU.mult), r=[pu, sg], w=[big])
            for tt in range(4):
                ti = st * 4 + tt
                xr = xres[ti % 2]
                P.dma(xr, xr[:], xsrc_t[ti], xsrc_t[ti][:, :])
                for half in range(2):
                    py = pb[4 + (2 * tt + half) % 3]
                    for f in range(11):
                        mm(py, py[:], big, aT[:, f, tt * 128:(tt + 1) * 128], slot, wd[:, f, half * 512:(half + 1) * 512],
                           f == 0, f == 10)
                    P.op("dve", lambda e, xr=xr, py=py, half=half: e.scalar_tensor_tensor(
                        out=xr[:, half * 512:(half + 1) * 512], in0=py[:], scalar=0.5,
                        in1=xr[:, half * 512:(half + 1) * 512], op0=ALU.mult, op1=ALU.add), r=[py, xr], w=[xr])
                if final:
                    ob = wk32[ti % 2]
                    stt = st1[ti % 2]
                    sq = wk32[2]
                    P.op("act", lambda e, xr=xr, stt=stt: e.activation(out=sq[:], in_=xr[:], func=AF.Square,
                                                                       accum_out=stt[:, 0:1]), r=[xr], w=[sq, stt])
                    P.op("dve", lambda e, stt=stt: e.tensor_scalar(out=stt[:, 1:2], in0=stt[:, 0:1], scalar1=1.0 / D,
                                                                   scalar2=1e-6, op0=ALU.mult, op1=ALU.add),
                         r=[stt], w=[stt])
                    P.op("act", lambda e, stt=stt: e.sqrt(out=stt[:, 2:3], in_=stt[:, 1:2]), r=[stt], w=[stt])
                    P.op("dve", lambda e, stt=stt: e.reciprocal(out=stt[:, 3:4], in_=stt[:, 2:3]), r=[stt], w=[stt])
                    P.op("dve", lambda e, xr=xr, stt=stt, ob=ob: e.scalar_tensor_tensor(
                        out=ob[:], in0=xr[:], scalar=stt[:, 3:4], in1=gain2[:], op0=ALU.mult, op1=ALU.mult),
                         r=[xr, stt, gain2], w=[ob])
                    P.dma(xdst_t[ti], xdst_t[ti][:, :], ob, ob[:], q="pool")
                else:
                    P.dma(xdst_t[ti], xdst_t[ti][:, :], xr, xr[:], q="pool")

    def rope(src_t, src_ap, H, R, Ct, Cap, St, Sap, out_t, out_ap, o32, t2):
        n = H * 64
        half = R // 2
        P.op("dve", lambda e: e.tensor_tensor(out=o32[:, 0:n], in0=src_ap, in1=Cap, op=ALU.mult), r=[Ct, src_t], w=[o32])
        s3 = src_ap.rearrange("p (h d) -> p h d", h=H)
        S3 = Sap.rearrange("p (h r) -> p h r", h=H)
        t3 = t2[:, 0:H * R].rearrange("p (h r) -> p h r", h=H)
        o3 = o32[:, 0:n].rearrange("p (h d) -> p h d", h=H)
        P.op("dve", lambda e: e.tensor_tensor(out=t3[:, :, 0:half], in0=s3[:, :, half:R], in1=S3[:, :, 0:half],
                                              op=ALU.mult), r=[St, src_t], w=[t2])
        P.op("dve", lambda e: e.tensor_tensor(out=t3[:, :, half:R], in0=s3[:, :, 0:half], in1=S3[:, :, half:R],
                                              op=ALU.mult), r=[St, src_t], w=[t2])
        P.op("pool", lambda e: e.tensor_tensor(out=o3[:, :, 0:R], in0=o3[:, :, 0:R], in1=t3, op=ALU.add),
             r=[o32, t2], w=[o32])
        P.op("act", lambda e: e.copy(out=out_ap, in_=o32[:, 0:n]), r=[o32], w=[out_t])

    def proj_pass0(slot):
        w_d = I["w_in"]
        w0 = load_w(slot, 0, w_d, w_d.ap[:, 0:1944].rearrange("(k p) f -> p k f", p=128), [128, 8, 1944])
        w1 = load_w(slot, 8 * 1944, w_d, w_d.ap[:, 1944:3888].rearrange("(k p) f -> p k f", p=128), [128, 8, 1944])
        yield

        def wcol(k, c0, c1):
            if c1 <= 1944:
                return w0[:, k, c0:c1]
            assert c0 >= 1944
            return w1[:, k, c0 - 1944:c1 - 1944]

        load_gain(gain, I["mix_norm"])
        hmT = hT[:, :, 0:128]
        P.dma(tZ, tZ[:], C["ZT"], C["ZT"][:, :])
        qT = P.sb([128, 4, 128], BF16, "qT")
        blocks = [(0, 512), (512, 1024), (1024, 1536), (1536, 1840), (1840, 2352), (2352, 2864), (2864, 3376), (3376, 3888)]
        for ti in range(NT if lim is None else lim):
            xt = xin[ti % 2]
            P.dma(xt, xt[:], x1_t[ti], x1_t[ti][:, :])
            P.dma(tCa, tCa[:], C["ropeCa"], C["ropeCa"][ti * 128:(ti + 1) * 128, :])
            P.dma(tSa, tSa[:], C["ropeSa"], C["ropeSa"][ti * 128:(ti + 1) * 128, :])
            P.dma(tCr, tCr[:], C["ropeCr"], C["ropeCr"][ti * 128:(ti + 1) * 128, :])
            P.dma(tSr, tSr[:], C["ropeSr"], C["ropeSr"][ti * 128:(ti + 1) * 128, :])
            hbt = hb[ti % 2]
            rmsnorm(xt, gain, hbt, st1[ti % 2])
            transpose8(hbt, hT, hmT, eng="act")
            P.dma(hmT_t[ti], hmT_t[ti][:], hT, hmT, q="pool")
            for bi, (c0, c1) in enumerate(blocks):
                if str(bi) in _os.environ.get("SKIPB", ""):
                    continue
                pp = pb[bi % 7]
                subs = [(c0, c1)] if not (c0 < 1944 < c1) else [(c0, 1944), (1944, c1)]
                for (s0, s1) in subs:
                    for k in range(8):
                        mm(pp, pp[:, s0 - c0:s1 - c0], hT, hT[:, k, 0:128], slot, wcol(k, s0, s1), k == 0, k == 7)
                o32, t2 = wk32[0], wk32[1]
                if bi in (0, 1):
                    ob = wk16[bi % 2]
                    rope(pp, pp[:, 0:512], 8, 16, tCa, tCa[:, :], tSa, tSa[:, :], ob, ob[:, 0:512], o32, t2)
                    transpose8(ob, qT, qT[:], nblk=4, eng="dve")
                    dst = QT_t[bi][ti]
                    P.dma(dst, dst[:].rearrange("(p r) t -> r p t", r=128), qT, qT[:], q="pool")
                elif bi == 2:
                    ob = wk16[2]
                    if "x" not in _os.environ.get("SKIPF", ""):
                        P.op("act", lambda e, pp=pp, ob=ob: e.copy(out=ob[:, 0:256], in_=pp[:, 0:256]), r=[pp], w=[ob])
                    if "r" not in _os.environ.get("SKIPF", ""):
                        rope(pp, pp[:, 256:384], 2, 16, tCa, tCa[:, 0:128], tSa, tSa[:, 0:32], ob, ob[:, 256:384], o32, t2)
                    if "y" not in _os.environ.get("SKIPF", ""):
                        P.op("act", lambda e, pp=pp, ob=ob: e.copy(out=ob[:, 384:512], in_=pp[:, 384:512]), r=[pp], w=[ob])
                    SK = _os.environ.get("SKIPF", "")
                    if "t" not in SK:
                        transpose8(ob, qT, qT[:, 0:3, :], nblk=3, eng="dve")
                    if "a" not in SK:
                        P.dma(kcT_t[ti], kcT_t[ti][:], qT, qT[:, 0, :], q="pool")
                    if "b" not in SK:
                        P.dma(vcT_t[ti], vcT_t[ti][:], qT, qT[:, 1, :], q="pool")
                        P.dma(ksT_t[ti], ksT_t[ti][:], qT, qT[:, 2, :], q="pool")
                    if "c" not in SK:
                        P.dma(vs_t[ti], vs_t[ti][:, :], ob, ob[:, 384:512], q="pool")
                elif bi == 3:
                    ob = wk16[2]
                    rope(pp, pp[:, 0:128], 2, 16, tCa, tCa[:, 0:128], tSa, tSa[:, 0:32], ob, ob[:, 0:128], o32, t2)
                    P.op("act", lambda e, pp=pp, ob=ob: e.copy(out=ob[:, 128:256], in_=pp[:, 128:256]), r=[pp], w=[ob])
                    gsb = sm[0]
                    P.op("act", lambda e, pp=pp, gsb=gsb: e.activation(out=gsb[:, 0:48], in_=pp[:, 256:304], func=AF.Sigmoid),
                         r=[pp], w=[gsb])
                    P.dma(ga_t[ti], ga_t[ti][:, :], gsb, gsb[:, 0:48], q="pool")
                    transpose8(ob, qT, qT[:, 0:1, :], nblk=1, eng="dve")
                    P.dma(kwT_t[ti], kwT_t[ti][:], qT, qT[:, 0, :], q="pool")
                    P.dma(vw_t[ti], vw_t[ti][:, :], ob, ob[:, 128:256], q="pool")
                elif bi in (4, 5):
                    ob = wk16[bi % 2]
                    rope(pp, pp[:, 0:512], 8, 64, tCr, tCr[:, :], tSr, tSr[:, :], ob, ob[:, 0:512], o32, t2)
                    transpose8(ob, qT, qT[:], nblk=4, eng="dve")
                    dst = (QrT_t if bi == 4 else KrT_t)[ti]
                    P.dma(dst, dst[:].rearrange("(p r) t -> r p t", r=128), qT, qT[:], q="pool")
                    if bi == 5:
                        kz = wk16[2]
                        P.op("pool", lambda e, kz=kz, o32=o32: e.tensor_tensor(out=kz[:, 0:512], in0=o32[:, 0:512],
                                                                              in1=tZ[:], op=ALU.mult), r=[o32, tZ], w=[kz])
                        P.dma(kz_t[ti], kz_t[ti][:, :], kz, kz[:, 0:512], q="pool")
                else:
                    ob = wk16[bi % 2]
                    P.op("act", lambda e, pp=pp, ob=ob: e.copy(out=ob[:, 0:512], in_=pp[:]), r=[pp], w=[ob])
                    hh = bi - 6
                    P.dma(vr_t[ti], vr_t[ti][:, hh * 512:(hh + 1) * 512], ob, ob[:, 0:512], q="pool")

    def proj_pass1(slot):
        w_d = I["w_in"]
        w0 = load_w(slot, 0, w_d, w_d.ap[:, 3888:3888 + 1536].rearrange("(k p) f -> p k f", p=128), [128, 8, 1536])
        w1 = load_w(slot, 8 * 1536, w_d, w_d.ap[:, 3888 + 1536:DIN].rearrange("(k p) f -> p k f", p=128), [128, 8, 1536])
        yield
        for ti in range(NT if lim is None else lim):
            P.dma(hT, hT[:, :, 0:128], hmT_t[ti], hmT_t[ti][:])
            for bi in range(6):
                pp = pb[bi % 7]
                wv = w0 if bi < 3 else w1
                cc = (bi % 3) * 512
                for k in range(8):
                    mm(pp, pp[:], hT, hT[:, k, 0:128], slot, wv[:, k, cc:cc + 512], k == 0, k == 7)
                o32 = wk32[bi % 3]
                fn = AF.Silu if bi < 2 else AF.Sigmoid
                P.op("act", lambda e, pp=pp, o32=o32, fn=fn: e.activation(out=o32[:, 0:512], in_=pp[:], func=fn),
                     r=[pp], w=[o32])
                if bi < 2:
                    P.dma(sgr_t[ti], sgr_t[ti][:, bi * 512:(bi + 1) * 512], o32, o32[:, 0:512], q="pool")
                else:
                    P.dma(gm_t[ti], gm_t[ti][:, (bi - 2) * 512:(bi - 1) * 512], o32, o32[:, 0:512], q="pool")


    def carve(parent, ap, name=""):
        t = T(ap, name)
        t.last_w = parent.last_w
        t.readers = list(parent.readers)
        return t

    def uncarve(parent, subs):
        for t in subs:
            parent.readers.extend(t.readers)
            if t.last_w is not None:
                parent.readers.append(t.last_w)

    kcT2_d = scratch("kcT2_d", [2, 64, 256], BF16)

    def cmp_phase(slot):
        subs = []

        def cv(off, n, parts=128, name=""):
            t = carve(slot, slot.ap[0:parts, off:off + n], name)
            subs.append(t)
            return t
        w1 = [cv(0, 8192, 64, "w1k"), cv(8192, 8192, 64, "w1v")]
        w2 = [cv(16384, 128, 128, "w2k"), cv(16512, 128, 128, "w2v")]
        tokT = [cv(16640, 4096, 64, "tokT0"), cv(20736, 4096, 64, "tokT1")]
        GT = cv(24832, 512, 128, "GT")
        peb = cv(25344, 64, 32, "peb")
        peT = cv(25408, 32, 64, "peT")
        kcb = cv(25440, 64, 128, "kcb")
        kcTs = cv(25504, 128, 64, "kcTs")
        tCc = sm[1]; tSc = sm[2]; bias = sm[3]
        srcs = [(I["cmp_k_w1"], I["cmp_k_w2"], kcT_d, kc_d), (I["cmp_v_w1"], I["cmp_v_w2"], vcT_d, vc_d)]
        P.dma(peb, peb[:, :], I["cmp_pos_emb"], I["cmp_pos_emb"][:, :], q="pool")
        tr(pbf, pbf[0:64, 0:32], peb, peb[:, :]) if False else P.op(
            "pe", lambda e: e.transpose(out=pbf[0:64, 0:32], in_=peb[:, :], identity=ident[0:32, 0:32]),
            r=[peb, ident], w=[pbf])
        P.op("act", lambda e: e.copy(out=peT[:, :], in_=pbf[0:64, 0:32]), r=[pbf], w=[peT])
        P.op("pool", lambda e: e.memset(GT[:, :], 0.0), w=[GT])
        GT3 = GT[:, :].rearrange("p (m n) -> p m n", m=2)
        for kv in range(2):
            w1_d, w2_d, tT_d, o_d = srcs[kv]
            w1v = w1[kv][:, :].rearrange("p (l j) -> p l j", l=32)
            P.dma(w1[kv], w1v, w1_d, w1_d.ap.rearrange("(l d) j -> d l j", d=64), q="pool")
            w2v = w2[kv][:, :].rearrange("p (m o) -> p m o", m=2)
            P.dma(w2[kv], w2v, w2_d, w2_d.ap.rearrange("(m p) o -> p m o", p=128), q="pool")
            pbias = pb[6]
            for m in range(2):
                for l in range(32):
                    mm(pbias, pbias[:, m:m + 1], w1[kv], w1v[:, l, m * 128:(m + 1) * 128], peT, peT[:, l:l + 1], l == 0, l == 31)
            P.op("act", lambda e, pbias=pbias: e.copy(out=bias[:, 0:2], in_=pbias[:, 0:2]), r=[pbias], w=[bias])
            for g in range(2):
                tk = tokT[g]
                P.dma(tk, tk[:, :], tT_d, tT_d.ap[g * 64:(g + 1) * 64, :])
                tok3 = tk[:, :].rearrange("p (n r) -> p n r", r=16)
                for m in range(2):
                    ph = pb[m]
                    for l in range(32):
                        a, rr_ = l // 16, l % 16
                        mm(ph, ph[:, 0:255], w1[kv], w1v[:, l, m * 128:(m + 1) * 128], tk, tok3[:, a:a + 255, rr_], l == 0, l == 31)
                    hbx, x2, z = wk32[0], wk32[1], wk32[2]
                    P.op("act", lambda e, ph=ph, m=m: e.activation(out=hbx[:, 0:255], in_=ph[:, 0:255], func=AF.Identity,
                                                                   bias=bias[:, m:m + 1]), r=[ph, bias], w=[hbx])
                    P.op("dve", lambda e: e.tensor_tensor(out=x2[:, 0:255], in0=hbx[:, 0:255], in1=hbx[:, 0:255], op=ALU.mult),
                         r=[hbx], w=[x2])
                    P.op("dve", lambda e: e.tensor_scalar(out=x2[:, 0:255], in0=x2[:, 0:255], scalar1=0.044715, scalar2=1.0,
                                                          op0=ALU.mult, op1=ALU.add), r=[x2], w=[x2])
                    P.op("dve", lambda e: e.tensor_tensor(out=z[:, 0:255], in0=x2[:, 0:255], in1=hbx[:, 0:255], op=ALU.mult),
                         r=[x2, hbx], w=[z])
                    P.op("act", lambda e: e.activation(out=z[:, 0:255], in_=z[:, 0:255], func=AF.Sigmoid, scale=1.5957691216),
                         r=[z], w=[z])
                    P.op("dve", lambda e, m=m: e.tensor_tensor(out=GT3[:, m, 0:255], in0=z[:, 0:255], in1=hbx[:, 0:255],
                                                               op=ALU.mult), r=[z, hbx], w=[GT])
                for j in range(2):
                    po = pb[2 + j]
                    for m in range(2):
                        mm(po, po[:, 0:64], GT, GT3[:, m, j * 128:(j + 1) * 128], w2[kv], w2v[:, m, :], m == 0, m == 1)
                    if kv == 0:
                        P.dma(tCc, tCc[:, 0:64], C["ropeCc"], C["ropeCc"][j * 128:(j + 1) * 128, :])
                        P.dma(tSc, tSc[:, 0:16], C["ropeSc"], C["ropeSc"][j * 128:(j + 1) * 128, :])
                        rope(po, po[:, 0:64], 1, 16, tCc, tCc[:, 0:64], tSc, tSc[:, 0:16], kcb, kcb[:, :], wk32[0], wk32[1])
                        P.op("pe", lambda e: e.transpose(out=pbf[0:64, 0:128], in_=kcb[:, :], identity=ident[:]),
                             r=[kcb, ident], w=[pbf])
                        P.op("act", lambda e: e.copy(out=kcTs[:, :], in_=pbf[0:64, 0:128]), r=[pbf], w=[kcTs])
                        P.dma(kcT2_d, kcT2_d.ap[g, :, j * 128:(j + 1) * 128], kcTs, kcTs[:, :], q="pool")
                    else:
                        P.op("act", lambda e, po=po: e.copy(out=kcb[:, :], in_=po[:, 0:64]), r=[po], w=[kcb])
                    P.dma(o_d, o_d.ap[g, j * 128:(j + 1) * 128, :], kcb, kcb[:, :], q="pool")
        uncarve(slot, subs)

    def nsa_phase(slot):
        subs = []

        def cv(off, n, parts=128, name=""):
            t = carve(slot, slot.ap[0:parts, off:off + n], name)
            subs.append(t)
            return t
        KsT = cv(0, 4096, 64, "KsT"); KwT = cv(4096, 4096, 64, "KwT")
        Vs = cv(8192, 2080, 128, "Vs"); Vw = cv(10272, 2080, 128, "Vw")
        KcT = cv(12352, 256, 64, "KcT"); Vc = cv(12608, 130, 128, "Vc")
        C2S = cv(12738, 128, 128, "C2S"); EM = cv(12866, 4096, 64, "EM")
        QTc = [cv(16962 + i * 1024, 1024, 64, f"QTc{i}") for i in range(2)]
        Et = [cv(19010 + i * 1024, 1024, 128, f"Et{i}") for i in range(3)]
        negm = cv(22082, 1024, 64, "negm")
        selb = cv(23106, 64, 128, "selb")
        ab = cv(23170, 512, 128, "ab")
        aTs = cv(23682, 512, 128, "aTs")
        TK = tCa; TA = tCr
        P.dma(TK, TK[:, 0:128], C["TK"], C["TK"][:, :])
        P.dma(TA, TA[:, 0:128], C["TA"], C["TA"][:, :])
        P.dma(C2S, C2S[:, :].rearrange("p (j s) -> p j s", j=2), C["C2S"], C["C2S"].ap.rearrange("(j p) s -> p j s", p=128), q="pool")
        P.dma(EM, EM[:, 0:2048], C["EM"], C["EM"][:, 0:2048], q="pool")
        P.dma(EM, EM[:, 2048:4096], C["EM"], C["EM"][:, 2048:4096], q="pool")
        Vs3 = Vs[:, :].rearrange("p (j d) -> p j d", d=65)
        Vw3 = Vw[:, :].rearrange("p (j d) -> p j d", d=65)
        Vc3 = Vc[:, :].rearrange("p (j d) -> p j d", d=65)
        C2S3 = C2S[:, :].rearrange("p (j s) -> p j s", j=2)
        SB = [(pb[0], pb[1]), (pb[2], pb[3])]
        OA, OB, OC = pb[4], pb[5], pb[6]
        OA3 = OA[:, 0:260].rearrange("p (h d) -> p h d", d=65)
        OB3 = OB[:, 0:260].rearrange("p (h d) -> p h d", d=65)
        acc = xin[0]; tmpo = xin[1]
        acc3 = acc[:, 0:512].rearrange("p (h d) -> p h d", d=64)
        tmp3 = tmpo[:, 0:512].rearrange("p (h d) -> p h d", d=64)
        den = sm[2]; coef = sm[3]; imp = sm[4]; imp2 = sm[5]; m8 = sm[6]; imp3 = sm[7]
        sidx = {"s": 0, "e": 0}

        def scores(lhs_t, lhs_ap, qt, mask_j=None):
            b = SB[sidx["s"] % 2]; sidx["s"] += 1
            q2 = qt[:, :]
            for hh in range(2):
                mm(b[hh], b[hh][:], lhs_t, lhs_ap, qt, q2[:, hh * 512:(hh + 1) * 512], True, mask_j is None)
            if mask_j is not None:
                for hh in range(2):
                    mm(b[hh], b[hh][:], EM, EM[:, mask_j * 128:(mask_j + 1) * 128], negm, negm[:, hh * 512:(hh + 1) * 512], False, True)
            et = Et[sidx["e"] % 3]; sidx["e"] += 1
            for hh in range(2):
                P.op("act", lambda e, hh=hh, et=et, b=b: e.activation(out=et[:, hh * 512:(hh + 1) * 512], in_=b[hh][:],
                                                                      func=AF.Exp, scale=0.125), r=[b[hh]], w=[et])
            return et

        def amask(et, cm, qstep, base):
            P.op("pool", lambda e: e.affine_select(out=et[:, :], in_=et[:, :], pattern=[[0, 8], [qstep, 128]],
                                                   compare_op=ALU.is_ge, fill=0.0, base=base, channel_multiplier=cm),
                 r=[et], w=[et])

        def pv(et, v_t, v_ap, first, last):
            for h in range(8):
                o_t = OA if h < 4 else OB
                o3 = OA3 if h < 4 else OB3
                mm(o_t, o3[:, h % 4, :], et, et[:, h * 128:(h + 1) * 128], v_t, v_ap, first and (h % 4 == 0), last, sgc=True)

        pend = []

        def flush():
            while pend:
                work, post = pend.pop(0)
                work()
                if post is not None:
                    post()

        def finish_branch(bidx, first, gat):
            P.op("dve", lambda e: e.tensor_scalar(out=den[:, 0:4], in0=OA3[:, :, 64], scalar1=1e-20, scalar2=None, op0=ALU.max),
                 r=[OA], w=[den])
            P.op("dve", lambda e: e.tensor_scalar(out=den[:, 4:8], in0=OB3[:, :, 64], scalar1=1e-20, scalar2=None, op0=ALU.max),
                 r=[OB], w=[den])
            P.op("dve", lambda e: e.reciprocal(out=den[:, 8:16], in_=den[:, 0:8]), r=[den], w=[den])
            g3 = gat[:, 0:24].rearrange("p (h b) -> p h b", b=3)
            P.op("dve", lambda e: e.tensor_tensor(out=coef[:, 0:8], in0=den[:, 8:16], in1=g3[:, :, bidx], op=ALU.mult),
                 r=[den, gat], w=[coef])
            dst, d3 = (acc, acc3) if first else (tmpo, tmp3)
            P.op("dve", lambda e: e.tensor_tensor(out=d3[:, 0:4, :], in0=OA3[:, :, 0:64],
                                                  in1=coef[:, 0:4].unsqueeze(2).broadcast_to([128, 4, 64]), op=ALU.mult),
                 r=[OA, coef], w=[dst])
            P.op("dve", lambda e: e.tensor_tensor(out=d3[:, 4:8, :], in0=OB3[:, :, 0:64],
                                                  in1=coef[:, 4:8].unsqueeze(2).broadcast_to([128, 4, 64]), op=ALU.mult),
                 r=[OB, coef], w=[dst])
            if not first:
                P.op("pool", lambda e: e.tensor_tensor(out=acc[:, 0:512], in0=acc[:, 0:512], in1=tmpo[:, 0:512], op=ALU.add),
                     r=[acc, tmpo], w=[acc])

        for g in range(2):
            P.dma(KsT, KsT[:, :], ksT_d, ksT_d.ap[g * 64:(g + 1) * 64, :])
            P.dma(KwT, KwT[:, :], kwT_d, kwT_d.ap[g * 64:(g + 1) * 64, :])
            P.dma(Vs, Vs3[:, :, 0:64], vs_d, vs_d.ap[:, g * 64:(g + 1) * 64].rearrange("(j p) d -> p j d", p=128))
            P.dma(Vw, Vw3[:, :, 0:64], vw_d, vw_d.ap[:, g * 64:(g + 1) * 64].rearrange("(j p) d -> p j d", p=128))
            P.op("pool", lambda e: e.memset(Vs3[:, :, 64:65], 1.0), w=[Vs])
            P.op("pool", lambda e: e.memset(Vw3[:, :, 64:65], 1.0), w=[Vw])
            P.dma(KcT, KcT[:, :], kcT2_d, kcT2_d.ap[g])
            P.dma(Vc, Vc3[:, :, 0:64], vc_d, vc_d.ap[g].rearrange("(j p) d -> p j d", p=128))
            P.op("pool", lambda e: e.memset(Vc3[:, :, 64:65], 1.0), w=[Vc])
            gats = [sm[1], sm[0]]
            for c in (range(NT) if lim is None else lim_c):
                qt = QTc[c % 2]
                gat = gats[c % 2]
                P.dma(qt, qt[:, :].rearrange("d (h t) -> d h t", h=8), QT_t[g][c],
                      QT_t[g][c][:].rearrange("(h d) t -> d h t", d=64))
                P.dma(gat, gat[:, 0:24], ga_t[c], ga_t[c][:, g * 24:(g + 1) * 24])

                def post_cmp(c=c, gat=gat):
                    finish_branch(0, True, gat)
                    for h in range(8):
                        if h == 0:
                            P.op("dve", lambda e: e.tensor_scalar(out=imp[:, 0:64], in0=OC[:, 0:64], scalar1=den[:, 8:9],
                                                                  scalar2=None, op0=ALU.mult), r=[OC, den], w=[imp])
                        else:
                            P.op("dve", lambda e, h=h: e.scalar_tensor_tensor(
                                out=imp[:, 0:64], in0=OC[:, h * 64:(h + 1) * 64], scalar=den[:, 8 + h:9 + h], in1=imp[:, 0:64],
                                op0=ALU.mult, op1=ALU.add), r=[OC, den, imp], w=[imp])
                    off = 64 - 2 * c
                    P.op("dve", lambda e: e.tensor_tensor(out=imp2[:, 0:64], in0=imp[:, 0:64], in1=TK[:, off:off + 64], op=ALU.mult),
                         r=[imp, TK], w=[imp2])
                    P.op("dve", lambda e: e.tensor_tensor(out=imp2[:, 0:64], in0=imp2[:, 0:64], in1=TA[:, off:off + 64], op=ALU.add),
                         r=[imp2, TA], w=[imp2])
                    P.op("dve", lambda e: e.memset(imp2[:, 0:1], 1.0e4), r=[imp2], w=[imp2])
                    P.op("dve", lambda e: e.max(out=m8[:, 0:8], in_=imp2[:, 0:64]), r=[imp2], w=[m8])
                    P.op("dve", lambda e: e.match_replace(out=imp3[:, 0:64], in_to_replace=m8[:, 0:8], in_values=imp2[:, 0:64],
                                                          imm_value=-3.0e4), r=[imp2, m8], w=[imp3])
                    P.op("dve", lambda e: e.max(out=m8[:, 8:16], in_=imp3[:, 0:64]), r=[imp3, m8], w=[m8])
                    P.op("dve", lambda e: e.tensor_scalar(out=selb[:, :], in0=imp2[:, 0:64], scalar1=m8[:, 15:16], scalar2=None,
                                                          op0=ALU.is_ge), r=[imp2, m8], w=[selb])
                    P.op("pe", lambda e: e.transpose(out=pbf[0:64, 0:128], in_=selb[:, :], identity=ident[:]),
                         r=[selb, ident], w=[pbf])
                    P.op("dve", lambda e: e.tensor_scalar(out=negm[:, :].rearrange("p (h q) -> p h q", h=8),
                                                          in0=pbf[0:64, 0:128].unsqueeze(1).broadcast_to([64, 8, 128]),
                                                          scalar1=-NEGBIG, scalar2=NEGBIG, op0=ALU.mult, op1=ALU.add),
                         r=[pbf], w=[negm])

                def post_win(gat=gat):
                    finish_branch(2, False, gat)

                def post_sel(c=c, gat=gat):
                    finish_branch(1, False, gat)
                    P.op("act", lambda e: e.copy(out=ab[:, :], in_=acc[:, 0:512]), r=[acc], w=[ab])
                    transpose8(ab, aTs, aTs[:, :].rearrange("p (k t) -> p k t", k=4), nblk=4, eng="act")
                    dst = aT_t[c]
                    P.dma(dst, dst[g * 512:(g + 1) * 512, :].rearrange("(k r) t -> r k t", r=128), aTs,
                          aTs[:, :].rearrange("p (k t) -> p k t", k=4), q="pool")

                njt = 1 if c <= 15 else 2
                for j in range(njt):
                    et = scores(KcT, KcT[:, j * 128:(j + 1) * 128], qt)
                    amask(et, -16, 1, -16 * (128 * j - 8 * c) - 31)
                    flush()

                    def work(et=et, j=j, njt=njt):
                        pv(et, Vc, Vc3[:, j, :], j == 0, j == njt - 1)
                        for h in range(8):
                            mm(OC, OC[:, h * 64:(h + 1) * 64], et, et[:, h * 128:(h + 1) * 128], C2S, C2S3[:, j, :],
                               j == 0 and h == 0, j == njt - 1, sgc=True)
                    pend.append((work, post_cmp if j == njt - 1 else None))
                j0 = max(0, c - 4)
                for j in range(j0, c + 1):
                    et = scores(KwT, KwT[:, j * 128:(j + 1) * 128], qt)
                    if j == c:
                        amask(et, -1, 1, 0)
                    if j == c - 4:
                        amask(et, 1, -1, -1)
                    flush()
                    pend.append((lambda et=et, j=j, j0=j0, c=c: pv(et, Vw, Vw3[:, j, :], j == j0, j == c),
                                 post_win if j == c else None))
                for j in range(c + 1):
                    if j == 0:
                        flush()
                    et = scores(KsT, KsT[:, j * 128:(j + 1) * 128], qt, mask_j=j)
                    if j == c:
                        amask(et, -1, 1, 0)
                    flush()
                    pend.append((lambda et=et, j=j, c=c: pv(et, Vs, Vs3[:, j, :], j == 0, j == c),
                                 post_sel if j == c else None))
            flush()
        uncarve(slot, subs)

    def ret_phase():
        subs = []
        bufs = []
        for i in range(2):
            q_ = carve(big, big.ap[0:64, i * 2560:i * 2560 + 1024], f"QrTc{i}")
            k_ = carve(big, big.ap[0:64, i * 2560 + 1024:i * 2560 + 2048], f"KrTc{i}")
            z_ = carve(big, big.ap[:, i * 2560 + 2048:i * 2560 + 2560], f"kzc{i}")
            subs += [q_, k_, z_]
            bufs.append((q_, k_, z_))
        dec = [tCr, tSr]
        P.dma(dec[0], dec[0][:, :], C["decayT"], C["decayT"][:, 0:512])
        P.dma(dec[1], dec[1][:, :], C["decayT"], C["decayT"][:, 512:1024])
        xit = sm[1]
        P.dma(xit, xit[:, 0:8], C["XI"], C["XI"][:, :])
        load_gain(gain2, I["ret_gn_gain"])
        Rst = xin[0]; osb = xin[1]; sgr = wk32[0]; sq = wk32[1]; tmp = wk32[2]
        Rbf = wk16[0]; ATs = wk16[1]; rob = wk16[2]
        stt = sm[2]
        P.op("pool", lambda e: e.memset(Rst[0:64, :], 0.0), w=[Rst])
        P.op("pool", lambda e: e.memset(Rbf[0:64, :], 0.0), w=[Rbf])
        osb3 = osb[:, :].rearrange("p (h e) -> p h e", h=8)
        sq3 = sq[:, :].rearrange("p (h e) -> p h e", h=8)
        tmp3 = tmp[:, :].rearrange("p (h e) -> p h e", h=8)
        for n in range(NT if lim is None else lim):
            q_, k_, z_ = bufs[n % 2]
            vt = hb[n % 2]
            P.dma(q_, q_[:, :].rearrange("d (h t) -> d h t", h=8), QrT_t[n], QrT_t[n][:].rearrange("(h d) t -> d h t", d=64))
            P.dma(k_, k_[:, :].rearrange("d (h t) -> d h t", h=8), KrT_t[n], KrT_t[n][:].rearrange("(h d) t -> d h t", d=64))
            P.dma(z_, z_[:, :], kz_t[n], kz_t[n][:, :])
            P.dma(vt, vt[:, :], vr_t[n], vr_t[n][:, :])
            P.dma(sgr, sgr[:, :], sgr_t[n], sgr_t[n][:, :])
            for h in range(8):
                b = pb[h // 4]
                mm(b, b[:, (h % 4) * 128:(h % 4 + 1) * 128], k_, k_[:, h * 128:(h + 1) * 128], q_, q_[:, h * 128:(h + 1) * 128], True, True)
            for hh in range(2):
                P.op("dve", lambda e, hh=hh: e.tensor_tensor(out=ATs[:, hh * 512:(hh + 1) * 512], in0=pb[hh][:], in1=dec[hh][:, :],
                                                             op=ALU.mult), r=[pb[hh], dec[hh]], w=[ATs])
            for h in range(8):
                b = pb[2 + h // 4]
                mm(b, b[:, (h % 4) * 128:(h % 4 + 1) * 128], ATs, ATs[:, h * 128:(h + 1) * 128], vt, vt[:, h * 128:(h + 1) * 128], True, True)
            if n > 0:
                for h in range(8):
                    b = pb[4 + h // 4]
                    mm(b, b[:, (h % 4) * 128:(h % 4 + 1) * 128], q_, q_[:, h * 128:(h + 1) * 128], Rbf, Rbf[0:64, h * 128:(h + 1) * 128], True, True)
            for hh in range(2):
                P.op("act", lambda e, hh=hh: e.copy(out=osb[:, hh * 512:(hh + 1) * 512], in_=pb[2 + hh][:]), r=[pb[2 + hh]], w=[osb])
            if n > 0:
                for hh in range(2):
                    P.op("dve", lambda e, hh=hh: e.tensor_tensor(
                        out=tmp3[:, hh * 4:(hh + 1) * 4, :], in0=pb[4 + hh][:].rearrange("p (h e) -> p h e", h=4),
                        in1=xit[:, hh * 4:(hh + 1) * 4].unsqueeze(2).broadcast_to([128, 4, 128]), op=ALU.mult),
                        r=[pb[4 + hh], xit], w=[tmp])
                P.op("pool", lambda e: e.tensor_tensor(out=osb[:, :], in0=osb[:, :], in1=tmp[:, :], op=ALU.add), r=[osb, tmp], w=[osb])
            for hh in range(2):
                b = pb[6]
                for h4 in range(4):
                    h = hh * 4 + h4
                    mm(b, b[0:64, h4 * 128:(h4 + 1) * 128], z_, z_[:, h * 64:(h + 1) * 64], vt, vt[:, h * 128:(h + 1) * 128], True, True)
                for h4 in range(4):
                    h = hh * 4 + h4
                    P.op("dve", lambda e, h=h, h4=h4, b=b: e.scalar_tensor_tensor(
                        out=Rst[0:64, h * 128:(h + 1) * 128], in0=Rst[0:64, h * 128:(h + 1) * 128], scalar=G_CHUNK[h],
                        in1=b[0:64, h4 * 128:(h4 + 1) * 128], op0=ALU.mult, op1=ALU.add), r=[b, Rst], w=[Rst])
            P.op("act", lambda e: e.copy(out=Rbf[0:64, :], in_=Rst[0:64, :]), r=[Rst], w=[Rbf])
            P.op("dve", lambda e: e.tensor_reduce(out=stt[:, 0:8], in_=osb3, axis=AX.X, op=ALU.add), r=[osb], w=[stt])
            P.op("act", lambda e: e.activation(out=sq[:, :], in_=osb[:, :], func=AF.Square), r=[osb], w=[sq])
            P.op("dve", lambda e: e.tensor_reduce(out=stt[:, 8:16], in_=sq3, axis=AX.X, op=ALU.add), r=[sq], w=[stt])
            P.op("dve", lambda e: e.tensor_scalar(out=stt[:, 0:8], in0=stt[:, 0:8], scalar1=1.0 / 128, scalar2=None, op0=ALU.mult),
                 r=[stt], w=[stt])
            P.op("dve", lambda e: e.tensor_tensor(out=stt[:, 16:24], in0=stt[:, 0:8], in1=stt[:, 0:8], op=ALU.mult), r=[stt], w=[stt])
            P.op("dve", lambda e: e.scalar_tensor_tensor(out=stt[:, 24:32], in0=stt[:, 8:16], scalar=1.0 / 128, in1=stt[:, 16:24],
                                                         op0=ALU.mult, op1=ALU.subtract), r=[stt], w=[stt])
            P.op("dve", lambda e: e.tensor_scalar(out=stt[:, 24:32], in0=stt[:, 24:32], scalar1=1e-5, scalar2=None, op0=ALU.add),
                 r=[stt], w=[stt])
            P.op("act", lambda e: e.sqrt(out=stt[:, 32:40], in_=stt[:, 24:32]), r=[stt], w=[stt])
            P.op("dve", lambda e: e.reciprocal(out=stt[:, 40:48], in_=stt[:, 32:40]), r=[stt], w=[stt])
            P.op("dve", lambda e: e.tensor_tensor(out=osb3, in0=osb3, in1=stt[:, 0:8].unsqueeze(2).broadcast_to([128, 8, 128]),
                                                  op=ALU.subtract), r=[osb, stt], w=[osb])
            P.op("pool", lambda e: e.tensor_tensor(out=osb3, in0=osb3, in1=stt[:, 40:48].unsqueeze(2).broadcast_to([128, 8, 128]),
                                                   op=ALU.mult), r=[osb, stt], w=[osb])
            P.op("dve", lambda e: e.tensor_tensor(out=osb[:, :], in0=osb[:, :], in1=gain2[:, :], op=ALU.mult), r=[osb, gain2], w=[osb])
            P.op("pool", lambda e: e.tensor_tensor(out=rob[:, :], in0=osb[:, :], in1=sgr[:, :], op=ALU.mult), r=[osb, sgr], w=[rob])
            transpose8(rob, hT, hT[:, :, 0:128], eng="act")
            dst = rT_t[n]
            P.dma(dst, dst[:].rearrange("(k r) t -> r k t", r=128), hT, hT[:, :, 0:128], q="pool")
        uncarve(big, subs)

    def merge_phase(slot):
        Wn = load_w(slot, 0, I["w_branch_nsa"], I["w_branch_nsa"].ap.rearrange("(k p) f -> p k f", p=128), [128, 8, 1024])
        Wr = load_w(slot, 8192, I["w_branch_ret"], I["w_branch_ret"].ap.rearrange("(k p) f -> p k f", p=128), [128, 8, 1024])
        Wo = load_w(slot, 16384, I["w_out"], I["w_out"].ap.rearrange("(k p) f -> p k f", p=128), [128, 8, 1024])
        yield
        subs = []
        bufs = []
        for i in range(2):
            a_ = carve(big, big.ap[:, i * 2048:i * 2048 + 1024], f"aTt{i}")
            r_ = carve(big, big.ap[:, i * 2048 + 1024:i * 2048 + 2048], f"rTt{i}")
            subs += [a_, r_]
            bufs.append((a_, r_))
        for ti in range(NT if lim is None else lim):
            a_, r_ = bufs[ti % 2]
            a3 = a_[:, :].rearrange("p (k t) -> p k t", k=8)
            r3 = r_[:, :].rearrange("p (k t) -> p k t", k=8)
            P.dma(a_, a3, aT_t[ti], aT_t[ti][:].rearrange("(k r) t -> r k t", r=128))
            P.dma(r_, r3, rT_t[ti], rT_t[ti][:].rearrange("(k r) t -> r k t", r=128))
            ga, gr = wk32[0], wk32[1]
            P.dma(ga, ga[:, :], gm_t[ti], gm_t[ti][:, 0:1024])
            P.dma(gr, gr[:, :], gm_t[ti], gm_t[ti][:, 1024:2048])
            x1s = wk32[2]
            P.dma(x1s, x1s[:, :], x1_t[ti], x1_t[ti][:, :])
            for half in range(2):
                for k in range(8):
                    mm(pb[half], pb[half][:], a_, a3[:, k, :], slot, Wn[:, k, half * 512:(half + 1) * 512], k == 0, k == 7)
                for k in range(8):
                    mm(pb[2 + half], pb[2 + half][:], r_, r3[:, k, :], slot, Wr[:, k, half * 512:(half + 1) * 512], k == 0, k == 7)
            m1, m2 = xin[0], xin[1]
            mb = hb[ti % 2]
            for half in range(2):
                cs = slice(half * 512, (half + 1) * 512)
                P.op("dve", lambda e, half=half, cs=cs: e.tensor_tensor(out=m1[:, cs], in0=pb[half][:], in1=ga[:, cs], op=ALU.mult),
                     r=[pb[half], ga], w=[m1])
                P.op("dve", lambda e, half=half, cs=cs: e.tensor_tensor(out=m2[:, cs], in0=pb[2 + half][:], in1=gr[:, cs], op=ALU.mult),
                     r=[pb[2 + half], gr], w=[m2])
            P.op("pool", lambda e, mb=mb: e.tensor_tensor(out=mb[:, :], in0=m1[:, :], in1=m2[:, :], op=ALU.add), r=[m1, m2], w=[mb])
            transpose8(mb, hT, hT[:, :, 0:128], eng="act")
            for half in range(2):
                po = pb[4 + half]
                for k in range(8):
                    mm(po, po[:], hT, hT[:, k, 0:128], slot, Wo[:, k, half * 512:(half + 1) * 512], k == 0, k == 7)
                P.op("dve", lambda e, po=po, half=half: e.tensor_tensor(out=x1s[:, half * 512:(half + 1) * 512], in0=po[:],
                                                                       in1=x1s[:, half * 512:(half + 1) * 512], op=ALU.add),
                     r=[po, x1s], w=[x1s])
            P.dma(x2_t[ti], x2_t[ti][:, :], x1s, x1s[:, :], q="pool")
        uncarve(big, subs)

    def begin(gen):
        next(gen)
        return gen

    def finish(gen):
        for _ in gen:
            pass

    allp = set(phases) == {"ffn1", "proj", "cmp", "nsa", "ret", "merge", "ffn2"} and not _os.environ.get("SKIPP1")
    out_t = [T(out_d.ap[i * 128:(i + 1) * 128], f"out{i}") for i in range(NT)]
    if allp:
        f1a = begin(ffn_pass(1, 0, wslot[0], x_t, x1_t, I["ffn1_norm"]))
        f1b = begin(ffn_pass(1, 1, wslot[1], x1_t, x1_t, I["ffn1_norm"]))
        finish(f1a)
        p0 = begin(proj_pass0(wslot[0]))
        finish(f1b)
        p1 = begin(proj_pass1(wslot[1]))
        finish(p0)
        finish(p1)
        cmp_phase(wslot[1])
        mg = begin(merge_phase(wslot[1]))
        nsa_phase(wslot[0])
        f2a = begin(ffn_pass(2, 0, wslot[0], x2_t, x2_t, I["ffn2_norm"]))
        ret_phase()
        finish(mg)
        f2b = begin(ffn_pass(2, 1, wslot[1], x2_t, out_t, I["ffn2_norm"], final=True))
        finish(f2a)
        finish(f2b)
    else:
        if "ffn1" in phases:
            finish(ffn_pass(1, 0, wslot[0], x_t, x1_t, I["ffn1_norm"]))
            finish(ffn_pass(1, 1, wslot[1], x1_t, x1_t, I["ffn1_norm"]))
        if "proj" in phases:
            finish(proj_pass0(wslot[0]))
            if not _os.environ.get("SKIPP1"):
                finish(proj_pass1(wslot[1]))
        if "cmp" in phases:
            cmp_phase(wslot[1])
        if "nsa" in phases:
            nsa_phase(wslot[0])
        if "ret" in phases:
            ret_phase()
        if "merge" in phases:
            finish(merge_phase(wslot[1]))
        if "ffn2" in phases:
            finish(ffn_pass(2, 0, wslot[0], x2_t, x2_t, I["ffn2_norm"]))
            finish(ffn_pass(2, 1, wslot[1], x2_t, out_t, I["ffn2_norm"], final=True))
    P.emit()
    return nc, P


_CACHE = {}


def kernel(**inputs):
    n_cores = 8
    if "nc" not in _CACHE:
        _CACHE["nc"] = build()[0]
        _CACHE["consts"] = {k: np.ascontiguousarray(v.reshape(CONST_SHAPES[k]).astype(np.float32))
                            for k, v in _consts().items()}
    nc = _CACHE["nc"]
    cst = _CACHE["consts"]
    shared = {}
    for k, shp in IN_SHAPES.items():
        if k == "x":
            continue
        shared[k] = np.ascontiguousarray(np.asarray(inputs[k], dtype=np.float32).reshape(shp))
    x = np.asarray(inputs["x"], dtype=np.float32)
    in_maps = []
    for c in range(n_cores):
        m = dict(shared)
        m.update(cst)
        m["x"] = np.ascontiguousarray(x[c % 4])
        in_maps.append(m)
    res = run_bass_kernel_spmd(nc, in_maps, core_ids=list(range(n_cores)))
    out = np.stack([np.asarray(res.results[b]["out"], dtype=np.float32) for b in range(4)], axis=0)
    return out
```

```python
import contextlib
import os as _os
import numpy as np
import concourse.bass as bass
import concourse.mybir as mybir
from concourse.bass_utils import run_bass_kernel_spmd

F32 = mybir.dt.float32
BF16 = mybir.dt.bfloat16
AF = mybir.ActivationFunctionType
ALU = mybir.AluOpType
AX = mybir.AxisListType

S = 4096
D = 1024
DFF = 2816
NT = S // 128
DIN = 6960


class T:
    __slots__ = ("ap", "name", "last_w", "readers", "excl")

    def __init__(self, ap, name="", excl=False):
        self.ap = ap
        self.name = name
        self.last_w = None
        self.readers = []
        self.excl = excl

    def __getitem__(self, k):
        return self.ap[k]


class Prog:
    ENGS = ("pe", "act", "dve", "pool", "sp")

    def __init__(self, nc, n_dma_sems=12, same_engine_sync=True):
        self.nc = nc
        self.ops = []
        self.es = contextlib.ExitStack()
        self.same_engine_sync = same_engine_sync
        self.n_dma_sems = n_dma_sems
        self.ncnt = 0

    def sb(self, shape, dt, name=None):
        self.ncnt += 1
        name = name or f"sb{self.ncnt}"
        t = self.es.enter_context(self.nc.sbuf_tensor(name, list(shape), dt))
        return T(t[:], name)

    def ps(self, shape, dt, name=None):
        self.ncnt += 1
        name = name or f"ps{self.ncnt}"
        t = self.es.enter_context(self.nc.psum_tensor(name, list(shape), dt))
        return T(t[:], name, excl=True)

    def dram(self, name, shape, dt, kind="Internal"):
        t = self.nc.dram_tensor(name, list(shape), dt, kind=kind)
        return T(t.ap(), name)

    def op(self, eng, fn, r=(), w=(), dma=False):
        idx = len(self.ops)
        deps = set()
        raw = set()
        for t in r:
            if t.last_w is not None:
                deps.add(t.last_w)
                raw.add(t.last_w)
            if t.excl:
                for rd in t.readers:
                    if self.ops[rd]["eng"] != eng:
                        deps.add(rd)
        for t in w:
            if t.last_w is not None:
                deps.add(t.last_w)
            deps.update(t.readers)
        for t in r:
            t.readers.append(idx)
        for t in w:
            t.last_w = idx
            t.readers = []
        deps.discard(idx)
        self.ops.append({"eng": eng, "fn": fn, "deps": deps, "dma": dma, "raw": raw})
        return idx

    def dma(self, out_t, out_ap, in_t, in_ap, q="sp", **kw):
        return self.op(q, lambda e: e.dma_start(out=out_ap, in_=in_ap, **kw),
                       r=[in_t], w=[out_t], dma=True)

    def emit(self):
        nc = self.nc
        ops = self.ops
        nops = len(ops)
        dma_use = {}
        for i, o in enumerate(ops):
            if o["dma"]:
                dma_use.setdefault(o["eng"], []).append(i)
        dma_sem = {}
        for q, lst in dma_use.items():
            for k, i in enumerate(lst):
                slot = k % self.n_dma_sems
                val = 16 * (k // self.n_dma_sems + 1)
                dma_sem[i] = (q, slot, val)
                if k >= self.n_dma_sems:
                    ops[i]["deps"].add(lst[k - self.n_dma_sems])
        waited = {e: {} for e in self.ENGS}
        waited_dma = {e: set() for e in self.ENGS}
        plan = [None] * nops
        signaling = set()
        for i, o in enumerate(ops):
            e = o["eng"]
            need = {}
            need_dma = []
            for d in sorted(o["deps"]):
                po = ops[d]
                if po["dma"]:
                    if d not in waited_dma[e]:
                        waited_dma[e].add(d)
                        need_dma.append(d)
                    continue
                p = po["eng"]
                if p == e and (e == "pe" or not self.same_engine_sync or d not in o["raw"]):
                    continue
                if waited[e].get(p, -1) >= d:
                    continue
                need[p] = max(need.get(p, -1), d)
            for p, d in need.items():
                waited[e][p] = d
                signaling.add(d)
            plan[i] = (need, need_dma)
        semval = {}
        cnt = {e: 0 for e in self.ENGS}
        for i, o in enumerate(ops):
            if o["dma"]:
                continue
            if i in signaling:
                cnt[o["eng"]] += 1
                semval[i] = cnt[o["eng"]]
        self.stats = {e: sum(1 for o in ops if o["eng"] == e) for e in self.ENGS}
        self.stats["signals"] = dict(cnt)
        es = self.es
        sems = {e: es.enter_context(nc.semaphore(f"sem_{e}")) for e in self.ENGS}
        dsems = {q: [es.enter_context(nc.semaphore(f"dsem_{q}{k}")) for k in range(self.n_dma_sems)]
                 for q in dma_use}
        block = es.enter_context(nc.Block())

        def body(ename):
            def f(eng):
                for i, o in enumerate(ops):
                    if o["eng"] != ename:
                        continue
                    need, need_dma = plan[i]
                    for p, d in need.items():
                        eng.wait_ge(sems[p], semval[d])
                    for d in need_dma:
                        q, slot, val = dma_sem[d]
                        eng.wait_ge(dsems[q][slot], val)
                    ins = o["fn"](eng)
                    if o["dma"]:
                        q, slot, val = dma_sem[i]
                        ins.then_inc(dsems[q][slot], 16)
                    elif i in signaling:
                        ins.then_inc(sems[ename], 1)
                for q, lst in dma_use.items():
                    if q != ename:
                        continue
                    last = {}
                    for i in lst:
                        _, slot, val = dma_sem[i]
                        last[slot] = val
                    for slot, val in last.items():
                        eng.wait_ge(dsems[q][slot], val)
            return f

        block.tensor(body("pe"))
        block.scalar(body("act"))
        block.vector(body("dve"))
        block.gpsimd(body("pool"))
        block.sync(body("sp"))

    def close(self):
        self.es.close()


def _consts():
    c = {}
    pos = np.arange(S, dtype=np.float32)

    def rope_tabs(p, rot, theta, nrep):
        half = rot // 2
        fr = (np.float32(theta) ** (-(np.arange(half, dtype=np.float32) * np.float32(2.0) / np.float32(rot)))).astype(np.float32)
        ang = p.astype(np.float32)[:, None] * fr[None, :]
        cs, sn = np.cos(ang).astype(np.float32), np.sin(ang).astype(np.float32)
        C = np.ones((len(p), 64), np.float32)
        C[:, 0:half] = cs
        C[:, half:rot] = cs
        Sg = np.concatenate([-sn, sn], 1)
        return np.tile(C, (1, nrep)), np.tile(Sg, (1, nrep))

    c["ropeCa"], c["ropeSa"] = rope_tabs(pos, 16, 500000.0, 8)
    pc = np.arange(256, dtype=np.float32) * 16 + 31
    c["ropeCc"], c["ropeSc"] = rope_tabs(pc, 16, 500000.0, 1)
    c["ropeCr"], c["ropeSr"] = rope_tabs(pos, 64, 10000.0, 8)
    lg = np.log(1.0 - 2.0 ** (-5.0 - np.arange(8, dtype=np.float64)))
    i = np.arange(128, dtype=np.float64)
    diff = i[None, :] - i[:, None]
    dT = np.zeros((128, 8, 128), np.float64)
    for h in range(8):
        dT[:, h, :] = np.where(diff >= 0, np.exp(np.maximum(diff, 0) * lg[h]), 0.0) * 0.125
    c["decayT"] = dT.astype(np.float32).reshape(128, 1024)
    zeta = np.exp((127.0 - i)[None, :] * lg[:, None])
    c["ZT"] = np.repeat(zeta.T[:, :, None], 64, axis=2).reshape(128, 512).astype(np.float32)
    xi = np.exp((i + 1.0)[None, :] * lg[:, None]) * 0.125
    c["XI"] = np.ascontiguousarray(xi.T).astype(np.float32)
    n = np.arange(256)
    cs_ = n * 16
    ss_ = np.arange(64) * 64
    ov = np.clip(np.minimum(cs_[:, None] + 32, ss_[None, :] + 64) - np.maximum(cs_[:, None], ss_[None, :]), 0, None)
    c2s = ov.astype(np.float32) / 32.0
    c2s[255] = 0.0
    c["C2S"] = c2s
    key = np.arange(S)
    c["EM"] = (key[None, :] // 64 == np.arange(64)[:, None]).astype(np.float32)
    sp = np.arange(128) - 64
    curp = (np.arange(128) >= 64).astype(np.int64)[:, None]
    forced = (sp[None, :] == curp) | (sp[None, :] == curp - 1)
    future = sp[None, :] > curp
    c["TK"] = (~(forced | future)).astype(np.float32)
    c["TA"] = np.where(forced, 1.0e4, np.where(future, -1.0e4, 0.0)).astype(np.float32)
    return c


CONST_SHAPES = {"ropeCa": [S, 512], "ropeSa": [S, 128], "ropeCc": [256, 64], "ropeSc": [256, 16],
                "ropeCr": [S, 512], "ropeSr": [S, 512], "decayT": [128, 1024], "ZT": [128, 512],
                "XI": [128, 8], "C2S": [256, 64], "EM": [64, S],
                "TK": [128, 128], "TA": [128, 128]}

IN_SHAPES = {"x": [S, D], "ffn1_norm": [1, D], "ffn1_w_gate": [D, DFF], "ffn1_w_up": [D, DFF],
             "ffn1_w_down": [DFF, D], "mix_norm": [1, D], "w_in": [D, DIN], "cmp_pos_emb": [32, 64],
             "cmp_k_w1": [2048, 256], "cmp_k_w2": [256, 64], "cmp_v_w1": [2048, 256], "cmp_v_w2": [256, 64],
             "ret_gn_gain": [1, 1024], "w_branch_nsa": [D, D], "w_branch_ret": [D, D], "w_out": [D, D],
             "ffn2_norm": [1, D], "ffn2_w_gate": [D, DFF], "ffn2_w_up": [D, DFF], "ffn2_w_down": [DFF, D],
             "final_norm": [1, D]}

G_CHUNK = [float(np.exp(128.0 * np.log(1.0 - 2.0 ** (-5.0 - h)))) for h in range(8)]
NEGBIG = -2.0e4


def build(phases=("ffn1", "proj", "cmp", "nsa", "ret", "merge", "ffn2"), dbg=(), lim=None):
    nc = bass.Bass("TRN2", target_bir_lowering=False)
    P = Prog(nc)
    lim_c = [0, 5, 17] if lim else None
    I = {k: P.dram(k, v, F32, kind="ExternalInput") for k, v in IN_SHAPES.items()}
    C = {k: P.dram(k, v, F32, kind="ExternalInput") for k, v in CONST_SHAPES.items()}
    out_d = P.dram("out", [S, D], F32, kind="ExternalOutput")

    def scratch(name, shape, dt):
        return P.dram(name, shape, dt, kind=("ExternalOutput" if name in dbg else "Internal"))

    def tiles(t, n=NT, rows=128):
        return [T(t.ap[i * rows:(i + 1) * rows], f"{t.name}{i}") for i in range(n)]

    def ctiles(t, n=NT, cols=128):
        return [T(t.ap[..., i * cols:(i + 1) * cols], f"{t.name}{i}") for i in range(n)]

    x_t = tiles(I["x"])
    x1_d = scratch("x1_d", [S, D], F32); x1_t = tiles(x1_d)
    x2_d = scratch("x2_d", [S, D], F32); x2_t = tiles(x2_d)
    hT_d = scratch("hT_d", [8, 128, 8, 512], BF16)
    hT_t = [T(hT_d.ap[i], f"hT{i}") for i in range(8)]
    hmT_d = scratch("hmT_d", [NT, 128, 8, 128], BF16)
    hmT_t = [T(hmT_d.ap[i], f"hmT{i}") for i in range(NT)]
    QT_d = scratch("QT_d", [2, 512, S], BF16)
    QT_t = [ctiles(T(QT_d.ap[g], f"QT{g}_")) for g in range(2)]
    kcT_d = scratch("kcT_d", [128, S], BF16); kcT_t = ctiles(kcT_d)
    vcT_d = scratch("vcT_d", [128, S], BF16); vcT_t = ctiles(vcT_d)
    ksT_d = scratch("ksT_d", [128, S], BF16); ksT_t = ctiles(ksT_d)
    kwT_d = scratch("kwT_d", [128, S], BF16); kwT_t = ctiles(kwT_d)
    vs_d = scratch("vs_d", [S, 128], BF16); vs_t = tiles(vs_d)
    vw_d = scratch("vw_d", [S, 128], BF16); vw_t = tiles(vw_d)
    ga_d = scratch("ga_d", [S, 48], F32); ga_t = tiles(ga_d)
    QrT_d = scratch("QrT_d", [512, S], BF16); QrT_t = ctiles(QrT_d)
    KrT_d = scratch("KrT_d", [512, S], BF16); KrT_t = ctiles(KrT_d)
    kz_d = scratch("kz_d", [S, 512], BF16); kz_t = tiles(kz_d)
    vr_d = scratch("vr_d", [S, 1024], BF16); vr_t = tiles(vr_d)
    sgr_d = scratch("sgr_d", [S, 1024], F32); sgr_t = tiles(sgr_d)
    gm_d = scratch("gm_d", [S, 2048], F32); gm_t = tiles(gm_d)
    kc_d = scratch("kc_d", [2, 256, 64], BF16)
    vc_d = scratch("vc_d", [2, 256, 64], BF16)
    aT_d = scratch("aT_d", [1024, S], BF16); aT_t = ctiles(aT_d)
    rT_d = scratch("rT_d", [1024, S], BF16); rT_t = ctiles(rT_d)

    WSLOT = 33792
    wslot = [P.sb([128, WSLOT], BF16, f"wslot{i}") for i in range(2)]
    ident = P.sb([128, 128], BF16, "ident")
    gain = P.sb([128, 1024], F32, "gain")
    gain2 = P.sb([128, 1024], F32, "gain2")
    xin = [P.sb([128, 1024], F32, f"xin{i}") for i in range(2)]
    xres = xin
    hb = [P.sb([128, 1024], BF16, f"hb{i}") for i in range(2)]
    st1 = [P.sb([128, 4], F32, f"st1_{i}") for i in range(2)]
    big = P.sb([128, 5632], BF16, "big")
    hT = P.sb([128, 8, 512], BF16, "hT")
    wk32 = [P.sb([128, 1024], F32, f"wk32_{i}") for i in range(3)]
    wk16 = [P.sb([128, 1024], BF16, f"wk16_{i}") for i in range(3)]
    sm = [P.sb([128, 64], F32, f"sm{i}") for i in range(8)]
    tCa = P.sb([128, 512], F32, "tCa"); tSa = P.sb([128, 128], F32, "tSa")
    tCr = P.sb([128, 512], F32, "tCr"); tSr = P.sb([128, 512], F32, "tSr")
    tZ = P.sb([128, 512], F32, "tZ")
    pb = [P.ps([128, 512], F32, f"pb{i}") for i in range(7)]
    pbf = P.ps([128, 1024], BF16, "pbf")

    P.op("pool", lambda e: e.memset(ident[:], 0.0), w=[ident])
    P.op("pool", lambda e: e.affine_select(out=ident[:], in_=ident[:], pattern=[[-1, 128]], compare_op=ALU.not_equal,
                                           fill=1.0, base=0, channel_multiplier=1), r=[ident], w=[ident])

    cnt = {"rr": 0}

    def rr(lst):
        cnt["rr"] += 1
        return lst[cnt["rr"] % len(lst)]

    def mm(out_t, out_ap, a_t, a_ap, b_t, b_ap, start, stop, sgc=False):
        P.op("pe", lambda e: e.matmul(out_ap, lhsT=a_ap, rhs=b_ap, start=start, stop=stop, skip_group_check=sgc),
             r=[a_t, b_t], w=[out_t])

    def tr(out_t, out_ap, in_t, in_ap):
        P.op("pe", lambda e: e.transpose(out=out_ap, in_=in_ap, identity=ident[:]), r=[in_t, ident], w=[out_t])

    def load_w(slot_t, off, src_t, src_ap, shape):
        n = int(np.prod(shape[1:]))
        dst = slot_t[:, off:off + n]
        if len(shape) == 3:
            dst = dst.rearrange("p (a b) -> p a b", a=shape[1])
        P.dma(slot_t, dst, src_t, src_ap, q="pool")
        return dst

    def rmsnorm(x_tile, gain_t, hb_t, stt):
        sq = wk32[2]
        P.op("act", lambda e: e.activation(out=sq[:], in_=x_tile[:], func=AF.Square, accum_out=stt[:, 0:1]),
             r=[x_tile], w=[sq, stt])
        P.op("dve", lambda e: e.tensor_scalar(out=stt[:, 1:2], in0=stt[:, 0:1], scalar1=1.0 / D, scalar2=1e-6,
                                              op0=ALU.mult, op1=ALU.add), r=[stt], w=[stt])
        P.op("act", lambda e: e.sqrt(out=stt[:, 2:3], in_=stt[:, 1:2]), r=[stt], w=[stt])
        P.op("dve", lambda e: e.reciprocal(out=stt[:, 3:4], in_=stt[:, 2:3]), r=[stt], w=[stt])
        P.op("dve", lambda e: e.scalar_tensor_tensor(out=hb_t[:], in0=x_tile[:], scalar=stt[:, 3:4], in1=gain_t[:],
                                                     op0=ALU.mult, op1=ALU.mult), r=[x_tile, stt, gain_t], w=[hb_t])

    def transpose8(src_t, dst_t, dst_ap, nblk=8, eng="act"):
        for k in range(nblk):
            tr(pbf, pbf[:, k * 128:(k + 1) * 128], src_t, src_t[:, k * 128:(k + 1) * 128])
        src = pbf[:, 0:nblk * 128].rearrange("p (k t) -> p k t", k=nblk)
        if eng == "act":
            P.op("act", lambda e: e.copy(out=dst_ap, in_=src), r=[pbf], w=[dst_t])
        else:
            P.op("dve", lambda e: e.tensor_copy(out=dst_ap, in_=src), r=[pbf], w=[dst_t])

    def load_gain(gt, src):
        P.dma(gt, gt[:], src, src.ap.partition_broadcast(128))

    def ffn_pass(which, hf, slot, xsrc_t, xdst_t, norm_gain, final=False):
        pre = f"ffn{which}_"
        wg_d, wu_d, wd_d = I[pre + "w_gate"], I[pre + "w_up"], I[pre + "w_down"]
        c0 = hf * 1408
        wg = load_w(slot, 0, wg_d, wg_d.ap[:, c0:c0 + 1408].rearrange("(k p) f -> p k f", p=128), [128, 8, 1408])
        wu = load_w(slot, 11264, wu_d, wu_d.ap[:, c0:c0 + 1408].rearrange("(k p) f -> p k f", p=128), [128, 8, 1408])
        wd = load_w(slot, 22528, wd_d, wd_d.ap[c0:c0 + 1408, :].rearrange("(k p) d -> p k d", p=128), [128, 11, 1024])
        yield
        aT = big[:, 0:11 * 512].rearrange("p (f t) -> p f t", f=11)
        if hf == 0:
            load_gain(gain, norm_gain)
        if final:
            load_gain(gain2, I["final_norm"])
        for st in range(8 if lim is None else 1):
            if hf == 0:
                for tt in range(4):
                    ti = st * 4 + tt
                    xt = xin[ti % 2]
                    P.dma(xt, xt[:], xsrc_t[ti], xsrc_t[ti][:, :])
                    hbt = hb[ti % 2]
                    rmsnorm(xt, gain, hbt, st1[ti % 2])
                    transpose8(hbt, hT, hT[:, :, tt * 128:(tt + 1) * 128], eng="act")
                P.dma(hT_t[st], hT_t[st][:], hT, hT[:], q="pool")
            else:
                P.dma(hT, hT[:], hT_t[st], hT_t[st][:])
            for f in range(11):
                pg = pb[(2 * f) % 4]
                pu = pb[(2 * f + 1) % 4]
                for k in range(8):
                    mm(pg, pg[:], slot, wg[:, k, f * 128:(f + 1) * 128], hT, hT[:, k, :], k == 0, k == 7)
                for k in range(8):
                    mm(pu, pu[:], slot, wu[:, k, f * 128:(f + 1) * 128], hT, hT[:, k, :], k == 0, k == 7)
                sg = wk32[f % 2]
                P.op("act", lambda e, sg=sg, pg=pg: e.activation(out=sg[:, 0:512], in_=pg[:], func=AF.Silu),
                     r=[pg], w=[sg])
                P.op("dve", lambda e, sg=sg, pu=pu, f=f: e.tensor_tensor(out=aT[:, f, :], in0=pu[:], in1=sg[:, 0:512],
                                                                        op=ALU.mult), r=[pu, sg], w=[big])
            for tt in range(4):
                ti = st * 4 + tt
                xr = xres[ti % 2]
                P.dma(xr, xr[:], xsrc_t[ti], xsrc_t[ti][:, :])
                for half in range(2):
                    py = pb[4 + (2 * tt + half) % 3]
                    for f in range(11):
                        mm(py, py[:], big, aT[:, f, tt * 128:(tt + 1) * 128], slot, wd[:, f, half * 512:(half + 1) * 512],
                           f == 0, f == 10)
                    P.op("dve", lambda e, xr=xr, py=py, half=half: e.scalar_tensor_tensor(
                        out=xr[:, half * 512:(half + 1) * 512], in0=py[:], scalar=0.5,
                        in1=xr[:, half * 512:(half + 1) * 512], op0=ALU.mult, op1=ALU.add), r=[py, xr], w=[xr])
                if final:
                    ob = wk32[ti % 2]
                    stt = st1[ti % 2]
                    sq = wk32[2]
                    P.op("act", lambda e, xr=xr, stt=stt: e.activation(out=sq[:], in_=xr[:], func=AF.Square,
                                                                       accum_out=stt[:, 0:1]), r=[xr], w=[sq, stt])
                    P.op("dve", lambda e, stt=stt: e.tensor_scalar(out=stt[:, 1:2], in0=stt[:, 0:1], scalar1=1.0 / D,
                                                                   scalar2=1e-6, op0=ALU.mult, op1=ALU.add),
                         r=[stt], w=[stt])
                    P.op("act", lambda e, stt=stt: e.sqrt(out=stt[:, 2:3], in_=stt[:, 1:2]), r=[stt], w=[stt])
                    P.op("dve", lambda e, stt=stt: e.reciprocal(out=stt[:, 3:4], in_=stt[:, 2:3]), r=[stt], w=[stt])
                    P.op("dve", lambda e, xr=xr, stt=stt, ob=ob: e.scalar_tensor_tensor(
                        out=ob[:], in0=xr[:], scalar=stt[:, 3:4], in1=gain2[:], op0=ALU.mult, op1=ALU.mult),
                         r=[xr, stt, gain2], w=[ob])
                    P.dma(xdst_t[ti], xdst_t[ti][:, :], ob, ob[:], q="pool")
                else:
                    P.dma(xdst_t[ti], xdst_t[ti][:, :], xr, xr[:], q="pool")

    def rope(src_t, src_ap, H, R, Ct, Cap, St, Sap, out_t, out_ap, o32, t2):
        n = H * 64
        half = R // 2
        P.op("dve", lambda e: e.tensor_tensor(out=o32[:, 0:n], in0=src_ap, in1=Cap, op=ALU.mult), r=[Ct, src_t], w=[o32])
        s3 = src_ap.rearrange("p (h d) -> p h d", h=H)
        S3 = Sap.rearrange("p (h r) -> p h r", h=H)
        t3 = t2[:, 0:H * R].rearrange("p (h r) -> p h r", h=H)
        o3 = o32[:, 0:n].rearrange("p (h d) -> p h d", h=H)
        P.op("dve", lambda e: e.tensor_tensor(out=t3[:, :, 0:half], in0=s3[:, :, half:R], in1=S3[:, :, 0:half],
                                              op=ALU.mult), r=[St, src_t], w=[t2])
        P.op("dve", lambda e: e.tensor_tensor(out=t3[:, :, half:R], in0=s3[:, :, 0:half], in1=S3[:, :, half:R],
                                              op=ALU.mult), r=[St, src_t], w=[t2])
        P.op("pool", lambda e: e.tensor_tensor(out=o3[:, :, 0:R], in0=o3[:, :, 0:R], in1=t3, op=ALU.add),
             r=[o32, t2], w=[o32])
        P.op("act", lambda e: e.copy(out=out_ap, in_=o32[:, 0:n]), r=[o32], w=[out_t])

    def proj_pass0(slot):
        w_d = I["w_in"]
        w0 = load_w(slot, 0, w_d, w_d.ap[:, 0:1944].rearrange("(k p) f -> p k f", p=128), [128, 8, 1944])
        w1 = load_w(slot, 8 * 1944, w_d, w_d.ap[:, 1944:3888].rearrange("(k p) f -> p k f", p=128), [128, 8, 1944])
        yield

        def wcol(k, c0, c1):
            if c1 <= 1944:
                return w0[:, k, c0:c1]
            assert c0 >= 1944
            return w1[:, k, c0 - 1944:c1 - 1944]

        load_gain(gain, I["mix_norm"])
        hmT = hT[:, :, 0:128]
        P.dma(tZ, tZ[:], C["ZT"], C["ZT"][:, :])
        qT = P.sb([128, 4, 128], BF16, "qT")
        blocks = [(0, 512), (512, 1024), (1024, 1536), (1536, 1840), (1840, 2352), (2352, 2864), (2864, 3376), (3376, 3888)]
        for ti in range(NT if lim is None else lim):
            xt = xin[ti % 2]
            P.dma(xt, xt[:], x1_t[ti], x1_t[ti][:, :])
            P.dma(tCa, tCa[:], C["ropeCa"], C["ropeCa"][ti * 128:(ti + 1) * 128, :])
            P.dma(tSa, tSa[:], C["ropeSa"], C["ropeSa"][ti * 128:(ti + 1) * 128, :])
            P.dma(tCr, tCr[:], C["ropeCr"], C["ropeCr"][ti * 128:(ti + 1) * 128, :])
            P.dma(tSr, tSr[:], C["ropeSr"], C["ropeSr"][ti * 128:(ti + 1) * 128, :])
            hbt = hb[ti % 2]
            rmsnorm(xt, gain, hbt, st1[ti % 2])
            transpose8(hbt, hT, hmT, eng="act")
            P.dma(hmT_t[ti], hmT_t[ti][:], hT, hmT, q="pool")
            for bi, (c0, c1) in enumerate(blocks):
                if str(bi) in _os.environ.get("SKIPB", ""):
                    continue
                pp = pb[bi % 7]
                subs = [(c0, c1)] if not (c0 < 1944 < c1) else [(c0, 1944), (1944, c1)]
                for (s0, s1) in subs:
                    for k in range(8):
                        mm(pp, pp[:, s0 - c0:s1 - c0], hT, hT[:, k, 0:128], slot, wcol(k, s0, s1), k == 0, k == 7)
                o32, t2 = wk32[0], wk32[1]
                if bi in (0, 1):
                    ob = wk16[bi % 2]
                    rope(pp, pp[:, 0:512], 8, 16, tCa, tCa[:, :], tSa, tSa[:, :], ob, ob[:, 0:512], o32, t2)
                    transpose8(ob, qT, qT[:], nblk=4, eng="dve")
                    dst = QT_t[bi][ti]
                    P.dma(dst, dst[:].rearrange("(p r) t -> r p t", r=128), qT, qT[:], q="pool")
                elif bi == 2:
                    ob = wk16[2]
                    if "x" not in _os.environ.get("SKIPF", ""):
                        P.op("act", lambda e, pp=pp, ob=ob: e.copy(out=ob[:, 0:256], in_=pp[:, 0:256]), r=[pp], w=[ob])
                    if "r" not in _os.environ.get("SKIPF", ""):
                        rope(pp, pp[:, 256:384], 2, 16, tCa, tCa[:, 0:128], tSa, tSa[:, 0:32], ob, ob[:, 256:384], o32, t2)
                    if "y" not in _os.environ.get("SKIPF", ""):
                        P.op("act", lambda e, pp=pp, ob=ob: e.copy(out=ob[:, 384:512], in_=pp[:, 384:512]), r=[pp], w=[ob])
                    SK = _os.environ.get("SKIPF", "")
                    if "t" not in SK:
                        transpose8(ob, qT, qT[:, 0:3, :], nblk=3, eng="dve")
                    if "a" not in SK:
                        P.dma(kcT_t[ti], kcT_t[ti][:], qT, qT[:, 0, :], q="pool")
                    if "b" not in SK:
                        P.dma(vcT_t[ti], vcT_t[ti][:], qT, qT[:, 1, :], q="pool")
                        P.dma(ksT_t[ti], ksT_t[ti][:], qT, qT[:, 2, :], q="pool")
                    if "c" not in SK:
                        P.dma(vs_t[ti], vs_t[ti][:, :], ob, ob[:, 384:512], q="pool")
                elif bi == 3:
                    ob = wk16[2]
                    rope(pp, pp[:, 0:128], 2, 16, tCa, tCa[:, 0:128], tSa, tSa[:, 0:32], ob, ob[:, 0:128], o32, t2)
                    P.op("act", lambda e, pp=pp, ob=ob: e.copy(out=ob[:, 128:256], in_=pp[:, 128:256]), r=[pp], w=[ob])
                    gsb = sm[0]
                    P.op("act", lambda e, pp=pp, gsb=gsb: e.activation(out=gsb[:, 0:48], in_=pp[:, 256:304], func=AF.Sigmoid),
                         r=[pp], w=[gsb])
                    P.dma(ga_t[ti], ga_t[ti][:, :], gsb, gsb[:, 0:48], q="pool")
                    transpose8(ob, qT, qT[:, 0:1, :], nblk=1, eng="dve")
                    P.dma(kwT_t[ti], kwT_t[ti][:], qT, qT[:, 0, :], q="pool")
                    P.dma(vw_t[ti], vw_t[ti][:, :], ob, ob[:, 128:256], q="pool")
                elif bi in (4, 5):
                    ob = wk16[bi % 2]
                    rope(pp, pp[:, 0:512], 8, 64, tCr, tCr[:, :], tSr, tSr[:, :], ob, ob[:, 0:512], o32, t2)
                    transpose8(ob, qT, qT[:], nblk=4, eng="dve")
                    dst = (QrT_t if bi == 4 else KrT_t)[ti]
                    P.dma(dst, dst[:].rearrange("(p r) t -> r p t", r=128), qT, qT[:], q="pool")
                    if bi == 5:
                        kz = wk16[2]
                        P.op("pool", lambda e, kz=kz, o32=o32: e.tensor_tensor(out=kz[:, 0:512], in0=o32[:, 0:512],
                                                                              in1=tZ[:], op=ALU.mult), r=[o32, tZ], w=[kz])
                        P.dma(kz_t[ti], kz_t[ti][:, :], kz, kz[:, 0:512], q="pool")
                else:
                    ob = wk16[bi % 2]
                    P.op("act", lambda e, pp=pp, ob=ob: e.copy(out=ob[:, 0:512], in_=pp[:]), r=[pp], w=[ob])
                    hh = bi - 6
                    P.dma(vr_t[ti], vr_t[ti][:, hh * 512:(hh + 1) * 512], ob, ob[:, 0:512], q="pool")

    def proj_pass1(slot):
        w_d = I["w_in"]
        w0 = load_w(slot, 0, w_d, w_d.ap[:, 3888:3888 + 1536].rearrange("(k p) f -> p k f", p=128), [128, 8, 1536])
        w1 = load_w(slot, 8 * 1536, w_d, w_d.ap[:, 3888 + 1536:DIN].rearrange("(k p) f -> p k f", p=128), [128, 8, 1536])
        yield
        for ti in range(NT if lim is None else lim):
            P.dma(hT, hT[:, :, 0:128], hmT_t[ti], hmT_t[ti][:])
            for bi in range(6):
                pp = pb[bi % 7]
                wv = w0 if bi < 3 else w1
                cc = (bi % 3) * 512
                for k in range(8):
                    mm(pp, pp[:], hT, hT[:, k, 0:128], slot, wv[:, k, cc:cc + 512], k == 0, k == 7)
                o32 = wk32[bi % 3]
                fn = AF.Silu if bi < 2 else AF.Sigmoid
                P.op("act", lambda e, pp=pp, o32=o32, fn=fn: e.activation(out=o32[:, 0:512], in_=pp[:], func=fn),
                     r=[pp], w=[o32])
                if bi < 2:
                    P.dma(sgr_t[ti], sgr_t[ti][:, bi * 512:(bi + 1) * 512], o32, o32[:, 0:512], q="pool")
                else:
                    P.dma(gm_t[ti], gm_t[ti][:, (bi - 2) * 512:(bi - 1) * 512], o32, o32[:, 0:512], q="pool")


    def carve(parent, ap, name=""):
        t = T(ap, name)
        t.last_w = parent.last_w
        t.readers = list(parent.readers)
        return t

    def uncarve(parent, subs):
        for t in subs:
            parent.readers.extend(t.readers)
            if t.last_w is not None:
                parent.readers.append(t.last_w)

    kcT2_d = scratch("kcT2_d", [2, 64, 256], BF16)

    def cmp_phase(slot):
        subs = []

        def cv(off, n, parts=128, name=""):
            t = carve(slot, slot.ap[0:parts, off:off + n], name)
            subs.append(t)
            return t
        w1 = [cv(0, 8192, 64, "w1k"), cv(8192, 8192, 64, "w1v")]
        w2 = [cv(16384, 128, 128, "w2k"), cv(16512, 128, 128, "w2v")]
        tokT = [cv(16640, 4096, 64, "tokT0"), cv(20736, 4096, 64, "tokT1")]
        GT = cv(24832, 512, 128, "GT")
        peb = cv(25344, 64, 32, "peb")
        peT = cv(25408, 32, 64, "peT")
        kcb = cv(25440, 64, 128, "kcb")
        kcTs = cv(25504, 128, 64, "kcTs")
        tCc = sm[1]; tSc = sm[2]; bias = sm[3]
        srcs = [(I["cmp_k_w1"], I["cmp_k_w2"], kcT_d, kc_d), (I["cmp_v_w1"], I["cmp_v_w2"], vcT_d, vc_d)]
        P.dma(peb, peb[:, :], I["cmp_pos_emb"], I["cmp_pos_emb"][:, :], q="pool")
        tr(pbf, pbf[0:64, 0:32], peb, peb[:, :]) if False else P.op(
            "pe", lambda e: e.transpose(out=pbf[0:64, 0:32], in_=peb[:, :], identity=ident[0:32, 0:32]),
            r=[peb, ident], w=[pbf])
        P.op("act", lambda e: e.copy(out=peT[:, :], in_=pbf[0:64, 0:32]), r=[pbf], w=[peT])
        P.op("pool", lambda e: e.memset(GT[:, :], 0.0), w=[GT])
        GT3 = GT[:, :].rearrange("p (m n) -> p m n", m=2)
        for kv in range(2):
            w1_d, w2_d, tT_d, o_d = srcs[kv]
            w1v = w1[kv][:, :].rearrange("p (l j) -> p l j", l=32)
            P.dma(w1[kv], w1v, w1_d, w1_d.ap.rearrange("(l d) j -> d l j", d=64), q="pool")
            w2v = w2[kv][:, :].rearrange("p (m o) -> p m o", m=2)
            P.dma(w2[kv], w2v, w2_d, w2_d.ap.rearrange("(m p) o -> p m o", p=128), q="pool")
            pbias = pb[6]
            for m in range(2):
                for l in range(32):
                    mm(pbias, pbias[:, m:m + 1], w1[kv], w1v[:, l, m * 128:(m + 1) * 128], peT, peT[:, l:l + 1], l == 0, l == 31)
            P.op("act", lambda e, pbias=pbias: e.copy(out=bias[:, 0:2], in_=pbias[:, 0:2]), r=[pbias], w=[bias])
            for g in range(2):
                tk = tokT[g]
                P.dma(tk, tk[:, :], tT_d, tT_d.ap[g * 64:(g + 1) * 64, :])
                tok3 = tk[:, :].rearrange("p (n r) -> p n r", r=16)
                for m in range(2):
                    ph = pb[m]
                    for l in range(32):
                        a, rr_ = l // 16, l % 16
                        mm(ph, ph[:, 0:255], w1[kv], w1v[:, l, m * 128:(m + 1) * 128], tk, tok3[:, a:a + 255, rr_], l == 0, l == 31)
                    hbx, x2, z = wk32[0], wk32[1], wk32[2]
                    P.op("act", lambda e, ph=ph, m=m: e.activation(out=hbx[:, 0:255], in_=ph[:, 0:255], func=AF.Identity,
                                                                   bias=bias[:, m:m + 1]), r=[ph, bias], w=[hbx])
                    P.op("dve", lambda e: e.tensor_tensor(out=x2[:, 0:255], in0=hbx[:, 0:255], in1=hbx[:, 0:255], op=ALU.mult),
                         r=[hbx], w=[x2])
                    P.op("dve", lambda e: e.tensor_scalar(out=x2[:, 0:255], in0=x2[:, 0:255], scalar1=0.044715, scalar2=1.0,
                                                          op0=ALU.mult, op1=ALU.add), r=[x2], w=[x2])
                    P.op("dve", lambda e: e.tensor_tensor(out=z[:, 0:255], in0=x2[:, 0:255], in1=hbx[:, 0:255], op=ALU.mult),
                         r=[x2, hbx], w=[z])
                    P.op("act", lambda e: e.activation(out=z[:, 0:255], in_=z[:, 0:255], func=AF.Sigmoid, scale=1.5957691216),
                         r=[z], w=[z])
                    P.op("dve", lambda e, m=m: e.tensor_tensor(out=GT3[:, m, 0:255], in0=z[:, 0:255], in1=hbx[:, 0:255],
                                                               op=ALU.mult), r=[z, hbx], w=[GT])
                for j in range(2):
                    po = pb[2 + j]
                    for m in range(2):
                        mm(po, po[:, 0:64], GT, GT3[:, m, j * 128:(j + 1) * 128], w2[kv], w2v[:, m, :], m == 0, m == 1)
                    if kv == 0:
                        P.dma(tCc, tCc[:, 0:64], C["ropeCc"], C["ropeCc"][j * 128:(j + 1) * 128, :])
                        P.dma(tSc, tSc[:, 0:16], C["ropeSc"], C["ropeSc"][j * 128:(j + 1) * 128, :])
                        rope(po, po[:, 0:64], 1, 16, tCc, tCc[:, 0:64], tSc, tSc[:, 0:16], kcb, kcb[:, :], wk32[0], wk32[1])
                        P.op("pe", lambda e: e.transpose(out=pbf[0:64, 0:128], in_=kcb[:, :], identity=ident[:]),
                             r=[kcb, ident], w=[pbf])
                        P.op("act", lambda e: e.copy(out=kcTs[:, :], in_=pbf[0:64, 0:128]), r=[pbf], w=[kcTs])
                        P.dma(kcT2_d, kcT2_d.ap[g, :, j * 128:(j + 1) * 128], kcTs, kcTs[:, :], q="pool")
                    else:
                        P.op("act", lambda e, po=po: e.copy(out=kcb[:, :], in_=po[:, 0:64]), r=[po], w=[kcb])
                    P.dma(o_d, o_d.ap[g, j * 128:(j + 1) * 128, :], kcb, kcb[:, :], q="pool")
        uncarve(slot, subs)

    def nsa_phase(slot):
        subs = []

        def cv(off, n, parts=128, name=""):
            t = carve(slot, slot.ap[0:parts, off:off + n], name)
            subs.append(t)
            return t
        KsT = cv(0, 4096, 64, "KsT"); KwT = cv(4096, 4096, 64, "KwT")
        Vs = cv(8192, 2080, 128, "Vs"); Vw = cv(10272, 2080, 128, "Vw")
        KcT = cv(12352, 256, 64, "KcT"); Vc = cv(12608, 130, 128, "Vc")
        C2S = cv(12738, 128, 128, "C2S"); EM = cv(12866, 4096, 64, "EM")
        QTc = [cv(16962 + i * 1024, 1024, 64, f"QTc{i}") for i in range(2)]
        Et = [cv(19010 + i * 1024, 1024, 128, f"Et{i}") for i in range(3)]
        negm = cv(22082, 1024, 64, "negm")
        selb = cv(23106, 64, 128, "selb")
        ab = cv(23170, 512, 128, "ab")
        aTs = cv(23682, 512, 128, "aTs")
        TK = tCa; TA = tCr
        P.dma(TK, TK[:, 0:128], C["TK"], C["TK"][:, :])
        P.dma(TA, TA[:, 0:128], C["TA"], C["TA"][:, :])
        P.dma(C2S, C2S[:, :].rearrange("p (j s) -> p j s", j=2), C["C2S"], C["C2S"].ap.rearrange("(j p) s -> p j s", p=128), q="pool")
        P.dma(EM, EM[:, 0:2048], C["EM"], C["EM"][:, 0:2048], q="pool")
        P.dma(EM, EM[:, 2048:4096], C["EM"], C["EM"][:, 2048:4096], q="pool")
        Vs3 = Vs[:, :].rearrange("p (j d) -> p j d", d=65)
        Vw3 = Vw[:, :].rearrange("p (j d) -> p j d", d=65)
        Vc3 = Vc[:, :].rearrange("p (j d) -> p j d", d=65)
        C2S3 = C2S[:, :].rearrange("p (j s) -> p j s", j=2)
        SB = [(pb[0], pb[1]), (pb[2], pb[3])]
        OA, OB, OC = pb[4], pb[5], pb[6]
        OA3 = OA[:, 0:260].rearrange("p (h d) -> p h d", d=65)
        OB3 = OB[:, 0:260].rearrange("p (h d) -> p h d", d=65)
        acc = xin[0]; tmpo = xin[1]
        acc3 = acc[:, 0:512].rearrange("p (h d) -> p h d", d=64)
        tmp3 = tmpo[:, 0:512].rearrange("p (h d) -> p h d", d=64)
        den = sm[2]; coef = sm[3]; imp = sm[4]; imp2 = sm[5]; m8 = sm[6]; imp3 = sm[7]
        sidx = {"s": 0, "e": 0}

        def scores(lhs_t, lhs_ap, qt, mask_j=None):
            b = SB[sidx["s"] % 2]; sidx["s"] += 1
            q2 = qt[:, :]
            for hh in range(2):
                mm(b[hh], b[hh][:], lhs_t, lhs_ap, qt, q2[:, hh * 512:(hh + 1) * 512], True, mask_j is None)
            if mask_j is not None:
                for hh in range(2):
                    mm(b[hh], b[hh][:], EM, EM[:, mask_j * 128:(mask_j + 1) * 128], negm, negm[:, hh * 512:(hh + 1) * 512], False, True)
            et = Et[sidx["e"] % 3]; sidx["e"] += 1
            for hh in range(2):
                P.op("act", lambda e, hh=hh, et=et, b=b: e.activation(out=et[:, hh * 512:(hh + 1) * 512], in_=b[hh][:],
                                                                      func=AF.Exp, scale=0.125), r=[b[hh]], w=[et])
            return et

        def amask(et, cm, qstep, base):
            P.op("pool", lambda e: e.affine_select(out=et[:, :], in_=et[:, :], pattern=[[0, 8], [qstep, 128]],
                                                   compare_op=ALU.is_ge, fill=0.0, base=base, channel_multiplier=cm),
                 r=[et], w=[et])

        def pv(et, v_t, v_ap, first, last):
            for h in range(8):
                o_t = OA if h < 4 else OB
                o3 = OA3 if h < 4 else OB3
                mm(o_t, o3[:, h % 4, :], et, et[:, h * 128:(h + 1) * 128], v_t, v_ap, first and (h % 4 == 0), last, sgc=True)

        pend = []

        def flush():
            while pend:
                work, post = pend.pop(0)
                work()
                if post is not None:
                    post()

        def finish_branch(bidx, first, gat):
            P.op("dve", lambda e: e.tensor_scalar(out=den[:, 0:4], in0=OA3[:, :, 64], scalar1=1e-20, scalar2=None, op0=ALU.max),
                 r=[OA], w=[den])
            P.op("dve", lambda e: e.tensor_scalar(out=den[:, 4:8], in0=OB3[:, :, 64], scalar1=1e-20, scalar2=None, op0=ALU.max),
                 r=[OB], w=[den])
            P.op("dve", lambda e: e.reciprocal(out=den[:, 8:16], in_=den[:, 0:8]), r=[den], w=[den])
            g3 = gat[:, 0:24].rearrange("p (h b) -> p h b", b=3)
            P.op("dve", lambda e: e.tensor_tensor(out=coef[:, 0:8], in0=den[:, 8:16], in1=g3[:, :, bidx], op=ALU.mult),
                 r=[den, gat], w=[coef])
            dst, d3 = (acc, acc3) if first else (tmpo, tmp3)
            P.op("dve", lambda e: e.tensor_tensor(out=d3[:, 0:4, :], in0=OA3[:, :, 0:64],
                                                  in1=coef[:, 0:4].unsqueeze(2).broadcast_to([128, 4, 64]), op=ALU.mult),
                 r=[OA, coef], w=[dst])
            P.op("dve", lambda e: e.tensor_tensor(out=d3[:, 4:8, :], in0=OB3[:, :, 0:64],
                                                  in1=coef[:, 4:8].unsqueeze(2).broadcast_to([128, 4, 64]), op=ALU.mult),
                 r=[OB, coef], w=[dst])
            if not first:
                P.op("pool", lambda e: e.tensor_tensor(out=acc[:, 0:512], in0=acc[:, 0:512], in1=tmpo[:, 0:512], op=ALU.add),
                     r=[acc, tmpo], w=[acc])

        for g in range(2):
            P.dma(KsT, KsT[:, :], ksT_d, ksT_d.ap[g * 64:(g + 1) * 64, :])
            P.dma(KwT, KwT[:, :], kwT_d, kwT_d.ap[g * 64:(g + 1) * 64, :])
            P.dma(Vs, Vs3[:, :, 0:64], vs_d, vs_d.ap[:, g * 64:(g + 1) * 64].rearrange("(j p) d -> p j d", p=128))
            P.dma(Vw, Vw3[:, :, 0:64], vw_d, vw_d.ap[:, g * 64:(g + 1) * 64].rearrange("(j p) d -> p j d", p=128))
            P.op("pool", lambda e: e.memset(Vs3[:, :, 64:65], 1.0), w=[Vs])
            P.op("pool", lambda e: e.memset(Vw3[:, :, 64:65], 1.0), w=[Vw])
            P.dma(KcT, KcT[:, :], kcT2_d, kcT2_d.ap[g])
            P.dma(Vc, Vc3[:, :, 0:64], vc_d, vc_d.ap[g].rearrange("(j p) d -> p j d", p=128))
            P.op("pool", lambda e: e.memset(Vc3[:, :, 64:65], 1.0), w=[Vc])
            gats = [sm[1], sm[0]]
            for c in (range(NT) if lim is None else lim_c):
                qt = QTc[c % 2]
                gat = gats[c % 2]
                P.dma(qt, qt[:, :].rearrange("d (h t) -> d h t", h=8), QT_t[g][c],
                      QT_t[g][c][:].rearrange("(h d) t -> d h t", d=64))
                P.dma(gat, gat[:, 0:24], ga_t[c], ga_t[c][:, g * 24:(g + 1) * 24])

                def post_cmp(c=c, gat=gat):
                    finish_branch(0, True, gat)
                    for h in range(8):
                        if h == 0:
                            P.op("dve", lambda e: e.tensor_scalar(out=imp[:, 0:64], in0=OC[:, 0:64], scalar1=den[:, 8:9],
                                                                  scalar2=None, op0=ALU.mult), r=[OC, den], w=[imp])
                        else:
                            P.op("dve", lambda e, h=h: e.scalar_tensor_tensor(
                                out=imp[:, 0:64], in0=OC[:, h * 64:(h + 1) * 64], scalar=den[:, 8 + h:9 + h], in1=imp[:, 0:64],
                                op0=ALU.mult, op1=ALU.add), r=[OC, den, imp], w=[imp])
                    off = 64 - 2 * c
                    P.op("dve", lambda e: e.tensor_tensor(out=imp2[:, 0:64], in0=imp[:, 0:64], in1=TK[:, off:off + 64], op=ALU.mult),
                         r=[imp, TK], w=[imp2])
                    P.op("dve", lambda e: e.tensor_tensor(out=imp2[:, 0:64], in0=imp2[:, 0:64], in1=TA[:, off:off + 64], op=ALU.add),
                         r=[imp2, TA], w=[imp2])
                    P.op("dve", lambda e: e.memset(imp2[:, 0:1], 1.0e4), r=[imp2], w=[imp2])
                    P.op("dve", lambda e: e.max(out=m8[:, 0:8], in_=imp2[:, 0:64]), r=[imp2], w=[m8])
                    P.op("dve", lambda e: e.match_replace(out=imp3[:, 0:64], in_to_replace=m8[:, 0:8], in_values=imp2[:, 0:64],
                                                          imm_value=-3.0e4), r=[imp2, m8], w=[imp3])
                    P.op("dve", lambda e: e.max(out=m8[:, 8:16], in_=imp3[:, 0:64]), r=[imp3, m8], w=[m8])
                    P.op("dve", lambda e: e.tensor_scalar(out=selb[:, :], in0=imp2[:, 0:64], scalar1=m8[:, 15:16], scalar2=None,
                                                          op0=ALU.is_ge), r=[imp2, m8], w=[selb])
                    P.op("pe", lambda e: e.transpose(out=pbf[0:64, 0:128], in_=selb[:, :], identity=ident[:]),
                         r=[selb, ident], w=[pbf])
                    P.op("dve", lambda e: e.tensor_scalar(out=negm[:, :].rearrange("p (h q) -> p h q", h=8),
                                                          in0=pbf[0:64, 0:128].unsqueeze(1).broadcast_to([64, 8, 128]),
                                                          scalar1=-NEGBIG, scalar2=NEGBIG, op0=ALU.mult, op1=ALU.add),
                         r=[pbf], w=[negm])

                def post_win(gat=gat):
                    finish_branch(2, False, gat)

                def post_sel(c=c, gat=gat):
                    finish_branch(1, False, gat)
                    P.op("act", lambda e: e.copy(out=ab[:, :], in_=acc[:, 0:512]), r=[acc], w=[ab])
                    transpose8(ab, aTs, aTs[:, :].rearrange("p (k t) -> p k t", k=4), nblk=4, eng="act")
                    dst = aT_t[c]
                    P.dma(dst, dst[g * 512:(g + 1) * 512, :].rearrange("(k r) t -> r k t", r=128), aTs,
                          aTs[:, :].rearrange("p (k t) -> p k t", k=4), q="pool")

                njt = 1 if c <= 15 else 2
                for j in range(njt):
                    et = scores(KcT, KcT[:, j * 128:(j + 1) * 128], qt)
                    amask(et, -16, 1, -16 * (128 * j - 8 * c) - 31)
                    flush()

                    def work(et=et, j=j, njt=njt):
                        pv(et, Vc, Vc3[:, j, :], j == 0, j == njt - 1)
                        for h in range(8):
                            mm(OC, OC[:, h * 64:(h + 1) * 64], et, et[:, h * 128:(h + 1) * 128], C2S, C2S3[:, j, :],
                               j == 0 and h == 0, j == njt - 1, sgc=True)
                    pend.append((work, post_cmp if j == njt - 1 else None))
                j0 = max(0, c - 4)
                for j in range(j0, c + 1):
                    et = scores(KwT, KwT[:, j * 128:(j + 1) * 128], qt)
                    if j == c:
                        amask(et, -1, 1, 0)
                    if j == c - 4:
                        amask(et, 1, -1, -1)
                    flush()
                    pend.append((lambda et=et, j=j, j0=j0, c=c: pv(et, Vw, Vw3[:, j, :], j == j0, j == c),
                                 post_win if j == c else None))
                for j in range(c + 1):
                    if j == 0:
                        flush()
                    et = scores(KsT, KsT[:, j * 128:(j + 1) * 128], qt, mask_j=j)
                    if j == c:
                        amask(et, -1, 1, 0)
                    flush()
                    pend.append((lambda et=et, j=j, c=c: pv(et, Vs, Vs3[:, j, :], j == 0, j == c),
                                 post_sel if j == c else None))
            flush()
        uncarve(slot, subs)

    def ret_phase():
        subs = []
        bufs = []
        for i in range(2):
            q_ = carve(big, big.ap[0:64, i * 2560:i * 2560 + 1024], f"QrTc{i}")
            k_ = carve(big, big.ap[0:64, i * 2560 + 1024:i * 2560 + 2048], f"KrTc{i}")
            z_ = carve(big, big.ap[:, i * 2560 + 2048:i * 2560 + 2560], f"kzc{i}")
            subs += [q_, k_, z_]
            bufs.append((q_, k_, z_))
        dec = [tCr, tSr]
        P.dma(dec[0], dec[0][:, :], C["decayT"], C["decayT"][:, 0:512])
        P.dma(dec[1], dec[1][:, :], C["decayT"], C["decayT"][:, 512:1024])
        xit = sm[1]
        P.dma(xit, xit[:, 0:8], C["XI"], C["XI"][:, :])
        load_gain(gain2, I["ret_gn_gain"])
        Rst = xin[0]; osb = xin[1]; sgr = wk32[0]; sq = wk32[1]; tmp = wk32[2]
        Rbf = wk16[0]; ATs = wk16[1]; rob = wk16[2]
        stt = sm[2]
        P.op("pool", lambda e: e.memset(Rst[0:64, :], 0.0), w=[Rst])
        P.op("pool", lambda e: e.memset(Rbf[0:64, :], 0.0), w=[Rbf])
        osb3 = osb[:, :].rearrange("p (h e) -> p h e", h=8)
        sq3 = sq[:, :].rearrange("p (h e) -> p h e", h=8)
        tmp3 = tmp[:, :].rearrange("p (h e) -> p h e", h=8)
        for n in range(NT if lim is None else lim):
            q_, k_, z_ = bufs[n % 2]
            vt = hb[n % 2]
            P.dma(q_, q_[:, :].rearrange("d (h t) -> d h t", h=8), QrT_t[n], QrT_t[n][:].rearrange("(h d) t -> d h t", d=64))
            P.dma(k_, k_[:, :].rearrange("d (h t) -> d h t", h=8), KrT_t[n], KrT_t[n][:].rearrange("(h d) t -> d h t", d=64))
            P.dma(z_, z_[:, :], kz_t[n], kz_t[n][:, :])
            P.dma(vt, vt[:, :], vr_t[n], vr_t[n][:, :])
            P.dma(sgr, sgr[:, :], sgr_t[n], sgr_t[n][:, :])
            for h in range(8):
                b = pb[h // 4]
                mm(b, b[:, (h % 4) * 128:(h % 4 + 1) * 128], k_, k_[:, h * 128:(h + 1) * 128], q_, q_[:, h * 128:(h + 1) * 128], True, True)
            for hh in range(2):
                P.op("dve", lambda e, hh=hh: e.tensor_tensor(out=ATs[:, hh * 512:(hh + 1) * 512], in0=pb[hh][:], in1=dec[hh][:, :],
                                                             op=ALU.mult), r=[pb[hh], dec[hh]], w=[ATs])
            for h in range(8):
                b = pb[2 + h // 4]
                mm(b, b[:, (h % 4) * 128:(h % 4 + 1) * 128], ATs, ATs[:, h * 128:(h + 1) * 128], vt, vt[:, h * 128:(h + 1) * 128], True, True)
            if n > 0:
                for h in range(8):
                    b = pb[4 + h // 4]
                    mm(b, b[:, (h % 4) * 128:(h % 4 + 1) * 128], q_, q_[:, h * 128:(h + 1) * 128], Rbf, Rbf[0:64, h * 128:(h + 1) * 128], True, True)
            for hh in range(2):
                P.op("act", lambda e, hh=hh: e.copy(out=osb[:, hh * 512:(hh + 1) * 512], in_=pb[2 + hh][:]), r=[pb[2 + hh]], w=[osb])
            if n > 0:
                for hh in range(2):
                    P.op("dve", lambda e, hh=hh: e.tensor_tensor(
                        out=tmp3[:, hh * 4:(hh + 1) * 4, :], in0=pb[4 + hh][:].rearrange("p (h e) -> p h e", h=4),
                        in1=xit[:, hh * 4:(hh + 1) * 4].unsqueeze(2).broadcast_to([128, 4, 128]), op=ALU.mult),
                        r=[pb[4 + hh], xit], w=[tmp])
                P.op("pool", lambda e: e.tensor_tensor(out=osb[:, :], in0=osb[:, :], in1=tmp[:, :], op=ALU.add), r=[osb, tmp], w=[osb])
            for hh in range(2):
                b = pb[6]
                for h4 in range(4):
                    h = hh * 4 + h4
                    mm(b, b[0:64, h4 * 128:(h4 + 1) * 128], z_, z_[:, h * 64:(h + 1) * 64], vt, vt[:, h * 128:(h + 1) * 128], True, True)
                for h4 in range(4):
                    h = hh * 4 + h4
                    P.op("dve", lambda e, h=h, h4=h4, b=b: e.scalar_tensor_tensor(
                        out=Rst[0:64, h * 128:(h + 1) * 128], in0=Rst[0:64, h * 128:(h + 1) * 128], scalar=G_CHUNK[h],
                        in1=b[0:64, h4 * 128:(h4 + 1) * 128], op0=ALU.mult, op1=ALU.add), r=[b, Rst], w=[Rst])
            P.op("act", lambda e: e.copy(out=Rbf[0:64, :], in_=Rst[0:64, :]), r=[Rst], w=[Rbf])
            P.op("dve", lambda e: e.tensor_reduce(out=stt[:, 0:8], in_=osb3, axis=AX.X, op=ALU.add), r=[osb], w=[stt])
            P.op("act", lambda e: e.activation(out=sq[:, :], in_=osb[:, :], func=AF.Square), r=[osb], w=[sq])
            P.op("dve", lambda e: e.tensor_reduce(out=stt[:, 8:16], in_=sq3, axis=AX.X, op=ALU.add), r=[sq], w=[stt])
            P.op("dve", lambda e: e.tensor_scalar(out=stt[:, 0:8], in0=stt[:, 0:8], scalar1=1.0 / 128, scalar2=None, op0=ALU.mult),
                 r=[stt], w=[stt])
            P.op("dve", lambda e: e.tensor_tensor(out=stt[:, 16:24], in0=stt[:, 0:8], in1=stt[:, 0:8], op=ALU.mult), r=[stt], w=[stt])
            P.op("dve", lambda e: e.scalar_tensor_tensor(out=stt[:, 24:32], in0=stt[:, 8:16], scalar=1.0 / 128, in1=stt[:, 16:24],
                                                         op0=ALU.mult, op1=ALU.subtract), r=[stt], w=[stt])
            P.op("dve", lambda e: e.tensor_scalar(out=stt[:, 24:32], in0=stt[:, 24:32], scalar1=1e-5, scalar2=None, op0=ALU.add),
                 r=[stt], w=[stt])
            P.op("act", lambda e: e.sqrt(out=stt[:, 32:40], in_=stt[:, 24:32]), r=[stt], w=[stt])
            P.op("dve", lambda e: e.reciprocal(out=stt[:, 40:48], in_=stt[:, 32:40]), r=[stt], w=[stt])
            P.op("dve", lambda e: e.tensor_tensor(out=osb3, in0=osb3, in1=stt[:, 0:8].unsqueeze(2).broadcast_to([128, 8, 128]),
                                                  op=ALU.subtract), r=[osb, stt], w=[osb])
            P.op("pool", lambda e: e.tensor_tensor(out=osb3, in0=osb3, in1=stt[:, 40:48].unsqueeze(2).broadcast_to([128, 8, 128]),
                                                   op=ALU.mult), r=[osb, stt], w=[osb])
            P.op("dve", lambda e: e.tensor_tensor(out=osb[:, :], in0=osb[:, :], in1=gain2[:, :], op=ALU.mult), r=[osb, gain2], w=[osb])
            P.op("pool", lambda e: e.tensor_tensor(out=rob[:, :], in0=osb[:, :], in1=sgr[:, :], op=ALU.mult), r=[osb, sgr], w=[rob])
            transpose8(rob, hT, hT[:, :, 0:128], eng="act")
            dst = rT_t[n]
            P.dma(dst, dst[:].rearrange("(k r) t -> r k t", r=128), hT, hT[:, :, 0:128], q="pool")
        uncarve(big, subs)

    def merge_phase(slot):
        Wn = load_w(slot, 0, I["w_branch_nsa"], I["w_branch_nsa"].ap.rearrange("(k p) f -> p k f", p=128), [128, 8, 1024])
        Wr = load_w(slot, 8192, I["w_branch_ret"], I["w_branch_ret"].ap.rearrange("(k p) f -> p k f", p=128), [128, 8, 1024])
        Wo = load_w(slot, 16384, I["w_out"], I["w_out"].ap.rearrange("(k p) f -> p k f", p=128), [128, 8, 1024])
        yield
        subs = []
        bufs = []
        for i in range(2):
            a_ = carve(big, big.ap[:, i * 2048:i * 2048 + 1024], f"aTt{i}")
            r_ = carve(big, big.ap[:, i * 2048 + 1024:i * 2048 + 2048], f"rTt{i}")
            subs += [a_, r_]
            bufs.append((a_, r_))
        for ti in range(NT if lim is None else lim):
            a_, r_ = bufs[ti % 2]
            a3 = a_[:, :].rearrange("p (k t) -> p k t", k=8)
            r3 = r_[:, :].rearrange("p (k t) -> p k t", k=8)
            P.dma(a_, a3, aT_t[ti], aT_t[ti][:].rearrange("(k r) t -> r k t", r=128))
            P.dma(r_, r3, rT_t[ti], rT_t[ti][:].rearrange("(k r) t -> r k t", r=128))
            ga, gr = wk32[0], wk32[1]
            P.dma(ga, ga[:, :], gm_t[ti], gm_t[ti][:, 0:1024])
            P.dma(gr, gr[:, :], gm_t[ti], gm_t[ti][:, 1024:2048])
            x1s = wk32[2]
            P.dma(x1s, x1s[:, :], x1_t[ti], x1_t[ti][:, :])
            for half in range(2):
                for k in range(8):
                    mm(pb[half], pb[half][:], a_, a3[:, k, :], slot, Wn[:, k, half * 512:(half + 1) * 512], k == 0, k == 7)
                for k in range(8):
                    mm(pb[2 + half], pb[2 + half][:], r_, r3[:, k, :], slot, Wr[:, k, half * 512:(half + 1) * 512], k == 0, k == 7)
            m1, m2 = xin[0], xin[1]
            mb = hb[ti % 2]
            for half in range(2):
                cs = slice(half * 512, (half + 1) * 512)
                P.op("dve", lambda e, half=half, cs=cs: e.tensor_tensor(out=m1[:, cs], in0=pb[half][:], in1=ga[:, cs], op=ALU.mult),
                     r=[pb[half], ga], w=[m1])
                P.op("dve", lambda e, half=half, cs=cs: e.tensor_tensor(out=m2[:, cs], in0=pb[2 + half][:], in1=gr[:, cs], op=ALU.mult),
                     r=[pb[2 + half], gr], w=[m2])
            P.op("pool", lambda e, mb=mb: e.tensor_tensor(out=mb[:, :], in0=m1[:, :], in1=m2[:, :], op=ALU.add), r=[m1, m2], w=[mb])
            transpose8(mb, hT, hT[:, :, 0:128], eng="act")
            for half in range(2):
                po = pb[4 + half]
                for k in range(8):
                    mm(po, po[:], hT, hT[:, k, 0:128], slot, Wo[:, k, half * 512:(half + 1) * 512], k == 0, k == 7)
                P.op("dve", lambda e, po=po, half=half: e.tensor_tensor(out=x1s[:, half * 512:(half + 1) * 512], in0=po[:],
                                                                       in1=x1s[:, half * 512:(half + 1) * 512], op=ALU.add),
                     r=[po, x1s], w=[x1s])
            P.dma(x2_t[ti], x2_t[ti][:, :], x1s, x1s[:, :], q="pool")
        uncarve(big, subs)

    def begin(gen):
        next(gen)
        return gen

    def finish(gen):
        for _ in gen:
            pass

    allp = set(phases) == {"ffn1", "proj", "cmp", "nsa", "ret", "merge", "ffn2"} and not _os.environ.get("SKIPP1")
    out_t = [T(out_d.ap[i * 128:(i + 1) * 128], f"out{i}") for i in range(NT)]
    if allp:
        f1a = begin(ffn_pass(1, 0, wslot[0], x_t, x1_t, I["ffn1_norm"]))
        f1b = begin(ffn_pass(1, 1, wslot[1], x1_t, x1_t, I["ffn1_norm"]))
        finish(f1a)
        p0 = begin(proj_pass0(wslot[0]))
        finish(f1b)
        p1 = begin(proj_pass1(wslot[1]))
        finish(p0)
        finish(p1)
        cmp_phase(wslot[1])
        mg = begin(merge_phase(wslot[1]))
        nsa_phase(wslot[0])
        f2a = begin(ffn_pass(2, 0, wslot[0], x2_t, x2_t, I["ffn2_norm"]))
        ret_phase()
        finish(mg)
        f2b = begin(ffn_pass(2, 1, wslot[1], x2_t, out_t, I["ffn2_norm"], final=True))
        finish(f2a)
        finish(f2b)
    else:
        if "ffn1" in phases:
            finish(ffn_pass(1, 0, wslot[0], x_t, x1_t, I["ffn1_norm"]))
            finish(ffn_pass(1, 1, wslot[1], x1_t, x1_t, I["ffn1_norm"]))
        if "proj" in phases:
            finish(proj_pass0(wslot[0]))
            if not _os.environ.get("SKIPP1"):
                finish(proj_pass1(wslot[1]))
        if "cmp" in phases:
            cmp_phase(wslot[1])
        if "nsa" in phases:
            nsa_phase(wslot[0])
        if "ret" in phases:
            ret_phase()
        if "merge" in phases:
            finish(merge_phase(wslot[1]))
        if "ffn2" in phases:
            finish(ffn_pass(2, 0, wslot[0], x2_t, x2_t, I["ffn2_norm"]))
            finish(ffn_pass(2, 1, wslot[1], x2_t, out_t, I["ffn2_norm"], final=True))
    P.emit()
    return nc, P


_CACHE = {}


def kernel(**inputs):
    n_cores = 8
    if "nc" not in _CACHE:
        _CACHE["nc"] = build()[0]
        _CACHE["consts"] = {k: np.ascontiguousarray(v.reshape(CONST_SHAPES[k]).astype(np.float32))
                            for k, v in _consts().items()}
    nc = _CACHE["nc"]
    cst = _CACHE["consts"]
    shared = {}
    for k, shp in IN_SHAPES.items():
        if k == "x":
            continue
        shared[k] = np.ascontiguousarray(np.asarray(inputs[k], dtype=np.float32).reshape(shp))
    x = np.asarray(inputs["x"], dtype=np.float32)
    in_maps = []
    for c in range(n_cores):
        m = dict(shared)
        m.update(cst)
        m["x"] = np.ascontiguousarray(x[c % 4])
        in_maps.append(m)
    res = run_bass_kernel_spmd(nc, in_maps, core_ids=list(range(n_cores)))
    out = np.stack([np.asarray(res.results[b]["out"], dtype=np.float32) for b in range(4)], axis=0)
    return out
```

```python
import contextlib
import os as _os
import numpy as np
import concourse.bass as bass
import concourse.mybir as mybir
from concourse.bass_utils import run_bass_kernel_spmd

F32 = mybir.dt.float32
BF16 = mybir.dt.bfloat16
AF = mybir.ActivationFunctionType
ALU = mybir.AluOpType
AX = mybir.AxisListType

S = 4096
D = 1024
DFF = 2816
NT = S // 128
DIN = 6960


class T:
    __slots__ = ("ap", "name", "last_w", "readers", "excl")

    def __init__(self, ap, name="", excl=False):
        self.ap = ap
        self.name = name
        self.last_w = None
        self.readers = []
        self.excl = excl

    def __getitem__(self, k):
        return self.ap[k]


class Prog:
    ENGS = ("pe", "act", "dve", "pool", "sp")

    def __init__(self, nc, n_dma_sems=12, same_engine_sync=True):
        self.nc = nc
        self.ops = []
        self.es = contextlib.ExitStack()
        self.same_engine_sync = same_engine_sync
        self.n_dma_sems = n_dma_sems
        self.ncnt = 0

    def sb(self, shape, dt, name=None):
        self.ncnt += 1
        name = name or f"sb{self.ncnt}"
        t = self.es.enter_context(self.nc.sbuf_tensor(name, list(shape), dt))
        return T(t[:], name)

    def ps(self, shape, dt, name=None):
        self.ncnt += 1
        name = name or f"ps{self.ncnt}"
        t = self.es.enter_context(self.nc.psum_tensor(name, list(shape), dt))
        return T(t[:], name, excl=True)

    def dram(self, name, shape, dt, kind="Internal"):
        t = self.nc.dram_tensor(name, list(shape), dt, kind=kind)
        return T(t.ap(), name)

    def op(self, eng, fn, r=(), w=(), dma=False):
        idx = len(self.ops)
        deps = set()
        raw = set()
        for t in r:
            if t.last_w is not None:
                deps.add(t.last_w)
                raw.add(t.last_w)
            if t.excl:
                for rd in t.readers:
                    if self.ops[rd]["eng"] != eng:
                        deps.add(rd)
        for t in w:
            if t.last_w is not None:
                deps.add(t.last_w)
            deps.update(t.readers)
        for t in r:
            t.readers.append(idx)
        for t in w:
            t.last_w = idx
            t.readers = []
        deps.discard(idx)
        self.ops.append({"eng": eng, "fn": fn, "deps": deps, "dma": dma, "raw": raw})
        return idx

    def dma(self, out_t, out_ap, in_t, in_ap, q="sp", **kw):
        return self.op(q, lambda e: e.dma_start(out=out_ap, in_=in_ap, **kw),
                       r=[in_t], w=[out_t], dma=True)

    def emit(self):
        nc = self.nc
        ops = self.ops
        nops = len(ops)
        dma_use = {}
        for i, o in enumerate(ops):
            if o["dma"]:
                dma_use.setdefault(o["eng"], []).append(i)
        dma_sem = {}
        for q, lst in dma_use.items():
            for k, i in enumerate(lst):
                slot = k % self.n_dma_sems
                val = 16 * (k // self.n_dma_sems + 1)
                dma_sem[i] = (q, slot, val)
                if k >= self.n_dma_sems:
                    ops[i]["deps"].add(lst[k - self.n_dma_sems])
        waited = {e: {} for e in self.ENGS}
        waited_dma = {e: set() for e in self.ENGS}
        plan = [None] * nops
        signaling = set()
        for i, o in enumerate(ops):
            e = o["eng"]
            need = {}
            need_dma = []
            for d in sorted(o["deps"]):
                po = ops[d]
                if po["dma"]:
                    if d not in waited_dma[e]:
                        waited_dma[e].add(d)
                        need_dma.append(d)
                    continue
                p = po["eng"]
                if p == e and (e == "pe" or not self.same_engine_sync or d not in o["raw"]):
                    continue
                if waited[e].get(p, -1) >= d:
                    continue
                need[p] = max(need.get(p, -1), d)
            for p, d in need.items():
                waited[e][p] = d
                signaling.add(d)
            plan[i] = (need, need_dma)
        semval = {}
        cnt = {e: 0 for e in self.ENGS}
        for i, o in enumerate(ops):
            if o["dma"]:
                continue
            if i in signaling:
                cnt[o["eng"]] += 1
                semval[i] = cnt[o["eng"]]
        self.stats = {e: sum(1 for o in ops if o["eng"] == e) for e in self.ENGS}
        self.stats["signals"] = dict(cnt)
        es = self.es
        sems = {e: es.enter_context(nc.semaphore(f"sem_{e}")) for e in self.ENGS}
        dsems = {q: [es.enter_context(nc.semaphore(f"dsem_{q}{k}")) for k in range(self.n_dma_sems)]
                 for q in dma_use}
        block = es.enter_context(nc.Block())

        def body(ename):
            def f(eng):
                for i, o in enumerate(ops):
                    if o["eng"] != ename:
                        continue
                    need, need_dma = plan[i]
                    for p, d in need.items():
                        eng.wait_ge(sems[p], semval[d])
                    for d in need_dma:
                        q, slot, val = dma_sem[d]
                        eng.wait_ge(dsems[q][slot], val)
                    ins = o["fn"](eng)
                    if o["dma"]:
                        q, slot, val = dma_sem[i]
                        ins.then_inc(dsems[q][slot], 16)
                    elif i in signaling:
                        ins.then_inc(sems[ename], 1)
                for q, lst in dma_use.items():
                    if q != ename:
                        continue
                    last = {}
                    for i in lst:
                        _, slot, val = dma_sem[i]
                        last[slot] = val
                    for slot, val in last.items():
                        eng.wait_ge(dsems[q][slot], val)
            return f

        block.tensor(body("pe"))
        block.scalar(body("act"))
        block.vector(body("dve"))
        block.gpsimd(body("pool"))
        block.sync(body("sp"))

    def close(self):
        self.es.close()


def _consts():
    c = {}
    pos = np.arange(S, dtype=np.float32)

    def rope_tabs(p, rot, theta, nrep):
        half = rot // 2
        fr = (np.float32(theta) ** (-(np.arange(half, dtype=np.float32) * np.float32(2.0) / np.float32(rot)))).astype(np.float32)
        ang = p.astype(np.float32)[:, None] * fr[None, :]
        cs, sn = np.cos(ang).astype(np.float32), np.sin(ang).astype(np.float32)
        C = np.ones((len(p), 64), np.float32)
        C[:, 0:half] = cs
        C[:, half:rot] = cs
        Sg = np.concatenate([-sn, sn], 1)
        return np.tile(C, (1, nrep)), np.tile(Sg, (1, nrep))

    c["ropeCa"], c["ropeSa"] = rope_tabs(pos, 16, 500000.0, 8)
    pc = np.arange(256, dtype=np.float32) * 16 + 31
    c["ropeCc"], c["ropeSc"] = rope_tabs(pc, 16, 500000.0, 1)
    c["ropeCr"], c["ropeSr"] = rope_tabs(pos, 64, 10000.0, 8)
    lg = np.log(1.0 - 2.0 ** (-5.0 - np.arange(8, dtype=np.float64)))
    i = np.arange(128, dtype=np.float64)
    diff = i[None, :] - i[:, None]
    dT = np.zeros((128, 8, 128), np.float64)
    for h in range(8):
        dT[:, h, :] = np.where(diff >= 0, np.exp(np.maximum(diff, 0) * lg[h]), 0.0) * 0.125
    c["decayT"] = dT.astype(np.float32).reshape(128, 1024)
    zeta = np.exp((127.0 - i)[None, :] * lg[:, None])
    c["ZT"] = np.repeat(zeta.T[:, :, None], 64, axis=2).reshape(128, 512).astype(np.float32)
    xi = np.exp((i + 1.0)[None, :] * lg[:, None]) * 0.125
    c["XI"] = np.ascontiguousarray(xi.T).astype(np.float32)
    gch = np.exp(128.0 * lg)
    c["GC"] = np.broadcast_to(gch[None, :, None], (64, 8, 128)).reshape(64, 1024).astype(np.float32)
    n = np.arange(256)
    cs_ = n * 16
    ss_ = np.arange(64) * 64
    ov = np.clip(np.minimum(cs_[:, None] + 32, ss_[None, :] + 64) - np.maximum(cs_[:, None], ss_[None, :]), 0, None)
    c2s = ov.astype(np.float32) / 32.0
    c2s[255] = 0.0
    c["C2S"] = c2s
    key = np.arange(S)
    c["EM"] = (key[None, :] // 64 == np.arange(64)[:, None]).astype(np.float32)
    sp = np.arange(128) - 64
    curp = (np.arange(128) >= 64).astype(np.int64)[:, None]
    forced = (sp[None, :] == curp) | (sp[None, :] == curp - 1)
    future = sp[None, :] > curp
    c["TK"] = (~(forced | future)).astype(np.float32)
    c["TA"] = np.where(forced, 1.0e4, np.where(future, -1.0e4, 0.0)).astype(np.float32)
    return c


CONST_SHAPES = {"ropeCa": [S, 512], "ropeSa": [S, 128], "ropeCc": [256, 64], "ropeSc": [256, 16],
                "ropeCr": [S, 512], "ropeSr": [S, 512], "decayT": [128, 1024], "ZT": [128, 512],
                "XI": [128, 8], "GC": [64, 1024], "C2S": [256, 64], "EM": [64, S],
                "TK": [128, 128], "TA": [128, 128]}

IN_SHAPES = {"x": [S, D], "ffn1_norm": [1, D], "ffn1_w_gate": [D, DFF], "ffn1_w_up": [D, DFF],
             "ffn1_w_down": [DFF, D], "mix_norm": [1, D], "w_in": [D, DIN], "cmp_pos_emb": [32, 64],
             "cmp_k_w1": [2048, 256], "cmp_k_w2": [256, 64], "cmp_v_w1": [2048, 256], "cmp_v_w2": [256, 64],
             "ret_gn_gain": [1, 1024], "w_branch_nsa": [D, D], "w_branch_ret": [D, D], "w_out": [D, D],
             "ffn2_norm": [1, D], "ffn2_w_gate": [D, DFF], "ffn2_w_up": [D, DFF], "ffn2_w_down": [DFF, D],
             "final_norm": [1, D]}

G_CHUNK = [float(np.exp(128.0 * np.log(1.0 - 2.0 ** (-5.0 - h)))) for h in range(8)]
NEGBIG = -2.0e4


def build(phases=("ffn1", "proj", "cmp", "nsa", "ret", "merge", "ffn2"), dbg=(), lim=None):
    nc = bass.Bass("TRN2", target_bir_lowering=False)
    P = Prog(nc)
    lim_c = [0, 5, 17] if lim else None
    I = {k: P.dram(k, v, F32, kind="ExternalInput") for k, v in IN_SHAPES.items()}
    C = {k: P.dram(k, v, F32, kind="ExternalInput") for k, v in CONST_SHAPES.items()}
    out_d = P.dram("out", [S, D], F32, kind="ExternalOutput")

    def scratch(name, shape, dt):
        return P.dram(name, shape, dt, kind=("ExternalOutput" if name in dbg else "Internal"))

    def tiles(t, n=NT, rows=128):
        return [T(t.ap[i * rows:(i + 1) * rows], f"{t.name}{i}") for i in range(n)]

    def ctiles(t, n=NT, cols=128):
        return [T(t.ap[..., i * cols:(i + 1) * cols], f"{t.name}{i}") for i in range(n)]

    x_t = tiles(I["x"])
    x1_d = scratch("x1_d", [S, D], F32); x1_t = tiles(x1_d)
    x2_d = scratch("x2_d", [S, D], F32); x2_t = tiles(x2_d)
    hT_d = scratch("hT_d", [8, 128, 8, 512], BF16)
    hT_t = [T(hT_d.ap[i], f"hT{i}") for i in range(8)]
    hmT_d = scratch("hmT_d", [NT, 128, 8, 128], BF16)
    hmT_t = [T(hmT_d.ap[i], f"hmT{i}") for i in range(NT)]
    QT_d = scratch("QT_d", [2, 512, S], BF16)
    QT_t = [ctiles(T(QT_d.ap[g], f"QT{g}_")) for g in range(2)]
    kcT_d = scratch("kcT_d", [128, S], BF16); kcT_t = ctiles(kcT_d)
    vcT_d = scratch("vcT_d", [128, S], BF16); vcT_t = ctiles(vcT_d)
    ksT_d = scratch("ksT_d", [128, S], BF16); ksT_t = ctiles(ksT_d)
    kwT_d = scratch("kwT_d", [128, S], BF16); kwT_t = ctiles(kwT_d)
    vs_d = scratch("vs_d", [S, 128], BF16); vs_t = tiles(vs_d)
    vw_d = scratch("vw_d", [S, 128], BF16); vw_t = tiles(vw_d)
    ga_d = scratch("ga_d", [S, 48], F32); ga_t = tiles(ga_d)
    QrT_d = scratch("QrT_d", [512, S], BF16); QrT_t = ctiles(QrT_d)
    KrT_d = scratch("KrT_d", [512, S], BF16); KrT_t = ctiles(KrT_d)
    kz_d = scratch("kz_d", [S, 512], BF16); kz_t = tiles(kz_d)
    vr_d = scratch("vr_d", [S, 1024], BF16); vr_t = tiles(vr_d)
    sgr_d = scratch("sgr_d", [S, 1024], F32); sgr_t = tiles(sgr_d)
    gm_d = scratch("gm_d", [S, 2048], F32); gm_t = tiles(gm_d)
    kc_d = scratch("kc_d", [2, 256, 64], BF16)
    vc_d = scratch("vc_d", [2, 256, 64], BF16)
    aT_d = scratch("aT_d", [1024, S], BF16); aT_t = ctiles(aT_d)
    rT_d = scratch("rT_d", [1024, S], BF16); rT_t = ctiles(rT_d)

    WSLOT = 33792
    wslot = [P.sb([128, WSLOT], BF16, f"wslot{i}") for i in range(2)]
    ident = P.sb([128, 128], BF16, "ident")
    gain = P.sb([128, 1024], F32, "gain")
    gain2 = P.sb([128, 1024], F32, "gain2")
    xin = [P.sb([128, 1024], F32, f"xin{i}") for i in range(2)]
    xres = xin
    hb = [P.sb([128, 1024], BF16, f"hb{i}") for i in range(2)]
    st1 = [P.sb([128, 4], F32, f"st1_{i}") for i in range(2)]
    big = P.sb([128, 5632], BF16, "big")
    hT = P.sb([128, 8, 512], BF16, "hT")
    wk32 = [P.sb([128, 1024], F32, f"wk32_{i}") for i in range(3)]
    wk16 = [P.sb([128, 1024], BF16, f"wk16_{i}") for i in range(3)]
    sm = [P.sb([128, 64], F32, f"sm{i}") for i in range(8)]
    tCa = P.sb([128, 512], F32, "tCa"); tSa = P.sb([128, 128], F32, "tSa")
    tCr = P.sb([128, 512], F32, "tCr"); tSr = P.sb([128, 512], F32, "tSr")
    tZ = P.sb([128, 512], F32, "tZ")
    pb = [P.ps([128, 512], F32, f"pb{i}") for i in range(7)]
    pbf = P.ps([128, 1024], BF16, "pbf")

    P.op("pool", lambda e: e.memset(ident[:], 0.0), w=[ident])
    P.op("pool", lambda e: e.affine_select(out=ident[:], in_=ident[:], pattern=[[-1, 128]], compare_op=ALU.not_equal,
                                           fill=1.0, base=0, channel_multiplier=1), r=[ident], w=[ident])

    cnt = {"rr": 0}

    def rr(lst):
        cnt["rr"] += 1
        return lst[cnt["rr"] % len(lst)]

    def mm(out_t, out_ap, a_t, a_ap, b_t, b_ap, start, stop, sgc=False):
        P.op("pe", lambda e: e.matmul(out_ap, lhsT=a_ap, rhs=b_ap, start=start, stop=stop, skip_group_check=sgc),
             r=[a_t, b_t], w=[out_t])

    def tr(out_t, out_ap, in_t, in_ap):
        P.op("pe", lambda e: e.transpose(out=out_ap, in_=in_ap, identity=ident[:]), r=[in_t, ident], w=[out_t])

    def load_w(slot_t, off, src_t, src_ap, shape):
        n = int(np.prod(shape[1:]))
        dst = slot_t[:, off:off + n]
        if len(shape) == 3:
            dst = dst.rearrange("p (a b) -> p a b", a=shape[1])
        P.dma(slot_t, dst, src_t, src_ap, q="pool")
        return dst

    def rmsnorm(x_tile, gain_t, hb_t, stt):
        sq = wk32[2]
        P.op("act", lambda e: e.activation(out=sq[:], in_=x_tile[:], func=AF.Square, accum_out=stt[:, 0:1]),
             r=[x_tile], w=[sq, stt])
        P.op("dve", lambda e: e.tensor_scalar(out=stt[:, 1:2], in0=stt[:, 0:1], scalar1=1.0 / D, scalar2=1e-6,
                                              op0=ALU.mult, op1=ALU.add), r=[stt], w=[stt])
        P.op("act", lambda e: e.sqrt(out=stt[:, 2:3], in_=stt[:, 1:2]), r=[stt], w=[stt])
        P.op("dve", lambda e: e.reciprocal(out=stt[:, 3:4], in_=stt[:, 2:3]), r=[stt], w=[stt])
        P.op("dve", lambda e: e.scalar_tensor_tensor(out=hb_t[:], in0=x_tile[:], scalar=stt[:, 3:4], in1=gain_t[:],
                                                     op0=ALU.mult, op1=ALU.mult), r=[x_tile, stt, gain_t], w=[hb_t])

    def transpose8(src_t, dst_t, dst_ap, nblk=8, eng="act"):
        for k in range(nblk):
            tr(pbf, pbf[:, k * 128:(k + 1) * 128], src_t, src_t[:, k * 128:(k + 1) * 128])
        src = pbf[:, 0:nblk * 128].rearrange("p (k t) -> p k t", k=nblk)
        if eng == "act":
            P.op("act", lambda e: e.copy(out=dst_ap, in_=src), r=[pbf], w=[dst_t])
        else:
            P.op("dve", lambda e: e.tensor_copy(out=dst_ap, in_=src), r=[pbf], w=[dst_t])

    def load_gain(gt, src):
        P.dma(gt, gt[:], src, src.ap.partition_broadcast(128))

    def carve(parent, ap, name=""):
        t = T(ap, name)
        t.last_w = parent.last_w
        t.readers = list(parent.readers)
        return t

    def uncarve(parent, subs):
        for t in subs:
            parent.readers.extend(t.readers)
            if t.last_w is not None:
                parent.readers.append(t.last_w)

    def ffn_pass(which, hf, slot, xsrc_t, xdst_t, norm_gain, final=False):
        pre = f"ffn{which}_"
        wg_d, wu_d, wd_d = I[pre + "w_gate"], I[pre + "w_up"], I[pre + "w_down"]
        c0 = hf * 1408
        wg = load_w(slot, 0, wg_d, wg_d.ap[:, c0:c0 + 1408].rearrange("(k p) f -> p k f", p=128), [128, 8, 1408])
        wu = load_w(slot, 11264, wu_d, wu_d.ap[:, c0:c0 + 1408].rearrange("(k p) f -> p k f", p=128), [128, 8, 1408])
        wd = load_w(slot, 22528, wd_d, wd_d.ap[c0:c0 + 1408, :].rearrange("(k p) d -> p k d", p=128), [128, 11, 1024])
        yield
        aT = big[:, 0:11 * 512].rearrange("p (f t) -> p f t", f=11)
        if hf == 0:
            load_gain(gain, norm_gain)
        if final:
            load_gain(gain2, I["final_norm"])
        for st in range(8 if lim is None else 1):
            if hf == 0:
                for tt in range(4):
                    ti = st * 4 + tt
                    xt = xin[ti % 2]
                    P.dma(xt, xt[:], xsrc_t[ti], xsrc_t[ti][:, :])
                    hbt = hb[ti % 2]
                    rmsnorm(xt, gain, hbt, st1[ti % 2])
                    transpose8(hbt, hT, hT[:, :, tt * 128:(tt + 1) * 128], eng="act")
                P.dma(hT_t[st], hT_t[st][:], hT, hT[:], q="pool")
            else:
                P.dma(hT, hT[:], hT_t[st], hT_t[st][:])
            for f in range(11):
                pg = pb[(2 * f) % 4]
                pu = pb[(2 * f + 1) % 4]
                for k in range(8):
                    mm(pg, pg[:], slot, wg[:, k, f * 128:(f + 1) * 128], hT, hT[:, k, :], k == 0, k == 7)
                for k in range(8):
                    mm(pu, pu[:], slot, wu[:, k, f * 128:(f + 1) * 128], hT, hT[:, k, :], k == 0, k == 7)
                sg = wk32[f % 2]
                P.op("act", lambda e, sg=sg, pg=pg: e.activation(out=sg[:, 0:512], in_=pg[:], func=AF.Silu),
                     r=[pg], w=[sg])
                P.op("dve", lambda e, sg=sg, pu=pu, f=f: e.tensor_tensor(out=aT[:, f, :], in0=pu[:], in1=sg[:, 0:512],
                                                                        op=ALU.mult), r=[pu, sg], w=[big])
            for tt in range(4):
                ti = st * 4 + tt
                xr = xres[ti % 2]
                P.dma(xr, xr[:], xsrc_t[ti], xsrc_t[ti][:, :])
                for half in range(2):
                    py = pb[4 + (2 * tt + half) % 3]
                    for f in range(11):
                        mm(py, py[:], big, aT[:, f, tt * 128:(tt + 1) * 128], slot, wd[:, f, half * 512:(half + 1) * 512],
                           f == 0, f == 10)
                    P.op("dve", lambda e, xr=xr, py=py, half=half: e.scalar_tensor_tensor(
                        out=xr[:, half * 512:(half + 1) * 512], in0=py[:], scalar=0.5,
                        in1=xr[:, half * 512:(half + 1) * 512], op0=ALU.mult, op1=ALU.add), r=[py, xr], w=[xr])
                if final:
                    ob = wk32[ti % 2]
                    stt = st1[ti % 2]
                    sq = wk32[2]
                    P.op("act", lambda e, xr=xr, stt=stt: e.activation(out=sq[:], in_=xr[:], func=AF.Square,
                                                                       accum_out=stt[:, 0:1]), r=[xr], w=[sq, stt])
                    P.op("dve", lambda e, stt=stt: e.tensor_scalar(out=stt[:, 1:2], in0=stt[:, 0:1], scalar1=1.0 / D,
                                                                   scalar2=1e-6, op0=ALU.mult, op1=ALU.add),
                         r=[stt], w=[stt])
                    P.op("act", lambda e, stt=stt: e.sqrt(out=stt[:, 2:3], in_=stt[:, 1:2]), r=[stt], w=[stt])
                    P.op("dve", lambda e, stt=stt: e.reciprocal(out=stt[:, 3:4], in_=stt[:, 2:3]), r=[stt], w=[stt])
                    P.op("dve", lambda e, xr=xr, stt=stt, ob=ob: e.scalar_tensor_tensor(
                        out=ob[:], in0=xr[:], scalar=stt[:, 3:4], in1=gain2[:], op0=ALU.mult, op1=ALU.mult),
                         r=[xr, stt, gain2], w=[ob])
                    P.dma(xdst_t[ti], xdst_t[ti][:, :], ob, ob[:], q="pool")
                else:
                    P.dma(xdst_t[ti], xdst_t[ti][:, :], xr, xr[:], q="pool")

    def rope(src_t, src_ap, H, R, Ct, Cap, St, Sap, out_t, out_ap, o32, t2):
        n = H * 64
        half = R // 2
        P.op("dve", lambda e: e.tensor_tensor(out=o32[:, 0:n], in0=src_ap, in1=Cap, op=ALU.mult), r=[Ct, src_t], w=[o32])
        s3 = src_ap.rearrange("p (h d) -> p h d", h=H)
        S3 = Sap.rearrange("p (h r) -> p h r", h=H)
        t3 = t2[:, 0:H * R].rearrange("p (h r) -> p h r", h=H)
        o3 = o32[:, 0:n].rearrange("p (h d) -> p h d", h=H)
        P.op("dve", lambda e: e.tensor_tensor(out=t3[:, :, 0:half], in0=s3[:, :, half:R], in1=S3[:, :, 0:half],
                                              op=ALU.mult), r=[St, src_t], w=[t2])
        P.op("dve", lambda e: e.tensor_tensor(out=t3[:, :, half:R], in0=s3[:, :, 0:half], in1=S3[:, :, half:R],
                                              op=ALU.mult), r=[St, src_t], w=[t2])
        P.op("pool", lambda e: e.tensor_tensor(out=o3[:, :, 0:R], in0=o3[:, :, 0:R], in1=t3, op=ALU.add),
             r=[o32, t2], w=[o32])
        P.op("act", lambda e: e.copy(out=out_ap, in_=o32[:, 0:n]), r=[o32], w=[out_t])

    def proj_pass0(slot):
        w_d = I["w_in"]
        w0 = load_w(slot, 0, w_d, w_d.ap[:, 0:1944].rearrange("(k p) f -> p k f", p=128), [128, 8, 1944])
        w1 = load_w(slot, 8 * 1944, w_d, w_d.ap[:, 1944:3888].rearrange("(k p) f -> p k f", p=128), [128, 8, 1944])
        yield

        def wcol(k, c0, c1):
            if c1 <= 1944:
                return w0[:, k, c0:c1]
            assert c0 >= 1944
            return w1[:, k, c0 - 1944:c1 - 1944]

        load_gain(gain, I["mix_norm"])
        P.dma(tZ, tZ[:], C["ZT"], C["ZT"][:, :])
        qTs = [P.sb([128, 4, 128], BF16, "qTa"), P.sb([128, 4, 128], BF16, "qTb")]
        blocks = [(0, 512), (512, 1024), (1024, 1536), (1536, 1840), (1840, 2352), (2352, 2864), (2864, 3376), (3376, 3888)]
        subs_c = []
        hbuf = [carve(hT, hT.ap[:, :, 0:128], "hTa"), carve(hT, hT.ap[:, :, 128:256], "hTb")]
        o32s = [carve(wk32[0], wk32[0].ap[:, 0:512], "o32a"), carve(wk32[0], wk32[0].ap[:, 512:1024], "o32b")]
        t2s = [carve(wk32[1], wk32[1].ap[:, 0:512], "t2a"), carve(wk32[1], wk32[1].ap[:, 512:1024], "t2b")]
        subs_c = [(hT, hbuf), (wk32[0], o32s), (wk32[1], t2s)]
        ntl = NT if lim is None else lim
        rcnt = {"r": 0, "q": 0}

        def prep_x(ti):
            xt = xin[ti % 2]
            P.dma(xt, xt[:], x1_t[ti], x1_t[ti][:, :])
            hbt = hb[ti % 2]
            rmsnorm(xt, gain, hbt, st1[ti % 2])
            hb_ = hbuf[ti % 2]
            transpose8(hbt, hb_, hb_[:, :, :], eng="act")
            P.dma(hmT_t[ti], hmT_t[ti][:], hb_, hb_[:, :, :], q="pool")

        def load_tabs_a(ti):
            P.dma(tCa, tCa[:], C["ropeCa"], C["ropeCa"][ti * 128:(ti + 1) * 128, :])
            P.dma(tSa, tSa[:], C["ropeSa"], C["ropeSa"][ti * 128:(ti + 1) * 128, :])

        def load_tabs_r(ti):
            P.dma(tCr, tCr[:], C["ropeCr"], C["ropeCr"][ti * 128:(ti + 1) * 128, :])
            P.dma(tSr, tSr[:], C["ropeSr"], C["ropeSr"][ti * 128:(ti + 1) * 128, :])

        def mm_block(ti, bi):
            c0, c1 = blocks[bi]
            pp = pb[bi % 7]
            hb_ = hbuf[ti % 2]
            subs = [(c0, c1)] if not (c0 < 1944 < c1) else [(c0, 1944), (1944, c1)]
            for (s0, s1) in subs:
                for k in range(8):
                    mm(pp, pp[:, s0 - c0:s1 - c0], hb_, hb_[:, k, :], slot, wcol(k, s0, s1), k == 0, k == 7)

        def post_block(ti, bi):
            pp = pb[bi % 7]
            rcnt["r"] += 1
            o32, t2 = o32s[rcnt["r"] % 2], t2s[rcnt["r"] % 2]

            def nextq():
                rcnt["q"] += 1
                return qTs[rcnt["q"] % 2]
            if bi in (0, 1):
                ob = wk16[bi % 2]
                qT = nextq()
                rope(pp, pp[:, 0:512], 8, 16, tCa, tCa[:, :], tSa, tSa[:, :], ob, ob[:, 0:512], o32, t2)
                transpose8(ob, qT, qT[:], nblk=4, eng="dve")
                dst = QT_t[bi][ti]
                P.dma(dst, dst[:].rearrange("(p r) t -> r p t", r=128), qT, qT[:], q="pool")
            elif bi == 2:
                ob = wk16[2]
                qT = nextq()
                P.op("act", lambda e: e.copy(out=ob[:, 0:256], in_=pp[:, 0:256]), r=[pp], w=[ob])
                rope(pp, pp[:, 256:384], 2, 16, tCa, tCa[:, 0:128], tSa, tSa[:, 0:32], ob, ob[:, 256:384], o32, t2)
                P.op("act", lambda e: e.copy(out=ob[:, 384:512], in_=pp[:, 384:512]), r=[pp], w=[ob])
                transpose8(ob, qT, qT[:, 0:3, :], nblk=3, eng="dve")
                P.dma(kcT_t[ti], kcT_t[ti][:], qT, qT[:, 0, :], q="pool")
                P.dma(vcT_t[ti], vcT_t[ti][:], qT, qT[:, 1, :], q="pool")
                P.dma(ksT_t[ti], ksT_t[ti][:], qT, qT[:, 2, :], q="pool")
                P.dma(vs_t[ti], vs_t[ti][:, :], ob, ob[:, 384:512], q="pool")
            elif bi == 3:
                ob = wk16[2]
                qT = nextq()
                rope(pp, pp[:, 0:128], 2, 16, tCa, tCa[:, 0:128], tSa, tSa[:, 0:32], ob, ob[:, 0:128], o32, t2)
                P.op("act", lambda e: e.copy(out=ob[:, 128:256], in_=pp[:, 128:256]), r=[pp], w=[ob])
                gsb = sm[0]
                P.op("act", lambda e: e.activation(out=gsb[:, 0:48], in_=pp[:, 256:304], func=AF.Sigmoid), r=[pp], w=[gsb])
                P.dma(ga_t[ti], ga_t[ti][:, :], gsb, gsb[:, 0:48], q="pool")
                transpose8(ob, qT, qT[:, 0:1, :], nblk=1, eng="dve")
                P.dma(kwT_t[ti], kwT_t[ti][:], qT, qT[:, 0, :], q="pool")
                P.dma(vw_t[ti], vw_t[ti][:, :], ob, ob[:, 128:256], q="pool")
            elif bi in (4, 5):
                ob = wk16[bi % 2]
                qT = nextq()
                rope(pp, pp[:, 0:512], 8, 64, tCr, tCr[:, :], tSr, tSr[:, :], ob, ob[:, 0:512], o32, t2)
                transpose8(ob, qT, qT[:], nblk=4, eng="dve")
                dst = (QrT_t if bi == 4 else KrT_t)[ti]
                P.dma(dst, dst[:].rearrange("(p r) t -> r p t", r=128), qT, qT[:], q="pool")
                if bi == 5:
                    kz = wk16[2]
                    P.op("pool", lambda e: e.tensor_tensor(out=kz[:, 0:512], in0=o32[:, 0:512], in1=tZ[:], op=ALU.mult),
                         r=[o32, tZ], w=[kz])
                    P.dma(kz_t[ti], kz_t[ti][:, :], kz, kz[:, 0:512], q="pool")
            else:
                ob = wk16[bi % 2]
                P.op("act", lambda e: e.copy(out=ob[:, 0:512], in_=pp[:]), r=[pp], w=[ob])
                hh = bi - 6
                P.dma(vr_t[ti], vr_t[ti][:, hh * 512:(hh + 1) * 512], ob, ob[:, 0:512], q="pool")

        load_tabs_a(0)
        load_tabs_r(0)
        prep_x(0)
        for ti in range(ntl):
            for bi in range(7):
                mm_block(ti, bi)
            post_block(ti, 0)
            mm_block(ti, 7)
            post_block(ti, 1)
            post_block(ti, 2)
            post_block(ti, 3)
            if ti + 1 < ntl:
                load_tabs_a(ti + 1)
                prep_x(ti + 1)
            post_block(ti, 4)
            post_block(ti, 5)
            if ti + 1 < ntl:
                load_tabs_r(ti + 1)
            post_block(ti, 6)
            post_block(ti, 7)
        for parent, subs in subs_c:
            uncarve(parent, subs)

    def proj_pass1(slot):
        w_d = I["w_in"]
        w0 = load_w(slot, 0, w_d, w_d.ap[:, 3888:3888 + 1536].rearrange("(k p) f -> p k f", p=128), [128, 8, 1536])
        w1 = load_w(slot, 8 * 1536, w_d, w_d.ap[:, 3888 + 1536:DIN].rearrange("(k p) f -> p k f", p=128), [128, 8, 1536])
        yield
        for ti in range(NT if lim is None else lim):
            P.dma(hT, hT[:, :, 0:128], hmT_t[ti], hmT_t[ti][:])
            for bi in range(6):
                pp = pb[bi % 7]
                wv = w0 if bi < 3 else w1
                cc = (bi % 3) * 512
                for k in range(8):
                    mm(pp, pp[:], hT, hT[:, k, 0:128], slot, wv[:, k, cc:cc + 512], k == 0, k == 7)
                o32 = wk32[bi % 3]
                fn = AF.Silu if bi < 2 else AF.Sigmoid
                P.op("act", lambda e, pp=pp, o32=o32, fn=fn: e.activation(out=o32[:, 0:512], in_=pp[:], func=fn),
                     r=[pp], w=[o32])
                if bi < 2:
                    P.dma(sgr_t[ti], sgr_t[ti][:, bi * 512:(bi + 1) * 512], o32, o32[:, 0:512], q="pool")
                else:
                    P.dma(gm_t[ti], gm_t[ti][:, (bi - 2) * 512:(bi - 1) * 512], o32, o32[:, 0:512], q="pool")


    kcT2_d = scratch("kcT2_d", [2, 64, 256], BF16)

    def cmp_phase(slot):
        subs = []

        def cv(off, n, parts=128, name=""):
            t = carve(slot, slot.ap[0:parts, off:off + n], name)
            subs.append(t)
            return t
        w1 = [cv(0, 8192, 64, "w1k"), cv(8192, 8192, 64, "w1v")]
        w2 = [cv(16384, 128, 128, "w2k"), cv(16512, 128, 128, "w2v")]
        tokT = [cv(16640, 4096, 64, "tokT0"), cv(20736, 4096, 64, "tokT1")]
        GT = cv(24832, 512, 128, "GT")
        peb = cv(25344, 64, 32, "peb")
        peT = cv(25408, 32, 64, "peT")
        kcb = cv(25440, 64, 128, "kcb")
        kcTs = cv(25504, 128, 64, "kcTs")
        tCc = sm[1]; tSc = sm[2]; bias = sm[3]
        srcs = [(I["cmp_k_w1"], I["cmp_k_w2"], kcT_d, kc_d), (I["cmp_v_w1"], I["cmp_v_w2"], vcT_d, vc_d)]
        P.dma(peb, peb[:, :], I["cmp_pos_emb"], I["cmp_pos_emb"][:, :], q="pool")
        tr(pbf, pbf[0:64, 0:32], peb, peb[:, :]) if False else P.op(
            "pe", lambda e: e.transpose(out=pbf[0:64, 0:32], in_=peb[:, :], identity=ident[0:32, 0:32]),
            r=[peb, ident], w=[pbf])
        P.op("act", lambda e: e.copy(out=peT[:, :], in_=pbf[0:64, 0:32]), r=[pbf], w=[peT])
        P.op("pool", lambda e: e.memset(GT[:, :], 0.0), w=[GT])
        GT3 = GT[:, :].rearrange("p (m n) -> p m n", m=2)
        for kv in range(2):
            w1_d, w2_d, tT_d, o_d = srcs[kv]
            w1v = w1[kv][:, :].rearrange("p (l j) -> p l j", l=32)
            P.dma(w1[kv], w1v, w1_d, w1_d.ap.rearrange("(l d) j -> d l j", d=64), q="pool")
            w2v = w2[kv][:, :].rearrange("p (m o) -> p m o", m=2)
            P.dma(w2[kv], w2v, w2_d, w2_d.ap.rearrange("(m p) o -> p m o", p=128), q="pool")
            pbias = pb[6]
            for m in range(2):
                for l in range(32):
                    mm(pbias, pbias[:, m:m + 1], w1[kv], w1v[:, l, m * 128:(m + 1) * 128], peT, peT[:, l:l + 1], l == 0, l == 31)
            P.op("act", lambda e, pbias=pbias: e.copy(out=bias[:, 0:2], in_=pbias[:, 0:2]), r=[pbias], w=[bias])
            for g in range(2):
                tk = tokT[g]
                P.dma(tk, tk[:, :], tT_d, tT_d.ap[g * 64:(g + 1) * 64, :])
                tok3 = tk[:, :].rearrange("p (n r) -> p n r", r=16)
                for m in range(2):
                    ph = pb[m]
                    for l in range(32):
                        a, rr_ = l // 16, l % 16
                        mm(ph, ph[:, 0:255], w1[kv], w1v[:, l, m * 128:(m + 1) * 128], tk, tok3[:, a:a + 255, rr_], l == 0, l == 31)
                    hbx, x2, z = wk32[0], wk32[1], wk32[2]
                    P.op("act", lambda e, ph=ph, m=m: e.activation(out=hbx[:, 0:255], in_=ph[:, 0:255], func=AF.Identity,
                                                                   bias=bias[:, m:m + 1]), r=[ph, bias], w=[hbx])
                    P.op("dve", lambda e: e.tensor_tensor(out=x2[:, 0:255], in0=hbx[:, 0:255], in1=hbx[:, 0:255], op=ALU.mult),
                         r=[hbx], w=[x2])
                    P.op("dve", lambda e: e.tensor_scalar(out=x2[:, 0:255], in0=x2[:, 0:255], scalar1=0.044715, scalar2=1.0,
                                                          op0=ALU.mult, op1=ALU.add), r=[x2], w=[x2])
                    P.op("dve", lambda e: e.tensor_tensor(out=z[:, 0:255], in0=x2[:, 0:255], in1=hbx[:, 0:255], op=ALU.mult),
                         r=[x2, hbx], w=[z])
                    P.op("act", lambda e: e.activation(out=z[:, 0:255], in_=z[:, 0:255], func=AF.Sigmoid, scale=1.5957691216),
                         r=[z], w=[z])
                    P.op("dve", lambda e, m=m: e.tensor_tensor(out=GT3[:, m, 0:255], in0=z[:, 0:255], in1=hbx[:, 0:255],
                                                               op=ALU.mult), r=[z, hbx], w=[GT])
                for j in range(2):
                    po = pb[2 + j]
                    for m in range(2):
                        mm(po, po[:, 0:64], GT, GT3[:, m, j * 128:(j + 1) * 128], w2[kv], w2v[:, m, :], m == 0, m == 1)
                    if kv == 0:
                        P.dma(tCc, tCc[:, 0:64], C["ropeCc"], C["ropeCc"][j * 128:(j + 1) * 128, :])
                        P.dma(tSc, tSc[:, 0:16], C["ropeSc"], C["ropeSc"][j * 128:(j + 1) * 128, :])
                        rope(po, po[:, 0:64], 1, 16, tCc, tCc[:, 0:64], tSc, tSc[:, 0:16], kcb, kcb[:, :], wk32[0], wk32[1])
                        P.op("pe", lambda e: e.transpose(out=pbf[0:64, 0:128], in_=kcb[:, :], identity=ident[:]),
                             r=[kcb, ident], w=[pbf])
                        P.op("act", lambda e: e.copy(out=kcTs[:, :], in_=pbf[0:64, 0:128]), r=[pbf], w=[kcTs])
                        P.dma(kcT2_d, kcT2_d.ap[g, :, j * 128:(j + 1) * 128], kcTs, kcTs[:, :], q="pool")
                    else:
                        P.op("act", lambda e, po=po: e.copy(out=kcb[:, :], in_=po[:, 0:64]), r=[po], w=[kcb])
                    P.dma(o_d, o_d.ap[g, j * 128:(j + 1) * 128, :], kcb, kcb[:, :], q="pool")
        uncarve(slot, subs)

    def nsa_phase(slot):
        subs = []

        def cv(off, n, parts=128, name=""):
            t = carve(slot, slot.ap[0:parts, off:off + n], name)
            subs.append(t)
            return t
        KsT = cv(0, 4096, 64, "KsT"); KwT = cv(4096, 4096, 64, "KwT")
        Vs = cv(8192, 2080, 128, "Vs"); Vw = cv(10272, 2080, 128, "Vw")
        KcT = cv(12352, 256, 64, "KcT"); Vc = cv(12608, 130, 128, "Vc")
        C2S = cv(12738, 128, 128, "C2S"); EM = cv(12866, 4096, 64, "EM")
        QTc = [cv(16962 + i * 1024, 1024, 64, f"QTc{i}") for i in range(2)]
        Et = [cv(19010 + i * 1024, 1024, 128, f"Et{i}") for i in range(3)]
        negm = cv(22082, 1024, 64, "negm")
        selb = cv(23106, 64, 128, "selb")
        ab = cv(23170, 512, 128, "ab")
        aTs = cv(23682, 512, 128, "aTs")
        TK = tCa; TA = tCr
        P.dma(TK, TK[:, 0:128], C["TK"], C["TK"][:, :])
        P.dma(TA, TA[:, 0:128], C["TA"], C["TA"][:, :])
        P.dma(C2S, C2S[:, :].rearrange("p (j s) -> p j s", j=2), C["C2S"], C["C2S"].ap.rearrange("(j p) s -> p j s", p=128), q="pool")
        P.dma(EM, EM[:, 0:2048], C["EM"], C["EM"][:, 0:2048], q="pool")
        P.dma(EM, EM[:, 2048:4096], C["EM"], C["EM"][:, 2048:4096], q="pool")
        Vs3 = Vs[:, :].rearrange("p (j d) -> p j d", d=65)
        Vw3 = Vw[:, :].rearrange("p (j d) -> p j d", d=65)
        Vc3 = Vc[:, :].rearrange("p (j d) -> p j d", d=65)
        C2S3 = C2S[:, :].rearrange("p (j s) -> p j s", j=2)
        SB = [(pb[0], pb[1]), (pb[2], pb[3])]
        OA, OB, OC = pb[4], pb[5], pb[6]
        OA3 = OA[:, 0:260].rearrange("p (h d) -> p h d", d=65)
        OB3 = OB[:, 0:260].rearrange("p (h d) -> p h d", d=65)
        acc = xin[0]; tmpo = xin[1]
        acc3 = acc[:, 0:512].rearrange("p (h d) -> p h d", d=64)
        tmp3 = tmpo[:, 0:512].rearrange("p (h d) -> p h d", d=64)
        den = sm[2]; coef = sm[3]; imp = sm[4]; imp2 = sm[5]; m8 = sm[6]; imp3 = sm[7]
        sidx = {"s": 0, "e": 0}

        def scores(lhs_t, lhs_ap, qt, mask_j=None):
            b = SB[sidx["s"] % 2]; sidx["s"] += 1
            q2 = qt[:, :]
            for hh in range(2):
                mm(b[hh], b[hh][:], lhs_t, lhs_ap, qt, q2[:, hh * 512:(hh + 1) * 512], True, mask_j is None)
            if mask_j is not None:
                for hh in range(2):
                    mm(b[hh], b[hh][:], EM, EM[:, mask_j * 128:(mask_j + 1) * 128], negm, negm[:, hh * 512:(hh + 1) * 512], False, True)
            et = Et[sidx["e"] % 3]; sidx["e"] += 1
            for hh in range(2):
                P.op("act", lambda e, hh=hh, et=et, b=b: e.activation(out=et[:, hh * 512:(hh + 1) * 512], in_=b[hh][:],
                                                                      func=AF.Exp, scale=0.125), r=[b[hh]], w=[et])
            return et

        def amask(et, cm, qstep, base):
            P.op("pool", lambda e: e.affine_select(out=et[:, :], in_=et[:, :], pattern=[[0, 8], [qstep, 128]],
                                                   compare_op=ALU.is_ge, fill=0.0, base=base, channel_multiplier=cm),
                 r=[et], w=[et])

        def pv(et, v_t, v_ap, first, last):
            for h in range(8):
                o_t = OA if h < 4 else OB
                o3 = OA3 if h < 4 else OB3
                mm(o_t, o3[:, h % 4, :], et, et[:, h * 128:(h + 1) * 128], v_t, v_ap, first and (h % 4 == 0), last, sgc=True)

        pend = []

        def flush():
            while pend:
                work, post = pend.pop(0)
                work()
                if post is not None:
                    post()

        def finish_branch(bidx, first, gat):
            P.op("dve", lambda e: e.tensor_scalar(out=den[:, 0:4], in0=OA3[:, :, 64], scalar1=1e-20, scalar2=None, op0=ALU.max),
                 r=[OA], w=[den])
            P.op("dve", lambda e: e.tensor_scalar(out=den[:, 4:8], in0=OB3[:, :, 64], scalar1=1e-20, scalar2=None, op0=ALU.max),
                 r=[OB], w=[den])
            P.op("dve", lambda e: e.reciprocal(out=den[:, 8:16], in_=den[:, 0:8]), r=[den], w=[den])
            g3 = gat[:, 0:24].rearrange("p (h b) -> p h b", b=3)
            P.op("dve", lambda e: e.tensor_tensor(out=coef[:, 0:8], in0=den[:, 8:16], in1=g3[:, :, bidx], op=ALU.mult),
                 r=[den, gat], w=[coef])
            dst, d3 = (acc, acc3) if first else (tmpo, tmp3)
            P.op("dve", lambda e: e.tensor_tensor(out=d3[:, 0:4, :], in0=OA3[:, :, 0:64],
                                                  in1=coef[:, 0:4].unsqueeze(2).broadcast_to([128, 4, 64]), op=ALU.mult),
                 r=[OA, coef], w=[dst])
            P.op("dve", lambda e: e.tensor_tensor(out=d3[:, 4:8, :], in0=OB3[:, :, 0:64],
                                                  in1=coef[:, 4:8].unsqueeze(2).broadcast_to([128, 4, 64]), op=ALU.mult),
                 r=[OB, coef], w=[dst])
            if not first:
                P.op("pool", lambda e: e.tensor_tensor(out=acc[:, 0:512], in0=acc[:, 0:512], in1=tmpo[:, 0:512], op=ALU.add),
                     r=[acc, tmpo], w=[acc])

        for g in range(2):
            P.dma(KsT, KsT[:, :], ksT_d, ksT_d.ap[g * 64:(g + 1) * 64, :])
            P.dma(KwT, KwT[:, :], kwT_d, kwT_d.ap[g * 64:(g + 1) * 64, :])
            P.dma(Vs, Vs3[:, :, 0:64], vs_d, vs_d.ap[:, g * 64:(g + 1) * 64].rearrange("(j p) d -> p j d", p=128))
            P.dma(Vw, Vw3[:, :, 0:64], vw_d, vw_d.ap[:, g * 64:(g + 1) * 64].rearrange("(j p) d -> p j d", p=128))
            P.op("pool", lambda e: e.memset(Vs3[:, :, 64:65], 1.0), w=[Vs])
            P.op("pool", lambda e: e.memset(Vw3[:, :, 64:65], 1.0), w=[Vw])
            P.dma(KcT, KcT[:, :], kcT2_d, kcT2_d.ap[g])
            P.dma(Vc, Vc3[:, :, 0:64], vc_d, vc_d.ap[g].rearrange("(j p) d -> p j d", p=128))
            P.op("pool", lambda e: e.memset(Vc3[:, :, 64:65], 1.0), w=[Vc])
            gats = [sm[1], sm[0]]
            for c in (range(NT) if lim is None else lim_c):
                qt = QTc[c % 2]
                gat = gats[c % 2]
                P.dma(qt, qt[:, :].rearrange("d (h t) -> d h t", h=8), QT_t[g][c],
                      QT_t[g][c][:].rearrange("(h d) t -> d h t", d=64))
                P.dma(gat, gat[:, 0:24], ga_t[c], ga_t[c][:, g * 24:(g + 1) * 24])

                def post_cmp(c=c, gat=gat):
                    finish_branch(0, True, gat)
                    for h in range(8):
                        if h == 0:
                            P.op("dve", lambda e: e.tensor_scalar(out=imp[:, 0:64], in0=OC[:, 0:64], scalar1=den[:, 8:9],
                                                                  scalar2=None, op0=ALU.mult), r=[OC, den], w=[imp])
                        else:
                            P.op("dve", lambda e, h=h: e.scalar_tensor_tensor(
                                out=imp[:, 0:64], in0=OC[:, h * 64:(h + 1) * 64], scalar=den[:, 8 + h:9 + h], in1=imp[:, 0:64],
                                op0=ALU.mult, op1=ALU.add), r=[OC, den, imp], w=[imp])
                    off = 64 - 2 * c
                    P.op("dve", lambda e: e.tensor_tensor(out=imp2[:, 0:64], in0=imp[:, 0:64], in1=TK[:, off:off + 64], op=ALU.mult),
                         r=[imp, TK], w=[imp2])
                    P.op("dve", lambda e: e.tensor_tensor(out=imp2[:, 0:64], in0=imp2[:, 0:64], in1=TA[:, off:off + 64], op=ALU.add),
                         r=[imp2, TA], w=[imp2])
                    P.op("dve", lambda e: e.memset(imp2[:, 0:1], 1.0e4), r=[imp2], w=[imp2])
                    P.op("dve", lambda e: e.max(out=m8[:, 0:8], in_=imp2[:, 0:64]), r=[imp2], w=[m8])
                    P.op("dve", lambda e: e.match_replace(out=imp3[:, 0:64], in_to_replace=m8[:, 0:8], in_values=imp2[:, 0:64],
                                                          imm_value=-3.0e4), r=[imp2, m8], w=[imp3])
                    P.op("dve", lambda e: e.max(out=m8[:, 8:16], in_=imp3[:, 0:64]), r=[imp3, m8], w=[m8])
                    P.op("dve", lambda e: e.tensor_scalar(out=selb[:, :], in0=imp2[:, 0:64], scalar1=m8[:, 15:16], scalar2=None,
                                                          op0=ALU.is_ge), r=[imp2, m8], w=[selb])
                    P.op("pe", lambda e: e.transpose(out=pbf[0:64, 0:128], in_=selb[:, :], identity=ident[:]),
                         r=[selb, ident], w=[pbf])
                    P.op("dve", lambda e: e.tensor_scalar(out=negm[:, :].rearrange("p (h q) -> p h q", h=8),
                                                          in0=pbf[0:64, 0:128].unsqueeze(1).broadcast_to([64, 8, 128]),
                                                          scalar1=-NEGBIG, scalar2=NEGBIG, op0=ALU.mult, op1=ALU.add),
                         r=[pbf], w=[negm])

                def post_win(gat=gat):
                    finish_branch(2, False, gat)

                def post_sel(c=c, gat=gat):
                    finish_branch(1, False, gat)
                    P.op("act", lambda e: e.copy(out=ab[:, :], in_=acc[:, 0:512]), r=[acc], w=[ab])
                    transpose8(ab, aTs, aTs[:, :].rearrange("p (k t) -> p k t", k=4), nblk=4, eng="act")
                    dst = aT_t[c]
                    P.dma(dst, dst[g * 512:(g + 1) * 512, :].rearrange("(k r) t -> r k t", r=128), aTs,
                          aTs[:, :].rearrange("p (k t) -> p k t", k=4), q="pool")

                njt = 1 if c <= 15 else 2
                for j in range(njt):
                    et = scores(KcT, KcT[:, j * 128:(j + 1) * 128], qt)
                    amask(et, -16, 1, -16 * (128 * j - 8 * c) - 31)
                    flush()

                    def work(et=et, j=j, njt=njt):
                        pv(et, Vc, Vc3[:, j, :], j == 0, j == njt - 1)
                        for h in range(8):
                            mm(OC, OC[:, h * 64:(h + 1) * 64], et, et[:, h * 128:(h + 1) * 128], C2S, C2S3[:, j, :],
                               j == 0 and h == 0, j == njt - 1, sgc=True)
                    pend.append((work, post_cmp if j == njt - 1 else None))
                j0 = max(0, c - 4)
                for j in range(j0, c + 1):
                    et = scores(KwT, KwT[:, j * 128:(j + 1) * 128], qt)
                    if j == c:
                        amask(et, -1, 1, 0)
                    if j == c - 4:
                        amask(et, 1, -1, -1)
                    flush()
                    pend.append((lambda et=et, j=j, j0=j0, c=c: pv(et, Vw, Vw3[:, j, :], j == j0, j == c),
                                 post_win if j == c else None))
                for j in range(c + 1):
                    if j == 0:
                        flush()
                    et = scores(KsT, KsT[:, j * 128:(j + 1) * 128], qt, mask_j=j)
                    if j == c:
                        amask(et, -1, 1, 0)
                    flush()
                    pend.append((lambda et=et, j=j, c=c: pv(et, Vs, Vs3[:, j, :], j == 0, j == c),
                                 post_sel if j == c else None))
            flush()
        uncarve(slot, subs)

    def ret_phase():
        subs = []
        bufs = []
        for i in range(2):
            q_ = carve(big, big.ap[0:64, i * 2560:i * 2560 + 1024], f"QrTc{i}")
            k_ = carve(big, big.ap[0:64, i * 2560 + 1024:i * 2560 + 2048], f"KrTc{i}")
            z_ = carve(big, big.ap[:, i * 2560 + 2048:i * 2560 + 2560], f"kzc{i}")
            subs += [q_, k_, z_]
            bufs.append((q_, k_, z_))
        dec = [tCr, tSr]
        gct = [tCa, tZ]
        P.dma(dec[0], dec[0][:, :], C["decayT"], C["decayT"][:, 0:512])
        P.dma(dec[1], dec[1][:, :], C["decayT"], C["decayT"][:, 512:1024])
        P.dma(gct[0], gct[0][0:64, :], C["GC"], C["GC"][:, 0:512])
        P.dma(gct[1], gct[1][0:64, :], C["GC"], C["GC"][:, 512:1024])
        xit = sm[1]
        P.dma(xit, xit[:, 0:8], C["XI"], C["XI"][:, :])
        load_gain(gain2, I["ret_gn_gain"])
        Rst = xin[0]; osbs = [xin[1], gain]; sgr = wk32[0]; sq = wk32[1]; tmp = wk32[2]
        Rbf = wk16[0]; ATs = wk16[1]; rob = wk16[2]
        stt = sm[2]
        P.op("pool", lambda e: e.memset(Rst[0:64, :], 0.0), w=[Rst])
        P.op("pool", lambda e: e.memset(Rbf[0:64, :], 0.0), w=[Rbf])
        sq3 = sq[:, :].rearrange("p (h e) -> p h e", h=8)
        tmp3 = tmp[:, :].rearrange("p (h e) -> p h e", h=8)

        def stage1(n):
            q_, k_, z_ = bufs[n % 2]
            vt = hb[n % 2]
            osb = osbs[n % 2]
            P.dma(q_, q_[:, :].rearrange("d (h t) -> d h t", h=8), QrT_t[n], QrT_t[n][:].rearrange("(h d) t -> d h t", d=64))
            P.dma(k_, k_[:, :].rearrange("d (h t) -> d h t", h=8), KrT_t[n], KrT_t[n][:].rearrange("(h d) t -> d h t", d=64))
            P.dma(z_, z_[:, :], kz_t[n], kz_t[n][:, :])
            P.dma(vt, vt[:, :], vr_t[n], vr_t[n][:, :])
            for h in range(8):
                b = pb[h // 4]
                mm(b, b[:, (h % 4) * 128:(h % 4 + 1) * 128], k_, k_[:, h * 128:(h + 1) * 128], q_, q_[:, h * 128:(h + 1) * 128], True, True)
            if n > 0:
                for h in range(8):
                    b = pb[4 + h // 4]
                    mm(b, b[:, (h % 4) * 128:(h % 4 + 1) * 128], q_, q_[:, h * 128:(h + 1) * 128], Rbf, Rbf[0:64, h * 128:(h + 1) * 128], True, True)
            for hh in range(2):
                P.op("dve", lambda e, hh=hh: e.tensor_tensor(out=ATs[:, hh * 512:(hh + 1) * 512], in0=pb[hh][:], in1=dec[hh][:, :],
                                                             op=ALU.mult), r=[pb[hh], dec[hh]], w=[ATs])
            for h in range(8):
                b = pb[2 + h // 4]
                mm(b, b[:, (h % 4) * 128:(h % 4 + 1) * 128], ATs, ATs[:, h * 128:(h + 1) * 128], vt, vt[:, h * 128:(h + 1) * 128], True, True)
            for hh in range(2):
                b = pb[6]
                for h4 in range(4):
                    h = hh * 4 + h4
                    mm(b, b[0:64, h4 * 128:(h4 + 1) * 128], z_, z_[:, h * 64:(h + 1) * 64], vt, vt[:, h * 128:(h + 1) * 128], True, True)
                P.op("pool", lambda e, hh=hh: e.tensor_tensor(out=Rst[0:64, hh * 512:(hh + 1) * 512], in0=Rst[0:64, hh * 512:(hh + 1) * 512],
                                                              in1=gct[hh][0:64, :], op=ALU.mult), r=[Rst, gct[hh]], w=[Rst])
                P.op("dve", lambda e, hh=hh, b=b: e.tensor_tensor(out=Rst[0:64, hh * 512:(hh + 1) * 512], in0=b[0:64, :],
                                                                  in1=Rst[0:64, hh * 512:(hh + 1) * 512], op=ALU.add), r=[b, Rst], w=[Rst])
            P.op("act", lambda e: e.copy(out=Rbf[0:64, :], in_=Rst[0:64, :]), r=[Rst], w=[Rbf])
            for hh in range(2):
                P.op("act", lambda e, hh=hh: e.copy(out=osb[:, hh * 512:(hh + 1) * 512], in_=pb[2 + hh][:]), r=[pb[2 + hh]], w=[osb])
            if n > 0:
                for hh in range(2):
                    P.op("dve", lambda e, hh=hh: e.tensor_tensor(
                        out=tmp3[:, hh * 4:(hh + 1) * 4, :], in0=pb[4 + hh][:].rearrange("p (h e) -> p h e", h=4),
                        in1=xit[:, hh * 4:(hh + 1) * 4].unsqueeze(2).broadcast_to([128, 4, 128]), op=ALU.mult),
                        r=[pb[4 + hh], xit], w=[tmp])
                P.op("pool", lambda e: e.tensor_tensor(out=osb[:, :], in0=osb[:, :], in1=tmp[:, :], op=ALU.add), r=[osb, tmp], w=[osb])

        def stage2(n):
            osb = osbs[n % 2]
            osb3 = osb[:, :].rearrange("p (h e) -> p h e", h=8)
            P.dma(sgr, sgr[:, :], sgr_t[n], sgr_t[n][:, :])
            P.op("dve", lambda e: e.tensor_reduce(out=stt[:, 0:8], in_=osb3, axis=AX.X, op=ALU.add), r=[osb], w=[stt])
            P.op("act", lambda e: e.activation(out=sq[:, :], in_=osb[:, :], func=AF.Square), r=[osb], w=[sq])
            P.op("dve", lambda e: e.tensor_reduce(out=stt[:, 8:16], in_=sq3, axis=AX.X, op=ALU.add), r=[sq], w=[stt])
            P.op("dve", lambda e: e.tensor_scalar(out=stt[:, 0:8], in0=stt[:, 0:8], scalar1=1.0 / 128, scalar2=None, op0=ALU.mult),
                 r=[stt], w=[stt])
            P.op("dve", lambda e: e.tensor_tensor(out=stt[:, 16:24], in0=stt[:, 0:8], in1=stt[:, 0:8], op=ALU.mult), r=[stt], w=[stt])
            P.op("dve", lambda e: e.scalar_tensor_tensor(out=stt[:, 24:32], in0=stt[:, 8:16], scalar=1.0 / 128, in1=stt[:, 16:24],
                                                         op0=ALU.mult, op1=ALU.subtract), r=[stt], w=[stt])
            P.op("dve", lambda e: e.tensor_scalar(out=stt[:, 24:32], in0=stt[:, 24:32], scalar1=1e-5, scalar2=None, op0=ALU.add),
                 r=[stt], w=[stt])
            P.op("act", lambda e: e.sqrt(out=stt[:, 32:40], in_=stt[:, 24:32]), r=[stt], w=[stt])
            P.op("dve", lambda e: e.reciprocal(out=stt[:, 40:48], in_=stt[:, 32:40]), r=[stt], w=[stt])
            P.op("pool", lambda e: e.tensor_tensor(out=osb3, in0=osb3, in1=stt[:, 0:8].unsqueeze(2).broadcast_to([128, 8, 128]),
                                                   op=ALU.subtract), r=[osb, stt], w=[osb])
            P.op("dve", lambda e: e.tensor_tensor(out=osb3, in0=osb3, in1=stt[:, 40:48].unsqueeze(2).broadcast_to([128, 8, 128]),
                                                  op=ALU.mult), r=[osb, stt], w=[osb])
            P.op("pool", lambda e: e.tensor_tensor(out=osb[:, :], in0=osb[:, :], in1=gain2[:, :], op=ALU.mult), r=[osb, gain2], w=[osb])
            P.op("pool", lambda e: e.tensor_tensor(out=rob[:, :], in0=osb[:, :], in1=sgr[:, :], op=ALU.mult), r=[osb, sgr], w=[rob])
            transpose8(rob, hT, hT[:, :, 0:128], eng="act")
            dst = rT_t[n]
            P.dma(dst, dst[:].rearrange("(k r) t -> r k t", r=128), hT, hT[:, :, 0:128], q="pool")

        nn = NT if lim is None else lim
        for n in range(nn):
            stage1(n)
            if n > 0:
                stage2(n - 1)
        stage2(nn - 1)
        uncarve(big, subs)

    def merge_phase(slot):
        Wn = load_w(slot, 0, I["w_branch_nsa"], I["w_branch_nsa"].ap.rearrange("(k p) f -> p k f", p=128), [128, 8, 1024])
        Wr = load_w(slot, 8192, I["w_branch_ret"], I["w_branch_ret"].ap.rearrange("(k p) f -> p k f", p=128), [128, 8, 1024])
        Wo = load_w(slot, 16384, I["w_out"], I["w_out"].ap.rearrange("(k p) f -> p k f", p=128), [128, 8, 1024])
        yield
        subs = []
        bufs = []
        for i in range(2):
            a_ = carve(big, big.ap[:, i * 2048:i * 2048 + 1024], f"aTt{i}")
            r_ = carve(big, big.ap[:, i * 2048 + 1024:i * 2048 + 2048], f"rTt{i}")
            subs += [a_, r_]
            bufs.append((a_, r_))
        for ti in range(NT if lim is None else lim):
            a_, r_ = bufs[ti % 2]
            a3 = a_[:, :].rearrange("p (k t) -> p k t", k=8)
            r3 = r_[:, :].rearrange("p (k t) -> p k t", k=8)
            P.dma(a_, a3, aT_t[ti], aT_t[ti][:].rearrange("(k r) t -> r k t", r=128))
            P.dma(r_, r3, rT_t[ti], rT_t[ti][:].rearrange("(k r) t -> r k t", r=128))
            ga, gr = wk32[0], wk32[1]
            P.dma(ga, ga[:, :], gm_t[ti], gm_t[ti][:, 0:1024])
            P.dma(gr, gr[:, :], gm_t[ti], gm_t[ti][:, 1024:2048])
            x1s = wk32[2]
            P.dma(x1s, x1s[:, :], x1_t[ti], x1_t[ti][:, :])
            for half in range(2):
                for k in range(8):
                    mm(pb[half], pb[half][:], a_, a3[:, k, :], slot, Wn[:, k, half * 512:(half + 1) * 512], k == 0, k == 7)
                for k in range(8):
                    mm(pb[2 + half], pb[2 + half][:], r_, r3[:, k, :], slot, Wr[:, k, half * 512:(half + 1) * 512], k == 0, k == 7)
            m1, m2 = xin[0], xin[1]
            mb = hb[ti % 2]
            for half in range(2):
                cs = slice(half * 512, (half + 1) * 512)
                P.op("dve", lambda e, half=half, cs=cs: e.tensor_tensor(out=m1[:, cs], in0=pb[half][:], in1=ga[:, cs], op=ALU.mult),
                     r=[pb[half], ga], w=[m1])
                P.op("dve", lambda e, half=half, cs=cs: e.tensor_tensor(out=m2[:, cs], in0=pb[2 + half][:], in1=gr[:, cs], op=ALU.mult),
                     r=[pb[2 + half], gr], w=[m2])
            P.op("pool", lambda e, mb=mb: e.tensor_tensor(out=mb[:, :], in0=m1[:, :], in1=m2[:, :], op=ALU.add), r=[m1, m2], w=[mb])
            transpose8(mb, hT, hT[:, :, 0:128], eng="act")
            for half in range(2):
                po = pb[4 + half]
                for k in range(8):
                    mm(po, po[:], hT, hT[:, k, 0:128], slot, Wo[:, k, half * 512:(half + 1) * 512], k == 0, k == 7)
                P.op("dve", lambda e, po=po, half=half: e.tensor_tensor(out=x1s[:, half * 512:(half + 1) * 512], in0=po[:],
                                                                       in1=x1s[:, half * 512:(half + 1) * 512], op=ALU.add),
                     r=[po, x1s], w=[x1s])
            P.dma(x2_t[ti], x2_t[ti][:, :], x1s, x1s[:, :], q="pool")
        uncarve(big, subs)

    def begin(gen):
        next(gen)
        return gen

    def finish(gen):
        for _ in gen:
            pass

    allp = set(phases) == {"ffn1", "proj", "cmp", "nsa", "ret", "merge", "ffn2"} and not _os.environ.get("SKIPP1")
    out_t = [T(out_d.ap[i * 128:(i + 1) * 128], f"out{i}") for i in range(NT)]
    if allp:
        f1a = begin(ffn_pass(1, 0, wslot[0], x_t, x1_t, I["ffn1_norm"]))
        f1b = begin(ffn_pass(1, 1, wslot[1], x1_t, x1_t, I["ffn1_norm"]))
        finish(f1a)
        p0 = begin(proj_pass0(wslot[0]))
        finish(f1b)
        p1 = begin(proj_pass1(wslot[1]))
        finish(p0)
        finish(p1)
        cmp_phase(wslot[1])
        mg = begin(merge_phase(wslot[1]))
        nsa_phase(wslot[0])
        f2a = begin(ffn_pass(2, 0, wslot[0], x2_t, x2_t, I["ffn2_norm"]))
        ret_phase()
        finish(mg)
        f2b = begin(ffn_pass(2, 1, wslot[1], x2_t, out_t, I["ffn2_norm"], final=True))
        finish(f2a)
        finish(f2b)
    else:
        if "ffn1" in phases:
            finish(ffn_pass(1, 0, wslot[0], x_t, x1_t, I["ffn1_norm"]))
            finish(ffn_pass(1, 1, wslot[1], x1_t, x1_t, I["ffn1_norm"]))
        if "proj" in phases:
            finish(proj_pass0(wslot[0]))
            if not _os.environ.get("SKIPP1"):
                finish(proj_pass1(wslot[1]))
        if "cmp" in phases:
            cmp_phase(wslot[1])
        if "nsa" in phases:
            nsa_phase(wslot[0])
        if "ret" in phases:
            ret_phase()
        if "merge" in phases:
            finish(merge_phase(wslot[1]))
        if "ffn2" in phases:
            finish(ffn_pass(2, 0, wslot[0], x2_t, x2_t, I["ffn2_norm"]))
            finish(ffn_pass(2, 1, wslot[1], x2_t, out_t, I["ffn2_norm"], final=True))
    P.emit()
    return nc, P


_CACHE = {}


def kernel(**inputs):
    n_cores = 8
    if "nc" not in _CACHE:
        _CACHE["nc"] = build()[0]
        _CACHE["consts"] = {k: np.ascontiguousarray(v.reshape(CONST_SHAPES[k]).astype(np.float32))
                            for k, v in _consts().items()}
    nc = _CACHE["nc"]
    cst = _CACHE["consts"]
    shared = {}
    for k, shp in IN_SHAPES.items():
        if k == "x":
            continue
        shared[k] = np.ascontiguousarray(np.asarray(inputs[k], dtype=np.float32).reshape(shp))
    x = np.asarray(inputs["x"], dtype=np.float32)
    in_maps = []
    for c in range(n_cores):
        m = dict(shared)
        m.update(cst)
        m["x"] = np.ascontiguousarray(x[c % 4])
        in_maps.append(m)
    res = run_bass_kernel_spmd(nc, in_maps, core_ids=list(range(n_cores)))
    out = np.stack([np.asarray(res.results[b]["out"], dtype=np.float32) for b in range(4)], axis=0)
    return out
```

```python
import contextlib
import os as _os
import numpy as np
import concourse.bass as bass
import concourse.mybir as mybir
from concourse.bass_utils import run_bass_kernel_spmd

F32 = mybir.dt.float32
BF16 = mybir.dt.bfloat16
AF = mybir.ActivationFunctionType
ALU = mybir.AluOpType
AX = mybir.AxisListType

S = 4096
D = 1024
DFF = 2816
NT = S // 128
DIN = 6960


class T:
    __slots__ = ("ap", "name", "last_w", "readers", "excl")

    def __init__(self, ap, name="", excl=False):
        self.ap = ap
        self.name = name
        self.last_w = None
        self.readers = []
        self.excl = excl

    def __getitem__(self, k):
        return self.ap[k]


class Prog:
    ENGS = ("pe", "act", "dve", "pool", "sp")

    def __init__(self, nc, n_dma_sems=12, same_engine_sync=True):
        self.nc = nc
        self.ops = []
        self.es = contextlib.ExitStack()
        self.same_engine_sync = same_engine_sync
        self.n_dma_sems = n_dma_sems
        self.ncnt = 0

    def sb(self, shape, dt, name=None):
        self.ncnt += 1
        name = name or f"sb{self.ncnt}"
        t = self.es.enter_context(self.nc.sbuf_tensor(name, list(shape), dt))
        return T(t[:], name)

    def ps(self, shape, dt, name=None):
        self.ncnt += 1
        name = name or f"ps{self.ncnt}"
        t = self.es.enter_context(self.nc.psum_tensor(name, list(shape), dt))
        return T(t[:], name, excl=True)

    def dram(self, name, shape, dt, kind="Internal"):
        t = self.nc.dram_tensor(name, list(shape), dt, kind=kind)
        return T(t.ap(), name)

    def op(self, eng, fn, r=(), w=(), dma=False):
        idx = len(self.ops)
        deps = set()
        raw = set()
        for t in r:
            if t.last_w is not None:
                deps.add(t.last_w)
                raw.add(t.last_w)
            if t.excl:
                for rd in t.readers:
                    if self.ops[rd]["eng"] != eng:
                        deps.add(rd)
        for t in w:
            if t.last_w is not None:
                deps.add(t.last_w)
            deps.update(t.readers)
        for t in r:
            t.readers.append(idx)
        for t in w:
            t.last_w = idx
            t.readers = []
        deps.discard(idx)
        self.ops.append({"eng": eng, "fn": fn, "deps": deps, "dma": dma, "raw": raw})
        return idx

    def dma(self, out_t, out_ap, in_t, in_ap, q="sp", **kw):
        return self.op(q, lambda e: e.dma_start(out=out_ap, in_=in_ap, **kw),
                       r=[in_t], w=[out_t], dma=True)

    def emit(self):
        nc = self.nc
        ops = self.ops
        nops = len(ops)
        dma_use = {}
        for i, o in enumerate(ops):
            if o["dma"]:
                dma_use.setdefault(o["eng"], []).append(i)
        dma_sem = {}
        for q, lst in dma_use.items():
            for k, i in enumerate(lst):
                slot = k % self.n_dma_sems
                val = 16 * (k // self.n_dma_sems + 1)
                dma_sem[i] = (q, slot, val)
                if k >= self.n_dma_sems:
                    ops[i]["deps"].add(lst[k - self.n_dma_sems])
        waited = {e: {} for e in self.ENGS}
        waited_dma = {e: set() for e in self.ENGS}
        plan = [None] * nops
        signaling = set()
        for i, o in enumerate(ops):
            e = o["eng"]
            need = {}
            need_dma = []
            for d in sorted(o["deps"]):
                po = ops[d]
                if po["dma"]:
                    if d not in waited_dma[e]:
                        waited_dma[e].add(d)
                        need_dma.append(d)
                    continue
                p = po["eng"]
                if p == e and (e == "pe" or not self.same_engine_sync or d not in o["raw"]):
                    continue
                if waited[e].get(p, -1) >= d:
                    continue
                need[p] = max(need.get(p, -1), d)
            for p, d in need.items():
                waited[e][p] = d
                signaling.add(d)
            plan[i] = (need, need_dma)
        semval = {}
        cnt = {e: 0 for e in self.ENGS}
        for i, o in enumerate(ops):
            if o["dma"]:
                continue
            if i in signaling:
                cnt[o["eng"]] += 1
                semval[i] = cnt[o["eng"]]
        self.stats = {e: sum(1 for o in ops if o["eng"] == e) for e in self.ENGS}
        self.stats["signals"] = dict(cnt)
        es = self.es
        sems = {e: es.enter_context(nc.semaphore(f"sem_{e}")) for e in self.ENGS}
        dsems = {q: [es.enter_context(nc.semaphore(f"dsem_{q}{k}")) for k in range(self.n_dma_sems)]
                 for q in dma_use}
        block = es.enter_context(nc.Block())

        def body(ename):
            def f(eng):
                for i, o in enumerate(ops):
                    if o["eng"] != ename:
                        continue
                    need, need_dma = plan[i]
                    for p, d in need.items():
                        eng.wait_ge(sems[p], semval[d])
                    for d in need_dma:
                        q, slot, val = dma_sem[d]
                        eng.wait_ge(dsems[q][slot], val)
                    ins = o["fn"](eng)
                    if o["dma"]:
                        q, slot, val = dma_sem[i]
                        ins.then_inc(dsems[q][slot], 16)
                    elif i in signaling:
                        ins.then_inc(sems[ename], 1)
                for q, lst in dma_use.items():
                    if q != ename:
                        continue
                    last = {}
                    for i in lst:
                        _, slot, val = dma_sem[i]
                        last[slot] = val
                    for slot, val in last.items():
                        eng.wait_ge(dsems[q][slot], val)
            return f

        block.tensor(body("pe"))
        block.scalar(body("act"))
        block.vector(body("dve"))
        block.gpsimd(body("pool"))
        block.sync(body("sp"))

    def close(self):
        self.es.close()


def _consts():
    c = {}
    pos = np.arange(S, dtype=np.float32)

    def rope_tabs(p, rot, theta, nrep):
        half = rot // 2
        fr = (np.float32(theta) ** (-(np.arange(half, dtype=np.float32) * np.float32(2.0) / np.float32(rot)))).astype(np.float32)
        ang = p.astype(np.float32)[:, None] * fr[None, :]
        cs, sn = np.cos(ang).astype(np.float32), np.sin(ang).astype(np.float32)
        C = np.ones((len(p), 64), np.float32)
        C[:, 0:half] = cs
        C[:, half:rot] = cs
        Sg = np.concatenate([-sn, sn], 1)
        return np.tile(C, (1, nrep)), np.tile(Sg, (1, nrep))

    c["ropeCa"], c["ropeSa"] = rope_tabs(pos, 16, 500000.0, 8)
    pc = np.arange(256, dtype=np.float32) * 16 + 31
    c["ropeCc"], c["ropeSc"] = rope_tabs(pc, 16, 500000.0, 1)
    c["ropeCr"], c["ropeSr"] = rope_tabs(pos, 64, 10000.0, 8)
    lg = np.log(1.0 - 2.0 ** (-5.0 - np.arange(8, dtype=np.float64)))
    i = np.arange(128, dtype=np.float64)
    diff = i[None, :] - i[:, None]
    dT = np.zeros((128, 8, 128), np.float64)
    for h in range(8):
        dT[:, h, :] = np.where(diff >= 0, np.exp(np.maximum(diff, 0) * lg[h]), 0.0) * 0.125
    c["decayT"] = dT.astype(np.float32).reshape(128, 1024)
    zeta = np.exp((127.0 - i)[None, :] * lg[:, None])
    c["ZT"] = np.repeat(zeta.T[:, :, None], 64, axis=2).reshape(128, 512).astype(np.float32)
    xi = np.exp((i + 1.0)[None, :] * lg[:, None]) * 0.125
    c["XI"] = np.ascontiguousarray(xi.T).astype(np.float32)
    gch = np.exp(128.0 * lg)
    c["GC"] = np.broadcast_to(gch[None, :, None], (64, 8, 128)).reshape(64, 1024).astype(np.float32)
    n = np.arange(256)
    cs_ = n * 16
    ss_ = np.arange(64) * 64
    ov = np.clip(np.minimum(cs_[:, None] + 32, ss_[None, :] + 64) - np.maximum(cs_[:, None], ss_[None, :]), 0, None)
    c2s = ov.astype(np.float32) / 32.0
    c2s[255] = 0.0
    c["C2S"] = c2s
    key = np.arange(S)
    c["EM"] = (key[None, :] // 64 == np.arange(64)[:, None]).astype(np.float32)
    sp = np.arange(128) - 64
    curp = (np.arange(128) >= 64).astype(np.int64)[:, None]
    forced = (sp[None, :] == curp) | (sp[None, :] == curp - 1)
    future = sp[None, :] > curp
    c["TK"] = (~(forced | future)).astype(np.float32)
    c["TA"] = np.where(forced, 1.0e4, np.where(future, -1.0e4, 0.0)).astype(np.float32)
    return c


CONST_SHAPES = {"ropeCa": [S, 512], "ropeSa": [S, 128], "ropeCc": [256, 64], "ropeSc": [256, 16],
                "ropeCr": [S, 512], "ropeSr": [S, 512], "decayT": [128, 1024], "ZT": [128, 512],
                "XI": [128, 8], "GC": [64, 1024], "C2S": [256, 64], "EM": [64, S],
                "TK": [128, 128], "TA": [128, 128]}

IN_SHAPES = {"x": [S, D], "ffn1_norm": [1, D], "ffn1_w_gate": [D, DFF], "ffn1_w_up": [D, DFF],
             "ffn1_w_down": [DFF, D], "mix_norm": [1, D], "w_in": [D, DIN], "cmp_pos_emb": [32, 64],
             "cmp_k_w1": [2048, 256], "cmp_k_w2": [256, 64], "cmp_v_w1": [2048, 256], "cmp_v_w2": [256, 64],
             "ret_gn_gain": [1, 1024], "w_branch_nsa": [D, D], "w_branch_ret": [D, D], "w_out": [D, D],
             "ffn2_norm": [1, D], "ffn2_w_gate": [D, DFF], "ffn2_w_up": [D, DFF], "ffn2_w_down": [DFF, D],
             "final_norm": [1, D]}

G_CHUNK = [float(np.exp(128.0 * np.log(1.0 - 2.0 ** (-5.0 - h)))) for h in range(8)]
NEGBIG = -2.0e4


def build(phases=("ffn1", "proj", "cmp", "nsa", "ret", "merge", "ffn2"), dbg=(), lim=None):
    nc = bass.Bass("TRN2", target_bir_lowering=False)
    P = Prog(nc)
    lim_c = [0, 5, 17] if lim else None
    I = {k: P.dram(k, v, F32, kind="ExternalInput") for k, v in IN_SHAPES.items()}
    C = {k: P.dram(k, v, F32, kind="ExternalInput") for k, v in CONST_SHAPES.items()}
    out_d = P.dram("out", [S, D], F32, kind="ExternalOutput")

    def scratch(name, shape, dt):
        return P.dram(name, shape, dt, kind=("ExternalOutput" if name in dbg else "Internal"))

    def tiles(t, n=NT, rows=128):
        return [T(t.ap[i * rows:(i + 1) * rows], f"{t.name}{i}") for i in range(n)]

    def ctiles(t, n=NT, cols=128):
        return [T(t.ap[..., i * cols:(i + 1) * cols], f"{t.name}{i}") for i in range(n)]

    x_t = tiles(I["x"])
    x1_d = scratch("x1_d", [S, D], F32); x1_t = tiles(x1_d)
    x2_d = scratch("x2_d", [S, D], F32); x2_t = tiles(x2_d)
    hT_d = scratch("hT_d", [8, 128, 8, 512], BF16)
    hT_t = [T(hT_d.ap[i], f"hT{i}") for i in range(8)]
    hmT_d = scratch("hmT_d", [NT, 128, 8, 128], BF16)
    hmT_t = [T(hmT_d.ap[i], f"hmT{i}") for i in range(NT)]
    QT_d = scratch("QT_d", [2, 512, S], BF16)
    QT_t = [ctiles(T(QT_d.ap[g], f"QT{g}_")) for g in range(2)]
    kcT_d = scratch("kcT_d", [128, S], BF16); kcT_t = ctiles(kcT_d)
    vcT_d = scratch("vcT_d", [128, S], BF16); vcT_t = ctiles(vcT_d)
    ksT_d = scratch("ksT_d", [128, S], BF16); ksT_t = ctiles(ksT_d)
    kwT_d = scratch("kwT_d", [128, S], BF16); kwT_t = ctiles(kwT_d)
    vs_d = scratch("vs_d", [S, 128], BF16); vs_t = tiles(vs_d)
    vw_d = scratch("vw_d", [S, 128], BF16); vw_t = tiles(vw_d)
    ga_d = scratch("ga_d", [S, 48], F32); ga_t = tiles(ga_d)
    QrT_d = scratch("QrT_d", [512, S], BF16); QrT_t = ctiles(QrT_d)
    KrT_d = scratch("KrT_d", [512, S], BF16); KrT_t = ctiles(KrT_d)
    kz_d = scratch("kz_d", [S, 512], BF16); kz_t = tiles(kz_d)
    vr_d = scratch("vr_d", [S, 1024], BF16); vr_t = tiles(vr_d)
    sgr_d = scratch("sgr_d", [S, 1024], F32); sgr_t = tiles(sgr_d)
    gm_d = scratch("gm_d", [S, 2048], F32); gm_t = tiles(gm_d)
    kc_d = scratch("kc_d", [2, 256, 64], BF16)
    vc_d = scratch("vc_d", [2, 256, 64], BF16)
    aT_d = scratch("aT_d", [1024, S], BF16); aT_t = ctiles(aT_d)
    rT_d = scratch("rT_d", [1024, S], BF16); rT_t = ctiles(rT_d)

    WSLOT = 33792
    wslot = [P.sb([128, WSLOT], BF16, f"wslot{i}") for i in range(2)]
    ident = P.sb([128, 128], BF16, "ident")
    gain = P.sb([128, 1024], F32, "gain")
    gain2 = P.sb([128, 1024], F32, "gain2")
    xin = [P.sb([128, 1024], F32, f"xin{i}") for i in range(2)]
    xres = xin
    hb = [P.sb([128, 1024], BF16, f"hb{i}") for i in range(2)]
    st1 = [P.sb([128, 4], F32, f"st1_{i}") for i in range(2)]
    big = P.sb([128, 5632], BF16, "big")
    hT = P.sb([128, 8, 512], BF16, "hT")
    wk32 = [P.sb([128, 1024], F32, f"wk32_{i}") for i in range(3)]
    wk16 = [P.sb([128, 1024], BF16, f"wk16_{i}") for i in range(3)]
    sm = [P.sb([128, 64], F32, f"sm{i}") for i in range(8)]
    tCa = P.sb([128, 512], F32, "tCa"); tSa = P.sb([128, 128], F32, "tSa")
    tCr = P.sb([128, 512], F32, "tCr"); tSr = P.sb([128, 512], F32, "tSr")
    tZ = P.sb([128, 512], F32, "tZ")
    pb = [P.ps([128, 512], F32, f"pb{i}") for i in range(7)]
    pbf = P.ps([128, 1024], BF16, "pbf")

    negh = P.sb([128, 8], F32, "negh")
    P.op("pool", lambda e: e.memset(negh[:], -0.5), w=[negh])
    P.op("pool", lambda e: e.memset(ident[:], 0.0), w=[ident])
    P.op("pool", lambda e: e.affine_select(out=ident[:], in_=ident[:], pattern=[[-1, 128]], compare_op=ALU.not_equal,
                                           fill=1.0, base=0, channel_multiplier=1), r=[ident], w=[ident])

    cnt = {"rr": 0}

    def rr(lst):
        cnt["rr"] += 1
        return lst[cnt["rr"] % len(lst)]

    def mm(out_t, out_ap, a_t, a_ap, b_t, b_ap, start, stop, sgc=False):
        P.op("pe", lambda e: e.matmul(out_ap, lhsT=a_ap, rhs=b_ap, start=start, stop=stop, skip_group_check=sgc),
             r=[a_t, b_t], w=[out_t])

    def tr(out_t, out_ap, in_t, in_ap):
        P.op("pe", lambda e: e.transpose(out=out_ap, in_=in_ap, identity=ident[:]), r=[in_t, ident], w=[out_t])

    def load_w(slot_t, off, src_t, src_ap, shape):
        n = int(np.prod(shape[1:]))
        dst = slot_t[:, off:off + n]
        if len(shape) == 3:
            dst = dst.rearrange("p (a b) -> p a b", a=shape[1])
        P.dma(slot_t, dst, src_t, src_ap, q="pool")
        return dst

    def rmsnorm(x_tile, gain_t, hb_t, stt):
        sq = wk32[2]
        P.op("act", lambda e: e.activation(out=sq[:], in_=x_tile[:], func=AF.Square, accum_out=stt[:, 0:1]),
             r=[x_tile], w=[sq, stt])
        P.op("dve", lambda e: e.tensor_scalar(out=stt[:, 1:2], in0=stt[:, 0:1], scalar1=1.0 / D, scalar2=1e-6,
                                              op0=ALU.mult, op1=ALU.add), r=[stt], w=[stt])
        P.op("pool", lambda e: e.tensor_tensor(out=stt[:, 3:4], in0=stt[:, 1:2], in1=negh[:, 0:1], op=ALU.pow),
             r=[stt, negh], w=[stt])
        P.op("dve", lambda e: e.scalar_tensor_tensor(out=hb_t[:], in0=x_tile[:], scalar=stt[:, 3:4], in1=gain_t[:],
                                                     op0=ALU.mult, op1=ALU.mult), r=[x_tile, stt, gain_t], w=[hb_t])

    def transpose8(src_t, dst_t, dst_ap, nblk=8, eng="act"):
        for k in range(nblk):
            tr(pbf, pbf[:, k * 128:(k + 1) * 128], src_t, src_t[:, k * 128:(k + 1) * 128])
        src = pbf[:, 0:nblk * 128].rearrange("p (k t) -> p k t", k=nblk)
        if eng == "act":
            P.op("act", lambda e: e.copy(out=dst_ap, in_=src), r=[pbf], w=[dst_t])
        else:
            P.op("dve", lambda e: e.tensor_copy(out=dst_ap, in_=src), r=[pbf], w=[dst_t])

    def load_gain(gt, src):
        P.dma(gt, gt[:], src, src.ap.partition_broadcast(128))

    def carve(parent, ap, name=""):
        t = T(ap, name)
        t.last_w = parent.last_w
        t.readers = list(parent.readers)
        return t

    def uncarve(parent, subs):
        for t in subs:
            parent.readers.extend(t.readers)
            if t.last_w is not None:
                parent.readers.append(t.last_w)

    def ffn_pass(which, hf, slot, xsrc_t, xdst_t, norm_gain, final=False):
        pre = f"ffn{which}_"
        wg_d, wu_d, wd_d = I[pre + "w_gate"], I[pre + "w_up"], I[pre + "w_down"]
        c0 = hf * 1408
        wg = load_w(slot, 0, wg_d, wg_d.ap[:, c0:c0 + 1408].rearrange("(k p) f -> p k f", p=128), [128, 8, 1408])
        wu = load_w(slot, 11264, wu_d, wu_d.ap[:, c0:c0 + 1408].rearrange("(k p) f -> p k f", p=128), [128, 8, 1408])
        wd = load_w(slot, 22528, wd_d, wd_d.ap[c0:c0 + 1408, :].rearrange("(k p) d -> p k d", p=128), [128, 11, 1024])
        yield
        aT = big[:, 0:11 * 512].rearrange("p (f t) -> p f t", f=11)
        if hf == 0:
            load_gain(gain, norm_gain)
        if final:
            load_gain(gain2, I["final_norm"])
        for st in range(8 if lim is None else 1):
            if hf == 0:
                for tt in range(4):
                    ti = st * 4 + tt
                    xt = xin[ti % 2]
                    P.dma(xt, xt[:], xsrc_t[ti], xsrc_t[ti][:, :])
                    hbt = hb[ti % 2]
                    rmsnorm(xt, gain, hbt, st1[ti % 2])
                    transpose8(hbt, hT, hT[:, :, tt * 128:(tt + 1) * 128], eng="act")
                P.dma(hT_t[st], hT_t[st][:], hT, hT[:], q="pool")
            else:
                P.dma(hT, hT[:], hT_t[st], hT_t[st][:])
            for f in range(11):
                pg = pb[(2 * f) % 4]
                pu = pb[(2 * f + 1) % 4]
                for k in range(8):
                    mm(pg, pg[:], slot, wg[:, k, f * 128:(f + 1) * 128], hT, hT[:, k, :], k == 0, k == 7)
                for k in range(8):
                    mm(pu, pu[:], slot, wu[:, k, f * 128:(f + 1) * 128], hT, hT[:, k, :], k == 0, k == 7)
                sg = wk32[f % 2]
                P.op("act", lambda e, sg=sg, pg=pg: e.activation(out=sg[:, 0:512], in_=pg[:], func=AF.Silu),
                     r=[pg], w=[sg])
                P.op("dve", lambda e, sg=sg, pu=pu, f=f: e.tensor_tensor(out=aT[:, f, :], in0=pu[:], in1=sg[:, 0:512],
                                                                        op=ALU.mult), r=[pu, sg], w=[big])
            for tt in range(4):
                ti = st * 4 + tt
                xr = xres[ti % 2]
                P.dma(xr, xr[:], xsrc_t[ti], xsrc_t[ti][:, :])
                for half in range(2):
                    py = pb[4 + (2 * tt + half) % 3]
                    for f in range(11):
                        mm(py, py[:], big, aT[:, f, tt * 128:(tt + 1) * 128], slot, wd[:, f, half * 512:(half + 1) * 512],
                           f == 0, f == 10)
                    P.op("dve", lambda e, xr=xr, py=py, half=half: e.scalar_tensor_tensor(
                        out=xr[:, half * 512:(half + 1) * 512], in0=py[:], scalar=0.5,
                        in1=xr[:, half * 512:(half + 1) * 512], op0=ALU.mult, op1=ALU.add), r=[py, xr], w=[xr])
                if final:
                    ob = wk32[ti % 2]
                    stt = st1[ti % 2]
                    sq = wk32[2]
                    P.op("act", lambda e, xr=xr, stt=stt: e.activation(out=sq[:], in_=xr[:], func=AF.Square,
                                                                       accum_out=stt[:, 0:1]), r=[xr], w=[sq, stt])
                    P.op("dve", lambda e, stt=stt: e.tensor_scalar(out=stt[:, 1:2], in0=stt[:, 0:1], scalar1=1.0 / D,
                                                                   scalar2=1e-6, op0=ALU.mult, op1=ALU.add),
                         r=[stt], w=[stt])
                    P.op("pool", lambda e, stt=stt: e.tensor_tensor(out=stt[:, 3:4], in0=stt[:, 1:2], in1=negh[:, 0:1], op=ALU.pow),
                         r=[stt, negh], w=[stt])
                    P.op("dve", lambda e, xr=xr, stt=stt, ob=ob: e.scalar_tensor_tensor(
                        out=ob[:], in0=xr[:], scalar=stt[:, 3:4], in1=gain2[:], op0=ALU.mult, op1=ALU.mult),
                         r=[xr, stt, gain2], w=[ob])
                    P.dma(xdst_t[ti], xdst_t[ti][:, :], ob, ob[:], q="pool")
                else:
                    P.dma(xdst_t[ti], xdst_t[ti][:, :], xr, xr[:], q="pool")

    def rope(src_t, src_ap, H, R, Ct, Cap, St, Sap, out_t, out_ap, o32, t2):
        n = H * 64
        half = R // 2
        P.op("dve", lambda e: e.tensor_tensor(out=o32[:, 0:n], in0=src_ap, in1=Cap, op=ALU.mult), r=[Ct, src_t], w=[o32])
        s3 = src_ap.rearrange("p (h d) -> p h d", h=H)
        S3 = Sap.rearrange("p (h r) -> p h r", h=H)
        t3 = t2[:, 0:H * R].rearrange("p (h r) -> p h r", h=H)
        o3 = o32[:, 0:n].rearrange("p (h d) -> p h d", h=H)
        P.op("dve", lambda e: e.tensor_tensor(out=t3[:, :, 0:half], in0=s3[:, :, half:R], in1=S3[:, :, 0:half],
                                              op=ALU.mult), r=[St, src_t], w=[t2])
        P.op("dve", lambda e: e.tensor_tensor(out=t3[:, :, half:R], in0=s3[:, :, 0:half], in1=S3[:, :, half:R],
                                              op=ALU.mult), r=[St, src_t], w=[t2])
        P.op("dve", lambda e: e.tensor_tensor(out=o3[:, :, 0:R], in0=o3[:, :, 0:R], in1=t3, op=ALU.add),
             r=[o32, t2], w=[o32])
        P.op("act", lambda e: e.copy(out=out_ap, in_=o32[:, 0:n]), r=[o32], w=[out_t])

    def proj_pass0(slot):
        w_d = I["w_in"]
        w0 = load_w(slot, 0, w_d, w_d.ap[:, 0:1944].rearrange("(k p) f -> p k f", p=128), [128, 8, 1944])
        w1 = load_w(slot, 8 * 1944, w_d, w_d.ap[:, 1944:3888].rearrange("(k p) f -> p k f", p=128), [128, 8, 1944])
        yield

        def wcol(k, c0, c1):
            if c1 <= 1944:
                return w0[:, k, c0:c1]
            assert c0 >= 1944
            return w1[:, k, c0 - 1944:c1 - 1944]

        load_gain(gain, I["mix_norm"])
        P.dma(tZ, tZ[:], C["ZT"], C["ZT"][:, :])
        qTs = [P.sb([128, 4, 128], BF16, "qTa"), P.sb([128, 4, 128], BF16, "qTb")]
        blocks = [(0, 512), (512, 1024), (1024, 1536), (1536, 1840), (1840, 2352), (2352, 2864), (2864, 3376), (3376, 3888)]
        subs_c = []
        hbuf = [carve(hT, hT.ap[:, :, 0:128], "hTa"), carve(hT, hT.ap[:, :, 128:256], "hTb")]
        o32s = [carve(wk32[0], wk32[0].ap[:, 0:512], "o32a"), carve(wk32[0], wk32[0].ap[:, 512:1024], "o32b")]
        t2s = [carve(wk32[1], wk32[1].ap[:, 0:512], "t2a"), carve(wk32[1], wk32[1].ap[:, 512:1024], "t2b")]
        subs_c = [(hT, hbuf), (wk32[0], o32s), (wk32[1], t2s)]
        ntl = NT if lim is None else lim
        rcnt = {"r": 0, "q": 0}

        def prep_x(ti):
            xt = xin[ti % 2]
            P.dma(xt, xt[:], x1_t[ti], x1_t[ti][:, :])
            hbt = hb[ti % 2]
            rmsnorm(xt, gain, hbt, st1[ti % 2])
            hb_ = hbuf[ti % 2]
            transpose8(hbt, hb_, hb_[:, :, :], eng="act")
            P.dma(hmT_t[ti], hmT_t[ti][:], hb_, hb_[:, :, :], q="sp")

        def load_tabs_a(ti):
            P.dma(tCa, tCa[:], C["ropeCa"], C["ropeCa"][ti * 128:(ti + 1) * 128, :])
            P.dma(tSa, tSa[:], C["ropeSa"], C["ropeSa"][ti * 128:(ti + 1) * 128, :])

        def load_tabs_r(ti):
            P.dma(tCr, tCr[:], C["ropeCr"], C["ropeCr"][ti * 128:(ti + 1) * 128, :])
            P.dma(tSr, tSr[:], C["ropeSr"], C["ropeSr"][ti * 128:(ti + 1) * 128, :])

        def mm_block(ti, bi):
            c0, c1 = blocks[bi]
            pp = pb[bi % 7]
            hb_ = hbuf[ti % 2]
            subs = [(c0, c1)] if not (c0 < 1944 < c1) else [(c0, 1944), (1944, c1)]
            for (s0, s1) in subs:
                for k in range(8):
                    mm(pp, pp[:, s0 - c0:s1 - c0], hb_, hb_[:, k, :], slot, wcol(k, s0, s1), k == 0, k == 7)

        def post_a(ti, bi):
            pp = pb[bi % 7]
            rcnt["r"] += 1
            o32, t2 = o32s[rcnt["r"] % 2], t2s[rcnt["r"] % 2]

            def nextq():
                rcnt["q"] += 1
                return qTs[rcnt["q"] % 2]
            if bi in (0, 1):
                ob = wk16[bi % 2]
                rope(pp, pp[:, 0:512], 8, 16, tCa, tCa[:, :], tSa, tSa[:, :], ob, ob[:, 0:512], o32, t2)

                def fb():
                    qT = nextq()
                    transpose8(ob, qT, qT[:], nblk=4, eng="dve")
                    dst = QT_t[bi][ti]
                    P.dma(dst, dst[:].rearrange("(p r) t -> r p t", r=128), qT, qT[:], q="sp")
                return fb
            elif bi == 2:
                ob = wk16[2]
                P.op("act", lambda e: e.copy(out=ob[:, 0:256], in_=pp[:, 0:256]), r=[pp], w=[ob])
                rope(pp, pp[:, 256:384], 2, 16, tCa, tCa[:, 0:128], tSa, tSa[:, 0:32], ob, ob[:, 256:384], o32, t2)
                P.op("act", lambda e: e.copy(out=ob[:, 384:512], in_=pp[:, 384:512]), r=[pp], w=[ob])

                def fb():
                    qT = nextq()
                    transpose8(ob, qT, qT[:, 0:3, :], nblk=3, eng="dve")
                    P.dma(kcT_t[ti], kcT_t[ti][:], qT, qT[:, 0, :], q="sp")
                    P.dma(vcT_t[ti], vcT_t[ti][:], qT, qT[:, 1, :], q="sp")
                    P.dma(ksT_t[ti], ksT_t[ti][:], qT, qT[:, 2, :], q="sp")
                    P.dma(vs_t[ti], vs_t[ti][:, :], ob, ob[:, 384:512], q="sp")
                return fb
            elif bi == 3:
                ob = hb[(ti + 1) % 2]
                ob = wk16[2]
                rope(pp, pp[:, 0:128], 2, 16, tCa, tCa[:, 0:128], tSa, tSa[:, 0:32], ob, ob[:, 0:128], o32, t2)
                P.op("act", lambda e: e.copy(out=ob[:, 128:256], in_=pp[:, 128:256]), r=[pp], w=[ob])
                gsb = sm[0]
                P.op("act", lambda e: e.activation(out=gsb[:, 0:48], in_=pp[:, 256:304], func=AF.Sigmoid), r=[pp], w=[gsb])

                def fb():
                    qT = nextq()
                    P.dma(ga_t[ti], ga_t[ti][:, :], gsb, gsb[:, 0:48], q="sp")
                    transpose8(ob, qT, qT[:, 0:1, :], nblk=1, eng="dve")
                    P.dma(kwT_t[ti], kwT_t[ti][:], qT, qT[:, 0, :], q="sp")
                    P.dma(vw_t[ti], vw_t[ti][:, :], ob, ob[:, 128:256], q="sp")
                return fb
            elif bi in (4, 5):
                ob = wk16[bi % 2]
                rope(pp, pp[:, 0:512], 8, 64, tCr, tCr[:, :], tSr, tSr[:, :], ob, ob[:, 0:512], o32, t2)
                kz = None
                if bi == 5:
                    kz = hb[(ti + 1) % 2] if False else wk16[2]
                    P.op("pool", lambda e: e.tensor_tensor(out=kz[:, 0:512], in0=o32[:, 0:512], in1=tZ[:], op=ALU.mult),
                         r=[o32, tZ], w=[kz])

                def fb():
                    qT = nextq()
                    transpose8(ob, qT, qT[:], nblk=4, eng="dve")
                    dst = (QrT_t if bi == 4 else KrT_t)[ti]
                    P.dma(dst, dst[:].rearrange("(p r) t -> r p t", r=128), qT, qT[:], q="sp")
                    if bi == 5:
                        P.dma(kz_t[ti], kz_t[ti][:, :], kz, kz[:, 0:512], q="sp")
                return fb
            else:
                ob = wk16[bi % 2]
                P.op("act", lambda e: e.copy(out=ob[:, 0:512], in_=pp[:]), r=[pp], w=[ob])
                hh = bi - 6

                def fb():
                    P.dma(vr_t[ti], vr_t[ti][:, hh * 512:(hh + 1) * 512], ob, ob[:, 0:512], q="sp")
                return fb

        load_tabs_a(0)
        load_tabs_r(0)
        prep_x(0)
        for ti in range(ntl):
            for bi in range(7):
                mm_block(ti, bi)
            f0 = post_a(ti, 0)
            mm_block(ti, 7)
            f1 = post_a(ti, 1)
            f0()
            f2 = post_a(ti, 2)
            f1()
            f2()
            f3 = post_a(ti, 3)
            if ti + 1 < ntl:
                prep_x(ti + 1)
            f3()
            if ti + 1 < ntl:
                load_tabs_a(ti + 1)
            f4 = post_a(ti, 4)
            f5 = post_a(ti, 5)
            f4()
            if ti + 1 < ntl:
                load_tabs_r(ti + 1)
            f6 = post_a(ti, 6)
            f5()
            f7 = post_a(ti, 7)
            f6()
            f7()
        for parent, subs in subs_c:
            uncarve(parent, subs)

    def proj_pass1(slot):
        w_d = I["w_in"]
        w0 = load_w(slot, 0, w_d, w_d.ap[:, 3888:3888 + 1536].rearrange("(k p) f -> p k f", p=128), [128, 8, 1536])
        w1 = load_w(slot, 8 * 1536, w_d, w_d.ap[:, 3888 + 1536:DIN].rearrange("(k p) f -> p k f", p=128), [128, 8, 1536])
        yield
        for ti in range(NT if lim is None else lim):
            P.dma(hT, hT[:, :, 0:128], hmT_t[ti], hmT_t[ti][:])
            for bi in range(6):
                pp = pb[bi % 7]
                wv = w0 if bi < 3 else w1
                cc = (bi % 3) * 512
                for k in range(8):
                    mm(pp, pp[:], hT, hT[:, k, 0:128], slot, wv[:, k, cc:cc + 512], k == 0, k == 7)
                o32 = wk32[bi % 3]
                fn = AF.Silu if bi < 2 else AF.Sigmoid
                P.op("act", lambda e, pp=pp, o32=o32, fn=fn: e.activation(out=o32[:, 0:512], in_=pp[:], func=fn),
                     r=[pp], w=[o32])
                if bi < 2:
                    P.dma(sgr_t[ti], sgr_t[ti][:, bi * 512:(bi + 1) * 512], o32, o32[:, 0:512], q="pool")
                else:
                    P.dma(gm_t[ti], gm_t[ti][:, (bi - 2) * 512:(bi - 1) * 512], o32, o32[:, 0:512], q="pool")


    kcT2_d = scratch("kcT2_d", [2, 64, 256], BF16)

    def cmp_phase(slot):
        subs = []

        def cv(off, n, parts=128, name=""):
            t = carve(slot, slot.ap[0:parts, off:off + n], name)
            subs.append(t)
            return t
        w1 = [cv(0, 8192, 64, "w1k"), cv(8192, 8192, 64, "w1v")]
        w2 = [cv(16384, 128, 128, "w2k"), cv(16512, 128, 128, "w2v")]
        tokT = [cv(16640, 4096, 64, "tokT0"), cv(20736, 4096, 64, "tokT1")]
        GT = cv(24832, 512, 128, "GT")
        peb = cv(25344, 64, 32, "peb")
        peT = cv(25408, 32, 64, "peT")
        kcb = cv(25440, 64, 128, "kcb")
        kcTs = cv(25504, 128, 64, "kcTs")
        tCc = sm[1]; tSc = sm[2]; bias = sm[3]
        srcs = [(I["cmp_k_w1"], I["cmp_k_w2"], kcT_d, kc_d), (I["cmp_v_w1"], I["cmp_v_w2"], vcT_d, vc_d)]
        P.dma(peb, peb[:, :], I["cmp_pos_emb"], I["cmp_pos_emb"][:, :], q="pool")
        tr(pbf, pbf[0:64, 0:32], peb, peb[:, :]) if False else P.op(
            "pe", lambda e: e.transpose(out=pbf[0:64, 0:32], in_=peb[:, :], identity=ident[0:32, 0:32]),
            r=[peb, ident], w=[pbf])
        P.op("act", lambda e: e.copy(out=peT[:, :], in_=pbf[0:64, 0:32]), r=[pbf], w=[peT])
        P.op("pool", lambda e: e.memset(GT[:, :], 0.0), w=[GT])
        GT3 = GT[:, :].rearrange("p (m n) -> p m n", m=2)
        for kv in range(2):
            w1_d, w2_d, tT_d, o_d = srcs[kv]
            w1v = w1[kv][:, :].rearrange("p (l j) -> p l j", l=32)
            P.dma(w1[kv], w1v, w1_d, w1_d.ap.rearrange("(l d) j -> d l j", d=64), q="pool")
            w2v = w2[kv][:, :].rearrange("p (m o) -> p m o", m=2)
            P.dma(w2[kv], w2v, w2_d, w2_d.ap.rearrange("(m p) o -> p m o", p=128), q="pool")
            pbias = pb[6]
            for m in range(2):
                for l in range(32):
                    mm(pbias, pbias[:, m:m + 1], w1[kv], w1v[:, l, m * 128:(m + 1) * 128], peT, peT[:, l:l + 1], l == 0, l == 31)
            P.op("act", lambda e, pbias=pbias: e.copy(out=bias[:, 0:2], in_=pbias[:, 0:2]), r=[pbias], w=[bias])
            for g in range(2):
                tk = tokT[g]
                P.dma(tk, tk[:, :], tT_d, tT_d.ap[g * 64:(g + 1) * 64, :])
                tok3 = tk[:, :].rearrange("p (n r) -> p n r", r=16)
                for m in range(2):
                    ph = pb[m]
                    for l in range(32):
                        a, rr_ = l // 16, l % 16
                        mm(ph, ph[:, 0:255], w1[kv], w1v[:, l, m * 128:(m + 1) * 128], tk, tok3[:, a:a + 255, rr_], l == 0, l == 31)
                    hbx, x2, z = wk32[0], wk32[1], wk32[2]
                    P.op("act", lambda e, ph=ph, m=m: e.activation(out=hbx[:, 0:255], in_=ph[:, 0:255], func=AF.Identity,
                                                                   bias=bias[:, m:m + 1]), r=[ph, bias], w=[hbx])
                    P.op("dve", lambda e: e.tensor_tensor(out=x2[:, 0:255], in0=hbx[:, 0:255], in1=hbx[:, 0:255], op=ALU.mult),
                         r=[hbx], w=[x2])
                    P.op("dve", lambda e: e.tensor_scalar(out=x2[:, 0:255], in0=x2[:, 0:255], scalar1=0.044715, scalar2=1.0,
                                                          op0=ALU.mult, op1=ALU.add), r=[x2], w=[x2])
                    P.op("dve", lambda e: e.tensor_tensor(out=z[:, 0:255], in0=x2[:, 0:255], in1=hbx[:, 0:255], op=ALU.mult),
                         r=[x2, hbx], w=[z])
                    P.op("act", lambda e: e.activation(out=z[:, 0:255], in_=z[:, 0:255], func=AF.Sigmoid, scale=1.5957691216),
                         r=[z], w=[z])
                    P.op("dve", lambda e, m=m: e.tensor_tensor(out=GT3[:, m, 0:255], in0=z[:, 0:255], in1=hbx[:, 0:255],
                                                               op=ALU.mult), r=[z, hbx], w=[GT])
                for j in range(2):
                    po = pb[2 + j]
                    for m in range(2):
                        mm(po, po[:, 0:64], GT, GT3[:, m, j * 128:(j + 1) * 128], w2[kv], w2v[:, m, :], m == 0, m == 1)
                    if kv == 0:
                        P.dma(tCc, tCc[:, 0:64], C["ropeCc"], C["ropeCc"][j * 128:(j + 1) * 128, :])
                        P.dma(tSc, tSc[:, 0:16], C["ropeSc"], C["ropeSc"][j * 128:(j + 1) * 128, :])
                        rope(po, po[:, 0:64], 1, 16, tCc, tCc[:, 0:64], tSc, tSc[:, 0:16], kcb, kcb[:, :], wk32[0], wk32[1])
                        P.op("pe", lambda e: e.transpose(out=pbf[0:64, 0:128], in_=kcb[:, :], identity=ident[:]),
                             r=[kcb, ident], w=[pbf])
                        P.op("act", lambda e: e.copy(out=kcTs[:, :], in_=pbf[0:64, 0:128]), r=[pbf], w=[kcTs])
                        P.dma(kcT2_d, kcT2_d.ap[g, :, j * 128:(j + 1) * 128], kcTs, kcTs[:, :], q="pool")
                    else:
                        P.op("act", lambda e, po=po: e.copy(out=kcb[:, :], in_=po[:, 0:64]), r=[po], w=[kcb])
                    P.dma(o_d, o_d.ap[g, j * 128:(j + 1) * 128, :], kcb, kcb[:, :], q="pool")
        uncarve(slot, subs)

    def nsa_phase(slot):
        subs = []

        def cv(off, n, parts=128, name=""):
            t = carve(slot, slot.ap[0:parts, off:off + n], name)
            subs.append(t)
            return t
        KsT = cv(0, 4096, 64, "KsT"); KwT = cv(4096, 4096, 64, "KwT")
        Vs = cv(8192, 2080, 128, "Vs"); Vw = cv(10272, 2080, 128, "Vw")
        KcT = cv(12352, 256, 64, "KcT"); Vc = cv(12608, 130, 128, "Vc")
        C2S = cv(12738, 128, 128, "C2S"); EM = cv(12866, 4096, 64, "EM")
        QTc = [cv(16962 + i * 1024, 1024, 64, f"QTc{i}") for i in range(2)]
        Et = [cv(19010 + i * 1024, 1024, 128, f"Et{i}") for i in range(3)]
        negm = cv(22082, 1024, 64, "negm")
        selb = cv(23106, 64, 128, "selb")
        ab = cv(23170, 512, 128, "ab")
        aTs = cv(23682, 512, 128, "aTs")
        TK = tCa; TA = tCr
        P.dma(TK, TK[:, 0:128], C["TK"], C["TK"][:, :])
        P.dma(TA, TA[:, 0:128], C["TA"], C["TA"][:, :])
        P.dma(C2S, C2S[:, :].rearrange("p (j s) -> p j s", j=2), C["C2S"], C["C2S"].ap.rearrange("(j p) s -> p j s", p=128), q="pool")
        P.dma(EM, EM[:, 0:2048], C["EM"], C["EM"][:, 0:2048], q="pool")
        P.dma(EM, EM[:, 2048:4096], C["EM"], C["EM"][:, 2048:4096], q="pool")
        Vs3 = Vs[:, :].rearrange("p (j d) -> p j d", d=65)
        Vw3 = Vw[:, :].rearrange("p (j d) -> p j d", d=65)
        Vc3 = Vc[:, :].rearrange("p (j d) -> p j d", d=65)
        C2S3 = C2S[:, :].rearrange("p (j s) -> p j s", j=2)
        SB = [(pb[0], pb[1]), (pb[2], pb[3])]
        OA, OB, OC = pb[4], pb[5], pb[6]
        OA3 = OA[:, 0:260].rearrange("p (h d) -> p h d", d=65)
        OB3 = OB[:, 0:260].rearrange("p (h d) -> p h d", d=65)
        acc = xin[0]; tmpo = xin[1]
        acc3 = acc[:, 0:512].rearrange("p (h d) -> p h d", d=64)
        tmp3 = tmpo[:, 0:512].rearrange("p (h d) -> p h d", d=64)
        den = sm[2]; coef = sm[3]; imp = sm[4]; imp2 = sm[5]; m8 = sm[6]; imp3 = sm[7]
        sidx = {"s": 0, "e": 0}

        def scores(lhs_t, lhs_ap, qt, mask_j=None):
            b = SB[sidx["s"] % 2]; sidx["s"] += 1
            q2 = qt[:, :]
            for hh in range(2):
                mm(b[hh], b[hh][:], lhs_t, lhs_ap, qt, q2[:, hh * 512:(hh + 1) * 512], True, mask_j is None)
            if mask_j is not None:
                for hh in range(2):
                    mm(b[hh], b[hh][:], EM, EM[:, mask_j * 128:(mask_j + 1) * 128], negm, negm[:, hh * 512:(hh + 1) * 512], False, True)
            et = Et[sidx["e"] % 3]; sidx["e"] += 1
            for hh in range(2):
                P.op("act", lambda e, hh=hh, et=et, b=b: e.activation(out=et[:, hh * 512:(hh + 1) * 512], in_=b[hh][:],
                                                                      func=AF.Exp, scale=0.125), r=[b[hh]], w=[et])
            return et

        def amask(et, cm, qstep, base):
            P.op("pool", lambda e: e.affine_select(out=et[:, :], in_=et[:, :], pattern=[[0, 8], [qstep, 128]],
                                                   compare_op=ALU.is_ge, fill=0.0, base=base, channel_multiplier=cm),
                 r=[et], w=[et])

        def pv(et, v_t, v_ap, first, last):
            for h in range(8):
                o_t = OA if h < 4 else OB
                o3 = OA3 if h < 4 else OB3
                mm(o_t, o3[:, h % 4, :], et, et[:, h * 128:(h + 1) * 128], v_t, v_ap, first and (h % 4 == 0), last, sgc=True)

        pend = []

        def flush():
            while pend:
                work, post = pend.pop(0)
                work()
                if post is not None:
                    post()

        def finish_branch(bidx, first, gat):
            P.op("dve", lambda e: e.tensor_scalar(out=den[:, 0:4], in0=OA3[:, :, 64], scalar1=1e-20, scalar2=None, op0=ALU.max),
                 r=[OA], w=[den])
            P.op("dve", lambda e: e.tensor_scalar(out=den[:, 4:8], in0=OB3[:, :, 64], scalar1=1e-20, scalar2=None, op0=ALU.max),
                 r=[OB], w=[den])
            P.op("dve", lambda e: e.reciprocal(out=den[:, 8:16], in_=den[:, 0:8]), r=[den], w=[den])
            g3 = gat[:, 0:24].rearrange("p (h b) -> p h b", b=3)
            P.op("dve", lambda e: e.tensor_tensor(out=coef[:, 0:8], in0=den[:, 8:16], in1=g3[:, :, bidx], op=ALU.mult),
                 r=[den, gat], w=[coef])
            dst, d3 = (acc, acc3) if first else (tmpo, tmp3)
            P.op("dve", lambda e: e.tensor_tensor(out=d3[:, 0:4, :], in0=OA3[:, :, 0:64],
                                                  in1=coef[:, 0:4].unsqueeze(2).broadcast_to([128, 4, 64]), op=ALU.mult),
                 r=[OA, coef], w=[dst])
            P.op("dve", lambda e: e.tensor_tensor(out=d3[:, 4:8, :], in0=OB3[:, :, 0:64],
                                                  in1=coef[:, 4:8].unsqueeze(2).broadcast_to([128, 4, 64]), op=ALU.mult),
                 r=[OB, coef], w=[dst])
            if not first:
                P.op("pool", lambda e: e.tensor_tensor(out=acc[:, 0:512], in0=acc[:, 0:512], in1=tmpo[:, 0:512], op=ALU.add),
                     r=[acc, tmpo], w=[acc])

        for g in range(2):
            P.dma(KsT, KsT[:, :], ksT_d, ksT_d.ap[g * 64:(g + 1) * 64, :])
            P.dma(KwT, KwT[:, :], kwT_d, kwT_d.ap[g * 64:(g + 1) * 64, :])
            P.dma(Vs, Vs3[:, :, 0:64], vs_d, vs_d.ap[:, g * 64:(g + 1) * 64].rearrange("(j p) d -> p j d", p=128))
            P.dma(Vw, Vw3[:, :, 0:64], vw_d, vw_d.ap[:, g * 64:(g + 1) * 64].rearrange("(j p) d -> p j d", p=128))
            P.op("pool", lambda e: e.memset(Vs3[:, :, 64:65], 1.0), w=[Vs])
            P.op("pool", lambda e: e.memset(Vw3[:, :, 64:65], 1.0), w=[Vw])
            P.dma(KcT, KcT[:, :], kcT2_d, kcT2_d.ap[g])
            P.dma(Vc, Vc3[:, :, 0:64], vc_d, vc_d.ap[g].rearrange("(j p) d -> p j d", p=128))
            P.op("pool", lambda e: e.memset(Vc3[:, :, 64:65], 1.0), w=[Vc])
            gats = [sm[1], sm[0]]
            for c in (range(NT) if lim is None else lim_c):
                qt = QTc[c % 2]
                gat = gats[c % 2]
                P.dma(qt, qt[:, :].rearrange("d (h t) -> d h t", h=8), QT_t[g][c],
                      QT_t[g][c][:].rearrange("(h d) t -> d h t", d=64))
                P.dma(gat, gat[:, 0:24], ga_t[c], ga_t[c][:, g * 24:(g + 1) * 24])

                def post_cmp(c=c, gat=gat):
                    finish_branch(0, True, gat)
                    for h in range(8):
                        if h == 0:
                            P.op("dve", lambda e: e.tensor_scalar(out=imp[:, 0:64], in0=OC[:, 0:64], scalar1=den[:, 8:9],
                                                                  scalar2=None, op0=ALU.mult), r=[OC, den], w=[imp])
                        else:
                            P.op("dve", lambda e, h=h: e.scalar_tensor_tensor(
                                out=imp[:, 0:64], in0=OC[:, h * 64:(h + 1) * 64], scalar=den[:, 8 + h:9 + h], in1=imp[:, 0:64],
                                op0=ALU.mult, op1=ALU.add), r=[OC, den, imp], w=[imp])
                    off = 64 - 2 * c
                    P.op("dve", lambda e: e.tensor_tensor(out=imp2[:, 0:64], in0=imp[:, 0:64], in1=TK[:, off:off + 64], op=ALU.mult),
                         r=[imp, TK], w=[imp2])
                    P.op("dve", lambda e: e.tensor_tensor(out=imp2[:, 0:64], in0=imp2[:, 0:64], in1=TA[:, off:off + 64], op=ALU.add),
                         r=[imp2, TA], w=[imp2])
                    P.op("dve", lambda e: e.memset(imp2[:, 0:1], 1.0e4), r=[imp2], w=[imp2])
                    P.op("dve", lambda e: e.max(out=m8[:, 0:8], in_=imp2[:, 0:64]), r=[imp2], w=[m8])
                    P.op("dve", lambda e: e.match_replace(out=imp3[:, 0:64], in_to_replace=m8[:, 0:8], in_values=imp2[:, 0:64],
                                                          imm_value=-3.0e4), r=[imp2, m8], w=[imp3])
                    P.op("dve", lambda e: e.max(out=m8[:, 8:16], in_=imp3[:, 0:64]), r=[imp3, m8], w=[m8])
                    P.op("dve", lambda e: e.tensor_scalar(out=selb[:, :], in0=imp2[:, 0:64], scalar1=m8[:, 15:16], scalar2=None,
                                                          op0=ALU.is_ge), r=[imp2, m8], w=[selb])
                    P.op("pe", lambda e: e.transpose(out=pbf[0:64, 0:128], in_=selb[:, :], identity=ident[:]),
                         r=[selb, ident], w=[pbf])
                    P.op("dve", lambda e: e.tensor_scalar(out=negm[:, :].rearrange("p (h q) -> p h q", h=8),
                                                          in0=pbf[0:64, 0:128].unsqueeze(1).broadcast_to([64, 8, 128]),
                                                          scalar1=-NEGBIG, scalar2=NEGBIG, op0=ALU.mult, op1=ALU.add),
                         r=[pbf], w=[negm])

                def post_win(gat=gat):
                    finish_branch(2, False, gat)

                def post_sel(c=c, gat=gat):
                    finish_branch(1, False, gat)
                    P.op("act", lambda e: e.copy(out=ab[:, :], in_=acc[:, 0:512]), r=[acc], w=[ab])
                    transpose8(ab, aTs, aTs[:, :].rearrange("p (k t) -> p k t", k=4), nblk=4, eng="act")
                    dst = aT_t[c]
                    P.dma(dst, dst[g * 512:(g + 1) * 512, :].rearrange("(k r) t -> r k t", r=128), aTs,
                          aTs[:, :].rearrange("p (k t) -> p k t", k=4), q="pool")

                njt = 1 if c <= 15 else 2
                for j in range(njt):
                    et = scores(KcT, KcT[:, j * 128:(j + 1) * 128], qt)
                    amask(et, -16, 1, -16 * (128 * j - 8 * c) - 31)
                    flush()

                    def work(et=et, j=j, njt=njt):
                        pv(et, Vc, Vc3[:, j, :], j == 0, j == njt - 1)
                        for h in range(8):
                            mm(OC, OC[:, h * 64:(h + 1) * 64], et, et[:, h * 128:(h + 1) * 128], C2S, C2S3[:, j, :],
                               j == 0 and h == 0, j == njt - 1, sgc=True)
                    pend.append((work, post_cmp if j == njt - 1 else None))
                j0 = max(0, c - 4)
                for j in range(j0, c + 1):
                    et = scores(KwT, KwT[:, j * 128:(j + 1) * 128], qt)
                    if j == c:
                        amask(et, -1, 1, 0)
                    if j == c - 4:
                        amask(et, 1, -1, -1)
                    flush()
                    pend.append((lambda et=et, j=j, j0=j0, c=c: pv(et, Vw, Vw3[:, j, :], j == j0, j == c),
                                 post_win if j == c else None))
                for j in range(c + 1):
                    if j == 0:
                        flush()
                    et = scores(KsT, KsT[:, j * 128:(j + 1) * 128], qt, mask_j=j)
                    if j == c:
                        amask(et, -1, 1, 0)
                    flush()
                    pend.append((lambda et=et, j=j, c=c: pv(et, Vs, Vs3[:, j, :], j == 0, j == c),
                                 post_sel if j == c else None))
            flush()
        uncarve(slot, subs)

    def ret_phase():
        subs = []
        bufs = []
        for i in range(2):
            q_ = carve(big, big.ap[0:64, i * 2560:i * 2560 + 1024], f"QrTc{i}")
            k_ = carve(big, big.ap[0:64, i * 2560 + 1024:i * 2560 + 2048], f"KrTc{i}")
            z_ = carve(big, big.ap[:, i * 2560 + 2048:i * 2560 + 2560], f"kzc{i}")
            subs += [q_, k_, z_]
            bufs.append((q_, k_, z_))
        dec = [tCr, tSr]
        gct = [tCa, tZ]
        P.dma(dec[0], dec[0][:, :], C["decayT"], C["decayT"][:, 0:512])
        P.dma(dec[1], dec[1][:, :], C["decayT"], C["decayT"][:, 512:1024])
        P.dma(gct[0], gct[0][0:64, :], C["GC"], C["GC"][:, 0:512])
        P.dma(gct[1], gct[1][0:64, :], C["GC"], C["GC"][:, 512:1024])
        xit = sm[1]
        P.dma(xit, xit[:, 0:8], C["XI"], C["XI"][:, :])
        load_gain(gain2, I["ret_gn_gain"])
        Rst = xin[0]; osbs = [xin[1], gain]; sgr = wk32[0]; sq = wk32[1]; tmp = wk32[2]
        Rbf = wk16[0]; ATs = wk16[1]; rob = wk16[2]
        stt = sm[2]
        P.op("pool", lambda e: e.memset(Rst[0:64, :], 0.0), w=[Rst])
        P.op("pool", lambda e: e.memset(Rbf[0:64, :], 0.0), w=[Rbf])
        sq3 = sq[:, :].rearrange("p (h e) -> p h e", h=8)
        tmp3 = tmp[:, :].rearrange("p (h e) -> p h e", h=8)

        def stage1(n):
            q_, k_, z_ = bufs[n % 2]
            vt = hb[n % 2]
            osb = osbs[n % 2]
            P.dma(q_, q_[:, :].rearrange("d (h t) -> d h t", h=8), QrT_t[n], QrT_t[n][:].rearrange("(h d) t -> d h t", d=64))
            P.dma(k_, k_[:, :].rearrange("d (h t) -> d h t", h=8), KrT_t[n], KrT_t[n][:].rearrange("(h d) t -> d h t", d=64))
            P.dma(z_, z_[:, :], kz_t[n], kz_t[n][:, :])
            P.dma(vt, vt[:, :], vr_t[n], vr_t[n][:, :])
            for h in range(8):
                b = pb[h // 4]
                mm(b, b[:, (h % 4) * 128:(h % 4 + 1) * 128], k_, k_[:, h * 128:(h + 1) * 128], q_, q_[:, h * 128:(h + 1) * 128], True, True)
            if n > 0:
                for h in range(8):
                    b = pb[4 + h // 4]
                    mm(b, b[:, (h % 4) * 128:(h % 4 + 1) * 128], q_, q_[:, h * 128:(h + 1) * 128], Rbf, Rbf[0:64, h * 128:(h + 1) * 128], True, True)
            for hh in range(2):
                P.op("dve", lambda e, hh=hh: e.tensor_tensor(out=ATs[:, hh * 512:(hh + 1) * 512], in0=pb[hh][:], in1=dec[hh][:, :],
                                                             op=ALU.mult), r=[pb[hh], dec[hh]], w=[ATs])
            for h in range(8):
                b = pb[2 + h // 4]
                mm(b, b[:, (h % 4) * 128:(h % 4 + 1) * 128], ATs, ATs[:, h * 128:(h + 1) * 128], vt, vt[:, h * 128:(h + 1) * 128], True, True)
            for hh in range(2):
                b = pb[6]
                for h4 in range(4):
                    h = hh * 4 + h4
                    mm(b, b[0:64, h4 * 128:(h4 + 1) * 128], z_, z_[:, h * 64:(h + 1) * 64], vt, vt[:, h * 128:(h + 1) * 128], True, True)
                P.op("pool", lambda e, hh=hh: e.tensor_tensor(out=Rst[0:64, hh * 512:(hh + 1) * 512], in0=Rst[0:64, hh * 512:(hh + 1) * 512],
                                                              in1=gct[hh][0:64, :], op=ALU.mult), r=[Rst, gct[hh]], w=[Rst])
                P.op("dve", lambda e, hh=hh, b=b: e.tensor_tensor(out=Rst[0:64, hh * 512:(hh + 1) * 512], in0=b[0:64, :],
                                                                  in1=Rst[0:64, hh * 512:(hh + 1) * 512], op=ALU.add), r=[b, Rst], w=[Rst])
            P.op("act", lambda e: e.copy(out=Rbf[0:64, :], in_=Rst[0:64, :]), r=[Rst], w=[Rbf])
            for hh in range(2):
                P.op("act", lambda e, hh=hh: e.copy(out=osb[:, hh * 512:(hh + 1) * 512], in_=pb[2 + hh][:]), r=[pb[2 + hh]], w=[osb])
            if n > 0:
                for hh in range(2):
                    P.op("dve", lambda e, hh=hh: e.tensor_tensor(
                        out=tmp3[:, hh * 4:(hh + 1) * 4, :], in0=pb[4 + hh][:].rearrange("p (h e) -> p h e", h=4),
                        in1=xit[:, hh * 4:(hh + 1) * 4].unsqueeze(2).broadcast_to([128, 4, 128]), op=ALU.mult),
                        r=[pb[4 + hh], xit], w=[tmp])
                P.op("pool", lambda e: e.tensor_tensor(out=osb[:, :], in0=osb[:, :], in1=tmp[:, :], op=ALU.add), r=[osb, tmp], w=[osb])

        def stage2(n):
            osb = osbs[n % 2]
            osb3 = osb[:, :].rearrange("p (h e) -> p h e", h=8)
            P.dma(sgr, sgr[:, :], sgr_t[n], sgr_t[n][:, :])
            P.op("dve", lambda e: e.tensor_reduce(out=stt[:, 0:8], in_=osb3, axis=AX.X, op=ALU.add), r=[osb], w=[stt])
            P.op("act", lambda e: e.activation(out=sq[:, :], in_=osb[:, :], func=AF.Square), r=[osb], w=[sq])
            P.op("dve", lambda e: e.tensor_reduce(out=stt[:, 8:16], in_=sq3, axis=AX.X, op=ALU.add), r=[sq], w=[stt])
            P.op("dve", lambda e: e.tensor_scalar(out=stt[:, 0:8], in0=stt[:, 0:8], scalar1=1.0 / 128, scalar2=None, op0=ALU.mult),
                 r=[stt], w=[stt])
            P.op("dve", lambda e: e.tensor_tensor(out=stt[:, 16:24], in0=stt[:, 0:8], in1=stt[:, 0:8], op=ALU.mult), r=[stt], w=[stt])
            P.op("dve", lambda e: e.scalar_tensor_tensor(out=stt[:, 24:32], in0=stt[:, 8:16], scalar=1.0 / 128, in1=stt[:, 16:24],
                                                         op0=ALU.mult, op1=ALU.subtract), r=[stt], w=[stt])
            P.op("dve", lambda e: e.tensor_scalar(out=stt[:, 24:32], in0=stt[:, 24:32], scalar1=1e-5, scalar2=None, op0=ALU.add),
                 r=[stt], w=[stt])
            P.op("pool", lambda e: e.tensor_tensor(out=stt[:, 40:48], in0=stt[:, 24:32], in1=negh[:, 0:8], op=ALU.pow),
                 r=[stt, negh], w=[stt])
            P.op("pool", lambda e: e.tensor_tensor(out=osb3, in0=osb3, in1=stt[:, 0:8].unsqueeze(2).broadcast_to([128, 8, 128]),
                                                   op=ALU.subtract), r=[osb, stt], w=[osb])
            P.op("dve", lambda e: e.tensor_tensor(out=osb3, in0=osb3, in1=stt[:, 40:48].unsqueeze(2).broadcast_to([128, 8, 128]),
                                                  op=ALU.mult), r=[osb, stt], w=[osb])
            P.op("pool", lambda e: e.tensor_tensor(out=osb[:, :], in0=osb[:, :], in1=gain2[:, :], op=ALU.mult), r=[osb, gain2], w=[osb])
            P.op("pool", lambda e: e.tensor_tensor(out=rob[:, :], in0=osb[:, :], in1=sgr[:, :], op=ALU.mult), r=[osb, sgr], w=[rob])
            transpose8(rob, hT, hT[:, :, 0:128], eng="act")
            dst = rT_t[n]
            P.dma(dst, dst[:].rearrange("(k r) t -> r k t", r=128), hT, hT[:, :, 0:128], q="pool")

        nn = NT if lim is None else lim
        for n in range(nn):
            stage1(n)
            if n > 0:
                stage2(n - 1)
        stage2(nn - 1)
        uncarve(big, subs)

    def merge_phase(slot):
        Wn = load_w(slot, 0, I["w_branch_nsa"], I["w_branch_nsa"].ap.rearrange("(k p) f -> p k f", p=128), [128, 8, 1024])
        Wr = load_w(slot, 8192, I["w_branch_ret"], I["w_branch_ret"].ap.rearrange("(k p) f -> p k f", p=128), [128, 8, 1024])
        Wo = load_w(slot, 16384, I["w_out"], I["w_out"].ap.rearrange("(k p) f -> p k f", p=128), [128, 8, 1024])
        yield
        subs = []
        bufs = []
        for i in range(2):
            a_ = carve(big, big.ap[:, i * 2048:i * 2048 + 1024], f"aTt{i}")
            r_ = carve(big, big.ap[:, i * 2048 + 1024:i * 2048 + 2048], f"rTt{i}")
            subs += [a_, r_]
            bufs.append((a_, r_))
        for ti in range(NT if lim is None else lim):
            a_, r_ = bufs[ti % 2]
            a3 = a_[:, :].rearrange("p (k t) -> p k t", k=8)
            r3 = r_[:, :].rearrange("p (k t) -> p k t", k=8)
            P.dma(a_, a3, aT_t[ti], aT_t[ti][:].rearrange("(k r) t -> r k t", r=128))
            P.dma(r_, r3, rT_t[ti], rT_t[ti][:].rearrange("(k r) t -> r k t", r=128))
            ga, gr = wk32[0], wk32[1]
            P.dma(ga, ga[:, :], gm_t[ti], gm_t[ti][:, 0:1024])
            P.dma(gr, gr[:, :], gm_t[ti], gm_t[ti][:, 1024:2048])
            x1s = wk32[2]
            P.dma(x1s, x1s[:, :], x1_t[ti], x1_t[ti][:, :])
            for half in range(2):
                for k in range(8):
                    mm(pb[half], pb[half][:], a_, a3[:, k, :], slot, Wn[:, k, half * 512:(half + 1) * 512], k == 0, k == 7)
                for k in range(8):
                    mm(pb[2 + half], pb[2 + half][:], r_, r3[:, k, :], slot, Wr[:, k, half * 512:(half + 1) * 512], k == 0, k == 7)
            m1, m2 = xin[0], xin[1]
            mb = hb[ti % 2]
            for half in range(2):
                cs = slice(half * 512, (half + 1) * 512)
                P.op("dve", lambda e, half=half, cs=cs: e.tensor_tensor(out=m1[:, cs], in0=pb[half][:], in1=ga[:, cs], op=ALU.mult),
                     r=[pb[half], ga], w=[m1])
                P.op("dve", lambda e, half=half, cs=cs: e.tensor_tensor(out=m2[:, cs], in0=pb[2 + half][:], in1=gr[:, cs], op=ALU.mult),
                     r=[pb[2 + half], gr], w=[m2])
            P.op("pool", lambda e, mb=mb: e.tensor_tensor(out=mb[:, :], in0=m1[:, :], in1=m2[:, :], op=ALU.add), r=[m1, m2], w=[mb])
            transpose8(mb, hT, hT[:, :, 0:128], eng="act")
            for half in range(2):
                po = pb[4 + half]
                for k in range(8):
                    mm(po, po[:], hT, hT[:, k, 0:128], slot, Wo[:, k, half * 512:(half + 1) * 512], k == 0, k == 7)
                P.op("dve", lambda e, po=po, half=half: e.tensor_tensor(out=x1s[:, half * 512:(half + 1) * 512], in0=po[:],
                                                                       in1=x1s[:, half * 512:(half + 1) * 512], op=ALU.add),
                     r=[po, x1s], w=[x1s])
            P.dma(x2_t[ti], x2_t[ti][:, :], x1s, x1s[:, :], q="pool")
        uncarve(big, subs)

    def begin(gen):
        next(gen)
        return gen

    def finish(gen):
        for _ in gen:
            pass

    allp = set(phases) == {"ffn1", "proj", "cmp", "nsa", "ret", "merge", "ffn2"} and not _os.environ.get("SKIPP1")
    out_t = [T(out_d.ap[i * 128:(i + 1) * 128], f"out{i}") for i in range(NT)]
    if allp:
        f1a = begin(ffn_pass(1, 0, wslot[0], x_t, x1_t, I["ffn1_norm"]))
        f1b = begin(ffn_pass(1, 1, wslot[1], x1_t, x1_t, I["ffn1_norm"]))
        finish(f1a)
        p0 = begin(proj_pass0(wslot[0]))
        finish(f1b)
        p1 = begin(proj_pass1(wslot[1]))
        finish(p0)
        finish(p1)
        cmp_phase(wslot[1])
        mg = begin(merge_phase(wslot[1]))
        nsa_phase(wslot[0])
        f2a = begin(ffn_pass(2, 0, wslot[0], x2_t, x2_t, I["ffn2_norm"]))
        ret_phase()
        finish(mg)
        f2b = begin(ffn_pass(2, 1, wslot[1], x2_t, out_t, I["ffn2_norm"], final=True))
        finish(f2a)
        finish(f2b)
    else:
        if "ffn1" in phases:
            finish(ffn_pass(1, 0, wslot[0], x_t, x1_t, I["ffn1_norm"]))
            finish(ffn_pass(1, 1, wslot[1], x1_t, x1_t, I["ffn1_norm"]))
        if "proj" in phases:
            finish(proj_pass0(wslot[0]))
            if not _os.environ.get("SKIPP1"):
                finish(proj_pass1(wslot[1]))
        if "cmp" in phases:
            cmp_phase(wslot[1])
        if "nsa" in phases:
            nsa_phase(wslot[0])
        if "ret" in phases:
            ret_phase()
        if "merge" in phases:
            finish(merge_phase(wslot[1]))
        if "ffn2" in phases:
            finish(ffn_pass(2, 0, wslot[0], x2_t, x2_t, I["ffn2_norm"]))
            finish(ffn_pass(2, 1, wslot[1], x2_t, out_t, I["ffn2_norm"], final=True))
    P.emit()
    return nc, P


_CACHE = {}


def kernel(**inputs):
    n_cores = 8
    if "nc" not in _CACHE:
        _CACHE["nc"] = build()[0]
        _CACHE["consts"] = {k: np.ascontiguousarray(v.reshape(CONST_SHAPES[k]).astype(np.float32))
                            for k, v in _consts().items()}
    nc = _CACHE["nc"]
    cst = _CACHE["consts"]
    shared = {}
    for k, shp in IN_SHAPES.items():
        if k == "x":
            continue
        shared[k] = np.ascontiguousarray(np.asarray(inputs[k], dtype=np.float32).reshape(shp))
    x = np.asarray(inputs["x"], dtype=np.float32)
    in_maps = []
    for c in range(n_cores):
        m = dict(shared)
        m.update(cst)
        m["x"] = np.ascontiguousarray(x[c % 4])
        in_maps.append(m)
    res = run_bass_kernel_spmd(nc, in_maps, core_ids=list(range(n_cores)))
    out = np.stack([np.asarray(res.results[b]["out"], dtype=np.float32) for b in range(4)], axis=0)
    return out
```

```python
import contextlib
import os as _os
import numpy as np
import concourse.bass as bass
import concourse.mybir as mybir
from concourse.bass_utils import run_bass_kernel_spmd

F32 = mybir.dt.float32
BF16 = mybir.dt.bfloat16
AF = mybir.ActivationFunctionType
ALU = mybir.AluOpType
AX = mybir.AxisListType

S = 4096
D = 1024
DFF = 2816
NT = S // 128
DIN = 6960


class T:
    __slots__ = ("ap", "name", "last_w", "readers", "excl")

    def __init__(self, ap, name="", excl=False):
        self.ap = ap
        self.name = name
        self.last_w = None
        self.readers = []
        self.excl = excl

    def __getitem__(self, k):
        return self.ap[k]


class Prog:
    ENGS = ("pe", "act", "dve", "pool", "sp")

    def __init__(self, nc, n_dma_sems=12, same_engine_sync=True):
        self.nc = nc
        self.ops = []
        self.es = contextlib.ExitStack()
        self.same_engine_sync = same_engine_sync
        self.n_dma_sems = n_dma_sems
        self.ncnt = 0

    def sb(self, shape, dt, name=None):
        self.ncnt += 1
        name = name or f"sb{self.ncnt}"
        t = self.es.enter_context(self.nc.sbuf_tensor(name, list(shape), dt))
        return T(t[:], name)

    def ps(self, shape, dt, name=None):
        self.ncnt += 1
        name = name or f"ps{self.ncnt}"
        t = self.es.enter_context(self.nc.psum_tensor(name, list(shape), dt))
        return T(t[:], name, excl=True)

    def dram(self, name, shape, dt, kind="Internal"):
        t = self.nc.dram_tensor(name, list(shape), dt, kind=kind)
        return T(t.ap(), name)

    def op(self, eng, fn, r=(), w=(), dma=False):
        idx = len(self.ops)
        deps = set()
        raw = set()
        for t in r:
            if t.last_w is not None:
                deps.add(t.last_w)
                raw.add(t.last_w)
            if t.excl:
                for rd in t.readers:
                    if self.ops[rd]["eng"] != eng:
                        deps.add(rd)
        for t in w:
            if t.last_w is not None:
                deps.add(t.last_w)
            deps.update(t.readers)
        for t in r:
            t.readers.append(idx)
        for t in w:
            t.last_w = idx
            t.readers = []
        deps.discard(idx)
        self.ops.append({"eng": eng, "fn": fn, "deps": deps, "dma": dma, "raw": raw, "tag": getattr(self, "tag", "")})
        return idx

    def dma(self, out_t, out_ap, in_t, in_ap, q="sp", **kw):
        return self.op(q, lambda e: e.dma_start(out=out_ap, in_=in_ap, **kw),
                       r=[in_t], w=[out_t], dma=True)

    def emit(self):
        nc = self.nc
        ops = self.ops
        nops = len(ops)
        dma_use = {}
        for i, o in enumerate(ops):
            if o["dma"]:
                dma_use.setdefault(o["eng"], []).append(i)
        dma_sem = {}
        for q, lst in dma_use.items():
            for k, i in enumerate(lst):
                slot = k % self.n_dma_sems
                val = 16 * (k // self.n_dma_sems + 1)
                dma_sem[i] = (q, slot, val)
                if k >= self.n_dma_sems:
                    ops[i]["deps"].add(lst[k - self.n_dma_sems])
        waited = {e: {} for e in self.ENGS}
        waited_dma = {e: set() for e in self.ENGS}
        plan = [None] * nops
        signaling = set()
        for i, o in enumerate(ops):
            e = o["eng"]
            need = {}
            need_dma = []
            for d in sorted(o["deps"]):
                po = ops[d]
                if po["dma"]:
                    if d not in waited_dma[e]:
                        waited_dma[e].add(d)
                        need_dma.append(d)
                    continue
                p = po["eng"]
                if p == e and (e == "pe" or not self.same_engine_sync or d not in o["raw"]):
                    continue
                if waited[e].get(p, -1) >= d:
                    continue
                need[p] = max(need.get(p, -1), d)
            for p, d in need.items():
                waited[e][p] = d
                signaling.add(d)
            plan[i] = (need, need_dma)
        semval = {}
        cnt = {e: 0 for e in self.ENGS}
        for i, o in enumerate(ops):
            if o["dma"]:
                continue
            if i in signaling:
                cnt[o["eng"]] += 1
                semval[i] = cnt[o["eng"]]
        self.stats = {e: sum(1 for o in ops if o["eng"] == e) for e in self.ENGS}
        self.stats["signals"] = dict(cnt)
        es = self.es
        sems = {e: es.enter_context(nc.semaphore(f"sem_{e}")) for e in self.ENGS}
        dsems = {q: [es.enter_context(nc.semaphore(f"dsem_{q}{k}")) for k in range(self.n_dma_sems)]
                 for q in dma_use}
        block = es.enter_context(nc.Block())

        def body(ename):
            def f(eng):
                for i, o in enumerate(ops):
                    if o["eng"] != ename:
                        continue
                    need, need_dma = plan[i]
                    for p, d in need.items():
                        eng.wait_ge(sems[p], semval[d])
                    for d in need_dma:
                        q, slot, val = dma_sem[d]
                        eng.wait_ge(dsems[q][slot], val)
                    ins = o["fn"](eng)
                    if o["dma"]:
                        q, slot, val = dma_sem[i]
                        ins.then_inc(dsems[q][slot], 16)
                    elif i in signaling:
                        ins.then_inc(sems[ename], 1)
                for q, lst in dma_use.items():
                    if q != ename:
                        continue
                    last = {}
                    for i in lst:
                        _, slot, val = dma_sem[i]
                        last[slot] = val
                    for slot, val in last.items():
                        eng.wait_ge(dsems[q][slot], val)
            return f

        block.tensor(body("pe"))
        block.scalar(body("act"))
        block.vector(body("dve"))
        block.gpsimd(body("pool"))
        block.sync(body("sp"))

    def close(self):
        self.es.close()


def _consts():
    c = {}
    pos = np.arange(S, dtype=np.float32)

    def rope_tabs(p, rot, theta, nrep):
        half = rot // 2
        fr = (np.float32(theta) ** (-(np.arange(half, dtype=np.float32) * np.float32(2.0) / np.float32(rot)))).astype(np.float32)
        ang = p.astype(np.float32)[:, None] * fr[None, :]
        cs, sn = np.cos(ang).astype(np.float32), np.sin(ang).astype(np.float32)
        C = np.ones((len(p), 64), np.float32)
        C[:, 0:half] = cs
        C[:, half:rot] = cs
        Sg = np.concatenate([-sn, sn], 1)
        return np.tile(C, (1, nrep)), np.tile(Sg, (1, nrep))

    c["ropeCa"], c["ropeSa"] = rope_tabs(pos, 16, 500000.0, 8)
    pc = np.arange(256, dtype=np.float32) * 16 + 31
    c["ropeCc"], c["ropeSc"] = rope_tabs(pc, 16, 500000.0, 1)
    c["ropeCr"], c["ropeSr"] = rope_tabs(pos, 64, 10000.0, 8)
    lg = np.log(1.0 - 2.0 ** (-5.0 - np.arange(8, dtype=np.float64)))
    i = np.arange(128, dtype=np.float64)
    diff = i[None, :] - i[:, None]
    dT = np.zeros((128, 8, 128), np.float64)
    for h in range(8):
        dT[:, h, :] = np.where(diff >= 0, np.exp(np.maximum(diff, 0) * lg[h]), 0.0) * 0.125
    c["decayT"] = dT.astype(np.float32).reshape(128, 1024)
    zeta = np.exp((127.0 - i)[None, :] * lg[:, None])
    c["ZT"] = np.repeat(zeta.T[:, :, None], 64, axis=2).reshape(128, 512).astype(np.float32)
    xi = np.exp((i + 1.0)[None, :] * lg[:, None]) * 0.125
    c["XI"] = np.ascontiguousarray(xi.T).astype(np.float32)
    gch = np.exp(128.0 * lg)
    c["GC"] = np.broadcast_to(gch[None, :, None], (64, 8, 128)).reshape(64, 1024).astype(np.float32)
    n = np.arange(256)
    cs_ = n * 16
    ss_ = np.arange(64) * 64
    ov = np.clip(np.minimum(cs_[:, None] + 32, ss_[None, :] + 64) - np.maximum(cs_[:, None], ss_[None, :]), 0, None)
    c2s = ov.astype(np.float32) / 32.0
    c2s[255] = 0.0
    c["C2S"] = c2s
    key = np.arange(S)
    c["EM"] = (key[None, :] // 64 == np.arange(64)[:, None]).astype(np.float32)
    sp = np.arange(128) - 64
    curp = (np.arange(128) >= 64).astype(np.int64)[:, None]
    forced = (sp[None, :] == curp) | (sp[None, :] == curp - 1)
    future = sp[None, :] > curp
    c["TK"] = (~(forced | future)).astype(np.float32)
    c["TA"] = np.where(forced, 1.0e4, np.where(future, -1.0e4, 0.0)).astype(np.float32)
    return c


CONST_SHAPES = {"ropeCa": [S, 512], "ropeSa": [S, 128], "ropeCc": [256, 64], "ropeSc": [256, 16],
                "ropeCr": [S, 512], "ropeSr": [S, 512], "decayT": [128, 1024], "ZT": [128, 512],
                "XI": [128, 8], "GC": [64, 1024], "C2S": [256, 64], "EM": [64, S],
                "TK": [128, 128], "TA": [128, 128]}

IN_SHAPES = {"x": [S, D], "ffn1_norm": [1, D], "ffn1_w_gate": [D, DFF], "ffn1_w_up": [D, DFF],
             "ffn1_w_down": [DFF, D], "mix_norm": [1, D], "w_in": [D, DIN], "cmp_pos_emb": [32, 64],
             "cmp_k_w1": [2048, 256], "cmp_k_w2": [256, 64], "cmp_v_w1": [2048, 256], "cmp_v_w2": [256, 64],
             "ret_gn_gain": [1, 1024], "w_branch_nsa": [D, D], "w_branch_ret": [D, D], "w_out": [D, D],
             "ffn2_norm": [1, D], "ffn2_w_gate": [D, DFF], "ffn2_w_up": [D, DFF], "ffn2_w_down": [DFF, D],
             "final_norm": [1, D]}

G_CHUNK = [float(np.exp(128.0 * np.log(1.0 - 2.0 ** (-5.0 - h)))) for h in range(8)]
NEGBIG = -2.0e4


def build(phases=("ffn1", "proj", "cmp", "nsa", "ret", "merge", "ffn2"), dbg=(), lim=None):
    nc = bass.Bass("TRN2", target_bir_lowering=False)
    P = Prog(nc)
    lim_c = [0, 5, 17] if lim else None
    I = {k: P.dram(k, v, F32, kind="ExternalInput") for k, v in IN_SHAPES.items()}
    C = {k: P.dram(k, v, F32, kind="ExternalInput") for k, v in CONST_SHAPES.items()}
    out_d = P.dram("out", [S, D], F32, kind="ExternalOutput")

    def scratch(name, shape, dt):
        return P.dram(name, shape, dt, kind=("ExternalOutput" if name in dbg else "Internal"))

    def tiles(t, n=NT, rows=128):
        return [T(t.ap[i * rows:(i + 1) * rows], f"{t.name}{i}") for i in range(n)]

    def ctiles(t, n=NT, cols=128):
        return [T(t.ap[..., i * cols:(i + 1) * cols], f"{t.name}{i}") for i in range(n)]

    x_t = tiles(I["x"])
    x1_d = scratch("x1_d", [S, D], F32); x1_t = tiles(x1_d)
    x2_d = scratch("x2_d", [S, D], F32); x2_t = tiles(x2_d)
    hT_d = scratch("hT_d", [8, 128, 8, 512], BF16)
    hT_t = [T(hT_d.ap[i], f"hT{i}") for i in range(8)]
    hmT_d = scratch("hmT_d", [NT, 128, 8, 128], BF16)
    hmT_t = [T(hmT_d.ap[i], f"hmT{i}") for i in range(NT)]
    QT_d = scratch("QT_d", [2, 512, S], BF16)
    QT_t = [ctiles(T(QT_d.ap[g], f"QT{g}_")) for g in range(2)]
    kcT_d = scratch("kcT_d", [128, S], BF16); kcT_t = ctiles(kcT_d)
    vcT_d = scratch("vcT_d", [128, S], BF16); vcT_t = ctiles(vcT_d)
    ksT_d = scratch("ksT_d", [128, S], BF16); ksT_t = ctiles(ksT_d)
    kwT_d = scratch("kwT_d", [128, S], BF16); kwT_t = ctiles(kwT_d)
    vs_d = scratch("vs_d", [S, 128], BF16); vs_t = tiles(vs_d)
    vw_d = scratch("vw_d", [S, 128], BF16); vw_t = tiles(vw_d)
    ga_d = scratch("ga_d", [S, 48], F32); ga_t = tiles(ga_d)
    QrT_d = scratch("QrT_d", [512, S], BF16); QrT_t = ctiles(QrT_d)
    KrT_d = scratch("KrT_d", [512, S], BF16); KrT_t = ctiles(KrT_d)
    kz_d = scratch("kz_d", [S, 512], BF16); kz_t = tiles(kz_d)
    vr_d = scratch("vr_d", [S, 1024], BF16); vr_t = tiles(vr_d)
    sgr_d = scratch("sgr_d", [S, 1024], F32); sgr_t = tiles(sgr_d)
    gm_d = scratch("gm_d", [S, 2048], F32); gm_t = tiles(gm_d)
    kc_d = scratch("kc_d", [2, 256, 64], BF16)
    vc_d = scratch("vc_d", [2, 256, 64], BF16)
    aT_d = scratch("aT_d", [1024, S], BF16); aT_t = ctiles(aT_d)
    rT_d = scratch("rT_d", [1024, S], BF16); rT_t = ctiles(rT_d)

    WSLOT = 33792
    wslot = [P.sb([128, WSLOT], BF16, f"wslot{i}") for i in range(2)]
    ident = P.sb([128, 128], BF16, "ident")
    gain = P.sb([128, 1024], F32, "gain")
    gain2 = P.sb([128, 1024], F32, "gain2")
    xin = [P.sb([128, 1024], F32, f"xin{i}") for i in range(2)]
    xres = xin
    hb = [P.sb([128, 1024], BF16, f"hb{i}") for i in range(2)]
    st1 = [P.sb([128, 4], F32, f"st1_{i}") for i in range(2)]
    big = P.sb([128, 5632], BF16, "big")
    hT = P.sb([128, 8, 512], BF16, "hT")
    wk32 = [P.sb([128, 1024], F32, f"wk32_{i}") for i in range(3)]
    wk16 = [P.sb([128, 1024], BF16, f"wk16_{i}") for i in range(3)]
    sm = [P.sb([128, 64], F32, f"sm{i}") for i in range(8)]
    tCa = P.sb([128, 512], F32, "tCa"); tSa = P.sb([128, 128], F32, "tSa")
    tCr = P.sb([128, 512], F32, "tCr"); tSr = P.sb([128, 512], F32, "tSr")
    tZ = P.sb([128, 512], F32, "tZ")
    pb = [P.ps([128, 512], F32, f"pb{i}") for i in range(7)]
    pbf = P.ps([128, 1024], BF16, "pbf")

    negh = P.sb([128, 8], F32, "negh")
    P.op("pool", lambda e: e.memset(negh[:], -0.5), w=[negh])
    P.op("pool", lambda e: e.memset(ident[:], 0.0), w=[ident])
    P.op("pool", lambda e: e.affine_select(out=ident[:], in_=ident[:], pattern=[[-1, 128]], compare_op=ALU.not_equal,
                                           fill=1.0, base=0, channel_multiplier=1), r=[ident], w=[ident])

    cnt = {"rr": 0}

    def rr(lst):
        cnt["rr"] += 1
        return lst[cnt["rr"] % len(lst)]

    def mm(out_t, out_ap, a_t, a_ap, b_t, b_ap, start, stop, sgc=False):
        P.op("pe", lambda e: e.matmul(out_ap, lhsT=a_ap, rhs=b_ap, start=start, stop=stop, skip_group_check=sgc),
             r=[a_t, b_t], w=[out_t])

    def tr(out_t, out_ap, in_t, in_ap):
        P.op("pe", lambda e: e.transpose(out=out_ap, in_=in_ap, identity=ident[:]), r=[in_t, ident], w=[out_t])

    def load_w(slot_t, off, src_t, src_ap, shape):
        n = int(np.prod(shape[1:]))
        dst = slot_t[:, off:off + n]
        if len(shape) == 3:
            dst = dst.rearrange("p (a b) -> p a b", a=shape[1])
        P.dma(slot_t, dst, src_t, src_ap, q="pool")
        return dst

    def rmsnorm(x_tile, gain_t, hb_t, stt):
        sq = wk32[2]
        P.op("act", lambda e: e.activation(out=sq[:], in_=x_tile[:], func=AF.Square, accum_out=stt[:, 0:1]),
             r=[x_tile], w=[sq, stt])
        P.op("dve", lambda e: e.tensor_scalar(out=stt[:, 1:2], in0=stt[:, 0:1], scalar1=1.0 / D, scalar2=1e-6,
                                              op0=ALU.mult, op1=ALU.add), r=[stt], w=[stt])
        P.op("pool", lambda e: e.tensor_tensor(out=stt[:, 3:4], in0=stt[:, 1:2], in1=negh[:, 0:1], op=ALU.pow),
             r=[stt, negh], w=[stt])
        P.op("dve", lambda e: e.scalar_tensor_tensor(out=hb_t[:], in0=x_tile[:], scalar=stt[:, 3:4], in1=gain_t[:],
                                                     op0=ALU.mult, op1=ALU.mult), r=[x_tile, stt, gain_t], w=[hb_t])

    pb6bf = pb[6].ap.bitcast(BF16)

    def transpose8(src_t, dst_t, dst_ap, nblk=8, eng="act", alt=False):
        bank_t, bank = (pb[6], pb6bf) if alt else (pbf, pbf.ap)
        for k in range(nblk):
            tr(bank_t, bank[:, k * 128:(k + 1) * 128], src_t, src_t[:, k * 128:(k + 1) * 128])
        src = bank[:, 0:nblk * 128].rearrange("p (k t) -> p k t", k=nblk)
        if eng == "act":
            P.op("act", lambda e: e.copy(out=dst_ap, in_=src), r=[bank_t], w=[dst_t])
        else:
            P.op("dve", lambda e: e.tensor_copy(out=dst_ap, in_=src), r=[bank_t], w=[dst_t])

    def load_gain(gt, src):
        P.dma(gt, gt[:], src, src.ap.partition_broadcast(128))

    def carve(parent, ap, name=""):
        t = T(ap, name)
        t.last_w = parent.last_w
        t.readers = list(parent.readers)
        return t

    def uncarve(parent, subs):
        for t in subs:
            parent.readers.extend(t.readers)
            if t.last_w is not None:
                parent.readers.append(t.last_w)

    def ffn_pass(which, hf, slot, xsrc_t, xdst_t, norm_gain, final=False):
        pre = f"ffn{which}_"
        wg_d, wu_d, wd_d = I[pre + "w_gate"], I[pre + "w_up"], I[pre + "w_down"]
        c0 = hf * 1408
        wg = load_w(slot, 0, wg_d, wg_d.ap[:, c0:c0 + 1408].rearrange("(k p) f -> p k f", p=128), [128, 8, 1408])
        wu = load_w(slot, 11264, wu_d, wu_d.ap[:, c0:c0 + 1408].rearrange("(k p) f -> p k f", p=128), [128, 8, 1408])
        wd = load_w(slot, 22528, wd_d, wd_d.ap[c0:c0 + 1408, :].rearrange("(k p) d -> p k d", p=128), [128, 11, 1024])
        yield
        aT = big[:, 0:11 * 512].rearrange("p (f t) -> p f t", f=11)
        if hf == 0:
            load_gain(gain, norm_gain)
        if final:
            load_gain(gain2, I["final_norm"])
        for st in range(8 if lim is None else 1):
            if hf == 0:
                for tt in range(4):
                    ti = st * 4 + tt
                    xt = xin[ti % 2]
                    P.dma(xt, xt[:], xsrc_t[ti], xsrc_t[ti][:, :])
                    hbt = hb[ti % 2]
                    rmsnorm(xt, gain, hbt, st1[ti % 2])
                    transpose8(hbt, hT, hT[:, :, tt * 128:(tt + 1) * 128], eng="act")
                P.dma(hT_t[st], hT_t[st][:], hT, hT[:], q="pool")
            else:
                P.dma(hT, hT[:], hT_t[st], hT_t[st][:])
            for f in range(11):
                pg = pb[(2 * f) % 4]
                pu = pb[(2 * f + 1) % 4]
                for k in range(8):
                    mm(pg, pg[:], slot, wg[:, k, f * 128:(f + 1) * 128], hT, hT[:, k, :], k == 0, k == 7)
                for k in range(8):
                    mm(pu, pu[:], slot, wu[:, k, f * 128:(f + 1) * 128], hT, hT[:, k, :], k == 0, k == 7)
                sg = wk32[f % 2]
                P.op("act", lambda e, sg=sg, pg=pg: e.activation(out=sg[:, 0:512], in_=pg[:], func=AF.Silu),
                     r=[pg], w=[sg])
                P.op("dve", lambda e, sg=sg, pu=pu, f=f: e.tensor_tensor(out=aT[:, f, :], in0=pu[:], in1=sg[:, 0:512],
                                                                        op=ALU.mult), r=[pu, sg], w=[big])
            for tt in range(4):
                ti = st * 4 + tt
                xr = xres[ti % 2]
                P.dma(xr, xr[:], xsrc_t[ti], xsrc_t[ti][:, :])
                for half in range(2):
                    py = pb[4 + (2 * tt + half) % 3]
                    for f in range(11):
                        mm(py, py[:], big, aT[:, f, tt * 128:(tt + 1) * 128], slot, wd[:, f, half * 512:(half + 1) * 512],
                           f == 0, f == 10)
                    P.op("dve", lambda e, xr=xr, py=py, half=half: e.scalar_tensor_tensor(
                        out=xr[:, half * 512:(half + 1) * 512], in0=py[:], scalar=0.5,
                        in1=xr[:, half * 512:(half + 1) * 512], op0=ALU.mult, op1=ALU.add), r=[py, xr], w=[xr])
                if final:
                    ob = wk32[ti % 2]
                    stt = st1[ti % 2]
                    sq = wk32[2]
                    P.op("act", lambda e, xr=xr, stt=stt: e.activation(out=sq[:], in_=xr[:], func=AF.Square,
                                                                       accum_out=stt[:, 0:1]), r=[xr], w=[sq, stt])
                    P.op("dve", lambda e, stt=stt: e.tensor_scalar(out=stt[:, 1:2], in0=stt[:, 0:1], scalar1=1.0 / D,
                                                                   scalar2=1e-6, op0=ALU.mult, op1=ALU.add),
                         r=[stt], w=[stt])
                    P.op("pool", lambda e, stt=stt: e.tensor_tensor(out=stt[:, 3:4], in0=stt[:, 1:2], in1=negh[:, 0:1], op=ALU.pow),
                         r=[stt, negh], w=[stt])
                    P.op("dve", lambda e, xr=xr, stt=stt, ob=ob: e.scalar_tensor_tensor(
                        out=ob[:], in0=xr[:], scalar=stt[:, 3:4], in1=gain2[:], op0=ALU.mult, op1=ALU.mult),
                         r=[xr, stt, gain2], w=[ob])
                    P.dma(xdst_t[ti], xdst_t[ti][:, :], ob, ob[:], q="pool")
                else:
                    P.dma(xdst_t[ti], xdst_t[ti][:, :], xr, xr[:], q="pool")

    def rope(src_t, src_ap, H, R, Ct, Cap, St, Sap, out_t, out_ap, o32, t2):
        n = H * 64
        half = R // 2
        P.op("dve", lambda e: e.tensor_tensor(out=o32[:, 0:n], in0=src_ap, in1=Cap, op=ALU.mult), r=[Ct, src_t], w=[o32])
        s3 = src_ap.rearrange("p (h d) -> p h d", h=H)
        S3 = Sap.rearrange("p (h r) -> p h r", h=H)
        t3 = t2[:, 0:H * R].rearrange("p (h r) -> p h r", h=H)
        o3 = o32[:, 0:n].rearrange("p (h d) -> p h d", h=H)
        P.op("dve", lambda e: e.tensor_tensor(out=t3[:, :, 0:half], in0=s3[:, :, half:R], in1=S3[:, :, 0:half],
                                              op=ALU.mult), r=[St, src_t], w=[t2])
        P.op("dve", lambda e: e.tensor_tensor(out=t3[:, :, half:R], in0=s3[:, :, 0:half], in1=S3[:, :, half:R],
                                              op=ALU.mult), r=[St, src_t], w=[t2])
        P.op("dve", lambda e: e.tensor_tensor(out=o3[:, :, 0:R], in0=o3[:, :, 0:R], in1=t3, op=ALU.add),
             r=[o32, t2], w=[o32])
        P.op("act", lambda e: e.copy(out=out_ap, in_=o32[:, 0:n]), r=[o32], w=[out_t])

    def proj_pass0(slot):
        w_d = I["w_in"]
        w0 = load_w(slot, 0, w_d, w_d.ap[:, 0:1944].rearrange("(k p) f -> p k f", p=128), [128, 8, 1944])
        w1 = load_w(slot, 8 * 1944, w_d, w_d.ap[:, 1944:3888].rearrange("(k p) f -> p k f", p=128), [128, 8, 1944])
        yield

        def wcol(k, c0, c1):
            if c1 <= 1944:
                return w0[:, k, c0:c1]
            assert c0 >= 1944
            return w1[:, k, c0 - 1944:c1 - 1944]

        load_gain(gain, I["mix_norm"])
        P.dma(tZ, tZ[:], C["ZT"], C["ZT"][:, :])
        qTs = [P.sb([128, 4, 128], BF16, "qTa"), P.sb([128, 4, 128], BF16, "qTb")]
        blocks = [(0, 512), (512, 1024), (1024, 1536), (1536, 1840), (1840, 2352), (2352, 2864), (2864, 3376), (3376, 3888)]
        subs_c = []
        hbuf = [carve(hT, hT.ap[:, :, 0:128], "hTa"), carve(hT, hT.ap[:, :, 128:256], "hTb")]
        o32s = [carve(wk32[0], wk32[0].ap[:, 0:512], "o32a"), carve(wk32[0], wk32[0].ap[:, 512:1024], "o32b")]
        t2s = [carve(wk32[1], wk32[1].ap[:, 0:512], "t2a"), carve(wk32[1], wk32[1].ap[:, 512:1024], "t2b")]
        subs_c = [(hT, hbuf), (wk32[0], o32s), (wk32[1], t2s)]
        ntl = NT if lim is None else lim
        rcnt = {"r": 0, "q": 0}

        def prep_x(ti):
            P.tag = f"p0.t{ti}.prep"
            xt = xin[ti % 2]
            P.dma(xt, xt[:], x1_t[ti], x1_t[ti][:, :])
            hbt = hb[ti % 2]
            rmsnorm(xt, gain, hbt, st1[ti % 2])
            hb_ = hbuf[ti % 2]
            transpose8(hbt, hb_, hb_[:, :, :], eng="act")
            P.dma(hmT_t[ti], hmT_t[ti][:], hb_, hb_[:, :, :], q="sp")

        def load_tabs_a(ti):
            P.dma(tCa, tCa[:], C["ropeCa"], C["ropeCa"][ti * 128:(ti + 1) * 128, :])
            P.dma(tSa, tSa[:], C["ropeSa"], C["ropeSa"][ti * 128:(ti + 1) * 128, :])

        def load_tabs_r(ti):
            P.dma(tCr, tCr[:], C["ropeCr"], C["ropeCr"][ti * 128:(ti + 1) * 128, :])
            P.dma(tSr, tSr[:], C["ropeSr"], C["ropeSr"][ti * 128:(ti + 1) * 128, :])

        def mm_block(ti, bi):
            P.tag = f"p0.t{ti}.mm{bi}"
            c0, c1 = blocks[bi]
            pp = pb[bi % 6]
            hb_ = hbuf[ti % 2]
            subs = [(c0, c1)] if not (c0 < 1944 < c1) else [(c0, 1944), (1944, c1)]
            for (s0, s1) in subs:
                for k in range(8):
                    mm(pp, pp[:, s0 - c0:s1 - c0], hb_, hb_[:, k, :], slot, wcol(k, s0, s1), k == 0, k == 7)

        def post_a(ti, bi):
            P.tag = f"p0.t{ti}.A{bi}"
            return post_a_(ti, bi)

        def post_a_(ti, bi):
            pp = pb[bi % 6]
            rcnt["r"] += 1
            o32, t2 = o32s[rcnt["r"] % 2], t2s[rcnt["r"] % 2]

            def nextq():
                rcnt["q"] += 1
                P.tag = f"p0.t{ti}.B{bi}"
                return qTs[rcnt["q"] % 2]
            if bi in (0, 1):
                ob = wk16[bi % 2]
                rope(pp, pp[:, 0:512], 8, 16, tCa, tCa[:, :], tSa, tSa[:, :], ob, ob[:, 0:512], o32, t2)

                def fb():
                    qT = nextq()
                    transpose8(ob, qT, qT[:], nblk=4, eng="dve", alt=rcnt["q"] % 2 == 1)
                    dst = QT_t[bi][ti]
                    P.dma(dst, dst[:].rearrange("(p r) t -> r p t", r=128), qT, qT[:], q="sp")
                return fb
            elif bi == 2:
                ob = wk16[2]
                P.op("act", lambda e: e.copy(out=ob[:, 0:256], in_=pp[:, 0:256]), r=[pp], w=[ob])
                rope(pp, pp[:, 256:384], 2, 16, tCa, tCa[:, 0:128], tSa, tSa[:, 0:32], ob, ob[:, 256:384], o32, t2)
                P.op("act", lambda e: e.copy(out=ob[:, 384:512], in_=pp[:, 384:512]), r=[pp], w=[ob])

                def fb():
                    qT = nextq()
                    transpose8(ob, qT, qT[:, 0:3, :], nblk=3, eng="dve", alt=rcnt["q"] % 2 == 1)
                    P.dma(kcT_t[ti], kcT_t[ti][:], qT, qT[:, 0, :], q="sp")
                    P.dma(vcT_t[ti], vcT_t[ti][:], qT, qT[:, 1, :], q="sp")
                    P.dma(ksT_t[ti], ksT_t[ti][:], qT, qT[:, 2, :], q="sp")
                    P.dma(vs_t[ti], vs_t[ti][:, :], ob, ob[:, 384:512], q="sp")
                return fb
            elif bi == 3:
                ob = hb[(ti + 1) % 2]
                ob = wk16[2]
                rope(pp, pp[:, 0:128], 2, 16, tCa, tCa[:, 0:128], tSa, tSa[:, 0:32], ob, ob[:, 0:128], o32, t2)
                P.op("act", lambda e: e.copy(out=ob[:, 128:256], in_=pp[:, 128:256]), r=[pp], w=[ob])
                gsb = sm[0]
                P.op("act", lambda e: e.activation(out=gsb[:, 0:48], in_=pp[:, 256:304], func=AF.Sigmoid), r=[pp], w=[gsb])

                def fb():
                    qT = nextq()
                    P.dma(ga_t[ti], ga_t[ti][:, :], gsb, gsb[:, 0:48], q="sp")
                    transpose8(ob, qT, qT[:, 0:1, :], nblk=1, eng="dve", alt=rcnt["q"] % 2 == 1)
                    P.dma(kwT_t[ti], kwT_t[ti][:], qT, qT[:, 0, :], q="sp")
                    P.dma(vw_t[ti], vw_t[ti][:, :], ob, ob[:, 128:256], q="sp")
                return fb
            elif bi in (4, 5):
                ob = wk16[bi % 2]
                rope(pp, pp[:, 0:512], 8, 64, tCr, tCr[:, :], tSr, tSr[:, :], ob, ob[:, 0:512], o32, t2)
                kz = None
                if bi == 5:
                    kz = hb[(ti + 1) % 2] if False else wk16[2]
                    P.op("pool", lambda e: e.tensor_tensor(out=kz[:, 0:512], in0=o32[:, 0:512], in1=tZ[:], op=ALU.mult),
                         r=[o32, tZ], w=[kz])

                def fb():
                    qT = nextq()
                    transpose8(ob, qT, qT[:], nblk=4, eng="dve", alt=rcnt["q"] % 2 == 1)
                    dst = (QrT_t if bi == 4 else KrT_t)[ti]
                    P.dma(dst, dst[:].rearrange("(p r) t -> r p t", r=128), qT, qT[:], q="sp")
                    if bi == 5:
                        P.dma(kz_t[ti], kz_t[ti][:, :], kz, kz[:, 0:512], q="sp")
                return fb
            else:
                ob = wk16[bi % 2]
                P.op("act", lambda e: e.copy(out=ob[:, 0:512], in_=pp[:]), r=[pp], w=[ob])
                hh = bi - 6

                def fb():
                    P.dma(vr_t[ti], vr_t[ti][:, hh * 512:(hh + 1) * 512], ob, ob[:, 0:512], q="sp")
                return fb

        load_tabs_a(0)
        load_tabs_r(0)
        prep_x(0)
        for ti in range(ntl):
            for bi in range(6):
                mm_block(ti, bi)
            f0 = post_a(ti, 0)
            mm_block(ti, 6)
            f1 = post_a(ti, 1)
            mm_block(ti, 7)
            f0()
            f2 = post_a(ti, 2)
            f1()
            f2()
            f3 = post_a(ti, 3)
            if ti + 1 < ntl:
                prep_x(ti + 1)
            f3()
            if ti + 1 < ntl:
                load_tabs_a(ti + 1)
            f4 = post_a(ti, 4)
            f5 = post_a(ti, 5)
            f4()
            if ti + 1 < ntl:
                load_tabs_r(ti + 1)
            f6 = post_a(ti, 6)
            f5()
            f7 = post_a(ti, 7)
            f6()
            f7()
        for parent, subs in subs_c:
            uncarve(parent, subs)

    def proj_pass1(slot):
        w_d = I["w_in"]
        w0 = load_w(slot, 0, w_d, w_d.ap[:, 3888:3888 + 1536].rearrange("(k p) f -> p k f", p=128), [128, 8, 1536])
        w1 = load_w(slot, 8 * 1536, w_d, w_d.ap[:, 3888 + 1536:DIN].rearrange("(k p) f -> p k f", p=128), [128, 8, 1536])
        yield
        for ti in range(NT if lim is None else lim):
            P.dma(hT, hT[:, :, 0:128], hmT_t[ti], hmT_t[ti][:])
            for bi in range(6):
                pp = pb[bi % 7]
                wv = w0 if bi < 3 else w1
                cc = (bi % 3) * 512
                for k in range(8):
                    mm(pp, pp[:], hT, hT[:, k, 0:128], slot, wv[:, k, cc:cc + 512], k == 0, k == 7)
                o32 = wk32[bi % 3]
                fn = AF.Silu if bi < 2 else AF.Sigmoid
                P.op("act", lambda e, pp=pp, o32=o32, fn=fn: e.activation(out=o32[:, 0:512], in_=pp[:], func=fn),
                     r=[pp], w=[o32])
                if bi < 2:
                    P.dma(sgr_t[ti], sgr_t[ti][:, bi * 512:(bi + 1) * 512], o32, o32[:, 0:512], q="pool")
                else:
                    P.dma(gm_t[ti], gm_t[ti][:, (bi - 2) * 512:(bi - 1) * 512], o32, o32[:, 0:512], q="pool")


    kcT2_d = scratch("kcT2_d", [2, 64, 256], BF16)

    def cmp_phase(slot):
        subs = []

        def cv(off, n, parts=128, name=""):
            t = carve(slot, slot.ap[0:parts, off:off + n], name)
            subs.append(t)
            return t
        w1 = [cv(0, 8192, 64, "w1k"), cv(8192, 8192, 64, "w1v")]
        w2 = [cv(16384, 128, 128, "w2k"), cv(16512, 128, 128, "w2v")]
        tokT = [cv(16640, 4096, 64, "tokT0"), cv(20736, 4096, 64, "tokT1")]
        GT = cv(24832, 512, 128, "GT")
        peb = cv(25344, 64, 32, "peb")
        peT = cv(25408, 32, 64, "peT")
        kcb = cv(25440, 64, 128, "kcb")
        kcTs = cv(25504, 128, 64, "kcTs")
        tCc = sm[1]; tSc = sm[2]; bias = sm[3]
        srcs = [(I["cmp_k_w1"], I["cmp_k_w2"], kcT_d, kc_d), (I["cmp_v_w1"], I["cmp_v_w2"], vcT_d, vc_d)]
        P.dma(peb, peb[:, :], I["cmp_pos_emb"], I["cmp_pos_emb"][:, :], q="pool")
        tr(pbf, pbf[0:64, 0:32], peb, peb[:, :]) if False else P.op(
            "pe", lambda e: e.transpose(out=pbf[0:64, 0:32], in_=peb[:, :], identity=ident[0:32, 0:32]),
            r=[peb, ident], w=[pbf])
        P.op("act", lambda e: e.copy(out=peT[:, :], in_=pbf[0:64, 0:32]), r=[pbf], w=[peT])
        P.op("pool", lambda e: e.memset(GT[:, :], 0.0), w=[GT])
        GT3 = GT[:, :].rearrange("p (m n) -> p m n", m=2)
        for kv in range(2):
            w1_d, w2_d, tT_d, o_d = srcs[kv]
            w1v = w1[kv][:, :].rearrange("p (l j) -> p l j", l=32)
            P.dma(w1[kv], w1v, w1_d, w1_d.ap.rearrange("(l d) j -> d l j", d=64), q="pool")
            w2v = w2[kv][:, :].rearrange("p (m o) -> p m o", m=2)
            P.dma(w2[kv], w2v, w2_d, w2_d.ap.rearrange("(m p) o -> p m o", p=128), q="pool")
            pbias = pb[6]
            for m in range(2):
                for l in range(32):
                    mm(pbias, pbias[:, m:m + 1], w1[kv], w1v[:, l, m * 128:(m + 1) * 128], peT, peT[:, l:l + 1], l == 0, l == 31)
            P.op("act", lambda e, pbias=pbias: e.copy(out=bias[:, 0:2], in_=pbias[:, 0:2]), r=[pbias], w=[bias])
            for g in range(2):
                tk = tokT[g]
                P.dma(tk, tk[:, :], tT_d, tT_d.ap[g * 64:(g + 1) * 64, :])
                tok3 = tk[:, :].rearrange("p (n r) -> p n r", r=16)
                for m in range(2):
                    ph = pb[m]
                    for l in range(32):
                        a, rr_ = l // 16, l % 16
                        mm(ph, ph[:, 0:255], w1[kv], w1v[:, l, m * 128:(m + 1) * 128], tk, tok3[:, a:a + 255, rr_], l == 0, l == 31)
                    hbx, x2, z = wk32[0], wk32[1], wk32[2]
                    P.op("act", lambda e, ph=ph, m=m: e.activation(out=hbx[:, 0:255], in_=ph[:, 0:255], func=AF.Identity,
                                                                   bias=bias[:, m:m + 1]), r=[ph, bias], w=[hbx])
                    P.op("dve", lambda e: e.tensor_tensor(out=x2[:, 0:255], in0=hbx[:, 0:255], in1=hbx[:, 0:255], op=ALU.mult),
                         r=[hbx], w=[x2])
                    P.op("dve", lambda e: e.tensor_scalar(out=x2[:, 0:255], in0=x2[:, 0:255], scalar1=0.044715, scalar2=1.0,
                                                          op0=ALU.mult, op1=ALU.add), r=[x2], w=[x2])
                    P.op("dve", lambda e: e.tensor_tensor(out=z[:, 0:255], in0=x2[:, 0:255], in1=hbx[:, 0:255], op=ALU.mult),
                         r=[x2, hbx], w=[z])
                    P.op("act", lambda e: e.activation(out=z[:, 0:255], in_=z[:, 0:255], func=AF.Sigmoid, scale=1.5957691216),
                         r=[z], w=[z])
                    P.op("dve", lambda e, m=m: e.tensor_tensor(out=GT3[:, m, 0:255], in0=z[:, 0:255], in1=hbx[:, 0:255],
                                                               op=ALU.mult), r=[z, hbx], w=[GT])
                for j in range(2):
                    po = pb[2 + j]
                    for m in range(2):
                        mm(po, po[:, 0:64], GT, GT3[:, m, j * 128:(j + 1) * 128], w2[kv], w2v[:, m, :], m == 0, m == 1)
                    if kv == 0:
                        P.dma(tCc, tCc[:, 0:64], C["ropeCc"], C["ropeCc"][j * 128:(j + 1) * 128, :])
                        P.dma(tSc, tSc[:, 0:16], C["ropeSc"], C["ropeSc"][j * 128:(j + 1) * 128, :])
                        rope(po, po[:, 0:64], 1, 16, tCc, tCc[:, 0:64], tSc, tSc[:, 0:16], kcb, kcb[:, :], wk32[0], wk32[1])
                        P.op("pe", lambda e: e.transpose(out=pbf[0:64, 0:128], in_=kcb[:, :], identity=ident[:]),
                             r=[kcb, ident], w=[pbf])
                        P.op("act", lambda e: e.copy(out=kcTs[:, :], in_=pbf[0:64, 0:128]), r=[pbf], w=[kcTs])
                        P.dma(kcT2_d, kcT2_d.ap[g, :, j * 128:(j + 1) * 128], kcTs, kcTs[:, :], q="pool")
                    else:
                        P.op("act", lambda e, po=po: e.copy(out=kcb[:, :], in_=po[:, 0:64]), r=[po], w=[kcb])
                    P.dma(o_d, o_d.ap[g, j * 128:(j + 1) * 128, :], kcb, kcb[:, :], q="pool")
        uncarve(slot, subs)

    def nsa_phase(slot):
        subs = []

        def cv(off, n, parts=128, name=""):
            t = carve(slot, slot.ap[0:parts, off:off + n], name)
            subs.append(t)
            return t
        KsT = cv(0, 4096, 64, "KsT"); KwT = cv(4096, 4096, 64, "KwT")
        Vs = cv(8192, 2080, 128, "Vs"); Vw = cv(10272, 2080, 128, "Vw")
        KcT = cv(12352, 256, 64, "KcT"); Vc = cv(12608, 130, 128, "Vc")
        C2S = cv(12738, 128, 128, "C2S"); EM = cv(12866, 4096, 64, "EM")
        QTc = [cv(16962 + i * 1024, 1024, 64, f"QTc{i}") for i in range(2)]
        Et = [cv(19010 + i * 1024, 1024, 128, f"Et{i}") for i in range(3)]
        negm = cv(22082, 1024, 64, "negm")
        selb = cv(23106, 64, 128, "selb")
        ab = cv(23170, 512, 128, "ab")
        aTs = cv(23682, 512, 128, "aTs")
        TK = tCa; TA = tCr
        P.dma(TK, TK[:, 0:128], C["TK"], C["TK"][:, :])
        P.dma(TA, TA[:, 0:128], C["TA"], C["TA"][:, :])
        P.dma(C2S, C2S[:, :].rearrange("p (j s) -> p j s", j=2), C["C2S"], C["C2S"].ap.rearrange("(j p) s -> p j s", p=128), q="pool")
        P.dma(EM, EM[:, 0:2048], C["EM"], C["EM"][:, 0:2048], q="pool")
        P.dma(EM, EM[:, 2048:4096], C["EM"], C["EM"][:, 2048:4096], q="pool")
        Vs3 = Vs[:, :].rearrange("p (j d) -> p j d", d=65)
        Vw3 = Vw[:, :].rearrange("p (j d) -> p j d", d=65)
        Vc3 = Vc[:, :].rearrange("p (j d) -> p j d", d=65)
        C2S3 = C2S[:, :].rearrange("p (j s) -> p j s", j=2)
        SB = [(pb[0], pb[1]), (pb[2], pb[3])]
        OA, OB, OC = pb[4], pb[5], pb[6]
        OA3 = OA[:, 0:260].rearrange("p (h d) -> p h d", d=65)
        OB3 = OB[:, 0:260].rearrange("p (h d) -> p h d", d=65)
        acc = xin[0]; tmpo = xin[1]
        acc3 = acc[:, 0:512].rearrange("p (h d) -> p h d", d=64)
        tmp3 = tmpo[:, 0:512].rearrange("p (h d) -> p h d", d=64)
        den = sm[2]; coef = sm[3]; imp = sm[4]; imp2 = sm[5]; m8 = sm[6]; imp3 = sm[7]
        sidx = {"s": 0, "e": 0}

        def scores(lhs_t, lhs_ap, qt, mask_j=None):
            b = SB[sidx["s"] % 2]; sidx["s"] += 1
            q2 = qt[:, :]
            for hh in range(2):
                mm(b[hh], b[hh][:], lhs_t, lhs_ap, qt, q2[:, hh * 512:(hh + 1) * 512], True, mask_j is None)
            if mask_j is not None:
                for hh in range(2):
                    mm(b[hh], b[hh][:], EM, EM[:, mask_j * 128:(mask_j + 1) * 128], negm, negm[:, hh * 512:(hh + 1) * 512], False, True)
            et = Et[sidx["e"] % 3]; sidx["e"] += 1
            for hh in range(2):
                P.op("act", lambda e, hh=hh, et=et, b=b: e.activation(out=et[:, hh * 512:(hh + 1) * 512], in_=b[hh][:],
                                                                      func=AF.Exp, scale=0.125), r=[b[hh]], w=[et])
            return et

        def amask(et, cm, qstep, base):
            P.op("pool", lambda e: e.affine_select(out=et[:, :], in_=et[:, :], pattern=[[0, 8], [qstep, 128]],
                                                   compare_op=ALU.is_ge, fill=0.0, base=base, channel_multiplier=cm),
                 r=[et], w=[et])

        def pv(et, v_t, v_ap, first, last):
            for h in range(8):
                o_t = OA if h < 4 else OB
                o3 = OA3 if h < 4 else OB3
                mm(o_t, o3[:, h % 4, :], et, et[:, h * 128:(h + 1) * 128], v_t, v_ap, first and (h % 4 == 0), last, sgc=True)

        pend = []

        def flush():
            while pend:
                work, post = pend.pop(0)
                work()
                if post is not None:
                    post()

        def finish_branch(bidx, first, gat):
            P.op("dve", lambda e: e.tensor_scalar(out=den[:, 0:4], in0=OA3[:, :, 64], scalar1=1e-20, scalar2=None, op0=ALU.max),
                 r=[OA], w=[den])
            P.op("dve", lambda e: e.tensor_scalar(out=den[:, 4:8], in0=OB3[:, :, 64], scalar1=1e-20, scalar2=None, op0=ALU.max),
                 r=[OB], w=[den])
            P.op("dve", lambda e: e.reciprocal(out=den[:, 8:16], in_=den[:, 0:8]), r=[den], w=[den])
            g3 = gat[:, 0:24].rearrange("p (h b) -> p h b", b=3)
            P.op("dve", lambda e: e.tensor_tensor(out=coef[:, 0:8], in0=den[:, 8:16], in1=g3[:, :, bidx], op=ALU.mult),
                 r=[den, gat], w=[coef])
            dst, d3 = (acc, acc3) if first else (tmpo, tmp3)
            P.op("dve", lambda e: e.tensor_tensor(out=d3[:, 0:4, :], in0=OA3[:, :, 0:64],
                                                  in1=coef[:, 0:4].unsqueeze(2).broadcast_to([128, 4, 64]), op=ALU.mult),
                 r=[OA, coef], w=[dst])
            P.op("dve", lambda e: e.tensor_tensor(out=d3[:, 4:8, :], in0=OB3[:, :, 0:64],
                                                  in1=coef[:, 4:8].unsqueeze(2).broadcast_to([128, 4, 64]), op=ALU.mult),
                 r=[OB, coef], w=[dst])
            if not first:
                P.op("pool", lambda e: e.tensor_tensor(out=acc[:, 0:512], in0=acc[:, 0:512], in1=tmpo[:, 0:512], op=ALU.add),
                     r=[acc, tmpo], w=[acc])

        for g in range(2):
            P.dma(KsT, KsT[:, :], ksT_d, ksT_d.ap[g * 64:(g + 1) * 64, :])
            P.dma(KwT, KwT[:, :], kwT_d, kwT_d.ap[g * 64:(g + 1) * 64, :])
            P.dma(Vs, Vs3[:, :, 0:64], vs_d, vs_d.ap[:, g * 64:(g + 1) * 64].rearrange("(j p) d -> p j d", p=128))
            P.dma(Vw, Vw3[:, :, 0:64], vw_d, vw_d.ap[:, g * 64:(g + 1) * 64].rearrange("(j p) d -> p j d", p=128))
            P.op("pool", lambda e: e.memset(Vs3[:, :, 64:65], 1.0), w=[Vs])
            P.op("pool", lambda e: e.memset(Vw3[:, :, 64:65], 1.0), w=[Vw])
            P.dma(KcT, KcT[:, :], kcT2_d, kcT2_d.ap[g])
            P.dma(Vc, Vc3[:, :, 0:64], vc_d, vc_d.ap[g].rearrange("(j p) d -> p j d", p=128))
            P.op("pool", lambda e: e.memset(Vc3[:, :, 64:65], 1.0), w=[Vc])
            gats = [sm[1], sm[0]]
            for c in (range(NT) if lim is None else lim_c):
                qt = QTc[c % 2]
                gat = gats[c % 2]
                P.dma(qt, qt[:, :].rearrange("d (h t) -> d h t", h=8), QT_t[g][c],
                      QT_t[g][c][:].rearrange("(h d) t -> d h t", d=64))
                P.dma(gat, gat[:, 0:24], ga_t[c], ga_t[c][:, g * 24:(g + 1) * 24])

                def post_cmp(c=c, gat=gat):
                    finish_branch(0, True, gat)
                    for h in range(8):
                        if h == 0:
                            P.op("dve", lambda e: e.tensor_scalar(out=imp[:, 0:64], in0=OC[:, 0:64], scalar1=den[:, 8:9],
                                                                  scalar2=None, op0=ALU.mult), r=[OC, den], w=[imp])
                        else:
                            P.op("dve", lambda e, h=h: e.scalar_tensor_tensor(
                                out=imp[:, 0:64], in0=OC[:, h * 64:(h + 1) * 64], scalar=den[:, 8 + h:9 + h], in1=imp[:, 0:64],
                                op0=ALU.mult, op1=ALU.add), r=[OC, den, imp], w=[imp])
                    off = 64 - 2 * c
                    P.op("dve", lambda e: e.tensor_tensor(out=imp2[:, 0:64], in0=imp[:, 0:64], in1=TK[:, off:off + 64], op=ALU.mult),
                         r=[imp, TK], w=[imp2])
                    P.op("dve", lambda e: e.tensor_tensor(out=imp2[:, 0:64], in0=imp2[:, 0:64], in1=TA[:, off:off + 64], op=ALU.add),
                         r=[imp2, TA], w=[imp2])
                    P.op("dve", lambda e: e.memset(imp2[:, 0:1], 1.0e4), r=[imp2], w=[imp2])
                    P.op("dve", lambda e: e.max(out=m8[:, 0:8], in_=imp2[:, 0:64]), r=[imp2], w=[m8])
                    P.op("dve", lambda e: e.match_replace(out=imp3[:, 0:64], in_to_replace=m8[:, 0:8], in_values=imp2[:, 0:64],
                                                          imm_value=-3.0e4), r=[imp2, m8], w=[imp3])
                    P.op("dve", lambda e: e.max(out=m8[:, 8:16], in_=imp3[:, 0:64]), r=[imp3, m8], w=[m8])
                    P.op("dve", lambda e: e.tensor_scalar(out=selb[:, :], in0=imp2[:, 0:64], scalar1=m8[:, 15:16], scalar2=None,
                                                          op0=ALU.is_ge), r=[imp2, m8], w=[selb])
                    P.op("pe", lambda e: e.transpose(out=pbf[0:64, 0:128], in_=selb[:, :], identity=ident[:]),
                         r=[selb, ident], w=[pbf])
                    P.op("dve", lambda e: e.tensor_scalar(out=negm[:, :].rearrange("p (h q) -> p h q", h=8),
                                                          in0=pbf[0:64, 0:128].unsqueeze(1).broadcast_to([64, 8, 128]),
                                                          scalar1=-NEGBIG, scalar2=NEGBIG, op0=ALU.mult, op1=ALU.add),
                         r=[pbf], w=[negm])

                def post_win(gat=gat):
                    finish_branch(2, False, gat)

                def post_sel(c=c, gat=gat):
                    finish_branch(1, False, gat)
                    P.op("act", lambda e: e.copy(out=ab[:, :], in_=acc[:, 0:512]), r=[acc], w=[ab])
                    transpose8(ab, aTs, aTs[:, :].rearrange("p (k t) -> p k t", k=4), nblk=4, eng="act")
                    dst = aT_t[c]
                    P.dma(dst, dst[g * 512:(g + 1) * 512, :].rearrange("(k r) t -> r k t", r=128), aTs,
                          aTs[:, :].rearrange("p (k t) -> p k t", k=4), q="pool")

                njt = 1 if c <= 15 else 2
                for j in range(njt):
                    et = scores(KcT, KcT[:, j * 128:(j + 1) * 128], qt)
                    amask(et, -16, 1, -16 * (128 * j - 8 * c) - 31)
                    flush()

                    def work(et=et, j=j, njt=njt):
                        pv(et, Vc, Vc3[:, j, :], j == 0, j == njt - 1)
                        for h in range(8):
                            mm(OC, OC[:, h * 64:(h + 1) * 64], et, et[:, h * 128:(h + 1) * 128], C2S, C2S3[:, j, :],
                               j == 0 and h == 0, j == njt - 1, sgc=True)
                    pend.append((work, post_cmp if j == njt - 1 else None))
                j0 = max(0, c - 4)
                for j in range(j0, c + 1):
                    et = scores(KwT, KwT[:, j * 128:(j + 1) * 128], qt)
                    if j == c:
                        amask(et, -1, 1, 0)
                    if j == c - 4:
                        amask(et, 1, -1, -1)
                    flush()
                    pend.append((lambda et=et, j=j, j0=j0, c=c: pv(et, Vw, Vw3[:, j, :], j == j0, j == c),
                                 post_win if j == c else None))
                for j in range(c + 1):
                    if j == 0:
                        flush()
                    et = scores(KsT, KsT[:, j * 128:(j + 1) * 128], qt, mask_j=j)
                    if j == c:
                        amask(et, -1, 1, 0)
                    flush()
                    pend.append((lambda et=et, j=j, c=c: pv(et, Vs, Vs3[:, j, :], j == 0, j == c),
                                 post_sel if j == c else None))
            flush()
        uncarve(slot, subs)

    def ret_phase():
        subs = []
        bufs = []
        for i in range(2):
            q_ = carve(big, big.ap[0:64, i * 2560:i * 2560 + 1024], f"QrTc{i}")
            k_ = carve(big, big.ap[0:64, i * 2560 + 1024:i * 2560 + 2048], f"KrTc{i}")
            z_ = carve(big, big.ap[:, i * 2560 + 2048:i * 2560 + 2560], f"kzc{i}")
            subs += [q_, k_, z_]
            bufs.append((q_, k_, z_))
        dec = [tCr, tSr]
        gct = [tCa, tZ]
        P.dma(dec[0], dec[0][:, :], C["decayT"], C["decayT"][:, 0:512])
        P.dma(dec[1], dec[1][:, :], C["decayT"], C["decayT"][:, 512:1024])
        P.dma(gct[0], gct[0][0:64, :], C["GC"], C["GC"][:, 0:512])
        P.dma(gct[1], gct[1][0:64, :], C["GC"], C["GC"][:, 512:1024])
        xit = sm[1]
        P.dma(xit, xit[:, 0:8], C["XI"], C["XI"][:, :])
        load_gain(gain2, I["ret_gn_gain"])
        Rst = xin[0]; osbs = [xin[1], gain]; sgr = wk32[0]; sq = wk32[1]; tmp = wk32[2]
        Rbf = wk16[0]; ATs = wk16[1]; rob = wk16[2]
        stt = sm[2]
        P.op("pool", lambda e: e.memset(Rst[0:64, :], 0.0), w=[Rst])
        P.op("pool", lambda e: e.memset(Rbf[0:64, :], 0.0), w=[Rbf])
        sq3 = sq[:, :].rearrange("p (h e) -> p h e", h=8)
        tmp3 = tmp[:, :].rearrange("p (h e) -> p h e", h=8)

        def stage1(n):
            q_, k_, z_ = bufs[n % 2]
            vt = hb[n % 2]
            osb = osbs[n % 2]
            P.dma(q_, q_[:, :].rearrange("d (h t) -> d h t", h=8), QrT_t[n], QrT_t[n][:].rearrange("(h d) t -> d h t", d=64))
            P.dma(k_, k_[:, :].rearrange("d (h t) -> d h t", h=8), KrT_t[n], KrT_t[n][:].rearrange("(h d) t -> d h t", d=64))
            P.dma(z_, z_[:, :], kz_t[n], kz_t[n][:, :])
            P.dma(vt, vt[:, :], vr_t[n], vr_t[n][:, :])
            for h in range(8):
                b = pb[h // 4]
                mm(b, b[:, (h % 4) * 128:(h % 4 + 1) * 128], k_, k_[:, h * 128:(h + 1) * 128], q_, q_[:, h * 128:(h + 1) * 128], True, True)
            if n > 0:
                for h in range(8):
                    b = pb[4 + h // 4]
                    mm(b, b[:, (h % 4) * 128:(h % 4 + 1) * 128], q_, q_[:, h * 128:(h + 1) * 128], Rbf, Rbf[0:64, h * 128:(h + 1) * 128], True, True)
            for hh in range(2):
                P.op("dve", lambda e, hh=hh: e.tensor_tensor(out=ATs[:, hh * 512:(hh + 1) * 512], in0=pb[hh][:], in1=dec[hh][:, :],
                                                             op=ALU.mult), r=[pb[hh], dec[hh]], w=[ATs])
            for h in range(8):
                b = pb[2 + h // 4]
                mm(b, b[:, (h % 4) * 128:(h % 4 + 1) * 128], ATs, ATs[:, h * 128:(h + 1) * 128], vt, vt[:, h * 128:(h + 1) * 128], True, True)
            for hh in range(2):
                b = pb[6]
                for h4 in range(4):
                    h = hh * 4 + h4
                    mm(b, b[0:64, h4 * 128:(h4 + 1) * 128], z_, z_[:, h * 64:(h + 1) * 64], vt, vt[:, h * 128:(h + 1) * 128], True, True)
                P.op("pool", lambda e, hh=hh: e.tensor_tensor(out=Rst[0:64, hh * 512:(hh + 1) * 512], in0=Rst[0:64, hh * 512:(hh + 1) * 512],
                                                              in1=gct[hh][0:64, :], op=ALU.mult), r=[Rst, gct[hh]], w=[Rst])
                P.op("dve", lambda e, hh=hh, b=b: e.tensor_tensor(out=Rst[0:64, hh * 512:(hh + 1) * 512], in0=b[0:64, :],
                                                                  in1=Rst[0:64, hh * 512:(hh + 1) * 512], op=ALU.add), r=[b, Rst], w=[Rst])
            P.op("act", lambda e: e.copy(out=Rbf[0:64, :], in_=Rst[0:64, :]), r=[Rst], w=[Rbf])
            for hh in range(2):
                P.op("act", lambda e, hh=hh: e.copy(out=osb[:, hh * 512:(hh + 1) * 512], in_=pb[2 + hh][:]), r=[pb[2 + hh]], w=[osb])
            if n > 0:
                for hh in range(2):
                    P.op("dve", lambda e, hh=hh: e.tensor_tensor(
                        out=tmp3[:, hh * 4:(hh + 1) * 4, :], in0=pb[4 + hh][:].rearrange("p (h e) -> p h e", h=4),
                        in1=xit[:, hh * 4:(hh + 1) * 4].unsqueeze(2).broadcast_to([128, 4, 128]), op=ALU.mult),
                        r=[pb[4 + hh], xit], w=[tmp])
                P.op("pool", lambda e: e.tensor_tensor(out=osb[:, :], in0=osb[:, :], in1=tmp[:, :], op=ALU.add), r=[osb, tmp], w=[osb])

        def stage2(n):
            osb = osbs[n % 2]
            osb3 = osb[:, :].rearrange("p (h e) -> p h e", h=8)
            P.dma(sgr, sgr[:, :], sgr_t[n], sgr_t[n][:, :])
            P.op("dve", lambda e: e.tensor_reduce(out=stt[:, 0:8], in_=osb3, axis=AX.X, op=ALU.add), r=[osb], w=[stt])
            P.op("act", lambda e: e.activation(out=sq[:, :], in_=osb[:, :], func=AF.Square), r=[osb], w=[sq])
            P.op("dve", lambda e: e.tensor_reduce(out=stt[:, 8:16], in_=sq3, axis=AX.X, op=ALU.add), r=[sq], w=[stt])
            P.op("dve", lambda e: e.tensor_scalar(out=stt[:, 0:8], in0=stt[:, 0:8], scalar1=1.0 / 128, scalar2=None, op0=ALU.mult),
                 r=[stt], w=[stt])
            P.op("dve", lambda e: e.tensor_tensor(out=stt[:, 16:24], in0=stt[:, 0:8], in1=stt[:, 0:8], op=ALU.mult), r=[stt], w=[stt])
            P.op("dve", lambda e: e.scalar_tensor_tensor(out=stt[:, 24:32], in0=stt[:, 8:16], scalar=1.0 / 128, in1=stt[:, 16:24],
                                                         op0=ALU.mult, op1=ALU.subtract), r=[stt], w=[stt])
            P.op("dve", lambda e: e.tensor_scalar(out=stt[:, 24:32], in0=stt[:, 24:32], scalar1=1e-5, scalar2=None, op0=ALU.add),
                 r=[stt], w=[stt])
            P.op("pool", lambda e: e.tensor_tensor(out=stt[:, 40:48], in0=stt[:, 24:32], in1=negh[:, 0:8], op=ALU.pow),
                 r=[stt, negh], w=[stt])
            P.op("dve", lambda e: e.scalar_tensor_tensor(out=stt[:, 48:56], in0=stt[:, 0:8], scalar=-1.0, in1=stt[:, 40:48],
                                                         op0=ALU.mult, op1=ALU.mult), r=[stt], w=[stt])
            for h in range(8):
                P.op("act", lambda e, h=h: e.activation(out=osb3[:, h, :], in_=osb3[:, h, :], func=AF.Identity,
                                                        scale=stt[:, 40 + h:41 + h], bias=stt[:, 48 + h:49 + h]),
                     r=[osb, stt], w=[osb])
            P.op("dve", lambda e: e.tensor_tensor(out=osb[:, :], in0=osb[:, :], in1=gain2[:, :], op=ALU.mult), r=[osb, gain2], w=[osb])
            P.op("pool", lambda e: e.tensor_tensor(out=rob[:, :], in0=osb[:, :], in1=sgr[:, :], op=ALU.mult), r=[osb, sgr], w=[rob])
            transpose8(rob, hT, hT[:, :, 0:128], eng="act")
            dst = rT_t[n]
            P.dma(dst, dst[:].rearrange("(k r) t -> r k t", r=128), hT, hT[:, :, 0:128], q="pool")

        nn = NT if lim is None else lim
        for n in range(nn):
            stage1(n)
            if n > 0:
                stage2(n - 1)
        stage2(nn - 1)
        uncarve(big, subs)

    def merge_phase(slot):
        Wn = load_w(slot, 0, I["w_branch_nsa"], I["w_branch_nsa"].ap.rearrange("(k p) f -> p k f", p=128), [128, 8, 1024])
        Wr = load_w(slot, 8192, I["w_branch_ret"], I["w_branch_ret"].ap.rearrange("(k p) f -> p k f", p=128), [128, 8, 1024])
        Wo = load_w(slot, 16384, I["w_out"], I["w_out"].ap.rearrange("(k p) f -> p k f", p=128), [128, 8, 1024])
        yield
        subs = []
        bufs = []
        for i in range(2):
            a_ = carve(big, big.ap[:, i * 2048:i * 2048 + 1024], f"aTt{i}")
            r_ = carve(big, big.ap[:, i * 2048 + 1024:i * 2048 + 2048], f"rTt{i}")
            subs += [a_, r_]
            bufs.append((a_, r_))
        for ti in range(NT if lim is None else lim):
            a_, r_ = bufs[ti % 2]
            a3 = a_[:, :].rearrange("p (k t) -> p k t", k=8)
            r3 = r_[:, :].rearrange("p (k t) -> p k t", k=8)
            P.dma(a_, a3, aT_t[ti], aT_t[ti][:].rearrange("(k r) t -> r k t", r=128))
            P.dma(r_, r3, rT_t[ti], rT_t[ti][:].rearrange("(k r) t -> r k t", r=128))
            ga, gr = wk32[0], wk32[1]
            P.dma(ga, ga[:, :], gm_t[ti], gm_t[ti][:, 0:1024])
            P.dma(gr, gr[:, :], gm_t[ti], gm_t[ti][:, 1024:2048])
            x1s = wk32[2]
            P.dma(x1s, x1s[:, :], x1_t[ti], x1_t[ti][:, :])
            for half in range(2):
                for k in range(8):
                    mm(pb[half], pb[half][:], a_, a3[:, k, :], slot, Wn[:, k, half * 512:(half + 1) * 512], k == 0, k == 7)
                for k in range(8):
                    mm(pb[2 + half], pb[2 + half][:], r_, r3[:, k, :], slot, Wr[:, k, half * 512:(half + 1) * 512], k == 0, k == 7)
            m1, m2 = xin[0], xin[1]
            mb = hb[ti % 2]
            for half in range(2):
                cs = slice(half * 512, (half + 1) * 512)
                P.op("dve", lambda e, half=half, cs=cs: e.tensor_tensor(out=m1[:, cs], in0=pb[half][:], in1=ga[:, cs], op=ALU.mult),
                     r=[pb[half], ga], w=[m1])
                P.op("dve", lambda e, half=half, cs=cs: e.tensor_tensor(out=m2[:, cs], in0=pb[2 + half][:], in1=gr[:, cs], op=ALU.mult),
                     r=[pb[2 + half], gr], w=[m2])
            P.op("pool", lambda e, mb=mb: e.tensor_tensor(out=mb[:, :], in0=m1[:, :], in1=m2[:, :], op=ALU.add), r=[m1, m2], w=[mb])
            transpose8(mb, hT, hT[:, :, 0:128], eng="act")
            for half in range(2):
                po = pb[4 + half]
                for k in range(8):
                    mm(po, po[:], hT, hT[:, k, 0:128], slot, Wo[:, k, half * 512:(half + 1) * 512], k == 0, k == 7)
                P.op("dve", lambda e, po=po, half=half: e.tensor_tensor(out=x1s[:, half * 512:(half + 1) * 512], in0=po[:],
                                                                       in1=x1s[:, half * 512:(half + 1) * 512], op=ALU.add),
                     r=[po, x1s], w=[x1s])
            P.dma(x2_t[ti], x2_t[ti][:, :], x1s, x1s[:, :], q="pool")
        uncarve(big, subs)

    def begin(gen):
        next(gen)
        return gen

    def finish(gen):
        for _ in gen:
            pass

    allp = set(phases) == {"ffn1", "proj", "cmp", "nsa", "ret", "merge", "ffn2"} and not _os.environ.get("SKIPP1")
    out_t = [T(out_d.ap[i * 128:(i + 1) * 128], f"out{i}") for i in range(NT)]
    if allp:
        f1a = begin(ffn_pass(1, 0, wslot[0], x_t, x1_t, I["ffn1_norm"]))
        f1b = begin(ffn_pass(1, 1, wslot[1], x1_t, x1_t, I["ffn1_norm"]))
        finish(f1a)
        p0 = begin(proj_pass0(wslot[0]))
        finish(f1b)
        p1 = begin(proj_pass1(wslot[1]))
        finish(p0)
        finish(p1)
        cmp_phase(wslot[1])
        mg = begin(merge_phase(wslot[1]))
        nsa_phase(wslot[0])
        f2a = begin(ffn_pass(2, 0, wslot[0], x2_t, x2_t, I["ffn2_norm"]))
        ret_phase()
        finish(mg)
        f2b = begin(ffn_pass(2, 1, wslot[1], x2_t, out_t, I["ffn2_norm"], final=True))
        finish(f2a)
        finish(f2b)
    else:
        if "ffn1" in phases:
            finish(ffn_pass(1, 0, wslot[0], x_t, x1_t, I["ffn1_norm"]))
            finish(ffn_pass(1, 1, wslot[1], x1_t, x1_t, I["ffn1_norm"]))
        if "proj" in phases:
            finish(proj_pass0(wslot[0]))
            if not _os.environ.get("SKIPP1"):
                finish(proj_pass1(wslot[1]))
        if "cmp" in phases:
            cmp_phase(wslot[1])
        if "nsa" in phases:
            nsa_phase(wslot[0])
        if "ret" in phases:
            ret_phase()
        if "merge" in phases:
            finish(merge_phase(wslot[1]))
        if "ffn2" in phases:
            finish(ffn_pass(2, 0, wslot[0], x2_t, x2_t, I["ffn2_norm"]))
            finish(ffn_pass(2, 1, wslot[1], x2_t, out_t, I["ffn2_norm"], final=True))
    P.emit()
    return nc, P


_CACHE = {}


def kernel(**inputs):
    n_cores = 8
    if "nc" not in _CACHE:
        _CACHE["nc"] = build()[0]
        _CACHE["consts"] = {k: np.ascontiguousarray(v.reshape(CONST_SHAPES[k]).astype(np.float32))
                            for k, v in _consts().items()}
    nc = _CACHE["nc"]
    cst = _CACHE["consts"]
    shared = {}
    for k, shp in IN_SHAPES.items():
        if k == "x":
            continue
        shared[k] = np.ascontiguousarray(np.asarray(inputs[k], dtype=np.float32).reshape(shp))
    x = np.asarray(inputs["x"], dtype=np.float32)
    in_maps = []
    for c in range(n_cores):
        m = dict(shared)
        m.update(cst)
        m["x"] = np.ascontiguousarray(x[c % 4])
        in_maps.append(m)
    res = run_bass_kernel_spmd(nc, in_maps, core_ids=list(range(n_cores)))
    out = np.stack([np.asarray(res.results[b]["out"], dtype=np.float32) for b in range(4)], axis=0)
    return out
```

```python
import contextlib
import os as _os
import numpy as np
import concourse.bass as bass
import concourse.mybir as mybir
from concourse.bass_utils import run_bass_kernel_spmd

F32 = mybir.dt.float32
BF16 = mybir.dt.bfloat16
AF = mybir.ActivationFunctionType
ALU = mybir.AluOpType
AX = mybir.AxisListType

S = 4096
D = 1024
DFF = 2816
NT = S // 128
DIN = 6960


class T:
    __slots__ = ("ap", "name", "last_w", "readers", "excl")

    def __init__(self, ap, name="", excl=False):
        self.ap = ap
        self.name = name
        self.last_w = None
        self.readers = []
        self.excl = excl

    def __getitem__(self, k):
        return self.ap[k]


class Prog:
    ENGS = ("pe", "act", "dve", "pool", "sp")

    def __init__(self, nc, n_dma_sems=12, same_engine_sync=True):
        self.nc = nc
        self.ops = []
        self.es = contextlib.ExitStack()
        self.same_engine_sync = same_engine_sync
        self.n_dma_sems = n_dma_sems
        self.ncnt = 0

    def sb(self, shape, dt, name=None):
        self.ncnt += 1
        name = name or f"sb{self.ncnt}"
        t = self.es.enter_context(self.nc.sbuf_tensor(name, list(shape), dt))
        return T(t[:], name)

    def ps(self, shape, dt, name=None):
        self.ncnt += 1
        name = name or f"ps{self.ncnt}"
        t = self.es.enter_context(self.nc.psum_tensor(name, list(shape), dt))
        return T(t[:], name, excl=True)

    def dram(self, name, shape, dt, kind="Internal"):
        t = self.nc.dram_tensor(name, list(shape), dt, kind=kind)
        return T(t.ap(), name)

    def op(self, eng, fn, r=(), w=(), dma=False):
        idx = len(self.ops)
        deps = set()
        raw = set()
        for t in r:
            if t.last_w is not None:
                deps.add(t.last_w)
                raw.add(t.last_w)
            if t.excl:
                for rd in t.readers:
                    if self.ops[rd]["eng"] != eng:
                        deps.add(rd)
        for t in w:
            if t.last_w is not None:
                deps.add(t.last_w)
            deps.update(t.readers)
        for t in r:
            t.readers.append(idx)
        for t in w:
            t.last_w = idx
            t.readers = []
        deps.discard(idx)
        self.ops.append({"eng": eng, "fn": fn, "deps": deps, "dma": dma, "raw": raw, "tag": getattr(self, "tag", "")})
        return idx

    def dma(self, out_t, out_ap, in_t, in_ap, q="sp", **kw):
        return self.op(q, lambda e: e.dma_start(out=out_ap, in_=in_ap, **kw),
                       r=[in_t], w=[out_t], dma=True)

    def emit(self):
        nc = self.nc
        ops = self.ops
        nops = len(ops)
        dma_use = {}
        for i, o in enumerate(ops):
            if o["dma"]:
                dma_use.setdefault(o["eng"], []).append(i)
        dma_sem = {}
        for q, lst in dma_use.items():
            for k, i in enumerate(lst):
                slot = k % self.n_dma_sems
                val = 16 * (k // self.n_dma_sems + 1)
                dma_sem[i] = (q, slot, val)
                if k >= self.n_dma_sems:
                    ops[i]["deps"].add(lst[k - self.n_dma_sems])
        waited = {e: {} for e in self.ENGS}
        waited_dma = {e: set() for e in self.ENGS}
        plan = [None] * nops
        signaling = set()
        for i, o in enumerate(ops):
            e = o["eng"]
            need = {}
            need_dma = []
            for d in sorted(o["deps"]):
                po = ops[d]
                if po["dma"]:
                    if d not in waited_dma[e]:
                        waited_dma[e].add(d)
                        need_dma.append(d)
                    continue
                p = po["eng"]
                if p == e and (e == "pe" or not self.same_engine_sync or d not in o["raw"]):
                    continue
                if waited[e].get(p, -1) >= d:
                    continue
                need[p] = max(need.get(p, -1), d)
            for p, d in need.items():
                waited[e][p] = d
                signaling.add(d)
            plan[i] = (need, need_dma)
        semval = {}
        cnt = {e: 0 for e in self.ENGS}
        for i, o in enumerate(ops):
            if o["dma"]:
                continue
            if i in signaling:
                cnt[o["eng"]] += 1
                semval[i] = cnt[o["eng"]]
        self.stats = {e: sum(1 for o in ops if o["eng"] == e) for e in self.ENGS}
        self.stats["signals"] = dict(cnt)
        es = self.es
        sems = {e: es.enter_context(nc.semaphore(f"sem_{e}")) for e in self.ENGS}
        dsems = {q: [es.enter_context(nc.semaphore(f"dsem_{q}{k}")) for k in range(self.n_dma_sems)]
                 for q in dma_use}
        block = es.enter_context(nc.Block())

        def body(ename):
            def f(eng):
                for i, o in enumerate(ops):
                    if o["eng"] != ename:
                        continue
                    need, need_dma = plan[i]
                    for p, d in need.items():
                        eng.wait_ge(sems[p], semval[d])
                    for d in need_dma:
                        q, slot, val = dma_sem[d]
                        eng.wait_ge(dsems[q][slot], val)
                    ins = o["fn"](eng)
                    if o["dma"]:
                        q, slot, val = dma_sem[i]
                        ins.then_inc(dsems[q][slot], 16)
                    elif i in signaling:
                        ins.then_inc(sems[ename], 1)
                for q, lst in dma_use.items():
                    if q != ename:
                        continue
                    last = {}
                    for i in lst:
                        _, slot, val = dma_sem[i]
                        last[slot] = val
                    for slot, val in last.items():
                        eng.wait_ge(dsems[q][slot], val)
            return f

        block.tensor(body("pe"))
        block.scalar(body("act"))
        block.vector(body("dve"))
        block.gpsimd(body("pool"))
        block.sync(body("sp"))

    def close(self):
        self.es.close()


def _consts():
    c = {}
    pos = np.arange(S, dtype=np.float32)

    def rope_tabs(p, rot, theta, nrep):
        half = rot // 2
        fr = (np.float32(theta) ** (-(np.arange(half, dtype=np.float32) * np.float32(2.0) / np.float32(rot)))).astype(np.float32)
        ang = p.astype(np.float32)[:, None] * fr[None, :]
        cs, sn = np.cos(ang).astype(np.float32), np.sin(ang).astype(np.float32)
        C = np.ones((len(p), 64), np.float32)
        C[:, 0:half] = cs
        C[:, half:rot] = cs
        Sg = np.concatenate([-sn, sn], 1)
        return np.tile(C, (1, nrep)), np.tile(Sg, (1, nrep))

    c["ropeCa"], c["ropeSa"] = rope_tabs(pos, 16, 500000.0, 8)
    pc = np.arange(256, dtype=np.float32) * 16 + 31
    c["ropeCc"], c["ropeSc"] = rope_tabs(pc, 16, 500000.0, 1)
    c["ropeCr"], c["ropeSr"] = rope_tabs(pos, 64, 10000.0, 8)
    lg = np.log(1.0 - 2.0 ** (-5.0 - np.arange(8, dtype=np.float64)))
    i = np.arange(128, dtype=np.float64)
    diff = i[None, :] - i[:, None]
    dT = np.zeros((128, 8, 128), np.float64)
    for h in range(8):
        dT[:, h, :] = np.where(diff >= 0, np.exp(np.maximum(diff, 0) * lg[h]), 0.0) * 0.125
    c["decayT"] = dT.astype(np.float32).reshape(128, 1024)
    zeta = np.exp((127.0 - i)[None, :] * lg[:, None])
    c["ZT"] = np.repeat(zeta.T[:, :, None], 64, axis=2).reshape(128, 512).astype(np.float32)
    xi = np.exp((i + 1.0)[None, :] * lg[:, None]) * 0.125
    c["XI"] = np.ascontiguousarray(xi.T).astype(np.float32)
    gch = np.exp(128.0 * lg)
    c["GC"] = np.broadcast_to(gch[None, :, None], (64, 8, 128)).reshape(64, 1024).astype(np.float32)
    n = np.arange(256)
    cs_ = n * 16
    ss_ = np.arange(64) * 64
    ov = np.clip(np.minimum(cs_[:, None] + 32, ss_[None, :] + 64) - np.maximum(cs_[:, None], ss_[None, :]), 0, None)
    c2s = ov.astype(np.float32) / 32.0
    c2s[255] = 0.0
    c["C2S"] = c2s
    key = np.arange(S)
    c["EM"] = (key[None, :] // 64 == np.arange(64)[:, None]).astype(np.float32)
    sp = np.arange(128) - 64
    curp = (np.arange(128) >= 64).astype(np.int64)[:, None]
    forced = (sp[None, :] == curp) | (sp[None, :] == curp - 1)
    future = sp[None, :] > curp
    c["TK"] = (~(forced | future)).astype(np.float32)
    c["TA"] = np.where(forced, 1.0e4, np.where(future, -1.0e4, 0.0)).astype(np.float32)
    return c


CONST_SHAPES = {"ropeCa": [S, 512], "ropeSa": [S, 128], "ropeCc": [256, 64], "ropeSc": [256, 16],
                "ropeCr": [S, 512], "ropeSr": [S, 512], "decayT": [128, 1024], "ZT": [128, 512],
                "XI": [128, 8], "GC": [64, 1024], "C2S": [256, 64], "EM": [64, S],
                "TK": [128, 128], "TA": [128, 128]}

IN_SHAPES = {"x": [S, D], "ffn1_norm": [1, D], "ffn1_w_gate": [D, DFF], "ffn1_w_up": [D, DFF],
             "ffn1_w_down": [DFF, D], "mix_norm": [1, D], "w_in": [D, DIN], "cmp_pos_emb": [32, 64],
             "cmp_k_w1": [2048, 256], "cmp_k_w2": [256, 64], "cmp_v_w1": [2048, 256], "cmp_v_w2": [256, 64],
             "ret_gn_gain": [1, 1024], "w_branch_nsa": [D, D], "w_branch_ret": [D, D], "w_out": [D, D],
             "ffn2_norm": [1, D], "ffn2_w_gate": [D, DFF], "ffn2_w_up": [D, DFF], "ffn2_w_down": [DFF, D],
             "final_norm": [1, D]}

G_CHUNK = [float(np.exp(128.0 * np.log(1.0 - 2.0 ** (-5.0 - h)))) for h in range(8)]
NEGBIG = -2.0e4


def build(phases=("ffn1", "proj", "cmp", "nsa", "ret", "merge", "ffn2"), dbg=(), lim=None):
    nc = bass.Bass("TRN2", target_bir_lowering=False)
    P = Prog(nc)
    lim_c = [0, 5, 17] if lim else None
    I = {k: P.dram(k, v, F32, kind="ExternalInput") for k, v in IN_SHAPES.items()}
    C = {k: P.dram(k, v, F32, kind="ExternalInput") for k, v in CONST_SHAPES.items()}
    out_d = P.dram("out", [S, D], F32, kind="ExternalOutput")

    def scratch(name, shape, dt):
        return P.dram(name, shape, dt, kind=("ExternalOutput" if name in dbg else "Internal"))

    def tiles(t, n=NT, rows=128):
        return [T(t.ap[i * rows:(i + 1) * rows], f"{t.name}{i}") for i in range(n)]

    def ctiles(t, n=NT, cols=128):
        return [T(t.ap[..., i * cols:(i + 1) * cols], f"{t.name}{i}") for i in range(n)]

    x_t = tiles(I["x"])
    x1_d = scratch("x1_d", [S, D], F32); x1_t = tiles(x1_d)
    x2_d = scratch("x2_d", [S, D], F32); x2_t = tiles(x2_d)
    hT_d = scratch("hT_d", [8, 128, 8, 512], BF16)
    hT_t = [T(hT_d.ap[i], f"hT{i}") for i in range(8)]
    hmT_d = scratch("hmT_d", [NT, 128, 8, 128], BF16)
    hmT_t = [T(hmT_d.ap[i], f"hmT{i}") for i in range(NT)]
    QT_d = scratch("QT_d", [2, 512, S], BF16)
    QT_t = [ctiles(T(QT_d.ap[g], f"QT{g}_")) for g in range(2)]
    kcT_d = scratch("kcT_d", [128, S], BF16); kcT_t = ctiles(kcT_d)
    vcT_d = scratch("vcT_d", [128, S], BF16); vcT_t = ctiles(vcT_d)
    ksT_d = scratch("ksT_d", [128, S], BF16); ksT_t = ctiles(ksT_d)
    kwT_d = scratch("kwT_d", [128, S], BF16); kwT_t = ctiles(kwT_d)
    vs_d = scratch("vs_d", [S, 128], BF16); vs_t = tiles(vs_d)
    vw_d = scratch("vw_d", [S, 128], BF16); vw_t = tiles(vw_d)
    ga_d = scratch("ga_d", [S, 48], F32); ga_t = tiles(ga_d)
    QrT_d = scratch("QrT_d", [512, S], BF16); QrT_t = ctiles(QrT_d)
    KrT_d = scratch("KrT_d", [512, S], BF16); KrT_t = ctiles(KrT_d)
    kz_d = scratch("kz_d", [S, 512], BF16); kz_t = tiles(kz_d)
    vr_d = scratch("vr_d", [S, 1024], BF16); vr_t = tiles(vr_d)
    sgr_d = scratch("sgr_d", [S, 1024], F32); sgr_t = tiles(sgr_d)
    gm_d = scratch("gm_d", [S, 2048], F32); gm_t = tiles(gm_d)
    kc_d = scratch("kc_d", [2, 256, 64], BF16)
    vc_d = scratch("vc_d", [2, 256, 64], BF16)
    aT_d = scratch("aT_d", [1024, S], BF16); aT_t = ctiles(aT_d)
    rT_d = scratch("rT_d", [1024, S], BF16); rT_t = ctiles(rT_d)

    WSLOT = 33792
    wslot = [P.sb([128, WSLOT], BF16, f"wslot{i}") for i in range(2)]
    ident = P.sb([128, 128], BF16, "ident")
    gain = P.sb([128, 1024], F32, "gain")
    gain2 = P.sb([128, 1024], F32, "gain2")
    xin = [P.sb([128, 1024], F32, f"xin{i}") for i in range(2)]
    xres = xin
    hb = [P.sb([128, 1024], BF16, f"hb{i}") for i in range(2)]
    st1 = [P.sb([128, 4], F32, f"st1_{i}") for i in range(2)]
    big = P.sb([128, 5632], BF16, "big")
    hT = P.sb([128, 8, 512], BF16, "hT")
    wk32 = [P.sb([128, 1024], F32, f"wk32_{i}") for i in range(3)]
    wk16 = [P.sb([128, 1024], BF16, f"wk16_{i}") for i in range(3)]
    sm = [P.sb([128, 64], F32, f"sm{i}") for i in range(8)]
    tCa = P.sb([128, 512], F32, "tCa"); tSa = P.sb([128, 128], F32, "tSa")
    tCr = P.sb([128, 512], F32, "tCr"); tSr = P.sb([128, 512], F32, "tSr")
    tZ = P.sb([128, 512], F32, "tZ")
    pb = [P.ps([128, 512], F32, f"pb{i}") for i in range(7)]
    pbf = P.ps([128, 1024], BF16, "pbf")

    negh = P.sb([128, 8], F32, "negh")
    P.op("pool", lambda e: e.memset(negh[:], -0.5), w=[negh])
    P.op("pool", lambda e: e.memset(ident[:], 0.0), w=[ident])
    P.op("pool", lambda e: e.affine_select(out=ident[:], in_=ident[:], pattern=[[-1, 128]], compare_op=ALU.not_equal,
                                           fill=1.0, base=0, channel_multiplier=1), r=[ident], w=[ident])

    cnt = {"rr": 0}

    def rr(lst):
        cnt["rr"] += 1
        return lst[cnt["rr"] % len(lst)]

    def mm(out_t, out_ap, a_t, a_ap, b_t, b_ap, start, stop, sgc=False):
        P.op("pe", lambda e: e.matmul(out_ap, lhsT=a_ap, rhs=b_ap, start=start, stop=stop, skip_group_check=sgc),
             r=[a_t, b_t], w=[out_t])

    def tr(out_t, out_ap, in_t, in_ap):
        P.op("pe", lambda e: e.transpose(out=out_ap, in_=in_ap, identity=ident[:]), r=[in_t, ident], w=[out_t])

    def load_w(slot_t, off, src_t, src_ap, shape):
        n = int(np.prod(shape[1:]))
        dst = slot_t[:, off:off + n]
        if len(shape) == 3:
            dst = dst.rearrange("p (a b) -> p a b", a=shape[1])
        P.dma(slot_t, dst, src_t, src_ap, q="pool")
        return dst

    def rmsnorm(x_tile, gain_t, hb_t, stt):
        sq = wk32[2]
        P.op("act", lambda e: e.activation(out=sq[:], in_=x_tile[:], func=AF.Square, accum_out=stt[:, 0:1]),
             r=[x_tile], w=[sq, stt])
        P.op("dve", lambda e: e.tensor_scalar(out=stt[:, 1:2], in0=stt[:, 0:1], scalar1=1.0 / D, scalar2=1e-6,
                                              op0=ALU.mult, op1=ALU.add), r=[stt], w=[stt])
        P.op("pool", lambda e: e.tensor_tensor(out=stt[:, 3:4], in0=stt[:, 1:2], in1=negh[:, 0:1], op=ALU.pow),
             r=[stt, negh], w=[stt])
        P.op("dve", lambda e: e.scalar_tensor_tensor(out=hb_t[:], in0=x_tile[:], scalar=stt[:, 3:4], in1=gain_t[:],
                                                     op0=ALU.mult, op1=ALU.mult), r=[x_tile, stt, gain_t], w=[hb_t])

    pb6bf = pb[6].ap.bitcast(BF16)

    def transpose8(src_t, dst_t, dst_ap, nblk=8, eng="act", alt=False):
        bank_t, bank = (pb[6], pb6bf) if alt else (pbf, pbf.ap)
        for k in range(nblk):
            tr(bank_t, bank[:, k * 128:(k + 1) * 128], src_t, src_t[:, k * 128:(k + 1) * 128])
        src = bank[:, 0:nblk * 128].rearrange("p (k t) -> p k t", k=nblk)
        if eng == "act":
            P.op("act", lambda e: e.copy(out=dst_ap, in_=src), r=[bank_t], w=[dst_t])
        else:
            P.op("dve", lambda e: e.tensor_copy(out=dst_ap, in_=src), r=[bank_t], w=[dst_t])

    def load_gain(gt, src):
        P.dma(gt, gt[:], src, src.ap.partition_broadcast(128))

    def carve(parent, ap, name=""):
        t = T(ap, name)
        t.last_w = parent.last_w
        t.readers = list(parent.readers)
        return t

    def uncarve(parent, subs):
        for t in subs:
            parent.readers.extend(t.readers)
            if t.last_w is not None:
                parent.readers.append(t.last_w)

    def ffn_pass(which, hf, slot, xsrc_t, xdst_t, norm_gain, final=False):
        pre = f"ffn{which}_"
        wg_d, wu_d, wd_d = I[pre + "w_gate"], I[pre + "w_up"], I[pre + "w_down"]
        c0 = hf * 1408
        wg = load_w(slot, 0, wg_d, wg_d.ap[:, c0:c0 + 1408].rearrange("(k p) f -> p k f", p=128), [128, 8, 1408])
        wu = load_w(slot, 11264, wu_d, wu_d.ap[:, c0:c0 + 1408].rearrange("(k p) f -> p k f", p=128), [128, 8, 1408])
        wd = load_w(slot, 22528, wd_d, wd_d.ap[c0:c0 + 1408, :].rearrange("(k p) d -> p k d", p=128), [128, 11, 1024])
        yield
        aT = big[:, 0:11 * 512].rearrange("p (f t) -> p f t", f=11)
        if hf == 0:
            load_gain(gain, norm_gain)
        if final:
            load_gain(gain2, I["final_norm"])
        for st in range(8 if lim is None else 1):
            if hf == 0:
                for tt in range(4):
                    ti = st * 4 + tt
                    xt = xin[ti % 2]
                    P.dma(xt, xt[:], xsrc_t[ti], xsrc_t[ti][:, :])
                    hbt = hb[ti % 2]
                    rmsnorm(xt, gain, hbt, st1[ti % 2])
                    transpose8(hbt, hT, hT[:, :, tt * 128:(tt + 1) * 128], eng="act")
                P.dma(hT_t[st], hT_t[st][:], hT, hT[:], q="pool")
            else:
                P.dma(hT, hT[:], hT_t[st], hT_t[st][:])
            for f in range(11):
                pg = pb[(2 * f) % 4]
                pu = pb[(2 * f + 1) % 4]
                for k in range(8):
                    mm(pg, pg[:], slot, wg[:, k, f * 128:(f + 1) * 128], hT, hT[:, k, :], k == 0, k == 7)
                for k in range(8):
                    mm(pu, pu[:], slot, wu[:, k, f * 128:(f + 1) * 128], hT, hT[:, k, :], k == 0, k == 7)
                sg = wk32[f % 2]
                P.op("act", lambda e, sg=sg, pg=pg: e.activation(out=sg[:, 0:512], in_=pg[:], func=AF.Silu),
                     r=[pg], w=[sg])
                P.op("dve", lambda e, sg=sg, pu=pu, f=f: e.tensor_tensor(out=aT[:, f, :], in0=pu[:], in1=sg[:, 0:512],
                                                                        op=ALU.mult), r=[pu, sg], w=[big])
            for tt in range(4):
                ti = st * 4 + tt
                xr = xres[ti % 2]
                P.dma(xr, xr[:], xsrc_t[ti], xsrc_t[ti][:, :])
                for half in range(2):
                    py = pb[4 + (2 * tt + half) % 3]
                    for f in range(11):
                        mm(py, py[:], big, aT[:, f, tt * 128:(tt + 1) * 128], slot, wd[:, f, half * 512:(half + 1) * 512],
                           f == 0, f == 10)
                    P.op("dve", lambda e, xr=xr, py=py, half=half: e.scalar_tensor_tensor(
                        out=xr[:, half * 512:(half + 1) * 512], in0=py[:], scalar=0.5,
                        in1=xr[:, half * 512:(half + 1) * 512], op0=ALU.mult, op1=ALU.add), r=[py, xr], w=[xr])
                if final:
                    ob = wk32[ti % 2]
                    stt = st1[ti % 2]
                    sq = wk32[2]
                    P.op("act", lambda e, xr=xr, stt=stt: e.activation(out=sq[:], in_=xr[:], func=AF.Square,
                                                                       accum_out=stt[:, 0:1]), r=[xr], w=[sq, stt])
                    P.op("dve", lambda e, stt=stt: e.tensor_scalar(out=stt[:, 1:2], in0=stt[:, 0:1], scalar1=1.0 / D,
                                                                   scalar2=1e-6, op0=ALU.mult, op1=ALU.add),
                         r=[stt], w=[stt])
                    P.op("pool", lambda e, stt=stt: e.tensor_tensor(out=stt[:, 3:4], in0=stt[:, 1:2], in1=negh[:, 0:1], op=ALU.pow),
                         r=[stt, negh], w=[stt])
                    P.op("dve", lambda e, xr=xr, stt=stt, ob=ob: e.scalar_tensor_tensor(
                        out=ob[:], in0=xr[:], scalar=stt[:, 3:4], in1=gain2[:], op0=ALU.mult, op1=ALU.mult),
                         r=[xr, stt, gain2], w=[ob])
                    P.dma(xdst_t[ti], xdst_t[ti][:, :], ob, ob[:], q="pool")
                else:
                    P.dma(xdst_t[ti], xdst_t[ti][:, :], xr, xr[:], q="pool")

    def rope(src_t, src_ap, H, R, Ct, Cap, St, Sap, out_t, out_ap, o32, t2):
        n = H * 64
        half = R // 2
        P.op("dve", lambda e: e.tensor_tensor(out=o32[:, 0:n], in0=src_ap, in1=Cap, op=ALU.mult), r=[Ct, src_t], w=[o32])
        s3 = src_ap.rearrange("p (h d) -> p h d", h=H)
        S3 = Sap.rearrange("p (h r) -> p h r", h=H)
        t3 = t2[:, 0:H * R].rearrange("p (h r) -> p h r", h=H)
        o3 = o32[:, 0:n].rearrange("p (h d) -> p h d", h=H)
        P.op("dve", lambda e: e.tensor_tensor(out=t3[:, :, 0:half], in0=s3[:, :, half:R], in1=S3[:, :, 0:half],
                                              op=ALU.mult), r=[St, src_t], w=[t2])
        P.op("dve", lambda e: e.tensor_tensor(out=t3[:, :, half:R], in0=s3[:, :, 0:half], in1=S3[:, :, half:R],
                                              op=ALU.mult), r=[St, src_t], w=[t2])
        P.op("dve", lambda e: e.tensor_tensor(out=o3[:, :, 0:R], in0=o3[:, :, 0:R], in1=t3, op=ALU.add),
             r=[o32, t2], w=[o32])
        P.op("act", lambda e: e.copy(out=out_ap, in_=o32[:, 0:n]), r=[o32], w=[out_t])

    def proj_pass0(slot):
        w_d = I["w_in"]
        w0 = load_w(slot, 0, w_d, w_d.ap[:, 0:1944].rearrange("(k p) f -> p k f", p=128), [128, 8, 1944])
        w1 = load_w(slot, 8 * 1944, w_d, w_d.ap[:, 1944:3888].rearrange("(k p) f -> p k f", p=128), [128, 8, 1944])
        yield

        def wcol(k, c0, c1):
            if c1 <= 1944:
                return w0[:, k, c0:c1]
            assert c0 >= 1944
            return w1[:, k, c0 - 1944:c1 - 1944]

        load_gain(gain, I["mix_norm"])
        P.dma(tZ, tZ[:], C["ZT"], C["ZT"][:, :])
        qTs = [P.sb([128, 4, 128], BF16, "qTa"), P.sb([128, 4, 128], BF16, "qTb")]
        blocks = [(0, 512), (512, 1024), (1024, 1536), (1536, 1840), (1840, 2352), (2352, 2864), (2864, 3376), (3376, 3888)]
        subs_c = []
        hbuf = [carve(hT, hT.ap[:, :, 0:128], "hTa"), carve(hT, hT.ap[:, :, 128:256], "hTb")]
        o32s = [carve(wk32[0], wk32[0].ap[:, 0:512], "o32a"), carve(wk32[0], wk32[0].ap[:, 512:1024], "o32b")]
        t2s = [carve(wk32[1], wk32[1].ap[:, 0:512], "t2a"), carve(wk32[1], wk32[1].ap[:, 512:1024], "t2b")]
        subs_c = [(hT, hbuf), (wk32[0], o32s), (wk32[1], t2s)]
        ntl = NT if lim is None else lim
        rcnt = {"r": 0, "q": 0}

        def prep_x(ti):
            P.tag = f"p0.t{ti}.prep"
            xt = xin[ti % 2]
            P.dma(xt, xt[:], x1_t[ti], x1_t[ti][:, :])
            hbt = hb[ti % 2]
            rmsnorm(xt, gain, hbt, st1[ti % 2])
            hb_ = hbuf[ti % 2]
            transpose8(hbt, hb_, hb_[:, :, :], eng="act")
            P.dma(hmT_t[ti], hmT_t[ti][:], hb_, hb_[:, :, :], q="sp")

        def load_tabs_a(ti):
            P.dma(tCa, tCa[:], C["ropeCa"], C["ropeCa"][ti * 128:(ti + 1) * 128, :])
            P.dma(tSa, tSa[:], C["ropeSa"], C["ropeSa"][ti * 128:(ti + 1) * 128, :])

        def load_tabs_r(ti):
            P.dma(tCr, tCr[:], C["ropeCr"], C["ropeCr"][ti * 128:(ti + 1) * 128, :])
            P.dma(tSr, tSr[:], C["ropeSr"], C["ropeSr"][ti * 128:(ti + 1) * 128, :])

        def mm_block(ti, bi):
            P.tag = f"p0.t{ti}.mm{bi}"
            c0, c1 = blocks[bi]
            pp = pb[bi % 6]
            hb_ = hbuf[ti % 2]
            subs = [(c0, c1)] if not (c0 < 1944 < c1) else [(c0, 1944), (1944, c1)]
            for (s0, s1) in subs:
                for k in range(8):
                    mm(pp, pp[:, s0 - c0:s1 - c0], hb_, hb_[:, k, :], slot, wcol(k, s0, s1), k == 0, k == 7)

        def post_a(ti, bi):
            P.tag = f"p0.t{ti}.A{bi}"
            return post_a_(ti, bi)

        def post_a_(ti, bi):
            pp = pb[bi % 6]
            rcnt["r"] += 1
            o32, t2 = o32s[rcnt["r"] % 2], t2s[rcnt["r"] % 2]

            def nextq():
                rcnt["q"] += 1
                P.tag = f"p0.t{ti}.B{bi}"
                return qTs[rcnt["q"] % 2]
            if bi in (0, 1):
                ob = wk16[bi % 2]
                rope(pp, pp[:, 0:512], 8, 16, tCa, tCa[:, :], tSa, tSa[:, :], ob, ob[:, 0:512], o32, t2)

                def fb():
                    qT = nextq()
                    transpose8(ob, qT, qT[:], nblk=4, eng="dve", alt=rcnt["q"] % 2 == 1)
                    dst = QT_t[bi][ti]
                    P.dma(dst, dst[:].rearrange("(p r) t -> r p t", r=128), qT, qT[:], q="sp")
                return fb
            elif bi == 2:
                ob = wk16[2]
                P.op("act", lambda e: e.copy(out=ob[:, 0:256], in_=pp[:, 0:256]), r=[pp], w=[ob])
                rope(pp, pp[:, 256:384], 2, 16, tCa, tCa[:, 0:128], tSa, tSa[:, 0:32], ob, ob[:, 256:384], o32, t2)
                P.op("act", lambda e: e.copy(out=ob[:, 384:512], in_=pp[:, 384:512]), r=[pp], w=[ob])

                def fb():
                    qT = nextq()
                    transpose8(ob, qT, qT[:, 0:3, :], nblk=3, eng="dve", alt=rcnt["q"] % 2 == 1)
                    P.dma(kcT_t[ti], kcT_t[ti][:], qT, qT[:, 0, :], q="sp")
                    P.dma(vcT_t[ti], vcT_t[ti][:], qT, qT[:, 1, :], q="sp")
                    P.dma(ksT_t[ti], ksT_t[ti][:], qT, qT[:, 2, :], q="sp")
                    P.dma(vs_t[ti], vs_t[ti][:, :], ob, ob[:, 384:512], q="sp")
                return fb
            elif bi == 3:
                ob = hb[(ti + 1) % 2]
                ob = wk16[2]
                rope(pp, pp[:, 0:128], 2, 16, tCa, tCa[:, 0:128], tSa, tSa[:, 0:32], ob, ob[:, 0:128], o32, t2)
                P.op("act", lambda e: e.copy(out=ob[:, 128:256], in_=pp[:, 128:256]), r=[pp], w=[ob])
                gsb = sm[0]
                P.op("act", lambda e: e.activation(out=gsb[:, 0:48], in_=pp[:, 256:304], func=AF.Sigmoid), r=[pp], w=[gsb])

                def fb():
                    qT = nextq()
                    P.dma(ga_t[ti], ga_t[ti][:, :], gsb, gsb[:, 0:48], q="sp")
                    transpose8(ob, qT, qT[:, 0:1, :], nblk=1, eng="dve", alt=rcnt["q"] % 2 == 1)
                    P.dma(kwT_t[ti], kwT_t[ti][:], qT, qT[:, 0, :], q="sp")
                    P.dma(vw_t[ti], vw_t[ti][:, :], ob, ob[:, 128:256], q="sp")
                return fb
            elif bi in (4, 5):
                ob = wk16[bi % 2]
                rope(pp, pp[:, 0:512], 8, 64, tCr, tCr[:, :], tSr, tSr[:, :], ob, ob[:, 0:512], o32, t2)
                kz = None
                if bi == 5:
                    kz = hb[(ti + 1) % 2] if False else wk16[2]
                    P.op("pool", lambda e: e.tensor_tensor(out=kz[:, 0:512], in0=o32[:, 0:512], in1=tZ[:], op=ALU.mult),
                         r=[o32, tZ], w=[kz])

                def fb():
                    qT = nextq()
                    transpose8(ob, qT, qT[:], nblk=4, eng="dve", alt=rcnt["q"] % 2 == 1)
                    dst = (QrT_t if bi == 4 else KrT_t)[ti]
                    P.dma(dst, dst[:].rearrange("(p r) t -> r p t", r=128), qT, qT[:], q="sp")
                    if bi == 5:
                        P.dma(kz_t[ti], kz_t[ti][:, :], kz, kz[:, 0:512], q="sp")
                return fb
            else:
                ob = wk16[bi % 2]
                P.op("act", lambda e: e.copy(out=ob[:, 0:512], in_=pp[:]), r=[pp], w=[ob])
                hh = bi - 6

                def fb():
                    P.dma(vr_t[ti], vr_t[ti][:, hh * 512:(hh + 1) * 512], ob, ob[:, 0:512], q="sp")
                return fb

        load_tabs_a(0)
        load_tabs_r(0)
        prep_x(0)
        for ti in range(ntl):
            for bi in range(6):
                mm_block(ti, bi)
            f0 = post_a(ti, 0)
            mm_block(ti, 6)
            f1 = post_a(ti, 1)
            mm_block(ti, 7)
            f0()
            f2 = post_a(ti, 2)
            f1()
            f2()
            f3 = post_a(ti, 3)
            if ti + 1 < ntl:
                prep_x(ti + 1)
            f3()
            if ti + 1 < ntl:
                load_tabs_a(ti + 1)
            f4 = post_a(ti, 4)
            f5 = post_a(ti, 5)
            f4()
            if ti + 1 < ntl:
                load_tabs_r(ti + 1)
            f6 = post_a(ti, 6)
            f5()
            f7 = post_a(ti, 7)
            f6()
            f7()
        for parent, subs in subs_c:
            uncarve(parent, subs)

    def proj_pass1(slot):
        w_d = I["w_in"]
        w0 = load_w(slot, 0, w_d, w_d.ap[:, 3888:3888 + 1536].rearrange("(k p) f -> p k f", p=128), [128, 8, 1536])
        w1 = load_w(slot, 8 * 1536, w_d, w_d.ap[:, 3888 + 1536:DIN].rearrange("(k p) f -> p k f", p=128), [128, 8, 1536])
        yield
        for ti in range(NT if lim is None else lim):
            P.dma(hT, hT[:, :, 0:128], hmT_t[ti], hmT_t[ti][:])
            for bi in range(6):
                pp = pb[bi % 7]
                wv = w0 if bi < 3 else w1
                cc = (bi % 3) * 512
                for k in range(8):
                    mm(pp, pp[:], hT, hT[:, k, 0:128], slot, wv[:, k, cc:cc + 512], k == 0, k == 7)
                o32 = wk32[bi % 3]
                fn = AF.Silu if bi < 2 else AF.Sigmoid
                P.op("act", lambda e, pp=pp, o32=o32, fn=fn: e.activation(out=o32[:, 0:512], in_=pp[:], func=fn),
                     r=[pp], w=[o32])
                if bi < 2:
                    P.dma(sgr_t[ti], sgr_t[ti][:, bi * 512:(bi + 1) * 512], o32, o32[:, 0:512], q="pool")
                else:
                    P.dma(gm_t[ti], gm_t[ti][:, (bi - 2) * 512:(bi - 1) * 512], o32, o32[:, 0:512], q="pool")


    kcT2_d = scratch("kcT2_d", [2, 64, 256], BF16)
    selT_d = scratch("selT_d", [2, 64, 128], BF16)
    selT_t = [T(selT_d.ap[i], f"selT{i}") for i in range(2)]

    def cmp_phase(slot):
        subs = []

        def cv(off, n, parts=128, name=""):
            t = carve(slot, slot.ap[0:parts, off:off + n], name)
            subs.append(t)
            return t
        w1 = [cv(0, 8192, 64, "w1k"), cv(8192, 8192, 64, "w1v")]
        w2 = [cv(16384, 128, 128, "w2k"), cv(16512, 128, 128, "w2v")]
        tokT = [cv(16640, 4096, 64, "tokT0"), cv(20736, 4096, 64, "tokT1")]
        GT = cv(24832, 512, 128, "GT")
        peb = cv(25344, 64, 32, "peb")
        peT = cv(25408, 32, 64, "peT")
        kcb = cv(25440, 64, 128, "kcb")
        kcTs = cv(25504, 128, 64, "kcTs")
        tCc = sm[1]; tSc = sm[2]; bias = sm[3]
        srcs = [(I["cmp_k_w1"], I["cmp_k_w2"], kcT_d, kc_d), (I["cmp_v_w1"], I["cmp_v_w2"], vcT_d, vc_d)]
        P.dma(peb, peb[:, :], I["cmp_pos_emb"], I["cmp_pos_emb"][:, :], q="pool")
        tr(pbf, pbf[0:64, 0:32], peb, peb[:, :]) if False else P.op(
            "pe", lambda e: e.transpose(out=pbf[0:64, 0:32], in_=peb[:, :], identity=ident[0:32, 0:32]),
            r=[peb, ident], w=[pbf])
        P.op("act", lambda e: e.copy(out=peT[:, :], in_=pbf[0:64, 0:32]), r=[pbf], w=[peT])
        P.op("pool", lambda e: e.memset(GT[:, :], 0.0), w=[GT])
        GT3 = GT[:, :].rearrange("p (m n) -> p m n", m=2)
        for kv in range(2):
            w1_d, w2_d, tT_d, o_d = srcs[kv]
            w1v = w1[kv][:, :].rearrange("p (l j) -> p l j", l=32)
            P.dma(w1[kv], w1v, w1_d, w1_d.ap.rearrange("(l d) j -> d l j", d=64), q="pool")
            w2v = w2[kv][:, :].rearrange("p (m o) -> p m o", m=2)
            P.dma(w2[kv], w2v, w2_d, w2_d.ap.rearrange("(m p) o -> p m o", p=128), q="pool")
            pbias = pb[6]
            for m in range(2):
                for l in range(32):
                    mm(pbias, pbias[:, m:m + 1], w1[kv], w1v[:, l, m * 128:(m + 1) * 128], peT, peT[:, l:l + 1], l == 0, l == 31)
            P.op("act", lambda e, pbias=pbias: e.copy(out=bias[:, 0:2], in_=pbias[:, 0:2]), r=[pbias], w=[bias])
            for g in range(2):
                tk = tokT[g]
                P.dma(tk, tk[:, :], tT_d, tT_d.ap[g * 64:(g + 1) * 64, :])
                tok3 = tk[:, :].rearrange("p (n r) -> p n r", r=16)
                for m in range(2):
                    ph = pb[m]
                    for l in range(32):
                        a, rr_ = l // 16, l % 16
                        mm(ph, ph[:, 0:255], w1[kv], w1v[:, l, m * 128:(m + 1) * 128], tk, tok3[:, a:a + 255, rr_], l == 0, l == 31)
                    hbx, x2, z = wk32[0], wk32[1], wk32[2]
                    P.op("act", lambda e, ph=ph, m=m: e.activation(out=hbx[:, 0:255], in_=ph[:, 0:255], func=AF.Identity,
                                                                   bias=bias[:, m:m + 1]), r=[ph, bias], w=[hbx])
                    P.op("dve", lambda e: e.tensor_tensor(out=x2[:, 0:255], in0=hbx[:, 0:255], in1=hbx[:, 0:255], op=ALU.mult),
                         r=[hbx], w=[x2])
                    P.op("dve", lambda e: e.tensor_scalar(out=x2[:, 0:255], in0=x2[:, 0:255], scalar1=0.044715, scalar2=1.0,
                                                          op0=ALU.mult, op1=ALU.add), r=[x2], w=[x2])
                    P.op("dve", lambda e: e.tensor_tensor(out=z[:, 0:255], in0=x2[:, 0:255], in1=hbx[:, 0:255], op=ALU.mult),
                         r=[x2, hbx], w=[z])
                    P.op("act", lambda e: e.activation(out=z[:, 0:255], in_=z[:, 0:255], func=AF.Sigmoid, scale=1.5957691216),
                         r=[z], w=[z])
                    P.op("dve", lambda e, m=m: e.tensor_tensor(out=GT3[:, m, 0:255], in0=z[:, 0:255], in1=hbx[:, 0:255],
                                                               op=ALU.mult), r=[z, hbx], w=[GT])
                for j in range(2):
                    po = pb[2 + j]
                    for m in range(2):
                        mm(po, po[:, 0:64], GT, GT3[:, m, j * 128:(j + 1) * 128], w2[kv], w2v[:, m, :], m == 0, m == 1)
                    if kv == 0:
                        P.dma(tCc, tCc[:, 0:64], C["ropeCc"], C["ropeCc"][j * 128:(j + 1) * 128, :])
                        P.dma(tSc, tSc[:, 0:16], C["ropeSc"], C["ropeSc"][j * 128:(j + 1) * 128, :])
                        rope(po, po[:, 0:64], 1, 16, tCc, tCc[:, 0:64], tSc, tSc[:, 0:16], kcb, kcb[:, :], wk32[0], wk32[1])
                        P.op("pe", lambda e: e.transpose(out=pbf[0:64, 0:128], in_=kcb[:, :], identity=ident[:]),
                             r=[kcb, ident], w=[pbf])
                        P.op("act", lambda e: e.copy(out=kcTs[:, :], in_=pbf[0:64, 0:128]), r=[pbf], w=[kcTs])
                        P.dma(kcT2_d, kcT2_d.ap[g, :, j * 128:(j + 1) * 128], kcTs, kcTs[:, :], q="pool")
                    else:
                        P.op("act", lambda e, po=po: e.copy(out=kcb[:, :], in_=po[:, 0:64]), r=[po], w=[kcb])
                    P.dma(o_d, o_d.ap[g, j * 128:(j + 1) * 128, :], kcb, kcb[:, :], q="pool")
        uncarve(slot, subs)

    def nsa_phase(slot):
        subs = []

        def cv(off, n, parts=128, name=""):
            t = carve(slot, slot.ap[0:parts, off:off + n], name)
            subs.append(t)
            return t
        KsT = cv(0, 4096, 64, "KsT"); KwT = cv(4096, 4096, 64, "KwT")
        Vs = cv(8192, 2080, 128, "Vs"); Vw = cv(10272, 2080, 128, "Vw")
        KcT = cv(12352, 256, 64, "KcT"); Vc = cv(12608, 130, 128, "Vc")
        C2S = cv(12738, 128, 128, "C2S")
        mfull = [cv(12866, 4096, 128, "mfull0"), cv(24194, 4096, 128, "mfull1")]
        tri = cv(28290, 128, 128, "tri")
        selTs = cv(28418, 128, 64, "selTs")
        QTc = [cv(16962 + i * 1024, 1024, 64, f"QTc{i}") for i in range(2)]
        Et = [cv(19010 + i * 1024, 1024, 128, f"Et{i}") for i in range(3)]
        selb = cv(23106, 64, 128, "selb")
        ab = cv(23170, 512, 128, "ab")
        aTs = cv(23682, 512, 128, "aTs")
        TK = tCa; TA = tCr
        P.dma(TK, TK[:, 0:128], C["TK"], C["TK"][:, :])
        P.dma(TA, TA[:, 0:128], C["TA"], C["TA"][:, :])
        P.dma(C2S, C2S[:, :].rearrange("p (j s) -> p j s", j=2), C["C2S"], C["C2S"].ap.rearrange("(j p) s -> p j s", p=128), q="pool")
        P.op("pool", lambda e: e.memset(tri[:, :], 1.0), w=[tri])
        P.op("pool", lambda e: e.affine_select(out=tri[:, :], in_=tri[:, :], pattern=[[1, 128]], compare_op=ALU.is_ge,
                                               fill=0.0, base=0, channel_multiplier=-1), r=[tri], w=[tri])
        Vs3 = Vs[:, :].rearrange("p (j d) -> p j d", d=65)
        Vw3 = Vw[:, :].rearrange("p (j d) -> p j d", d=65)
        Vc3 = Vc[:, :].rearrange("p (j d) -> p j d", d=65)
        C2S3 = C2S[:, :].rearrange("p (j s) -> p j s", j=2)
        SB = [(pb[0], pb[1]), (pb[2], pb[3])]
        OA, OB, OC = pb[4], pb[5], pb[6]
        OA3 = OA[:, 0:260].rearrange("p (h d) -> p h d", d=65)
        OB3 = OB[:, 0:260].rearrange("p (h d) -> p h d", d=65)
        acc = xin[0]; tmpo = xin[1]
        acc3 = acc[:, 0:512].rearrange("p (h d) -> p h d", d=64)
        tmp3 = tmpo[:, 0:512].rearrange("p (h d) -> p h d", d=64)
        den = sm[2]; coef = sm[3]; imp = sm[4]; imp2 = sm[5]; m8 = sm[6]; imp3 = sm[7]
        sidx = {"s": 0, "e": 0}

        def scores(lhs_t, lhs_ap, qt, mask=None):
            b = SB[sidx["s"] % 2]; sidx["s"] += 1
            q2 = qt[:, :]
            for hh in range(2):
                mm(b[hh], b[hh][:], lhs_t, lhs_ap, qt, q2[:, hh * 512:(hh + 1) * 512], True, True)
            et = Et[sidx["e"] % 3]; sidx["e"] += 1
            for hh in range(2):
                P.op("act", lambda e, hh=hh, et=et, b=b: e.activation(out=et[:, hh * 512:(hh + 1) * 512], in_=b[hh][:],
                                                                      func=AF.Exp, scale=0.125), r=[b[hh]], w=[et])
            if mask is not None:
                m_t, m_ap = mask
                P.op("dve", lambda e: e.tensor_tensor(out=et[:, :].rearrange("p (h q) -> p h q", h=8),
                                                      in0=et[:, :].rearrange("p (h q) -> p h q", h=8),
                                                      in1=m_ap.unsqueeze(1).broadcast_to([128, 8, 128]), op=ALU.mult),
                     r=[et, m_t], w=[et])
            return et

        def amask(et, cm, qstep, base):
            P.op("pool", lambda e: e.affine_select(out=et[:, :], in_=et[:, :], pattern=[[0, 8], [qstep, 128]],
                                                   compare_op=ALU.is_ge, fill=0.0, base=base, channel_multiplier=cm),
                 r=[et], w=[et])

        def pv(et, v_t, v_ap, first, last):
            for h in range(8):
                o_t = OA if h < 4 else OB
                o3 = OA3 if h < 4 else OB3
                mm(o_t, o3[:, h % 4, :], et, et[:, h * 128:(h + 1) * 128], v_t, v_ap, first and (h % 4 == 0), last, sgc=True)

        pend = []

        def flush(keep=1):
            while len(pend) > keep:
                work, post = pend.pop(0)
                work()
                if post is not None:
                    post()

        def finish_branch(bidx, first, gat):
            P.op("dve", lambda e: e.tensor_scalar(out=den[:, 0:4], in0=OA3[:, :, 64], scalar1=1e-20, scalar2=None, op0=ALU.max),
                 r=[OA], w=[den])
            P.op("dve", lambda e: e.tensor_scalar(out=den[:, 4:8], in0=OB3[:, :, 64], scalar1=1e-20, scalar2=None, op0=ALU.max),
                 r=[OB], w=[den])
            P.op("dve", lambda e: e.reciprocal(out=den[:, 8:16], in_=den[:, 0:8]), r=[den], w=[den])
            g3 = gat[:, 0:24].rearrange("p (h b) -> p h b", b=3)
            P.op("dve", lambda e: e.tensor_tensor(out=coef[:, 0:8], in0=den[:, 8:16], in1=g3[:, :, bidx], op=ALU.mult),
                 r=[den, gat], w=[coef])
            dst, d3 = (acc, acc3) if first else (tmpo, tmp3)
            P.op("dve", lambda e: e.tensor_tensor(out=d3[:, 0:4, :], in0=OA3[:, :, 0:64],
                                                  in1=coef[:, 0:4].unsqueeze(2).broadcast_to([128, 4, 64]), op=ALU.mult),
                 r=[OA, coef], w=[dst])
            P.op("dve", lambda e: e.tensor_tensor(out=d3[:, 4:8, :], in0=OB3[:, :, 0:64],
                                                  in1=coef[:, 4:8].unsqueeze(2).broadcast_to([128, 4, 64]), op=ALU.mult),
                 r=[OB, coef], w=[dst])
            if not first:
                P.op("pool", lambda e: e.tensor_tensor(out=acc[:, 0:512], in0=acc[:, 0:512], in1=tmpo[:, 0:512], op=ALU.add),
                     r=[acc, tmpo], w=[acc])

        for g in range(2):
            P.dma(KsT, KsT[:, :], ksT_d, ksT_d.ap[g * 64:(g + 1) * 64, :])
            P.dma(KwT, KwT[:, :], kwT_d, kwT_d.ap[g * 64:(g + 1) * 64, :])
            P.dma(Vs, Vs3[:, :, 0:64], vs_d, vs_d.ap[:, g * 64:(g + 1) * 64].rearrange("(j p) d -> p j d", p=128))
            P.dma(Vw, Vw3[:, :, 0:64], vw_d, vw_d.ap[:, g * 64:(g + 1) * 64].rearrange("(j p) d -> p j d", p=128))
            P.op("pool", lambda e: e.memset(Vs3[:, :, 64:65], 1.0), w=[Vs])
            P.op("pool", lambda e: e.memset(Vw3[:, :, 64:65], 1.0), w=[Vw])
            P.dma(KcT, KcT[:, :], kcT2_d, kcT2_d.ap[g])
            P.dma(Vc, Vc3[:, :, 0:64], vc_d, vc_d.ap[g].rearrange("(j p) d -> p j d", p=128))
            P.op("pool", lambda e: e.memset(Vc3[:, :, 64:65], 1.0), w=[Vc])
            gats = [sm[1], sm[0]]
            for c in (range(NT) if lim is None else lim_c):
                qt = QTc[c % 2]
                gat = gats[c % 2]
                P.dma(qt, qt[:, :].rearrange("d (h t) -> d h t", h=8), QT_t[g][c],
                      QT_t[g][c][:].rearrange("(h d) t -> d h t", d=64))
                P.dma(gat, gat[:, 0:24], ga_t[c], ga_t[c][:, g * 24:(g + 1) * 24])

                def post_cmp(c=c, gat=gat):
                    finish_branch(0, True, gat)
                    for h in range(8):
                        if h == 0:
                            P.op("dve", lambda e: e.tensor_scalar(out=imp[:, 0:64], in0=OC[:, 0:64], scalar1=den[:, 8:9],
                                                                  scalar2=None, op0=ALU.mult), r=[OC, den], w=[imp])
                        else:
                            P.op("dve", lambda e, h=h: e.scalar_tensor_tensor(
                                out=imp[:, 0:64], in0=OC[:, h * 64:(h + 1) * 64], scalar=den[:, 8 + h:9 + h], in1=imp[:, 0:64],
                                op0=ALU.mult, op1=ALU.add), r=[OC, den, imp], w=[imp])
                    off = 64 - 2 * c
                    P.op("dve", lambda e: e.tensor_tensor(out=imp2[:, 0:64], in0=imp[:, 0:64], in1=TK[:, off:off + 64], op=ALU.mult),
                         r=[imp, TK], w=[imp2])
                    P.op("dve", lambda e: e.tensor_tensor(out=imp2[:, 0:64], in0=imp2[:, 0:64], in1=TA[:, off:off + 64], op=ALU.add),
                         r=[imp2, TA], w=[imp2])
                    P.op("dve", lambda e: e.memset(imp2[:, 0:1], 1.0e4), r=[imp2], w=[imp2])
                    P.op("dve", lambda e: e.max(out=m8[:, 0:8], in_=imp2[:, 0:64]), r=[imp2], w=[m8])
                    P.op("dve", lambda e: e.match_replace(out=imp3[:, 0:64], in_to_replace=m8[:, 0:8], in_values=imp2[:, 0:64],
                                                          imm_value=-3.0e4), r=[imp2, m8], w=[imp3])
                    P.op("dve", lambda e: e.max(out=m8[:, 8:16], in_=imp3[:, 0:64]), r=[imp3, m8], w=[m8])
                    P.op("dve", lambda e: e.tensor_scalar(out=selb[:, :].rearrange("p (b j) -> p j b", b=2),
                                                          in0=imp2[:, 0:64].rearrange("p (j b) -> p j b", b=2),
                                                          scalar1=m8[:, 15:16], scalar2=None, op0=ALU.is_ge),
                         r=[imp2, m8], w=[selb])
                    P.op("pe", lambda e: e.transpose(out=pbf[0:64, 0:128], in_=selb[:, :], identity=ident[:]),
                         r=[selb, ident], w=[pbf])
                    P.op("dve", lambda e: e.tensor_copy(out=selTs[:, :], in_=pbf[0:64, 0:128]), r=[pbf], w=[selTs])
                    sd = selT_t[c % 2]
                    P.dma(sd, sd[:, :], selTs, selTs[:, :])
                    mf = mfull[c % 2]
                    for b2 in range(2):
                        src = sd[b2 * 32:b2 * 32 + c + 1, :].rearrange("(o r) q -> o (r q)", o=1).partition_broadcast(64)
                        P.dma(mf, mf[b2 * 64:(b2 + 1) * 64, 0:(c + 1) * 128], sd, src)
                    P.op("pool", lambda e: e.tensor_tensor(out=mf[:, c * 128:(c + 1) * 128], in0=mf[:, c * 128:(c + 1) * 128],
                                                           in1=tri[:, :], op=ALU.mult), r=[mf, tri], w=[mf])

                def post_win(gat=gat):
                    finish_branch(2, False, gat)

                def post_sel(c=c, gat=gat):
                    finish_branch(1, False, gat)
                    P.op("act", lambda e: e.copy(out=ab[:, :], in_=acc[:, 0:512]), r=[acc], w=[ab])
                    transpose8(ab, aTs, aTs[:, :].rearrange("p (k t) -> p k t", k=4), nblk=4, eng="act")
                    dst = aT_t[c]
                    P.dma(dst, dst[g * 512:(g + 1) * 512, :].rearrange("(k r) t -> r k t", r=128), aTs,
                          aTs[:, :].rearrange("p (k t) -> p k t", k=4), q="pool")

                njt = 1 if c <= 15 else 2
                for j in range(njt):
                    et = scores(KcT, KcT[:, j * 128:(j + 1) * 128], qt)
                    amask(et, -16, 1, -16 * (128 * j - 8 * c) - 31)
                    flush()

                    def work(et=et, j=j, njt=njt):
                        pv(et, Vc, Vc3[:, j, :], j == 0, j == njt - 1)
                        for h in range(8):
                            mm(OC, OC[:, h * 64:(h + 1) * 64], et, et[:, h * 128:(h + 1) * 128], C2S, C2S3[:, j, :],
                               j == 0 and h == 0, j == njt - 1, sgc=True)
                    pend.append((work, post_cmp if j == njt - 1 else None))
                j0 = max(0, c - 4)
                for j in range(j0, c + 1):
                    et = scores(KwT, KwT[:, j * 128:(j + 1) * 128], qt)
                    if j == c:
                        amask(et, -1, 1, 0)
                    if j == c - 4:
                        amask(et, 1, -1, -1)
                    flush()
                    pend.append((lambda et=et, j=j, j0=j0, c=c: pv(et, Vw, Vw3[:, j, :], j == j0, j == c),
                                 post_win if j == c else None))
                mf = mfull[c % 2]
                for j in range(c + 1):
                    if j == 0:
                        flush(0)
                    et = scores(KsT, KsT[:, j * 128:(j + 1) * 128], qt, mask=(mf, mf[:, j * 128:(j + 1) * 128]))
                    flush()
                    pend.append((lambda et=et, j=j, c=c: pv(et, Vs, Vs3[:, j, :], j == 0, j == c),
                                 post_sel if j == c else None))
            flush(0)
        uncarve(slot, subs)

    def ret_phase():
        subs = []
        bufs = []
        for i in range(2):
            q_ = carve(big, big.ap[0:64, i * 2560:i * 2560 + 1024], f"QrTc{i}")
            k_ = carve(big, big.ap[0:64, i * 2560 + 1024:i * 2560 + 2048], f"KrTc{i}")
            z_ = carve(big, big.ap[:, i * 2560 + 2048:i * 2560 + 2560], f"kzc{i}")
            subs += [q_, k_, z_]
            bufs.append((q_, k_, z_))
        dec = [tCr, tSr]
        gct = [tCa, tZ]
        P.dma(dec[0], dec[0][:, :], C["decayT"], C["decayT"][:, 0:512])
        P.dma(dec[1], dec[1][:, :], C["decayT"], C["decayT"][:, 512:1024])
        P.dma(gct[0], gct[0][0:64, :], C["GC"], C["GC"][:, 0:512])
        P.dma(gct[1], gct[1][0:64, :], C["GC"], C["GC"][:, 512:1024])
        xit = sm[1]
        P.dma(xit, xit[:, 0:8], C["XI"], C["XI"][:, :])
        load_gain(gain2, I["ret_gn_gain"])
        Rst = xin[0]; osbs = [xin[1], gain]; sgr = wk32[0]; sq = wk32[1]; tmp = wk32[2]
        Rbf = wk16[0]; ATs = wk16[1]; rob = wk16[2]
        stt = sm[2]
        P.op("pool", lambda e: e.memset(Rst[0:64, :], 0.0), w=[Rst])
        P.op("pool", lambda e: e.memset(Rbf[0:64, :], 0.0), w=[Rbf])
        sq3 = sq[:, :].rearrange("p (h e) -> p h e", h=8)
        tmp3 = tmp[:, :].rearrange("p (h e) -> p h e", h=8)

        def stage1(n):
            q_, k_, z_ = bufs[n % 2]
            vt = hb[n % 2]
            osb = osbs[n % 2]
            P.dma(q_, q_[:, :].rearrange("d (h t) -> d h t", h=8), QrT_t[n], QrT_t[n][:].rearrange("(h d) t -> d h t", d=64))
            P.dma(k_, k_[:, :].rearrange("d (h t) -> d h t", h=8), KrT_t[n], KrT_t[n][:].rearrange("(h d) t -> d h t", d=64))
            P.dma(z_, z_[:, :], kz_t[n], kz_t[n][:, :])
            P.dma(vt, vt[:, :], vr_t[n], vr_t[n][:, :])
            for h in range(8):
                b = pb[h // 4]
                mm(b, b[:, (h % 4) * 128:(h % 4 + 1) * 128], k_, k_[:, h * 128:(h + 1) * 128], q_, q_[:, h * 128:(h + 1) * 128], True, True)
            if n > 0:
                for h in range(8):
                    b = pb[4 + h // 4]
                    mm(b, b[:, (h % 4) * 128:(h % 4 + 1) * 128], q_, q_[:, h * 128:(h + 1) * 128], Rbf, Rbf[0:64, h * 128:(h + 1) * 128], True, True)
            for hh in range(2):
                P.op("dve", lambda e, hh=hh: e.tensor_tensor(out=ATs[:, hh * 512:(hh + 1) * 512], in0=pb[hh][:], in1=dec[hh][:, :],
                                                             op=ALU.mult), r=[pb[hh], dec[hh]], w=[ATs])
            for h in range(8):
                b = pb[2 + h // 4]
                mm(b, b[:, (h % 4) * 128:(h % 4 + 1) * 128], ATs, ATs[:, h * 128:(h + 1) * 128], vt, vt[:, h * 128:(h + 1) * 128], True, True)
            for hh in range(2):
                b = pb[6]
                for h4 in range(4):
                    h = hh * 4 + h4
                    mm(b, b[0:64, h4 * 128:(h4 + 1) * 128], z_, z_[:, h * 64:(h + 1) * 64], vt, vt[:, h * 128:(h + 1) * 128], True, True)
                P.op("pool", lambda e, hh=hh: e.tensor_tensor(out=Rst[0:64, hh * 512:(hh + 1) * 512], in0=Rst[0:64, hh * 512:(hh + 1) * 512],
                                                              in1=gct[hh][0:64, :], op=ALU.mult), r=[Rst, gct[hh]], w=[Rst])
                P.op("dve", lambda e, hh=hh, b=b: e.tensor_tensor(out=Rst[0:64, hh * 512:(hh + 1) * 512], in0=b[0:64, :],
                                                                  in1=Rst[0:64, hh * 512:(hh + 1) * 512], op=ALU.add), r=[b, Rst], w=[Rst])
            P.op("act", lambda e: e.copy(out=Rbf[0:64, :], in_=Rst[0:64, :]), r=[Rst], w=[Rbf])
            for hh in range(2):
                P.op("act", lambda e, hh=hh: e.copy(out=osb[:, hh * 512:(hh + 1) * 512], in_=pb[2 + hh][:]), r=[pb[2 + hh]], w=[osb])
            if n > 0:
                for hh in range(2):
                    P.op("dve", lambda e, hh=hh: e.tensor_tensor(
                        out=tmp3[:, hh * 4:(hh + 1) * 4, :], in0=pb[4 + hh][:].rearrange("p (h e) -> p h e", h=4),
                        in1=xit[:, hh * 4:(hh + 1) * 4].unsqueeze(2).broadcast_to([128, 4, 128]), op=ALU.mult),
                        r=[pb[4 + hh], xit], w=[tmp])
                P.op("pool", lambda e: e.tensor_tensor(out=osb[:, :], in0=osb[:, :], in1=tmp[:, :], op=ALU.add), r=[osb, tmp], w=[osb])

        def stage2(n):
            osb = osbs[n % 2]
            osb3 = osb[:, :].rearrange("p (h e) -> p h e", h=8)
            P.dma(sgr, sgr[:, :], sgr_t[n], sgr_t[n][:, :])
            P.op("dve", lambda e: e.tensor_reduce(out=stt[:, 0:8], in_=osb3, axis=AX.X, op=ALU.add), r=[osb], w=[stt])
            P.op("act", lambda e: e.activation(out=sq[:, :], in_=osb[:, :], func=AF.Square), r=[osb], w=[sq])
            P.op("dve", lambda e: e.tensor_reduce(out=stt[:, 8:16], in_=sq3, axis=AX.X, op=ALU.add), r=[sq], w=[stt])
            P.op("dve", lambda e: e.tensor_scalar(out=stt[:, 0:8], in0=stt[:, 0:8], scalar1=1.0 / 128, scalar2=None, op0=ALU.mult),
                 r=[stt], w=[stt])
            P.op("dve", lambda e: e.tensor_tensor(out=stt[:, 16:24], in0=stt[:, 0:8], in1=stt[:, 0:8], op=ALU.mult), r=[stt], w=[stt])
            P.op("dve", lambda e: e.scalar_tensor_tensor(out=stt[:, 24:32], in0=stt[:, 8:16], scalar=1.0 / 128, in1=stt[:, 16:24],
                                                         op0=ALU.mult, op1=ALU.subtract), r=[stt], w=[stt])
            P.op("dve", lambda e: e.tensor_scalar(out=stt[:, 24:32], in0=stt[:, 24:32], scalar1=1e-5, scalar2=None, op0=ALU.add),
                 r=[stt], w=[stt])
            P.op("pool", lambda e: e.tensor_tensor(out=stt[:, 40:48], in0=stt[:, 24:32], in1=negh[:, 0:8], op=ALU.pow),
                 r=[stt, negh], w=[stt])
            P.op("dve", lambda e: e.scalar_tensor_tensor(out=stt[:, 48:56], in0=stt[:, 0:8], scalar=-1.0, in1=stt[:, 40:48],
                                                         op0=ALU.mult, op1=ALU.mult), r=[stt], w=[stt])
            for h in range(8):
                P.op("act", lambda e, h=h: e.activation(out=osb3[:, h, :], in_=osb3[:, h, :], func=AF.Identity,
                                                        scale=stt[:, 40 + h:41 + h], bias=stt[:, 48 + h:49 + h]),
                     r=[osb, stt], w=[osb])
            P.op("dve", lambda e: e.tensor_tensor(out=osb[:, :], in0=osb[:, :], in1=gain2[:, :], op=ALU.mult), r=[osb, gain2], w=[osb])
            P.op("pool", lambda e: e.tensor_tensor(out=rob[:, :], in0=osb[:, :], in1=sgr[:, :], op=ALU.mult), r=[osb, sgr], w=[rob])
            transpose8(rob, hT, hT[:, :, 0:128], eng="act")
            dst = rT_t[n]
            P.dma(dst, dst[:].rearrange("(k r) t -> r k t", r=128), hT, hT[:, :, 0:128], q="pool")

        nn = NT if lim is None else lim
        for n in range(nn):
            stage1(n)
            if n > 0:
                stage2(n - 1)
        stage2(nn - 1)
        uncarve(big, subs)

    def merge_phase(slot):
        Wn = load_w(slot, 0, I["w_branch_nsa"], I["w_branch_nsa"].ap.rearrange("(k p) f -> p k f", p=128), [128, 8, 1024])
        Wr = load_w(slot, 8192, I["w_branch_ret"], I["w_branch_ret"].ap.rearrange("(k p) f -> p k f", p=128), [128, 8, 1024])
        Wo = load_w(slot, 16384, I["w_out"], I["w_out"].ap.rearrange("(k p) f -> p k f", p=128), [128, 8, 1024])
        yield
        subs = []
        bufs = []
        for i in range(2):
            a_ = carve(big, big.ap[:, i * 2048:i * 2048 + 1024], f"aTt{i}")
            r_ = carve(big, big.ap[:, i * 2048 + 1024:i * 2048 + 2048], f"rTt{i}")
            subs += [a_, r_]
            bufs.append((a_, r_))
        for ti in range(NT if lim is None else lim):
            a_, r_ = bufs[ti % 2]
            a3 = a_[:, :].rearrange("p (k t) -> p k t", k=8)
            r3 = r_[:, :].rearrange("p (k t) -> p k t", k=8)
            P.dma(a_, a3, aT_t[ti], aT_t[ti][:].rearrange("(k r) t -> r k t", r=128))
            P.dma(r_, r3, rT_t[ti], rT_t[ti][:].rearrange("(k r) t -> r k t", r=128))
            ga, gr = wk32[0], wk32[1]
            P.dma(ga, ga[:, :], gm_t[ti], gm_t[ti][:, 0:1024])
            P.dma(gr, gr[:, :], gm_t[ti], gm_t[ti][:, 1024:2048])
            x1s = wk32[2]
            P.dma(x1s, x1s[:, :], x1_t[ti], x1_t[ti][:, :])
            for half in range(2):
                for k in range(8):
                    mm(pb[half], pb[half][:], a_, a3[:, k, :], slot, Wn[:, k, half * 512:(half + 1) * 512], k == 0, k == 7)
                for k in range(8):
                    mm(pb[2 + half], pb[2 + half][:], r_, r3[:, k, :], slot, Wr[:, k, half * 512:(half + 1) * 512], k == 0, k == 7)
            m1, m2 = xin[0], xin[1]
            mb = hb[ti % 2]
            for half in range(2):
                cs = slice(half * 512, (half + 1) * 512)
                P.op("dve", lambda e, half=half, cs=cs: e.tensor_tensor(out=m1[:, cs], in0=pb[half][:], in1=ga[:, cs], op=ALU.mult),
                     r=[pb[half], ga], w=[m1])
                P.op("dve", lambda e, half=half, cs=cs: e.tensor_tensor(out=m2[:, cs], in0=pb[2 + half][:], in1=gr[:, cs], op=ALU.mult),
                     r=[pb[2 + half], gr], w=[m2])
            P.op("pool", lambda e, mb=mb: e.tensor_tensor(out=mb[:, :], in0=m1[:, :], in1=m2[:, :], op=ALU.add), r=[m1, m2], w=[mb])
            transpose8(mb, hT, hT[:, :, 0:128], eng="act")
            for half in range(2):
                po = pb[4 + half]
                for k in range(8):
                    mm(po, po[:], hT, hT[:, k, 0:128], slot, Wo[:, k, half * 512:(half + 1) * 512], k == 0, k == 7)
                P.op("dve", lambda e, po=po, half=half: e.tensor_tensor(out=x1s[:, half * 512:(half + 1) * 512], in0=po[:],
                                                                       in1=x1s[:, half * 512:(half + 1) * 512], op=ALU.add),
                     r=[po, x1s], w=[x1s])
            P.dma(x2_t[ti], x2_t[ti][:, :], x1s, x1s[:, :], q="pool")
        uncarve(big, subs)

    def begin(gen):
        next(gen)
        return gen

    def finish(gen):
        for _ in gen:
            pass

    allp = set(phases) == {"ffn1", "proj", "cmp", "nsa", "ret", "merge", "ffn2"} and not _os.environ.get("SKIPP1")
    out_t = [T(out_d.ap[i * 128:(i + 1) * 128], f"out{i}") for i in range(NT)]
    if allp:
        f1a = begin(ffn_pass(1, 0, wslot[0], x_t, x1_t, I["ffn1_norm"]))
        f1b = begin(ffn_pass(1, 1, wslot[1], x1_t, x1_t, I["ffn1_norm"]))
        finish(f1a)
        p0 = begin(proj_pass0(wslot[0]))
        finish(f1b)
        p1 = begin(proj_pass1(wslot[1]))
        finish(p0)
        finish(p1)
        cmp_phase(wslot[1])
        mg = begin(merge_phase(wslot[1]))
        nsa_phase(wslot[0])
        f2a = begin(ffn_pass(2, 0, wslot[0], x2_t, x2_t, I["ffn2_norm"]))
        ret_phase()
        finish(mg)
        f2b = begin(ffn_pass(2, 1, wslot[1], x2_t, out_t, I["ffn2_norm"], final=True))
        finish(f2a)
        finish(f2b)
    else:
        if "ffn1" in phases:
            finish(ffn_pass(1, 0, wslot[0], x_t, x1_t, I["ffn1_norm"]))
            finish(ffn_pass(1, 1, wslot[1], x1_t, x1_t, I["ffn1_norm"]))
        if "proj" in phases:
            finish(proj_pass0(wslot[0]))
            if not _os.environ.get("SKIPP1"):
                finish(proj_pass1(wslot[1]))
        if "cmp" in phases:
            cmp_phase(wslot[1])
        if "nsa" in phases:
            nsa_phase(wslot[0])
        if "ret" in phases:
            ret_phase()
        if "merge" in phases:
            finish(merge_phase(wslot[1]))
        if "ffn2" in phases:
            finish(ffn_pass(2, 0, wslot[0], x2_t, x2_t, I["ffn2_norm"]))
            finish(ffn_pass(2, 1, wslot[1], x2_t, out_t, I["ffn2_norm"], final=True))
    P.emit()
    return nc, P


_CACHE = {}


def kernel(**inputs):
    n_cores = 8
    if "nc" not in _CACHE:
        _CACHE["nc"] = build()[0]
        _CACHE["consts"] = {k: np.ascontiguousarray(v.reshape(CONST_SHAPES[k]).astype(np.float32))
                            for k, v in _consts().items()}
    nc = _CACHE["nc"]
    cst = _CACHE["consts"]
    shared = {}
    for k, shp in IN_SHAPES.items():
        if k == "x":
            continue
        shared[k] = np.ascontiguousarray(np.asarray(inputs[k], dtype=np.float32).reshape(shp))
    x = np.asarray(inputs["x"], dtype=np.float32)
    in_maps = []
    for c in range(n_cores):
        m = dict(shared)
        m.update(cst)
        m["x"] = np.ascontiguousarray(x[c % 4])
        in_maps.append(m)
    res = run_bass_kernel_spmd(nc, in_maps, core_ids=list(range(n_cores)))
    out = np.stack([np.asarray(res.results[b]["out"], dtype=np.float32) for b in range(4)], axis=0)
    return out
```

```python
import contextlib
import os as _os
import numpy as np
import concourse.bass as bass
import concourse.mybir as mybir
from concourse.bass_utils import run_bass_kernel_spmd

F32 = mybir.dt.float32
BF16 = mybir.dt.bfloat16
AF = mybir.ActivationFunctionType
ALU = mybir.AluOpType
AX = mybir.AxisListType

S = 4096
D = 1024
DFF = 2816
NT = S // 128
DIN = 6960


class T:
    __slots__ = ("ap", "name", "last_w", "readers", "excl")

    def __init__(self, ap, name="", excl=False):
        self.ap = ap
        self.name = name
        self.last_w = None
        self.readers = []
        self.excl = excl

    def __getitem__(self, k):
        return self.ap[k]


class Prog:
    ENGS = ("pe", "act", "dve", "pool", "sp")

    def __init__(self, nc, n_dma_sems=12, same_engine_sync=True):
        self.nc = nc
        self.ops = []
        self.es = contextlib.ExitStack()
        self.same_engine_sync = same_engine_sync
        self.n_dma_sems = n_dma_sems
        self.ncnt = 0

    def sb(self, shape, dt, name=None):
        self.ncnt += 1
        name = name or f"sb{self.ncnt}"
        t = self.es.enter_context(self.nc.sbuf_tensor(name, list(shape), dt))
        return T(t[:], name)

    def ps(self, shape, dt, name=None):
        self.ncnt += 1
        name = name or f"ps{self.ncnt}"
        t = self.es.enter_context(self.nc.psum_tensor(name, list(shape), dt))
        return T(t[:], name, excl=True)

    def dram(self, name, shape, dt, kind="Internal"):
        t = self.nc.dram_tensor(name, list(shape), dt, kind=kind)
        return T(t.ap(), name)

    def op(self, eng, fn, r=(), w=(), dma=False):
        idx = len(self.ops)
        deps = set()
        raw = set()
        for t in r:
            if t.last_w is not None:
                deps.add(t.last_w)
                raw.add(t.last_w)
            if t.excl:
                for rd in t.readers:
                    if self.ops[rd]["eng"] != eng:
                        deps.add(rd)
        for t in w:
            if t.last_w is not None:
                deps.add(t.last_w)
            deps.update(t.readers)
        for t in r:
            t.readers.append(idx)
        for t in w:
            t.last_w = idx
            t.readers = []
        deps.discard(idx)
        self.ops.append({"eng": eng, "fn": fn, "deps": deps, "dma": dma, "raw": raw, "tag": getattr(self, "tag", "")})
        return idx

    def dma(self, out_t, out_ap, in_t, in_ap, q="sp", **kw):
        return self.op(q, lambda e: e.dma_start(out=out_ap, in_=in_ap, **kw),
                       r=[in_t], w=[out_t], dma=True)

    def emit(self):
        nc = self.nc
        ops = self.ops
        nops = len(ops)
        dma_use = {}
        for i, o in enumerate(ops):
            if o["dma"]:
                dma_use.setdefault(o["eng"], []).append(i)
        dma_sem = {}
        for q, lst in dma_use.items():
            for k, i in enumerate(lst):
                slot = k % self.n_dma_sems
                val = 16 * (k // self.n_dma_sems + 1)
                dma_sem[i] = (q, slot, val)
                if k >= self.n_dma_sems:
                    ops[i]["deps"].add(lst[k - self.n_dma_sems])
        waited = {e: {} for e in self.ENGS}
        waited_dma = {e: set() for e in self.ENGS}
        plan = [None] * nops
        signaling = set()
        for i, o in enumerate(ops):
            e = o["eng"]
            need = {}
            need_dma = []
            for d in sorted(o["deps"]):
                po = ops[d]
                if po["dma"]:
                    if d not in waited_dma[e]:
                        waited_dma[e].add(d)
                        need_dma.append(d)
                    continue
                p = po["eng"]
                if p == e and (e == "pe" or not self.same_engine_sync or d not in o["raw"]):
                    continue
                if waited[e].get(p, -1) >= d:
                    continue
                need[p] = max(need.get(p, -1), d)
            for p, d in need.items():
                waited[e][p] = d
                signaling.add(d)
            plan[i] = (need, need_dma)
        semval = {}
        cnt = {e: 0 for e in self.ENGS}
        for i, o in enumerate(ops):
            if o["dma"]:
                continue
            if i in signaling:
                cnt[o["eng"]] += 1
                semval[i] = cnt[o["eng"]]
        self.stats = {e: sum(1 for o in ops if o["eng"] == e) for e in self.ENGS}
        self.stats["signals"] = dict(cnt)
        es = self.es
        sems = {e: es.enter_context(nc.semaphore(f"sem_{e}")) for e in self.ENGS}
        dsems = {q: [es.enter_context(nc.semaphore(f"dsem_{q}{k}")) for k in range(self.n_dma_sems)]
                 for q in dma_use}
        block = es.enter_context(nc.Block())

        def body(ename):
            def f(eng):
                for i, o in enumerate(ops):
                    if o["eng"] != ename:
                        continue
                    need, need_dma = plan[i]
                    for p, d in need.items():
                        eng.wait_ge(sems[p], semval[d])
                    for d in need_dma:
                        q, slot, val = dma_sem[d]
                        eng.wait_ge(dsems[q][slot], val)
                    ins = o["fn"](eng)
                    if o["dma"]:
                        q, slot, val = dma_sem[i]
                        ins.then_inc(dsems[q][slot], 16)
                    elif i in signaling:
                        ins.then_inc(sems[ename], 1)
                for q, lst in dma_use.items():
                    if q != ename:
                        continue
                    last = {}
                    for i in lst:
                        _, slot, val = dma_sem[i]
                        last[slot] = val
                    for slot, val in last.items():
                        eng.wait_ge(dsems[q][slot], val)
            return f

        block.tensor(body("pe"))
        block.scalar(body("act"))
        block.vector(body("dve"))
        block.gpsimd(body("pool"))
        block.sync(body("sp"))

    def close(self):
        self.es.close()


def _consts():
    c = {}
    pos = np.arange(S, dtype=np.float32)

    def rope_tabs(p, rot, theta, nrep):
        half = rot // 2
        fr = (np.float32(theta) ** (-(np.arange(half, dtype=np.float32) * np.float32(2.0) / np.float32(rot)))).astype(np.float32)
        ang = p.astype(np.float32)[:, None] * fr[None, :]
        cs, sn = np.cos(ang).astype(np.float32), np.sin(ang).astype(np.float32)
        C = np.ones((len(p), 64), np.float32)
        C[:, 0:half] = cs
        C[:, half:rot] = cs
        Sg = np.concatenate([-sn, sn], 1)
        return np.tile(C, (1, nrep)), np.tile(Sg, (1, nrep))

    c["ropeCa"], c["ropeSa"] = rope_tabs(pos, 16, 500000.0, 8)
    pc = np.arange(256, dtype=np.float32) * 16 + 31
    c["ropeCc"], c["ropeSc"] = rope_tabs(pc, 16, 500000.0, 1)
    c["ropeCr"], c["ropeSr"] = rope_tabs(pos, 64, 10000.0, 8)
    lg = np.log(1.0 - 2.0 ** (-5.0 - np.arange(8, dtype=np.float64)))
    i = np.arange(128, dtype=np.float64)
    diff = i[None, :] - i[:, None]
    dT = np.zeros((128, 8, 128), np.float64)
    for h in range(8):
        dT[:, h, :] = np.where(diff >= 0, np.exp(np.maximum(diff, 0) * lg[h]), 0.0) * 0.125
    c["decayT"] = dT.astype(np.float32).reshape(128, 1024)
    zeta = np.exp((127.0 - i)[None, :] * lg[:, None])
    c["ZT"] = np.repeat(zeta.T[:, :, None], 64, axis=2).reshape(128, 512).astype(np.float32)
    xi = np.exp((i + 1.0)[None, :] * lg[:, None]) * 0.125
    c["XI"] = np.ascontiguousarray(xi.T).astype(np.float32)
    gch = np.exp(128.0 * lg)
    c["GC"] = np.broadcast_to(gch[None, :, None], (64, 8, 128)).reshape(64, 1024).astype(np.float32)
    n = np.arange(256)
    cs_ = n * 16
    ss_ = np.arange(64) * 64
    ov = np.clip(np.minimum(cs_[:, None] + 32, ss_[None, :] + 64) - np.maximum(cs_[:, None], ss_[None, :]), 0, None)
    c2s = ov.astype(np.float32) / 32.0
    c2s[255] = 0.0
    c["C2S"] = c2s
    key = np.arange(S)
    c["EM"] = (key[None, :] // 64 == np.arange(64)[:, None]).astype(np.float32)
    sp = np.arange(128) - 64
    curp = (np.arange(128) >= 64).astype(np.int64)[:, None]
    forced = (sp[None, :] == curp) | (sp[None, :] == curp - 1)
    future = sp[None, :] > curp
    c["TK"] = (~(forced | future)).astype(np.float32)
    c["TA"] = np.where(forced, 1.0e4, np.where(future, -1.0e4, 0.0)).astype(np.float32)
    return c


CONST_SHAPES = {"ropeCa": [S, 512], "ropeSa": [S, 128], "ropeCc": [256, 64], "ropeSc": [256, 16],
                "ropeCr": [S, 512], "ropeSr": [S, 512], "decayT": [128, 1024], "ZT": [128, 512],
                "XI": [128, 8], "GC": [64, 1024], "C2S": [256, 64], "EM": [64, S],
                "TK": [128, 128], "TA": [128, 128]}

IN_SHAPES = {"x": [S, D], "ffn1_norm": [1, D], "ffn1_w_gate": [D, DFF], "ffn1_w_up": [D, DFF],
             "ffn1_w_down": [DFF, D], "mix_norm": [1, D], "w_in": [D, DIN], "cmp_pos_emb": [32, 64],
             "cmp_k_w1": [2048, 256], "cmp_k_w2": [256, 64], "cmp_v_w1": [2048, 256], "cmp_v_w2": [256, 64],
             "ret_gn_gain": [1, 1024], "w_branch_nsa": [D, D], "w_branch_ret": [D, D], "w_out": [D, D],
             "ffn2_norm": [1, D], "ffn2_w_gate": [D, DFF], "ffn2_w_up": [D, DFF], "ffn2_w_down": [DFF, D],
             "final_norm": [1, D]}

G_CHUNK = [float(np.exp(128.0 * np.log(1.0 - 2.0 ** (-5.0 - h)))) for h in range(8)]
NEGBIG = -2.0e4


def build(phases=("ffn1", "proj", "cmp", "nsa", "ret", "merge", "ffn2"), dbg=(), lim=None):
    nc = bass.Bass("TRN2", target_bir_lowering=False)
    P = Prog(nc)
    lim_c = [0, 5, 17] if lim else None
    I = {k: P.dram(k, v, F32, kind="ExternalInput") for k, v in IN_SHAPES.items()}
    C = {k: P.dram(k, v, F32, kind="ExternalInput") for k, v in CONST_SHAPES.items()}
    out_d = P.dram("out", [S, D], F32, kind="ExternalOutput")

    def scratch(name, shape, dt):
        return P.dram(name, shape, dt, kind=("ExternalOutput" if name in dbg else "Internal"))

    def tiles(t, n=NT, rows=128):
        return [T(t.ap[i * rows:(i + 1) * rows], f"{t.name}{i}") for i in range(n)]

    def ctiles(t, n=NT, cols=128):
        return [T(t.ap[..., i * cols:(i + 1) * cols], f"{t.name}{i}") for i in range(n)]

    x_t = tiles(I["x"])
    x1_d = scratch("x1_d", [S, D], F32); x1_t = tiles(x1_d)
    x2_d = scratch("x2_d", [S, D], F32); x2_t = tiles(x2_d)
    hT_d = scratch("hT_d", [8, 128, 8, 512], BF16)
    hT_t = [T(hT_d.ap[i], f"hT{i}") for i in range(8)]
    hmT_d = scratch("hmT_d", [NT, 128, 8, 128], BF16)
    hmT_t = [T(hmT_d.ap[i], f"hmT{i}") for i in range(NT)]
    QT_d = scratch("QT_d", [2, 512, S], BF16)
    QT_t = [ctiles(T(QT_d.ap[g], f"QT{g}_")) for g in range(2)]
    kcT_d = scratch("kcT_d", [128, S], BF16); kcT_t = ctiles(kcT_d)
    vcT_d = scratch("vcT_d", [128, S], BF16); vcT_t = ctiles(vcT_d)
    ksT_d = scratch("ksT_d", [128, S], BF16); ksT_t = ctiles(ksT_d)
    kwT_d = scratch("kwT_d", [128, S], BF16); kwT_t = ctiles(kwT_d)
    vs_d = scratch("vs_d", [S, 128], BF16); vs_t = tiles(vs_d)
    vw_d = scratch("vw_d", [S, 128], BF16); vw_t = tiles(vw_d)
    ga_d = scratch("ga_d", [S, 48], F32); ga_t = tiles(ga_d)
    QrT_d = scratch("QrT_d", [512, S], BF16); QrT_t = ctiles(QrT_d)
    KrT_d = scratch("KrT_d", [512, S], BF16); KrT_t = ctiles(KrT_d)
    kz_d = scratch("kz_d", [S, 512], BF16); kz_t = tiles(kz_d)
    vr_d = scratch("vr_d", [S, 1024], BF16); vr_t = tiles(vr_d)
    sgr_d = scratch("sgr_d", [S, 1024], F32); sgr_t = tiles(sgr_d)
    gm_d = scratch("gm_d", [S, 2048], F32); gm_t = tiles(gm_d)
    kc_d = scratch("kc_d", [2, 256, 64], BF16)
    vc_d = scratch("vc_d", [2, 256, 64], BF16)
    aT_d = scratch("aT_d", [1024, S], BF16); aT_t = ctiles(aT_d)
    rT_d = scratch("rT_d", [1024, S], BF16); rT_t = ctiles(rT_d)

    WSLOT = 33792
    wslot = [P.sb([128, WSLOT], BF16, f"wslot{i}") for i in range(2)]
    ident = P.sb([128, 128], BF16, "ident")
    gain = P.sb([128, 1024], F32, "gain")
    gain2 = P.sb([128, 1024], F32, "gain2")
    xin = [P.sb([128, 1024], F32, f"xin{i}") for i in range(2)]
    xres = xin
    hb = [P.sb([128, 1024], BF16, f"hb{i}") for i in range(2)]
    st1 = [P.sb([128, 4], F32, f"st1_{i}") for i in range(2)]
    big = P.sb([128, 5632], BF16, "big")
    hT = P.sb([128, 8, 512], BF16, "hT")
    wk32 = [P.sb([128, 1024], F32, f"wk32_{i}") for i in range(3)]
    wk16 = [P.sb([128, 1024], BF16, f"wk16_{i}") for i in range(3)]
    sm = [P.sb([128, 64], F32, f"sm{i}") for i in range(8)]
    tCa = P.sb([128, 512], F32, "tCa"); tSa = P.sb([128, 128], F32, "tSa")
    tCr = P.sb([128, 512], F32, "tCr"); tSr = P.sb([128, 512], F32, "tSr")
    tZ = P.sb([128, 512], F32, "tZ")
    pb = [P.ps([128, 512], F32, f"pb{i}") for i in range(7)]
    pbf = P.ps([128, 1024], BF16, "pbf")

    negh = P.sb([128, 8], F32, "negh")
    P.op("pool", lambda e: e.memset(negh[:], -0.5), w=[negh])
    P.op("pool", lambda e: e.memset(ident[:], 0.0), w=[ident])
    P.op("pool", lambda e: e.affine_select(out=ident[:], in_=ident[:], pattern=[[-1, 128]], compare_op=ALU.not_equal,
                                           fill=1.0, base=0, channel_multiplier=1), r=[ident], w=[ident])

    cnt = {"rr": 0}

    def rr(lst):
        cnt["rr"] += 1
        return lst[cnt["rr"] % len(lst)]

    def mm(out_t, out_ap, a_t, a_ap, b_t, b_ap, start, stop, sgc=False):
        P.op("pe", lambda e: e.matmul(out_ap, lhsT=a_ap, rhs=b_ap, start=start, stop=stop, skip_group_check=sgc),
             r=[a_t, b_t], w=[out_t])

    def tr(out_t, out_ap, in_t, in_ap):
        P.op("pe", lambda e: e.transpose(out=out_ap, in_=in_ap, identity=ident[:]), r=[in_t, ident], w=[out_t])

    def load_w(slot_t, off, src_t, src_ap, shape):
        n = int(np.prod(shape[1:]))
        dst = slot_t[:, off:off + n]
        if len(shape) == 3:
            dst = dst.rearrange("p (a b) -> p a b", a=shape[1])
        P.dma(slot_t, dst, src_t, src_ap, q="pool")
        return dst

    def rmsnorm(x_tile, gain_t, hb_t, stt):
        sq = wk32[2]
        P.op("act", lambda e: e.activation(out=sq[:], in_=x_tile[:], func=AF.Square, accum_out=stt[:, 0:1]),
             r=[x_tile], w=[sq, stt])
        P.op("dve", lambda e: e.tensor_scalar(out=stt[:, 1:2], in0=stt[:, 0:1], scalar1=1.0 / D, scalar2=1e-6,
                                              op0=ALU.mult, op1=ALU.add), r=[stt], w=[stt])
        P.op("pool", lambda e: e.tensor_tensor(out=stt[:, 3:4], in0=stt[:, 1:2], in1=negh[:, 0:1], op=ALU.pow),
             r=[stt, negh], w=[stt])
        P.op("dve", lambda e: e.scalar_tensor_tensor(out=hb_t[:], in0=x_tile[:], scalar=stt[:, 3:4], in1=gain_t[:],
                                                     op0=ALU.mult, op1=ALU.mult), r=[x_tile, stt, gain_t], w=[hb_t])

    pb6bf = pb[6].ap.bitcast(BF16)

    def transpose8(src_t, dst_t, dst_ap, nblk=8, eng="act", alt=False):
        bank_t, bank = (pb[6], pb6bf) if alt else (pbf, pbf.ap)
        for k in range(nblk):
            tr(bank_t, bank[:, k * 128:(k + 1) * 128], src_t, src_t[:, k * 128:(k + 1) * 128])
        src = bank[:, 0:nblk * 128].rearrange("p (k t) -> p k t", k=nblk)
        if eng == "act":
            P.op("act", lambda e: e.copy(out=dst_ap, in_=src), r=[bank_t], w=[dst_t])
        else:
            P.op("dve", lambda e: e.tensor_copy(out=dst_ap, in_=src), r=[bank_t], w=[dst_t])

    def load_gain(gt, src):
        P.dma(gt, gt[:], src, src.ap.partition_broadcast(128))

    def carve(parent, ap, name=""):
        t = T(ap, name)
        t.last_w = parent.last_w
        t.readers = list(parent.readers)
        return t

    def uncarve(parent, subs):
        for t in subs:
            parent.readers.extend(t.readers)
            if t.last_w is not None:
                parent.readers.append(t.last_w)

    def ffn_pass(which, hf, slot, xsrc_t, xdst_t, norm_gain, final=False):
        pre = f"ffn{which}_"
        wg_d, wu_d, wd_d = I[pre + "w_gate"], I[pre + "w_up"], I[pre + "w_down"]
        c0 = hf * 1408
        wg = load_w(slot, 0, wg_d, wg_d.ap[:, c0:c0 + 1408].rearrange("(k p) f -> p k f", p=128), [128, 8, 1408])
        wu = load_w(slot, 11264, wu_d, wu_d.ap[:, c0:c0 + 1408].rearrange("(k p) f -> p k f", p=128), [128, 8, 1408])
        wd = load_w(slot, 22528, wd_d, wd_d.ap[c0:c0 + 1408, :].rearrange("(k p) d -> p k d", p=128), [128, 11, 1024])
        yield
        aT = big[:, 0:11 * 512].rearrange("p (f t) -> p f t", f=11)
        if hf == 0:
            load_gain(gain, norm_gain)
        if final:
            load_gain(gain2, I["final_norm"])
        for st in range(8 if lim is None else 1):
            if hf == 0:
                for tt in range(4):
                    ti = st * 4 + tt
                    xt = xin[ti % 2]
                    P.dma(xt, xt[:], xsrc_t[ti], xsrc_t[ti][:, :])
                    hbt = hb[ti % 2]
                    rmsnorm(xt, gain, hbt, st1[ti % 2])
                    transpose8(hbt, hT, hT[:, :, tt * 128:(tt + 1) * 128], eng="act")
                P.dma(hT_t[st], hT_t[st][:], hT, hT[:], q="pool")
            else:
                P.dma(hT, hT[:], hT_t[st], hT_t[st][:])
            for f in range(11):
                pg = pb[(2 * f) % 4]
                pu = pb[(2 * f + 1) % 4]
                for k in range(8):
                    mm(pg, pg[:], slot, wg[:, k, f * 128:(f + 1) * 128], hT, hT[:, k, :], k == 0, k == 7)
                for k in range(8):
                    mm(pu, pu[:], slot, wu[:, k, f * 128:(f + 1) * 128], hT, hT[:, k, :], k == 0, k == 7)
                sg = wk32[f % 2]
                P.op("act", lambda e, sg=sg, pg=pg: e.activation(out=sg[:, 0:512], in_=pg[:], func=AF.Silu),
                     r=[pg], w=[sg])
                P.op("dve", lambda e, sg=sg, pu=pu, f=f: e.tensor_tensor(out=aT[:, f, :], in0=pu[:], in1=sg[:, 0:512],
                                                                        op=ALU.mult), r=[pu, sg], w=[big])
            for tt in range(4):
                ti = st * 4 + tt
                xr = xres[ti % 2]
                P.dma(xr, xr[:], xsrc_t[ti], xsrc_t[ti][:, :])
                for half in range(2):
                    py = pb[4 + (2 * tt + half) % 3]
                    for f in range(11):
                        mm(py, py[:], big, aT[:, f, tt * 128:(tt + 1) * 128], slot, wd[:, f, half * 512:(half + 1) * 512],
                           f == 0, f == 10)
                    P.op("dve", lambda e, xr=xr, py=py, half=half: e.scalar_tensor_tensor(
                        out=xr[:, half * 512:(half + 1) * 512], in0=py[:], scalar=0.5,
                        in1=xr[:, half * 512:(half + 1) * 512], op0=ALU.mult, op1=ALU.add), r=[py, xr], w=[xr])
                if final:
                    ob = wk32[ti % 2]
                    stt = st1[ti % 2]
                    sq = wk32[2]
                    P.op("act", lambda e, xr=xr, stt=stt: e.activation(out=sq[:], in_=xr[:], func=AF.Square,
                                                                       accum_out=stt[:, 0:1]), r=[xr], w=[sq, stt])
                    P.op("dve", lambda e, stt=stt: e.tensor_scalar(out=stt[:, 1:2], in0=stt[:, 0:1], scalar1=1.0 / D,
                                                                   scalar2=1e-6, op0=ALU.mult, op1=ALU.add),
                         r=[stt], w=[stt])
                    P.op("pool", lambda e, stt=stt: e.tensor_tensor(out=stt[:, 3:4], in0=stt[:, 1:2], in1=negh[:, 0:1], op=ALU.pow),
                         r=[stt, negh], w=[stt])
                    P.op("dve", lambda e, xr=xr, stt=stt, ob=ob: e.scalar_tensor_tensor(
                        out=ob[:], in0=xr[:], scalar=stt[:, 3:4], in1=gain2[:], op0=ALU.mult, op1=ALU.mult),
                         r=[xr, stt, gain2], w=[ob])
                    P.dma(xdst_t[ti], xdst_t[ti][:, :], ob, ob[:], q="pool")
                else:
                    P.dma(xdst_t[ti], xdst_t[ti][:, :], xr, xr[:], q="pool")

    def rope(src_t, src_ap, H, R, Ct, Cap, St, Sap, out_t, out_ap, o32, t2):
        n = H * 64
        half = R // 2
        P.op("dve", lambda e: e.tensor_tensor(out=o32[:, 0:n], in0=src_ap, in1=Cap, op=ALU.mult), r=[Ct, src_t], w=[o32])
        s3 = src_ap.rearrange("p (h d) -> p h d", h=H)
        S3 = Sap.rearrange("p (h r) -> p h r", h=H)
        t3 = t2[:, 0:H * R].rearrange("p (h r) -> p h r", h=H)
        o3 = o32[:, 0:n].rearrange("p (h d) -> p h d", h=H)
        P.op("dve", lambda e: e.tensor_tensor(out=t3[:, :, 0:half], in0=s3[:, :, half:R], in1=S3[:, :, 0:half],
                                              op=ALU.mult), r=[St, src_t], w=[t2])
        P.op("dve", lambda e: e.tensor_tensor(out=t3[:, :, half:R], in0=s3[:, :, 0:half], in1=S3[:, :, half:R],
                                              op=ALU.mult), r=[St, src_t], w=[t2])
        P.op("dve", lambda e: e.tensor_tensor(out=o3[:, :, 0:R], in0=o3[:, :, 0:R], in1=t3, op=ALU.add),
             r=[o32, t2], w=[o32])
        P.op("act", lambda e: e.copy(out=out_ap, in_=o32[:, 0:n]), r=[o32], w=[out_t])

    def proj_pass0(slot):
        w_d = I["w_in"]
        w0 = load_w(slot, 0, w_d, w_d.ap[:, 0:1944].rearrange("(k p) f -> p k f", p=128), [128, 8, 1944])
        w1 = load_w(slot, 8 * 1944, w_d, w_d.ap[:, 1944:3888].rearrange("(k p) f -> p k f", p=128), [128, 8, 1944])
        yield

        def wcol(k, c0, c1):
            if c1 <= 1944:
                return w0[:, k, c0:c1]
            assert c0 >= 1944
            return w1[:, k, c0 - 1944:c1 - 1944]

        load_gain(gain, I["mix_norm"])
        P.dma(tZ, tZ[:], C["ZT"], C["ZT"][:, :])
        qTs = [P.sb([128, 4, 128], BF16, "qTa"), P.sb([128, 4, 128], BF16, "qTb")]
        blocks = [(0, 512), (512, 1024), (1024, 1536), (1536, 1840), (1840, 2352), (2352, 2864), (2864, 3376), (3376, 3888)]
        subs_c = []
        hbuf = [carve(hT, hT.ap[:, :, 0:128], "hTa"), carve(hT, hT.ap[:, :, 128:256], "hTb")]
        o32s = [carve(wk32[0], wk32[0].ap[:, 0:512], "o32a"), carve(wk32[0], wk32[0].ap[:, 512:1024], "o32b")]
        t2s = [carve(wk32[1], wk32[1].ap[:, 0:512], "t2a"), carve(wk32[1], wk32[1].ap[:, 512:1024], "t2b")]
        subs_c = [(hT, hbuf), (wk32[0], o32s), (wk32[1], t2s)]
        ntl = NT if lim is None else lim
        rcnt = {"r": 0, "q": 0}

        def prep_x(ti):
            P.tag = f"p0.t{ti}.prep"
            xt = xin[ti % 2]
            P.dma(xt, xt[:], x1_t[ti], x1_t[ti][:, :])
            hbt = hb[ti % 2]
            rmsnorm(xt, gain, hbt, st1[ti % 2])
            hb_ = hbuf[ti % 2]
            transpose8(hbt, hb_, hb_[:, :, :], eng="act")
            P.dma(hmT_t[ti], hmT_t[ti][:], hb_, hb_[:, :, :], q="sp")

        def load_tabs_a(ti):
            P.dma(tCa, tCa[:], C["ropeCa"], C["ropeCa"][ti * 128:(ti + 1) * 128, :])
            P.dma(tSa, tSa[:], C["ropeSa"], C["ropeSa"][ti * 128:(ti + 1) * 128, :])

        def load_tabs_r(ti):
            P.dma(tCr, tCr[:], C["ropeCr"], C["ropeCr"][ti * 128:(ti + 1) * 128, :])
            P.dma(tSr, tSr[:], C["ropeSr"], C["ropeSr"][ti * 128:(ti + 1) * 128, :])

        def mm_block(ti, bi):
            P.tag = f"p0.t{ti}.mm{bi}"
            c0, c1 = blocks[bi]
            pp = pb[bi % 6]
            hb_ = hbuf[ti % 2]
            subs = [(c0, c1)] if not (c0 < 1944 < c1) else [(c0, 1944), (1944, c1)]
            for (s0, s1) in subs:
                for k in range(8):
                    mm(pp, pp[:, s0 - c0:s1 - c0], hb_, hb_[:, k, :], slot, wcol(k, s0, s1), k == 0, k == 7)

        def post_a(ti, bi):
            P.tag = f"p0.t{ti}.A{bi}"
            return post_a_(ti, bi)

        def post_a_(ti, bi):
            pp = pb[bi % 6]
            rcnt["r"] += 1
            o32, t2 = o32s[rcnt["r"] % 2], t2s[rcnt["r"] % 2]

            def nextq():
                rcnt["q"] += 1
                P.tag = f"p0.t{ti}.B{bi}"
                return qTs[rcnt["q"] % 2]
            if bi in (0, 1):
                ob = wk16[bi % 2]
                rope(pp, pp[:, 0:512], 8, 16, tCa, tCa[:, :], tSa, tSa[:, :], ob, ob[:, 0:512], o32, t2)

                def fb():
                    qT = nextq()
                    transpose8(ob, qT, qT[:], nblk=4, eng="dve", alt=rcnt["q"] % 2 == 1)
                    dst = QT_t[bi][ti]
                    P.dma(dst, dst[:].rearrange("(p r) t -> r p t", r=128), qT, qT[:], q="sp")
                return fb
            elif bi == 2:
                ob = wk16[2]
                P.op("act", lambda e: e.copy(out=ob[:, 0:256], in_=pp[:, 0:256]), r=[pp], w=[ob])
                rope(pp, pp[:, 256:384], 2, 16, tCa, tCa[:, 0:128], tSa, tSa[:, 0:32], ob, ob[:, 256:384], o32, t2)
                P.op("act", lambda e: e.copy(out=ob[:, 384:512], in_=pp[:, 384:512]), r=[pp], w=[ob])

                def fb():
                    qT = nextq()
                    transpose8(ob, qT, qT[:, 0:3, :], nblk=3, eng="dve", alt=rcnt["q"] % 2 == 1)
                    P.dma(kcT_t[ti], kcT_t[ti][:], qT, qT[:, 0, :], q="sp")
                    P.dma(vcT_t[ti], vcT_t[ti][:], qT, qT[:, 1, :], q="sp")
                    P.dma(ksT_t[ti], ksT_t[ti][:], qT, qT[:, 2, :], q="sp")
                    P.dma(vs_t[ti], vs_t[ti][:, :], ob, ob[:, 384:512], q="sp")
                return fb
            elif bi == 3:
                ob = hb[(ti + 1) % 2]
                ob = wk16[2]
                rope(pp, pp[:, 0:128], 2, 16, tCa, tCa[:, 0:128], tSa, tSa[:, 0:32], ob, ob[:, 0:128], o32, t2)
                P.op("act", lambda e: e.copy(out=ob[:, 128:256], in_=pp[:, 128:256]), r=[pp], w=[ob])
                gsb = sm[0]
                P.op("act", lambda e: e.activation(out=gsb[:, 0:48], in_=pp[:, 256:304], func=AF.Sigmoid), r=[pp], w=[gsb])

                def fb():
                    qT = nextq()
                    P.dma(ga_t[ti], ga_t[ti][:, :], gsb, gsb[:, 0:48], q="sp")
                    transpose8(ob, qT, qT[:, 0:1, :], nblk=1, eng="dve", alt=rcnt["q"] % 2 == 1)
                    P.dma(kwT_t[ti], kwT_t[ti][:], qT, qT[:, 0, :], q="sp")
                    P.dma(vw_t[ti], vw_t[ti][:, :], ob, ob[:, 128:256], q="sp")
                return fb
            elif bi in (4, 5):
                ob = wk16[bi % 2]
                rope(pp, pp[:, 0:512], 8, 64, tCr, tCr[:, :], tSr, tSr[:, :], ob, ob[:, 0:512], o32, t2)
                kz = None
                if bi == 5:
                    kz = hb[(ti + 1) % 2] if False else wk16[2]
                    P.op("pool", lambda e: e.tensor_tensor(out=kz[:, 0:512], in0=o32[:, 0:512], in1=tZ[:], op=ALU.mult),
                         r=[o32, tZ], w=[kz])

                def fb():
                    qT = nextq()
                    transpose8(ob, qT, qT[:], nblk=4, eng="dve", alt=rcnt["q"] % 2 == 1)
                    dst = (QrT_t if bi == 4 else KrT_t)[ti]
                    P.dma(dst, dst[:].rearrange("(p r) t -> r p t", r=128), qT, qT[:], q="sp")
                    if bi == 5:
                        P.dma(kz_t[ti], kz_t[ti][:, :], kz, kz[:, 0:512], q="sp")
                return fb
            else:
                ob = wk16[bi % 2]
                P.op("act", lambda e: e.copy(out=ob[:, 0:512], in_=pp[:]), r=[pp], w=[ob])
                hh = bi - 6

                def fb():
                    P.dma(vr_t[ti], vr_t[ti][:, hh * 512:(hh + 1) * 512], ob, ob[:, 0:512], q="sp")
                return fb

        load_tabs_a(0)
        load_tabs_r(0)
        prep_x(0)
        for ti in range(ntl):
            for bi in range(6):
                mm_block(ti, bi)
            f0 = post_a(ti, 0)
            mm_block(ti, 6)
            f1 = post_a(ti, 1)
            mm_block(ti, 7)
            f0()
            f2 = post_a(ti, 2)
            f1()
            f2()
            f3 = post_a(ti, 3)
            if ti + 1 < ntl:
                prep_x(ti + 1)
            f3()
            if ti + 1 < ntl:
                load_tabs_a(ti + 1)
            f4 = post_a(ti, 4)
            f5 = post_a(ti, 5)
            f4()
            if ti + 1 < ntl:
                load_tabs_r(ti + 1)
            f6 = post_a(ti, 6)
            f5()
            f7 = post_a(ti, 7)
            f6()
            f7()
        for parent, subs in subs_c:
            uncarve(parent, subs)

    def proj_pass1(slot):
        w_d = I["w_in"]
        w0 = load_w(slot, 0, w_d, w_d.ap[:, 3888:3888 + 1536].rearrange("(k p) f -> p k f", p=128), [128, 8, 1536])
        w1 = load_w(slot, 8 * 1536, w_d, w_d.ap[:, 3888 + 1536:DIN].rearrange("(k p) f -> p k f", p=128), [128, 8, 1536])
        yield
        for ti in range(NT if lim is None else lim):
            P.dma(hT, hT[:, :, 0:128], hmT_t[ti], hmT_t[ti][:])
            for bi in range(6):
                pp = pb[bi % 7]
                wv = w0 if bi < 3 else w1
                cc = (bi % 3) * 512
                for k in range(8):
                    mm(pp, pp[:], hT, hT[:, k, 0:128], slot, wv[:, k, cc:cc + 512], k == 0, k == 7)
                o32 = wk32[bi % 3]
                fn = AF.Silu if bi < 2 else AF.Sigmoid
                P.op("act", lambda e, pp=pp, o32=o32, fn=fn: e.activation(out=o32[:, 0:512], in_=pp[:], func=fn),
                     r=[pp], w=[o32])
                if bi < 2:
                    P.dma(sgr_t[ti], sgr_t[ti][:, bi * 512:(bi + 1) * 512], o32, o32[:, 0:512], q="pool")
                else:
                    P.dma(gm_t[ti], gm_t[ti][:, (bi - 2) * 512:(bi - 1) * 512], o32, o32[:, 0:512], q="pool")


    kcT2_d = scratch("kcT2_d", [2, 64, 256], BF16)
    selT_d = scratch("selT_d", [2, 64, 128], BF16)
    selT_t = [T(selT_d.ap[i], f"selT{i}") for i in range(2)]

    def cmp_phase(slot):
        subs = []

        def cv(off, n, parts=128, name=""):
            t = carve(slot, slot.ap[0:parts, off:off + n], name)
            subs.append(t)
            return t
        w1 = [cv(0, 8192, 64, "w1k"), cv(8192, 8192, 64, "w1v")]
        w2 = [cv(16384, 128, 128, "w2k"), cv(16512, 128, 128, "w2v")]
        tokT = [cv(16640, 4096, 64, "tokT0"), cv(20736, 4096, 64, "tokT1")]
        GT = cv(24832, 512, 128, "GT")
        peb = cv(25344, 64, 32, "peb")
        peT = cv(25408, 32, 64, "peT")
        kcb = cv(25440, 64, 128, "kcb")
        kcTs = cv(25504, 128, 64, "kcTs")
        tCc = sm[1]; tSc = sm[2]; bias = sm[3]
        srcs = [(I["cmp_k_w1"], I["cmp_k_w2"], kcT_d, kc_d), (I["cmp_v_w1"], I["cmp_v_w2"], vcT_d, vc_d)]
        P.dma(peb, peb[:, :], I["cmp_pos_emb"], I["cmp_pos_emb"][:, :], q="pool")
        tr(pbf, pbf[0:64, 0:32], peb, peb[:, :]) if False else P.op(
            "pe", lambda e: e.transpose(out=pbf[0:64, 0:32], in_=peb[:, :], identity=ident[0:32, 0:32]),
            r=[peb, ident], w=[pbf])
        P.op("act", lambda e: e.copy(out=peT[:, :], in_=pbf[0:64, 0:32]), r=[pbf], w=[peT])
        P.op("pool", lambda e: e.memset(GT[:, :], 0.0), w=[GT])
        GT3 = GT[:, :].rearrange("p (m n) -> p m n", m=2)
        for kv in range(2):
            w1_d, w2_d, tT_d, o_d = srcs[kv]
            w1v = w1[kv][:, :].rearrange("p (l j) -> p l j", l=32)
            P.dma(w1[kv], w1v, w1_d, w1_d.ap.rearrange("(l d) j -> d l j", d=64), q="pool")
            w2v = w2[kv][:, :].rearrange("p (m o) -> p m o", m=2)
            P.dma(w2[kv], w2v, w2_d, w2_d.ap.rearrange("(m p) o -> p m o", p=128), q="pool")
            pbias = pb[6]
            for m in range(2):
                for l in range(32):
                    mm(pbias, pbias[:, m:m + 1], w1[kv], w1v[:, l, m * 128:(m + 1) * 128], peT, peT[:, l:l + 1], l == 0, l == 31)
            P.op("act", lambda e, pbias=pbias: e.copy(out=bias[:, 0:2], in_=pbias[:, 0:2]), r=[pbias], w=[bias])
            for g in range(2):
                tk = tokT[g]
                P.dma(tk, tk[:, :], tT_d, tT_d.ap[g * 64:(g + 1) * 64, :])
                tok3 = tk[:, :].rearrange("p (n r) -> p n r", r=16)
                for m in range(2):
                    ph = pb[m]
                    for l in range(32):
                        a, rr_ = l // 16, l % 16
                        mm(ph, ph[:, 0:255], w1[kv], w1v[:, l, m * 128:(m + 1) * 128], tk, tok3[:, a:a + 255, rr_], l == 0, l == 31)
                    hbx, x2, z = wk32[0], wk32[1], wk32[2]
                    P.op("act", lambda e, ph=ph, m=m: e.activation(out=hbx[:, 0:255], in_=ph[:, 0:255], func=AF.Identity,
                                                                   bias=bias[:, m:m + 1]), r=[ph, bias], w=[hbx])
                    P.op("dve", lambda e: e.tensor_tensor(out=x2[:, 0:255], in0=hbx[:, 0:255], in1=hbx[:, 0:255], op=ALU.mult),
                         r=[hbx], w=[x2])
                    P.op("dve", lambda e: e.tensor_scalar(out=x2[:, 0:255], in0=x2[:, 0:255], scalar1=0.044715, scalar2=1.0,
                                                          op0=ALU.mult, op1=ALU.add), r=[x2], w=[x2])
                    P.op("dve", lambda e: e.tensor_tensor(out=z[:, 0:255], in0=x2[:, 0:255], in1=hbx[:, 0:255], op=ALU.mult),
                         r=[x2, hbx], w=[z])
                    P.op("act", lambda e: e.activation(out=z[:, 0:255], in_=z[:, 0:255], func=AF.Sigmoid, scale=1.5957691216),
                         r=[z], w=[z])
                    P.op("dve", lambda e, m=m: e.tensor_tensor(out=GT3[:, m, 0:255], in0=z[:, 0:255], in1=hbx[:, 0:255],
                                                               op=ALU.mult), r=[z, hbx], w=[GT])
                for j in range(2):
                    po = pb[2 + j]
                    for m in range(2):
                        mm(po, po[:, 0:64], GT, GT3[:, m, j * 128:(j + 1) * 128], w2[kv], w2v[:, m, :], m == 0, m == 1)
                    if kv == 0:
                        P.dma(tCc, tCc[:, 0:64], C["ropeCc"], C["ropeCc"][j * 128:(j + 1) * 128, :])
                        P.dma(tSc, tSc[:, 0:16], C["ropeSc"], C["ropeSc"][j * 128:(j + 1) * 128, :])
                        rope(po, po[:, 0:64], 1, 16, tCc, tCc[:, 0:64], tSc, tSc[:, 0:16], kcb, kcb[:, :], wk32[0], wk32[1])
                        P.op("pe", lambda e: e.transpose(out=pbf[0:64, 0:128], in_=kcb[:, :], identity=ident[:]),
                             r=[kcb, ident], w=[pbf])
                        P.op("act", lambda e: e.copy(out=kcTs[:, :], in_=pbf[0:64, 0:128]), r=[pbf], w=[kcTs])
                        P.dma(kcT2_d, kcT2_d.ap[g, :, j * 128:(j + 1) * 128], kcTs, kcTs[:, :], q="pool")
                    else:
                        P.op("act", lambda e, po=po: e.copy(out=kcb[:, :], in_=po[:, 0:64]), r=[po], w=[kcb])
                    P.dma(o_d, o_d.ap[g, j * 128:(j + 1) * 128, :], kcb, kcb[:, :], q="pool")
        uncarve(slot, subs)

    def nsa_phase(slot):
        subs = []

        def cv(off, n, parts=128, name=""):
            t = carve(slot, slot.ap[0:parts, off:off + n], name)
            subs.append(t)
            return t
        KsT = cv(0, 4096, 128, "KsT"); KwT = cv(4096, 4096, 128, "KwT")
        Vs = cv(8192, 2080, 128, "Vs"); Vw = cv(10272, 2080, 128, "Vw")
        KcT = cv(12352, 256, 128, "KcT"); Vc = cv(12608, 130, 128, "Vc")
        C2S = cv(12738, 128, 128, "C2S")
        mfull = [cv(12866, 4096, 128, "mfull0"), cv(24194, 4096, 128, "mfull1")]
        tri = cv(28290, 128, 128, "tri")
        selTs = cv(28418, 128, 64, "selTs")
        QTc = [cv(16962 + i * 1024, 512, 128, f"QTc{i}") for i in range(2)]
        Et = [cv(19010 + i * 1024, 1024, 128, f"Et{i}") for i in range(3)]
        selb = cv(23106, 64, 128, "selb")
        ab = cv(23170, 512, 128, "ab")
        aTs = cv(23682, 512, 128, "aTs")
        TK = tCa; TA = tCr
        P.dma(TK, TK[:, 0:128], C["TK"], C["TK"][:, :])
        P.dma(TA, TA[:, 0:128], C["TA"], C["TA"][:, :])
        P.dma(C2S, C2S[:, :].rearrange("p (j s) -> p j s", j=2), C["C2S"], C["C2S"].ap.rearrange("(j p) s -> p j s", p=128), q="pool")
        P.op("pool", lambda e: e.memset(tri[:, :], 1.0), w=[tri])
        P.op("pool", lambda e: e.affine_select(out=tri[:, :], in_=tri[:, :], pattern=[[1, 128]], compare_op=ALU.is_ge,
                                               fill=0.0, base=0, channel_multiplier=-1), r=[tri], w=[tri])
        Vs3 = Vs[:, :].rearrange("p (j d) -> p j d", d=65)
        Vw3 = Vw[:, :].rearrange("p (j d) -> p j d", d=65)
        Vc3 = Vc[:, :].rearrange("p (j d) -> p j d", d=65)
        C2S3 = C2S[:, :].rearrange("p (j s) -> p j s", j=2)
        SB = [(pb[0], pb[1]), (pb[2], pb[3])]
        OA, OB, OC = pb[4], pb[5], pb[6]
        OA3 = OA[:, 0:260].rearrange("p (h d) -> p h d", d=65)
        OB3 = OB[:, 0:260].rearrange("p (h d) -> p h d", d=65)
        acc = xin[0]; tmpo = xin[1]
        acc3 = acc[:, 0:512].rearrange("p (h d) -> p h d", d=64)
        tmp3 = tmpo[:, 0:512].rearrange("p (h d) -> p h d", d=64)
        den = sm[2]; coef = sm[3]; imp = sm[4]; imp2 = sm[5]; m8 = sm[6]; imp3 = sm[7]
        sidx = {"s": 0, "e": 0}

        def scores(lhs_t, lhs_ap, qt, mask=None):
            b = SB[sidx["s"] % 2]; sidx["s"] += 1
            for hh in range(2):
                mm(b[hh], b[hh][:], lhs_t, lhs_ap[hh * 64:(hh + 1) * 64], qt, qt[hh * 64:(hh + 1) * 64, :], True, True)
            et = Et[sidx["e"] % 3]; sidx["e"] += 1
            for hh in range(2):
                P.op("act", lambda e, hh=hh, et=et, b=b: e.activation(out=et[:, hh * 512:(hh + 1) * 512], in_=b[hh][:],
                                                                      func=AF.Exp, scale=0.125), r=[b[hh]], w=[et])
            if mask is not None:
                m_t, m_ap = mask
                P.op("dve", lambda e: e.tensor_tensor(out=et[:, :].rearrange("p (h q) -> p h q", h=8),
                                                      in0=et[:, :].rearrange("p (h q) -> p h q", h=8),
                                                      in1=m_ap.unsqueeze(1).broadcast_to([128, 8, 128]), op=ALU.mult),
                     r=[et, m_t], w=[et])
            return et

        def warm(n):
            b = SB[sidx["s"] % 2][0]
            for _w in range(n):
                mm(b, b[:, :], ident, ident[:, :], hb[0], hb[0][:, 0:512], True, True)

        def amask(et, cm, qstep, base):
            P.op("pool", lambda e: e.affine_select(out=et[:, :], in_=et[:, :], pattern=[[0, 8], [qstep, 128]],
                                                   compare_op=ALU.is_ge, fill=0.0, base=base, channel_multiplier=cm),
                 r=[et], w=[et])

        def pv(et, v_t, v_ap, first, last):
            for h in range(8):
                o_t = OA if h < 4 else OB
                o3 = OA3 if h < 4 else OB3
                mm(o_t, o3[:, h % 4, :], et, et[:, h * 128:(h + 1) * 128], v_t, v_ap, first and (h % 4 == 0), last, sgc=True)

        pend = []

        def flush(keep=1):
            while len(pend) > keep:
                work, post = pend.pop(0)
                work()
                if post is not None:
                    post()

        def finish_branch(bidx, first, gat):
            P.op("dve", lambda e: e.tensor_scalar(out=den[:, 0:4], in0=OA3[:, :, 64], scalar1=1e-20, scalar2=None, op0=ALU.max),
                 r=[OA], w=[den])
            P.op("dve", lambda e: e.tensor_scalar(out=den[:, 4:8], in0=OB3[:, :, 64], scalar1=1e-20, scalar2=None, op0=ALU.max),
                 r=[OB], w=[den])
            P.op("dve", lambda e: e.reciprocal(out=den[:, 8:16], in_=den[:, 0:8]), r=[den], w=[den])
            g3 = gat[:, 0:24].rearrange("p (h b) -> p h b", b=3)
            P.op("dve", lambda e: e.tensor_tensor(out=coef[:, 0:8], in0=den[:, 8:16], in1=g3[:, :, bidx], op=ALU.mult),
                 r=[den, gat], w=[coef])
            dst, d3 = (acc, acc3) if first else (tmpo, tmp3)
            P.op("dve", lambda e: e.tensor_tensor(out=d3[:, 0:4, :], in0=OA3[:, :, 0:64],
                                                  in1=coef[:, 0:4].unsqueeze(2).broadcast_to([128, 4, 64]), op=ALU.mult),
                 r=[OA, coef], w=[dst])
            P.op("dve", lambda e: e.tensor_tensor(out=d3[:, 4:8, :], in0=OB3[:, :, 0:64],
                                                  in1=coef[:, 4:8].unsqueeze(2).broadcast_to([128, 4, 64]), op=ALU.mult),
                 r=[OB, coef], w=[dst])
            if not first:
                P.op("pool", lambda e: e.tensor_tensor(out=acc[:, 0:512], in0=acc[:, 0:512], in1=tmpo[:, 0:512], op=ALU.add),
                     r=[acc, tmpo], w=[acc])

        for g in range(2):
            for hh in range(2):
                P.dma(KsT, KsT[hh * 64:(hh + 1) * 64, :], ksT_d, ksT_d.ap[g * 64:(g + 1) * 64, :])
                P.dma(KwT, KwT[hh * 64:(hh + 1) * 64, :], kwT_d, kwT_d.ap[g * 64:(g + 1) * 64, :])
            P.dma(Vs, Vs3[:, :, 0:64], vs_d, vs_d.ap[:, g * 64:(g + 1) * 64].rearrange("(j p) d -> p j d", p=128))
            P.dma(Vw, Vw3[:, :, 0:64], vw_d, vw_d.ap[:, g * 64:(g + 1) * 64].rearrange("(j p) d -> p j d", p=128))
            P.op("pool", lambda e: e.memset(Vs3[:, :, 64:65], 1.0), w=[Vs])
            P.op("pool", lambda e: e.memset(Vw3[:, :, 64:65], 1.0), w=[Vw])
            for hh in range(2):
                P.dma(KcT, KcT[hh * 64:(hh + 1) * 64, :], kcT2_d, kcT2_d.ap[g])
            P.dma(Vc, Vc3[:, :, 0:64], vc_d, vc_d.ap[g].rearrange("(j p) d -> p j d", p=128))
            P.op("pool", lambda e: e.memset(Vc3[:, :, 64:65], 1.0), w=[Vc])
            gats = [sm[1], sm[0]]
            for c in (range(NT) if lim is None else lim_c):
                qt = QTc[c % 2]
                gat = gats[c % 2]
                warm(int(_os.environ.get("NWARM0", "8")))
                for hh in range(2):
                    P.dma(qt, qt[hh * 64:(hh + 1) * 64, :].rearrange("d (h t) -> d h t", h=4), QT_t[g][c],
                          QT_t[g][c][hh * 256:(hh + 1) * 256, :].rearrange("(h d) t -> d h t", d=64))
                P.dma(gat, gat[:, 0:24], ga_t[c], ga_t[c][:, g * 24:(g + 1) * 24])

                def post_cmp(c=c, gat=gat):
                    finish_branch(0, True, gat)
                    for h in range(8):
                        if h == 0:
                            P.op("dve", lambda e: e.tensor_scalar(out=imp[:, 0:64], in0=OC[:, 0:64], scalar1=den[:, 8:9],
                                                                  scalar2=None, op0=ALU.mult), r=[OC, den], w=[imp])
                        else:
                            P.op("dve", lambda e, h=h: e.scalar_tensor_tensor(
                                out=imp[:, 0:64], in0=OC[:, h * 64:(h + 1) * 64], scalar=den[:, 8 + h:9 + h], in1=imp[:, 0:64],
                                op0=ALU.mult, op1=ALU.add), r=[OC, den, imp], w=[imp])
                    off = 64 - 2 * c
                    P.op("dve", lambda e: e.tensor_tensor(out=imp2[:, 0:64], in0=imp[:, 0:64], in1=TK[:, off:off + 64], op=ALU.mult),
                         r=[imp, TK], w=[imp2])
                    P.op("dve", lambda e: e.tensor_tensor(out=imp2[:, 0:64], in0=imp2[:, 0:64], in1=TA[:, off:off + 64], op=ALU.add),
                         r=[imp2, TA], w=[imp2])
                    P.op("dve", lambda e: e.memset(imp2[:, 0:1], 1.0e4), r=[imp2], w=[imp2])
                    P.op("dve", lambda e: e.max(out=m8[:, 0:8], in_=imp2[:, 0:64]), r=[imp2], w=[m8])
                    P.op("dve", lambda e: e.match_replace(out=imp3[:, 0:64], in_to_replace=m8[:, 0:8], in_values=imp2[:, 0:64],
                                                          imm_value=-3.0e4), r=[imp2, m8], w=[imp3])
                    P.op("dve", lambda e: e.max(out=m8[:, 8:16], in_=imp3[:, 0:64]), r=[imp3, m8], w=[m8])
                    P.op("dve", lambda e: e.tensor_scalar(out=selb[:, :].rearrange("p (b j) -> p j b", b=2),
                                                          in0=imp2[:, 0:64].rearrange("p (j b) -> p j b", b=2),
                                                          scalar1=m8[:, 15:16], scalar2=None, op0=ALU.is_ge),
                         r=[imp2, m8], w=[selb])
                    P.op("pe", lambda e: e.transpose(out=pbf[0:64, 0:128], in_=selb[:, :], identity=ident[:]),
                         r=[selb, ident], w=[pbf])
                    P.op("dve", lambda e: e.tensor_copy(out=selTs[:, :], in_=pbf[0:64, 0:128]), r=[pbf], w=[selTs])
                    sd = selT_t[c % 2]
                    P.dma(sd, sd[:, :], selTs, selTs[:, :])
                    mf = mfull[c % 2]
                    for b2 in range(2):
                        src = sd[b2 * 32:b2 * 32 + c + 1, :].rearrange("(o r) q -> o (r q)", o=1).partition_broadcast(64)
                        P.dma(mf, mf[b2 * 64:(b2 + 1) * 64, 0:(c + 1) * 128], sd, src)
                    P.op("pool", lambda e: e.tensor_tensor(out=mf[:, c * 128:(c + 1) * 128], in0=mf[:, c * 128:(c + 1) * 128],
                                                           in1=tri[:, :], op=ALU.mult), r=[mf, tri], w=[mf])

                def post_win(gat=gat):
                    finish_branch(2, False, gat)

                def post_sel(c=c, gat=gat):
                    finish_branch(1, False, gat)
                    P.op("act", lambda e: e.copy(out=ab[:, :], in_=acc[:, 0:512]), r=[acc], w=[ab])
                    transpose8(ab, aTs, aTs[:, :].rearrange("p (k t) -> p k t", k=4), nblk=4, eng="act")
                    dst = aT_t[c]
                    P.dma(dst, dst[g * 512:(g + 1) * 512, :].rearrange("(k r) t -> r k t", r=128), aTs,
                          aTs[:, :].rearrange("p (k t) -> p k t", k=4), q="pool")

                njt = 1 if c <= 15 else 2
                for j in range(njt):
                    et = scores(KcT, KcT[:, j * 128:(j + 1) * 128], qt)
                    amask(et, -16, 1, -16 * (128 * j - 8 * c) - 31)
                    flush()

                    def work(et=et, j=j, njt=njt):
                        pv(et, Vc, Vc3[:, j, :], j == 0, j == njt - 1)
                        for h in range(8):
                            mm(OC, OC[:, h * 64:(h + 1) * 64], et, et[:, h * 128:(h + 1) * 128], C2S, C2S3[:, j, :],
                               j == 0 and h == 0, j == njt - 1, sgc=True)
                    pend.append((work, post_cmp if j == njt - 1 else None))
                j0 = max(0, c - 4)
                for j in range(j0, c + 1):
                    et = scores(KwT, KwT[:, j * 128:(j + 1) * 128], qt)
                    if j == c:
                        amask(et, -1, 1, 0)
                    if j == c - 4:
                        amask(et, 1, -1, -1)
                    flush()
                    pend.append((lambda et=et, j=j, j0=j0, c=c: pv(et, Vw, Vw3[:, j, :], j == j0, j == c),
                                 post_win if j == c else None))
                mf = mfull[c % 2]
                for j in range(c + 1):
                    if j == 0:
                        flush(0)
                        warm(int(_os.environ.get("NWARM1", "16")))
                    et = scores(KsT, KsT[:, j * 128:(j + 1) * 128], qt, mask=(mf, mf[:, j * 128:(j + 1) * 128]))
                    flush()
                    pend.append((lambda et=et, j=j, c=c: pv(et, Vs, Vs3[:, j, :], j == 0, j == c),
                                 post_sel if j == c else None))
            flush(0)
        uncarve(slot, subs)

    def ret_phase():
        subs = []
        bufs = []
        for i in range(2):
            q_ = carve(big, big.ap[0:64, i * 2560:i * 2560 + 1024], f"QrTc{i}")
            k_ = carve(big, big.ap[0:64, i * 2560 + 1024:i * 2560 + 2048], f"KrTc{i}")
            z_ = carve(big, big.ap[:, i * 2560 + 2048:i * 2560 + 2560], f"kzc{i}")
            subs += [q_, k_, z_]
            bufs.append((q_, k_, z_))
        dec = [tCr, tSr]
        gct = [tCa, tZ]
        P.dma(dec[0], dec[0][:, :], C["decayT"], C["decayT"][:, 0:512])
        P.dma(dec[1], dec[1][:, :], C["decayT"], C["decayT"][:, 512:1024])
        P.dma(gct[0], gct[0][0:64, :], C["GC"], C["GC"][:, 0:512])
        P.dma(gct[1], gct[1][0:64, :], C["GC"], C["GC"][:, 512:1024])
        xit = sm[1]
        P.dma(xit, xit[:, 0:8], C["XI"], C["XI"][:, :])
        load_gain(gain2, I["ret_gn_gain"])
        Rst = xin[0]; osbs = [xin[1], gain]; sgr = wk32[0]; sq = wk32[1]; tmp = wk32[2]
        Rbf = wk16[0]; ATs = wk16[1]; rob = wk16[2]
        stt = sm[2]
        P.op("pool", lambda e: e.memset(Rst[0:64, :], 0.0), w=[Rst])
        P.op("pool", lambda e: e.memset(Rbf[0:64, :], 0.0), w=[Rbf])
        sq3 = sq[:, :].rearrange("p (h e) -> p h e", h=8)
        tmp3 = tmp[:, :].rearrange("p (h e) -> p h e", h=8)

        def stage1(n):
            q_, k_, z_ = bufs[n % 2]
            vt = hb[n % 2]
            osb = osbs[n % 2]
            P.dma(q_, q_[:, :].rearrange("d (h t) -> d h t", h=8), QrT_t[n], QrT_t[n][:].rearrange("(h d) t -> d h t", d=64))
            P.dma(k_, k_[:, :].rearrange("d (h t) -> d h t", h=8), KrT_t[n], KrT_t[n][:].rearrange("(h d) t -> d h t", d=64))
            P.dma(z_, z_[:, :], kz_t[n], kz_t[n][:, :])
            P.dma(vt, vt[:, :], vr_t[n], vr_t[n][:, :])
            for h in range(8):
                b = pb[h // 4]
                mm(b, b[:, (h % 4) * 128:(h % 4 + 1) * 128], k_, k_[:, h * 128:(h + 1) * 128], q_, q_[:, h * 128:(h + 1) * 128], True, True)
            if n > 0:
                for h in range(8):
                    b = pb[4 + h // 4]
                    mm(b, b[:, (h % 4) * 128:(h % 4 + 1) * 128], q_, q_[:, h * 128:(h + 1) * 128], Rbf, Rbf[0:64, h * 128:(h + 1) * 128], True, True)
            for hh in range(2):
                P.op("dve", lambda e, hh=hh: e.tensor_tensor(out=ATs[:, hh * 512:(hh + 1) * 512], in0=pb[hh][:], in1=dec[hh][:, :],
                                                             op=ALU.mult), r=[pb[hh], dec[hh]], w=[ATs])
            for h in range(8):
                b = pb[2 + h // 4]
                mm(b, b[:, (h % 4) * 128:(h % 4 + 1) * 128], ATs, ATs[:, h * 128:(h + 1) * 128], vt, vt[:, h * 128:(h + 1) * 128], True, True)
            for hh in range(2):
                b = pb[6]
                for h4 in range(4):
                    h = hh * 4 + h4
                    mm(b, b[0:64, h4 * 128:(h4 + 1) * 128], z_, z_[:, h * 64:(h + 1) * 64], vt, vt[:, h * 128:(h + 1) * 128], True, True)
                P.op("pool", lambda e, hh=hh: e.tensor_tensor(out=Rst[0:64, hh * 512:(hh + 1) * 512], in0=Rst[0:64, hh * 512:(hh + 1) * 512],
                                                              in1=gct[hh][0:64, :], op=ALU.mult), r=[Rst, gct[hh]], w=[Rst])
                P.op("dve", lambda e, hh=hh, b=b: e.tensor_tensor(out=Rst[0:64, hh * 512:(hh + 1) * 512], in0=b[0:64, :],
                                                                  in1=Rst[0:64, hh * 512:(hh + 1) * 512], op=ALU.add), r=[b, Rst], w=[Rst])
            P.op("act", lambda e: e.copy(out=Rbf[0:64, :], in_=Rst[0:64, :]), r=[Rst], w=[Rbf])
            for hh in range(2):
                P.op("act", lambda e, hh=hh: e.copy(out=osb[:, hh * 512:(hh + 1) * 512], in_=pb[2 + hh][:]), r=[pb[2 + hh]], w=[osb])
            if n > 0:
                for hh in range(2):
                    P.op("dve", lambda e, hh=hh: e.tensor_tensor(
                        out=tmp3[:, hh * 4:(hh + 1) * 4, :], in0=pb[4 + hh][:].rearrange("p (h e) -> p h e", h=4),
                        in1=xit[:, hh * 4:(hh + 1) * 4].unsqueeze(2).broadcast_to([128, 4, 128]), op=ALU.mult),
                        r=[pb[4 + hh], xit], w=[tmp])
                P.op("pool", lambda e: e.tensor_tensor(out=osb[:, :], in0=osb[:, :], in1=tmp[:, :], op=ALU.add), r=[osb, tmp], w=[osb])

        def stage2(n):
            osb = osbs[n % 2]
            osb3 = osb[:, :].rearrange("p (h e) -> p h e", h=8)
            P.dma(sgr, sgr[:, :], sgr_t[n], sgr_t[n][:, :])
            P.op("dve", lambda e: e.tensor_reduce(out=stt[:, 0:8], in_=osb3, axis=AX.X, op=ALU.add), r=[osb], w=[stt])
            P.op("act", lambda e: e.activation(out=sq[:, :], in_=osb[:, :], func=AF.Square), r=[osb], w=[sq])
            P.op("dve", lambda e: e.tensor_reduce(out=stt[:, 8:16], in_=sq3, axis=AX.X, op=ALU.add), r=[sq], w=[stt])
            P.op("dve", lambda e: e.tensor_scalar(out=stt[:, 0:8], in0=stt[:, 0:8], scalar1=1.0 / 128, scalar2=None, op0=ALU.mult),
                 r=[stt], w=[stt])
            P.op("dve", lambda e: e.tensor_tensor(out=stt[:, 16:24], in0=stt[:, 0:8], in1=stt[:, 0:8], op=ALU.mult), r=[stt], w=[stt])
            P.op("dve", lambda e: e.scalar_tensor_tensor(out=stt[:, 24:32], in0=stt[:, 8:16], scalar=1.0 / 128, in1=stt[:, 16:24],
                                                         op0=ALU.mult, op1=ALU.subtract), r=[stt], w=[stt])
            P.op("dve", lambda e: e.tensor_scalar(out=stt[:, 24:32], in0=stt[:, 24:32], scalar1=1e-5, scalar2=None, op0=ALU.add),
                 r=[stt], w=[stt])
            P.op("pool", lambda e: e.tensor_tensor(out=stt[:, 40:48], in0=stt[:, 24:32], in1=negh[:, 0:8], op=ALU.pow),
                 r=[stt, negh], w=[stt])
            P.op("dve", lambda e: e.scalar_tensor_tensor(out=stt[:, 48:56], in0=stt[:, 0:8], scalar=-1.0, in1=stt[:, 40:48],
                                                         op0=ALU.mult, op1=ALU.mult), r=[stt], w=[stt])
            for h in range(8):
                P.op("act", lambda e, h=h: e.activation(out=osb3[:, h, :], in_=osb3[:, h, :], func=AF.Identity,
                                                        scale=stt[:, 40 + h:41 + h], bias=stt[:, 48 + h:49 + h]),
                     r=[osb, stt], w=[osb])
            P.op("dve", lambda e: e.tensor_tensor(out=osb[:, :], in0=osb[:, :], in1=gain2[:, :], op=ALU.mult), r=[osb, gain2], w=[osb])
            P.op("pool", lambda e: e.tensor_tensor(out=rob[:, :], in0=osb[:, :], in1=sgr[:, :], op=ALU.mult), r=[osb, sgr], w=[rob])
            transpose8(rob, hT, hT[:, :, 0:128], eng="act")
            dst = rT_t[n]
            P.dma(dst, dst[:].rearrange("(k r) t -> r k t", r=128), hT, hT[:, :, 0:128], q="pool")

        nn = NT if lim is None else lim
        for n in range(nn):
            stage1(n)
            if n > 0:
                stage2(n - 1)
        stage2(nn - 1)
        uncarve(big, subs)

    def merge_phase(slot):
        Wn = load_w(slot, 0, I["w_branch_nsa"], I["w_branch_nsa"].ap.rearrange("(k p) f -> p k f", p=128), [128, 8, 1024])
        Wr = load_w(slot, 8192, I["w_branch_ret"], I["w_branch_ret"].ap.rearrange("(k p) f -> p k f", p=128), [128, 8, 1024])
        Wo = load_w(slot, 16384, I["w_out"], I["w_out"].ap.rearrange("(k p) f -> p k f", p=128), [128, 8, 1024])
        yield
        subs = []
        bufs = []
        for i in range(2):
            a_ = carve(big, big.ap[:, i * 2048:i * 2048 + 1024], f"aTt{i}")
            r_ = carve(big, big.ap[:, i * 2048 + 1024:i * 2048 + 2048], f"rTt{i}")
            subs += [a_, r_]
            bufs.append((a_, r_))
        for ti in range(NT if lim is None else lim):
            a_, r_ = bufs[ti % 2]
            a3 = a_[:, :].rearrange("p (k t) -> p k t", k=8)
            r3 = r_[:, :].rearrange("p (k t) -> p k t", k=8)
            P.dma(a_, a3, aT_t[ti], aT_t[ti][:].rearrange("(k r) t -> r k t", r=128))
            P.dma(r_, r3, rT_t[ti], rT_t[ti][:].rearrange("(k r) t -> r k t", r=128))
            ga, gr = wk32[0], wk32[1]
            P.dma(ga, ga[:, :], gm_t[ti], gm_t[ti][:, 0:1024])
            P.dma(gr, gr[:, :], gm_t[ti], gm_t[ti][:, 1024:2048])
            x1s = wk32[2]
            P.dma(x1s, x1s[:, :], x1_t[ti], x1_t[ti][:, :])
            for half in range(2):
                for k in range(8):
                    mm(pb[half], pb[half][:], a_, a3[:, k, :], slot, Wn[:, k, half * 512:(half + 1) * 512], k == 0, k == 7)
                for k in range(8):
                    mm(pb[2 + half], pb[2 + half][:], r_, r3[:, k, :], slot, Wr[:, k, half * 512:(half + 1) * 512], k == 0, k == 7)
            m1, m2 = xin[0], xin[1]
            mb = hb[ti % 2]
            for half in range(2):
                cs = slice(half * 512, (half + 1) * 512)
                P.op("dve", lambda e, half=half, cs=cs: e.tensor_tensor(out=m1[:, cs], in0=pb[half][:], in1=ga[:, cs], op=ALU.mult),
                     r=[pb[half], ga], w=[m1])
                P.op("dve", lambda e, half=half, cs=cs: e.tensor_tensor(out=m2[:, cs], in0=pb[2 + half][:], in1=gr[:, cs], op=ALU.mult),
                     r=[pb[2 + half], gr], w=[m2])
            P.op("pool", lambda e, mb=mb: e.tensor_tensor(out=mb[:, :], in0=m1[:, :], in1=m2[:, :], op=ALU.add), r=[m1, m2], w=[mb])
            transpose8(mb, hT, hT[:, :, 0:128], eng="act")
            for half in range(2):
                po = pb[4 + half]
                for k in range(8):
                    mm(po, po[:], hT, hT[:, k, 0:128], slot, Wo[:, k, half * 512:(half + 1) * 512], k == 0, k == 7)
                P.op("dve", lambda e, po=po, half=half: e.tensor_tensor(out=x1s[:, half * 512:(half + 1) * 512], in0=po[:],
                                                                       in1=x1s[:, half * 512:(half + 1) * 512], op=ALU.add),
                     r=[po, x1s], w=[x1s])
            P.dma(x2_t[ti], x2_t[ti][:, :], x1s, x1s[:, :], q="pool")
        uncarve(big, subs)

    def begin(gen):
        next(gen)
        return gen

    def finish(gen):
        for _ in gen:
            pass

    allp = set(phases) == {"ffn1", "proj", "cmp", "nsa", "ret", "merge", "ffn2"} and not _os.environ.get("SKIPP1")
    out_t = [T(out_d.ap[i * 128:(i + 1) * 128], f"out{i}") for i in range(NT)]
    if allp:
        f1a = begin(ffn_pass(1, 0, wslot[0], x_t, x1_t, I["ffn1_norm"]))
        f1b = begin(ffn_pass(1, 1, wslot[1], x1_t, x1_t, I["ffn1_norm"]))
        finish(f1a)
        p0 = begin(proj_pass0(wslot[0]))
        finish(f1b)
        p1 = begin(proj_pass1(wslot[1]))
        finish(p0)
        finish(p1)
        cmp_phase(wslot[1])
        mg = begin(merge_phase(wslot[1]))
        nsa_phase(wslot[0])
        f2a = begin(ffn_pass(2, 0, wslot[0], x2_t, x2_t, I["ffn2_norm"]))
        ret_phase()
        finish(mg)
        f2b = begin(ffn_pass(2, 1, wslot[1], x2_t, out_t, I["ffn2_norm"], final=True))
        finish(f2a)
        finish(f2b)
    else:
        if "ffn1" in phases:
            finish(ffn_pass(1, 0, wslot[0], x_t, x1_t, I["ffn1_norm"]))
            finish(ffn_pass(1, 1, wslot[1], x1_t, x1_t, I["ffn1_norm"]))
        if "proj" in phases:
            finish(proj_pass0(wslot[0]))
            if not _os.environ.get("SKIPP1"):
                finish(proj_pass1(wslot[1]))
        if "cmp" in phases:
            cmp_phase(wslot[1])
        if "nsa" in phases:
            nsa_phase(wslot[0])
        if "ret" in phases:
            ret_phase()
        if "merge" in phases:
            finish(merge_phase(wslot[1]))
        if "ffn2" in phases:
            finish(ffn_pass(2, 0, wslot[0], x2_t, x2_t, I["ffn2_norm"]))
            finish(ffn_pass(2, 1, wslot[1], x2_t, out_t, I["ffn2_norm"], final=True))
    P.emit()
    return nc, P


_CACHE = {}


def kernel(**inputs):
    n_cores = 8
    if "nc" not in _CACHE:
        _CACHE["nc"] = build()[0]
        _CACHE["consts"] = {k: np.ascontiguousarray(v.reshape(CONST_SHAPES[k]).astype(np.float32))
                            for k, v in _consts().items()}
    nc = _CACHE["nc"]
    cst = _CACHE["consts"]
    shared = {}
    for k, shp in IN_SHAPES.items():
        if k == "x":
            continue
        shared[k] = np.ascontiguousarray(np.asarray(inputs[k], dtype=np.float32).reshape(shp))
    x = np.asarray(inputs["x"], dtype=np.float32)
    in_maps = []
    for c in range(n_cores):
        m = dict(shared)
        m.update(cst)
        m["x"] = np.ascontiguousarray(x[c % 4])
        in_maps.append(m)
    res = run_bass_kernel_spmd(nc, in_maps, core_ids=list(range(n_cores)))
    out = np.stack([np.asarray(res.results[b]["out"], dtype=np.float32) for b in range(4)], axis=0)
    return out
```

```python
import contextlib
import os as _os
import numpy as np
import concourse.bass as bass
import concourse.mybir as mybir
from concourse.bass_utils import run_bass_kernel_spmd

F32 = mybir.dt.float32
BF16 = mybir.dt.bfloat16
AF = mybir.ActivationFunctionType
ALU = mybir.AluOpType
AX = mybir.AxisListType

S = 4096
D = 1024
DFF = 2816
NT = S // 128
DIN = 6960


class T:
    __slots__ = ("ap", "name", "last_w", "readers", "excl")

    def __init__(self, ap, name="", excl=False):
        self.ap = ap
        self.name = name
        self.last_w = None
        self.readers = []
        self.excl = excl

    def __getitem__(self, k):
        return self.ap[k]


class Prog:
    ENGS = ("pe", "act", "dve", "pool", "sp")

    def __init__(self, nc, n_dma_sems=12, same_engine_sync=True):
        self.nc = nc
        self.ops = []
        self.es = contextlib.ExitStack()
        self.same_engine_sync = same_engine_sync
        self.n_dma_sems = n_dma_sems
        self.ncnt = 0

    def sb(self, shape, dt, name=None):
        self.ncnt += 1
        name = name or f"sb{self.ncnt}"
        t = self.es.enter_context(self.nc.sbuf_tensor(name, list(shape), dt))
        return T(t[:], name)

    def ps(self, shape, dt, name=None):
        self.ncnt += 1
        name = name or f"ps{self.ncnt}"
        t = self.es.enter_context(self.nc.psum_tensor(name, list(shape), dt))
        return T(t[:], name, excl=True)

    def dram(self, name, shape, dt, kind="Internal"):
        t = self.nc.dram_tensor(name, list(shape), dt, kind=kind)
        return T(t.ap(), name)

    def op(self, eng, fn, r=(), w=(), dma=False):
        idx = len(self.ops)
        deps = set()
        raw = set()
        for t in r:
            if t.last_w is not None:
                deps.add(t.last_w)
                raw.add(t.last_w)
            if t.excl:
                for rd in t.readers:
                    if self.ops[rd]["eng"] != eng:
                        deps.add(rd)
        for t in w:
            if t.last_w is not None:
                deps.add(t.last_w)
            deps.update(t.readers)
        for t in r:
            t.readers.append(idx)
        for t in w:
            t.last_w = idx
            t.readers = []
        deps.discard(idx)
        self.ops.append({"eng": eng, "fn": fn, "deps": deps, "dma": dma, "raw": raw, "tag": getattr(self, "tag", "")})
        return idx

    def dma(self, out_t, out_ap, in_t, in_ap, q="sp", **kw):
        return self.op(q, lambda e: e.dma_start(out=out_ap, in_=in_ap, **kw),
                       r=[in_t], w=[out_t], dma=True)

    def emit(self):
        nc = self.nc
        ops = self.ops
        nops = len(ops)
        dma_use = {}
        for i, o in enumerate(ops):
            if o["dma"]:
                dma_use.setdefault(o["eng"], []).append(i)
        dma_sem = {}
        for q, lst in dma_use.items():
            for k, i in enumerate(lst):
                slot = k % self.n_dma_sems
                val = 16 * (k // self.n_dma_sems + 1)
                dma_sem[i] = (q, slot, val)
                if k >= self.n_dma_sems:
                    ops[i]["deps"].add(lst[k - self.n_dma_sems])
        waited = {e: {} for e in self.ENGS}
        waited_dma = {e: set() for e in self.ENGS}
        plan = [None] * nops
        signaling = set()
        for i, o in enumerate(ops):
            e = o["eng"]
            need = {}
            need_dma = []
            for d in sorted(o["deps"]):
                po = ops[d]
                if po["dma"]:
                    if d not in waited_dma[e]:
                        waited_dma[e].add(d)
                        need_dma.append(d)
                    continue
                p = po["eng"]
                if p == e and (e == "pe" or not self.same_engine_sync or d not in o["raw"]):
                    continue
                if waited[e].get(p, -1) >= d:
                    continue
                need[p] = max(need.get(p, -1), d)
            for p, d in need.items():
                waited[e][p] = d
                signaling.add(d)
            plan[i] = (need, need_dma)
        semval = {}
        cnt = {e: 0 for e in self.ENGS}
        for i, o in enumerate(ops):
            if o["dma"]:
                continue
            if i in signaling:
                cnt[o["eng"]] += 1
                semval[i] = cnt[o["eng"]]
        self.stats = {e: sum(1 for o in ops if o["eng"] == e) for e in self.ENGS}
        self.stats["signals"] = dict(cnt)
        es = self.es
        sems = {e: es.enter_context(nc.semaphore(f"sem_{e}")) for e in self.ENGS}
        dsems = {q: [es.enter_context(nc.semaphore(f"dsem_{q}{k}")) for k in range(self.n_dma_sems)]
                 for q in dma_use}
        block = es.enter_context(nc.Block())

        def body(ename):
            def f(eng):
                for i, o in enumerate(ops):
                    if o["eng"] != ename:
                        continue
                    need, need_dma = plan[i]
                    for p, d in need.items():
                        eng.wait_ge(sems[p], semval[d])
                    for d in need_dma:
                        q, slot, val = dma_sem[d]
                        eng.wait_ge(dsems[q][slot], val)
                    ins = o["fn"](eng)
                    if o["dma"]:
                        q, slot, val = dma_sem[i]
                        ins.then_inc(dsems[q][slot], 16)
                    elif i in signaling:
                        ins.then_inc(sems[ename], 1)
                for q, lst in dma_use.items():
                    if q != ename:
                        continue
                    last = {}
                    for i in lst:
                        _, slot, val = dma_sem[i]
                        last[slot] = val
                    for slot, val in last.items():
                        eng.wait_ge(dsems[q][slot], val)
            return f

        block.tensor(body("pe"))
        block.scalar(body("act"))
        block.vector(body("dve"))
        block.gpsimd(body("pool"))
        block.sync(body("sp"))

    def close(self):
        self.es.close()


def _consts():
    c = {}
    pos = np.arange(S, dtype=np.float32)

    def rope_tabs(p, rot, theta, nrep):
        half = rot // 2
        fr = (np.float32(theta) ** (-(np.arange(half, dtype=np.float32) * np.float32(2.0) / np.float32(rot)))).astype(np.float32)
        ang = p.astype(np.float32)[:, None] * fr[None, :]
        cs, sn = np.cos(ang).astype(np.float32), np.sin(ang).astype(np.float32)
        C = np.ones((len(p), 64), np.float32)
        C[:, 0:half] = cs
        C[:, half:rot] = cs
        Sg = np.concatenate([-sn, sn], 1)
        return np.tile(C, (1, nrep)), np.tile(Sg, (1, nrep))

    c["ropeCa"], c["ropeSa"] = rope_tabs(pos, 16, 500000.0, 8)
    pc = np.arange(256, dtype=np.float32) * 16 + 31
    c["ropeCc"], c["ropeSc"] = rope_tabs(pc, 16, 500000.0, 1)
    c["ropeCr"], c["ropeSr"] = rope_tabs(pos, 64, 10000.0, 8)
    lg = np.log(1.0 - 2.0 ** (-5.0 - np.arange(8, dtype=np.float64)))
    i = np.arange(128, dtype=np.float64)
    diff = i[None, :] - i[:, None]
    dT = np.zeros((128, 8, 128), np.float64)
    for h in range(8):
        dT[:, h, :] = np.where(diff >= 0, np.exp(np.maximum(diff, 0) * lg[h]), 0.0) * 0.125
    c["decayT"] = dT.astype(np.float32).reshape(128, 1024)
    zeta = np.exp((127.0 - i)[None, :] * lg[:, None])
    c["ZT"] = np.repeat(zeta.T[:, :, None], 64, axis=2).reshape(128, 512).astype(np.float32)
    xi = np.exp((i + 1.0)[None, :] * lg[:, None]) * 0.125
    c["XI"] = np.ascontiguousarray(xi.T).astype(np.float32)
    gch = np.exp(128.0 * lg)
    c["GC"] = np.broadcast_to(gch[None, :, None], (64, 8, 128)).reshape(64, 1024).astype(np.float32)
    n = np.arange(256)
    cs_ = n * 16
    ss_ = np.arange(64) * 64
    ov = np.clip(np.minimum(cs_[:, None] + 32, ss_[None, :] + 64) - np.maximum(cs_[:, None], ss_[None, :]), 0, None)
    c2s = ov.astype(np.float32) / 32.0
    c2s[255] = 0.0
    c["C2S"] = c2s
    key = np.arange(S)
    c["EM"] = (key[None, :] // 64 == np.arange(64)[:, None]).astype(np.float32)
    sp = np.arange(128) - 64
    curp = (np.arange(128) >= 64).astype(np.int64)[:, None]
    forced = (sp[None, :] == curp) | (sp[None, :] == curp - 1)
    future = sp[None, :] > curp
    c["TK"] = (~(forced | future)).astype(np.float32)
    c["TA"] = np.where(forced, 1.0e4, np.where(future, -1.0e4, 0.0)).astype(np.float32)
    return c


CONST_SHAPES = {"ropeCa": [S, 512], "ropeSa": [S, 128], "ropeCc": [256, 64], "ropeSc": [256, 16],
                "ropeCr": [S, 512], "ropeSr": [S, 512], "decayT": [128, 1024], "ZT": [128, 512],
                "XI": [128, 8], "GC": [64, 1024], "C2S": [256, 64], "EM": [64, S],
                "TK": [128, 128], "TA": [128, 128]}

IN_SHAPES = {"x": [S, D], "ffn1_norm": [1, D], "ffn1_w_gate": [D, DFF], "ffn1_w_up": [D, DFF],
             "ffn1_w_down": [DFF, D], "mix_norm": [1, D], "w_in": [D, DIN], "cmp_pos_emb": [32, 64],
             "cmp_k_w1": [2048, 256], "cmp_k_w2": [256, 64], "cmp_v_w1": [2048, 256], "cmp_v_w2": [256, 64],
             "ret_gn_gain": [1, 1024], "w_branch_nsa": [D, D], "w_branch_ret": [D, D], "w_out": [D, D],
             "ffn2_norm": [1, D], "ffn2_w_gate": [D, DFF], "ffn2_w_up": [D, DFF], "ffn2_w_down": [DFF, D],
             "final_norm": [1, D]}

G_CHUNK = [float(np.exp(128.0 * np.log(1.0 - 2.0 ** (-5.0 - h)))) for h in range(8)]
NEGBIG = -2.0e4


def build(phases=("ffn1", "proj", "cmp", "nsa", "ret", "merge", "ffn2"), dbg=(), lim=None):
    nc = bass.Bass("TRN2", target_bir_lowering=False)
    P = Prog(nc)
    lim_c = [0, 5, 17] if lim else None
    I = {k: P.dram(k, v, F32, kind="ExternalInput") for k, v in IN_SHAPES.items()}
    C = {k: P.dram(k, v, F32, kind="ExternalInput") for k, v in CONST_SHAPES.items()}
    out_d = P.dram("out", [S, D], F32, kind="ExternalOutput")

    def scratch(name, shape, dt):
        return P.dram(name, shape, dt, kind=("ExternalOutput" if name in dbg else "Internal"))

    def tiles(t, n=NT, rows=128):
        return [T(t.ap[i * rows:(i + 1) * rows], f"{t.name}{i}") for i in range(n)]

    def ctiles(t, n=NT, cols=128):
        return [T(t.ap[..., i * cols:(i + 1) * cols], f"{t.name}{i}") for i in range(n)]

    x_t = tiles(I["x"])
    x1_d = scratch("x1_d", [S, D], F32); x1_t = tiles(x1_d)
    x2_d = scratch("x2_d", [S, D], F32); x2_t = tiles(x2_d)
    hT_d = scratch("hT_d", [8, 128, 8, 512], BF16)
    hT_t = [T(hT_d.ap[i], f"hT{i}") for i in range(8)]
    hmT_d = scratch("hmT_d", [NT, 128, 8, 128], BF16)
    hmT_t = [T(hmT_d.ap[i], f"hmT{i}") for i in range(NT)]
    QT_d = scratch("QT_d", [2, 512, S], BF16)
    QT_t = [ctiles(T(QT_d.ap[g], f"QT{g}_")) for g in range(2)]
    kcT_d = scratch("kcT_d", [128, S], BF16); kcT_t = ctiles(kcT_d)
    vcT_d = scratch("vcT_d", [128, S], BF16); vcT_t = ctiles(vcT_d)
    ksT_d = scratch("ksT_d", [128, S], BF16); ksT_t = ctiles(ksT_d)
    kwT_d = scratch("kwT_d", [128, S], BF16); kwT_t = ctiles(kwT_d)
    vs_d = scratch("vs_d", [S, 128], BF16); vs_t = tiles(vs_d)
    vw_d = scratch("vw_d", [S, 128], BF16); vw_t = tiles(vw_d)
    ga_d = scratch("ga_d", [S, 48], F32); ga_t = tiles(ga_d)
    QrT_d = scratch("QrT_d", [512, S], BF16); QrT_t = ctiles(QrT_d)
    KrT_d = scratch("KrT_d", [512, S], BF16); KrT_t = ctiles(KrT_d)
    kz_d = scratch("kz_d", [S, 512], BF16); kz_t = tiles(kz_d)
    vr_d = scratch("vr_d", [S, 1024], BF16); vr_t = tiles(vr_d)
    sgr_d = scratch("sgr_d", [S, 1024], F32); sgr_t = tiles(sgr_d)
    gm_d = scratch("gm_d", [S, 2048], F32); gm_t = tiles(gm_d)
    kc_d = scratch("kc_d", [2, 256, 64], BF16)
    vc_d = scratch("vc_d", [2, 256, 64], BF16)
    aT_d = scratch("aT_d", [1024, S], BF16); aT_t = ctiles(aT_d)
    rT_d = scratch("rT_d", [1024, S], BF16); rT_t = ctiles(rT_d)

    WSLOT = 33792
    wslot = [P.sb([128, WSLOT], BF16, f"wslot{i}") for i in range(2)]
    ident = P.sb([128, 128], BF16, "ident")
    gain = P.sb([128, 1024], F32, "gain")
    gain2 = P.sb([128, 1024], F32, "gain2")
    xin = [P.sb([128, 1024], F32, f"xin{i}") for i in range(2)]
    xres = xin
    hb = [P.sb([128, 1024], BF16, f"hb{i}") for i in range(2)]
    st1 = [P.sb([128, 4], F32, f"st1_{i}") for i in range(2)]
    big = P.sb([128, 5632], BF16, "big")
    hT = P.sb([128, 8, 512], BF16, "hT")
    wk32 = [P.sb([128, 1024], F32, f"wk32_{i}") for i in range(3)]
    wk16 = [P.sb([128, 1024], BF16, f"wk16_{i}") for i in range(3)]
    sm = [P.sb([128, 64], F32, f"sm{i}") for i in range(8)]
    tCa = P.sb([128, 512], F32, "tCa"); tSa = P.sb([128, 128], F32, "tSa")
    tCr = P.sb([128, 512], F32, "tCr"); tSr = P.sb([128, 512], F32, "tSr")
    tZ = P.sb([128, 512], F32, "tZ")
    pb = [P.ps([128, 512], F32, f"pb{i}") for i in range(7)]
    pbf = P.ps([128, 1024], BF16, "pbf")

    negh = P.sb([128, 8], F32, "negh")
    P.op("pool", lambda e: e.memset(negh[:], -0.5), w=[negh])
    P.op("pool", lambda e: e.memset(ident[:], 0.0), w=[ident])
    P.op("pool", lambda e: e.affine_select(out=ident[:], in_=ident[:], pattern=[[-1, 128]], compare_op=ALU.not_equal,
                                           fill=1.0, base=0, channel_multiplier=1), r=[ident], w=[ident])

    cnt = {"rr": 0}

    def rr(lst):
        cnt["rr"] += 1
        return lst[cnt["rr"] % len(lst)]

    def mm(out_t, out_ap, a_t, a_ap, b_t, b_ap, start, stop, sgc=False):
        P.op("pe", lambda e: e.matmul(out_ap, lhsT=a_ap, rhs=b_ap, start=start, stop=stop, skip_group_check=sgc),
             r=[a_t, b_t], w=[out_t])

    def tr(out_t, out_ap, in_t, in_ap):
        P.op("pe", lambda e: e.transpose(out=out_ap, in_=in_ap, identity=ident[:]), r=[in_t, ident], w=[out_t])

    def load_w(slot_t, off, src_t, src_ap, shape):
        n = int(np.prod(shape[1:]))
        dst = slot_t[:, off:off + n]
        if len(shape) == 3:
            dst = dst.rearrange("p (a b) -> p a b", a=shape[1])
        P.dma(slot_t, dst, src_t, src_ap, q="pool")
        return dst

    def rmsnorm(x_tile, gain_t, hb_t, stt, sq=None):
        sq = hb_t
        P.op("act", lambda e: e.activation(out=sq[:], in_=x_tile[:], func=AF.Square, accum_out=stt[:, 0:1]),
             r=[x_tile], w=[sq, stt])
        P.op("dve", lambda e: e.tensor_scalar(out=stt[:, 1:2], in0=stt[:, 0:1], scalar1=1.0 / D, scalar2=1e-6,
                                              op0=ALU.mult, op1=ALU.add), r=[stt], w=[stt])
        P.op("pool", lambda e: e.tensor_tensor(out=stt[:, 3:4], in0=stt[:, 1:2], in1=negh[:, 0:1], op=ALU.pow),
             r=[stt, negh], w=[stt])
        P.op("dve", lambda e: e.scalar_tensor_tensor(out=hb_t[:], in0=x_tile[:], scalar=stt[:, 3:4], in1=gain_t[:],
                                                     op0=ALU.mult, op1=ALU.mult), r=[x_tile, stt, gain_t], w=[hb_t])

    pb6bf = pb[6].ap.bitcast(BF16)

    def transpose8(src_t, dst_t, dst_ap, nblk=8, eng="act", alt=False):
        bank_t, bank = (pb[6], pb6bf) if alt else (pbf, pbf.ap)
        for k in range(nblk):
            tr(bank_t, bank[:, k * 128:(k + 1) * 128], src_t, src_t[:, k * 128:(k + 1) * 128])
        src = bank[:, 0:nblk * 128].rearrange("p (k t) -> p k t", k=nblk)
        if eng == "act":
            P.op("act", lambda e: e.copy(out=dst_ap, in_=src), r=[bank_t], w=[dst_t])
        else:
            P.op("dve", lambda e: e.tensor_copy(out=dst_ap, in_=src), r=[bank_t], w=[dst_t])

    def load_gain(gt, src):
        P.dma(gt, gt[:], src, src.ap.partition_broadcast(128))

    def carve(parent, ap, name=""):
        t = T(ap, name)
        t.last_w = parent.last_w
        t.readers = list(parent.readers)
        return t

    def uncarve(parent, subs):
        for t in subs:
            parent.readers.extend(t.readers)
            if t.last_w is not None:
                parent.readers.append(t.last_w)

    def ffn_pass(which, hf, slot, xsrc_t, xdst_t, norm_gain, final=False):
        pre = f"ffn{which}_"
        wg_d, wu_d, wd_d = I[pre + "w_gate"], I[pre + "w_up"], I[pre + "w_down"]
        c0 = hf * 1408
        wg = load_w(slot, 0, wg_d, wg_d.ap[:, c0:c0 + 1408].rearrange("(k p) f -> p k f", p=128), [128, 8, 1408])
        wu = load_w(slot, 11264, wu_d, wu_d.ap[:, c0:c0 + 1408].rearrange("(k p) f -> p k f", p=128), [128, 8, 1408])
        wd = load_w(slot, 22528, wd_d, wd_d.ap[c0:c0 + 1408, :].rearrange("(k p) d -> p k d", p=128), [128, 11, 1024])
        yield
        aT = big[:, 0:11 * 512].rearrange("p (f t) -> p f t", f=11)
        if hf == 0:
            load_gain(gain, norm_gain)
        if final:
            load_gain(gain2, I["final_norm"])
        nst = 8 if lim is None else min(lim, 8)
        sgs = [carve(wk32[0], wk32[0].ap[:, 0:512], "sga"), carve(wk32[0], wk32[0].ap[:, 512:1024], "sgb")]
        xs = [xin[0], xin[1], wk32[1], wk32[2]]
        hbs = [hb[0], hb[1], wk16[0], wk16[1]]
        st4 = [st1[0], st1[1], sm[6], sm[7]]
        if hf == 0:
            xrs = [[(tCa, tCa[:, :]), (tCr, tCr[:, :])], [(tSr, tSr[:, :]), (tZ, tZ[:, :])]]
        else:
            xrs = [[(x_, x_[:, 0:512]), (x_, x_[:, 512:1024])] for x_ in xs]

        def norm_part(st):
            for tt in range(4):
                ti = st * 4 + tt
                xt = xs[tt]
                P.dma(xt, xt[:], xsrc_t[ti], xsrc_t[ti][:, :])
                rmsnorm(xt, gain, hbs[tt], st4[tt], sq=wk16[2])

        def trans_part(st):
            for tt in range(4):
                transpose8(hbs[tt], hT, hT[:, :, tt * 128:(tt + 1) * 128], eng="act")
            P.dma(hT_t[st], hT_t[st][:], hT, hT[:], q="pool")

        def gu_part(st):
            for f in range(11):
                pg = pb[(2 * f) % 4]
                pu = pb[(2 * f + 1) % 4]
                for k in range(8):
                    mm(pg, pg[:], slot, wg[:, k, f * 128:(f + 1) * 128], hT, hT[:, k, :], k == 0, k == 7)
                for k in range(8):
                    mm(pu, pu[:], slot, wu[:, k, f * 128:(f + 1) * 128], hT, hT[:, k, :], k == 0, k == 7)
                sg = sgs[f % 2]
                P.op("act", lambda e, sg=sg, pg=pg: e.activation(out=sg[:, :], in_=pg[:], func=AF.Silu), r=[pg], w=[sg])
                P.op("dve", lambda e, sg=sg, pu=pu, f=f: e.tensor_tensor(out=aT[:, f, :], in0=pu[:], in1=sg[:, :], op=ALU.mult),
                     r=[pu, sg], w=[big])

        def down_part(st):
            for tt in range(4):
                ti = st * 4 + tt
                xr = xrs[ti % len(xrs)]
                for half in range(2):
                    P.dma(xr[half][0], xr[half][1], xsrc_t[ti], xsrc_t[ti][:, half * 512:(half + 1) * 512])
                for half in range(2):
                    py = pb[4 + (2 * tt + half) % 3]
                    for f in range(11):
                        mm(py, py[:], big, aT[:, f, tt * 128:(tt + 1) * 128], slot, wd[:, f, half * 512:(half + 1) * 512],
                           f == 0, f == 10)
                    xh_t, xh = xr[half]
                    P.op("dve", lambda e, xh=xh, py=py: e.scalar_tensor_tensor(out=xh, in0=py[:], scalar=0.5, in1=xh,
                                                                              op0=ALU.mult, op1=ALU.add), r=[py, xh_t], w=[xh_t])
                if final:
                    xf = xr[0][0]
                    ob = gain
                    stt = st4[tt]
                    P.op("act", lambda e, xf=xf, stt=stt: e.activation(out=ob[:], in_=xf[:], func=AF.Square,
                                                                       accum_out=stt[:, 0:1]), r=[xf], w=[ob, stt])
                    P.op("dve", lambda e, stt=stt: e.tensor_scalar(out=stt[:, 1:2], in0=stt[:, 0:1], scalar1=1.0 / D,
                                                                   scalar2=1e-6, op0=ALU.mult, op1=ALU.add), r=[stt], w=[stt])
                    P.op("pool", lambda e, stt=stt: e.tensor_tensor(out=stt[:, 3:4], in0=stt[:, 1:2], in1=negh[:, 0:1], op=ALU.pow),
                         r=[stt, negh], w=[stt])
                    P.op("dve", lambda e, xf=xf, stt=stt: e.scalar_tensor_tensor(out=ob[:], in0=xf[:], scalar=stt[:, 3:4],
                                                                                in1=gain2[:], op0=ALU.mult, op1=ALU.mult),
                         r=[xf, stt, gain2], w=[ob])
                    P.dma(xdst_t[ti], xdst_t[ti][:, :], ob, ob[:], q="pool")
                else:
                    for half in range(2):
                        P.dma(xdst_t[ti], xdst_t[ti][:, half * 512:(half + 1) * 512], xr[half][0], xr[half][1], q="pool")

        if hf == 0:
            norm_part(0)
            trans_part(0)
            for st in range(nst):
                gu_part(st)
                if st + 1 < nst:
                    norm_part(st + 1)
                down_part(st)
                if st + 1 < nst:
                    trans_part(st + 1)
        else:
            P.dma(hT, hT[:], hT_t[0], hT_t[0][:])
            for st in range(nst):
                gu_part(st)
                if st + 1 < nst:
                    P.dma(hT, hT[:], hT_t[st + 1], hT_t[st + 1][:])
                down_part(st)
        uncarve(wk32[0], sgs)

    def rope(src_t, src_ap, H, R, Ct, Cap, St, Sap, out_t, out_ap, o32, t2):
        n = H * 64
        half = R // 2
        P.op("dve", lambda e: e.tensor_tensor(out=o32[:, 0:n], in0=src_ap, in1=Cap, op=ALU.mult), r=[Ct, src_t], w=[o32])
        s3 = src_ap.rearrange("p (h d) -> p h d", h=H)
        S3 = Sap.rearrange("p (h r) -> p h r", h=H)
        t3 = t2[:, 0:H * R].rearrange("p (h r) -> p h r", h=H)
        o3 = o32[:, 0:n].rearrange("p (h d) -> p h d", h=H)
        P.op("dve", lambda e: e.tensor_tensor(out=t3[:, :, 0:half], in0=s3[:, :, half:R], in1=S3[:, :, 0:half],
                                              op=ALU.mult), r=[St, src_t], w=[t2])
        P.op("dve", lambda e: e.tensor_tensor(out=t3[:, :, half:R], in0=s3[:, :, 0:half], in1=S3[:, :, half:R],
                                              op=ALU.mult), r=[St, src_t], w=[t2])
        P.op("dve", lambda e: e.tensor_tensor(out=o3[:, :, 0:R], in0=o3[:, :, 0:R], in1=t3, op=ALU.add),
             r=[o32, t2], w=[o32])
        P.op("act", lambda e: e.copy(out=out_ap, in_=o32[:, 0:n]), r=[o32], w=[out_t])

    def proj_pass0(slot):
        w_d = I["w_in"]
        w0 = load_w(slot, 0, w_d, w_d.ap[:, 0:1944].rearrange("(k p) f -> p k f", p=128), [128, 8, 1944])
        w1 = load_w(slot, 8 * 1944, w_d, w_d.ap[:, 1944:3888].rearrange("(k p) f -> p k f", p=128), [128, 8, 1944])
        yield

        def wcol(k, c0, c1):
            if c1 <= 1944:
                return w0[:, k, c0:c1]
            assert c0 >= 1944
            return w1[:, k, c0 - 1944:c1 - 1944]

        load_gain(gain, I["mix_norm"])
        P.dma(tZ, tZ[:], C["ZT"], C["ZT"][:, :])
        qTs = [P.sb([128, 4, 128], BF16, "qTa"), P.sb([128, 4, 128], BF16, "qTb")]
        blocks = [(0, 512), (512, 1024), (1024, 1536), (1536, 1840), (1840, 2352), (2352, 2864), (2864, 3376), (3376, 3888)]
        subs_c = []
        hbuf = [carve(hT, hT.ap[:, :, 0:128], "hTa"), carve(hT, hT.ap[:, :, 128:256], "hTb")]
        o32s = [carve(wk32[0], wk32[0].ap[:, 0:512], "o32a"), carve(wk32[0], wk32[0].ap[:, 512:1024], "o32b")]
        t2s = [carve(wk32[1], wk32[1].ap[:, 0:512], "t2a"), carve(wk32[1], wk32[1].ap[:, 512:1024], "t2b")]
        subs_c = [(hT, hbuf), (wk32[0], o32s), (wk32[1], t2s)]
        ntl = NT if lim is None else lim
        rcnt = {"r": 0, "q": 0}

        def prep_x(ti):
            P.tag = f"p0.t{ti}.prep"
            xt = xin[ti % 2]
            P.dma(xt, xt[:], x1_t[ti], x1_t[ti][:, :])
            hbt = hb[ti % 2]
            rmsnorm(xt, gain, hbt, st1[ti % 2])
            hb_ = hbuf[ti % 2]
            transpose8(hbt, hb_, hb_[:, :, :], eng="act")
            P.dma(hmT_t[ti], hmT_t[ti][:], hb_, hb_[:, :, :], q="sp")

        def load_tabs_a(ti):
            P.dma(tCa, tCa[:], C["ropeCa"], C["ropeCa"][ti * 128:(ti + 1) * 128, :])
            P.dma(tSa, tSa[:], C["ropeSa"], C["ropeSa"][ti * 128:(ti + 1) * 128, :])

        def load_tabs_r(ti):
            P.dma(tCr, tCr[:], C["ropeCr"], C["ropeCr"][ti * 128:(ti + 1) * 128, :])
            P.dma(tSr, tSr[:], C["ropeSr"], C["ropeSr"][ti * 128:(ti + 1) * 128, :])

        def mm_block(ti, bi):
            P.tag = f"p0.t{ti}.mm{bi}"
            c0, c1 = blocks[bi]
            pp = pb[bi % 6]
            hb_ = hbuf[ti % 2]
            subs = [(c0, c1)] if not (c0 < 1944 < c1) else [(c0, 1944), (1944, c1)]
            for (s0, s1) in subs:
                for k in range(8):
                    mm(pp, pp[:, s0 - c0:s1 - c0], hb_, hb_[:, k, :], slot, wcol(k, s0, s1), k == 0, k == 7)

        def post_a(ti, bi):
            P.tag = f"p0.t{ti}.A{bi}"
            return post_a_(ti, bi)

        def post_a_(ti, bi):
            pp = pb[bi % 6]
            rcnt["r"] += 1
            o32, t2 = o32s[rcnt["r"] % 2], t2s[rcnt["r"] % 2]

            def nextq():
                rcnt["q"] += 1
                P.tag = f"p0.t{ti}.B{bi}"
                return qTs[rcnt["q"] % 2]
            if bi in (0, 1):
                ob = wk16[bi % 2]
                rope(pp, pp[:, 0:512], 8, 16, tCa, tCa[:, :], tSa, tSa[:, :], ob, ob[:, 0:512], o32, t2)

                def fb():
                    qT = nextq()
                    transpose8(ob, qT, qT[:], nblk=4, eng="dve", alt=rcnt["q"] % 2 == 1)
                    dst = QT_t[bi][ti]
                    P.dma(dst, dst[:].rearrange("(p r) t -> r p t", r=128), qT, qT[:], q="sp")
                return fb
            elif bi == 2:
                ob = wk16[2]
                P.op("act", lambda e: e.copy(out=ob[:, 0:256], in_=pp[:, 0:256]), r=[pp], w=[ob])
                rope(pp, pp[:, 256:384], 2, 16, tCa, tCa[:, 0:128], tSa, tSa[:, 0:32], ob, ob[:, 256:384], o32, t2)
                P.op("act", lambda e: e.copy(out=ob[:, 384:512], in_=pp[:, 384:512]), r=[pp], w=[ob])

                def fb():
                    qT = nextq()
                    transpose8(ob, qT, qT[:, 0:3, :], nblk=3, eng="dve", alt=rcnt["q"] % 2 == 1)
                    P.dma(kcT_t[ti], kcT_t[ti][:], qT, qT[:, 0, :], q="sp")
                    P.dma(vcT_t[ti], vcT_t[ti][:], qT, qT[:, 1, :], q="sp")
                    P.dma(ksT_t[ti], ksT_t[ti][:], qT, qT[:, 2, :], q="sp")
                    P.dma(vs_t[ti], vs_t[ti][:, :], ob, ob[:, 384:512], q="sp")
                return fb
            elif bi == 3:
                ob = hb[(ti + 1) % 2]
                ob = wk16[2]
                rope(pp, pp[:, 0:128], 2, 16, tCa, tCa[:, 0:128], tSa, tSa[:, 0:32], ob, ob[:, 0:128], o32, t2)
                P.op("act", lambda e: e.copy(out=ob[:, 128:256], in_=pp[:, 128:256]), r=[pp], w=[ob])
                gsb = sm[0]
                P.op("act", lambda e: e.activation(out=gsb[:, 0:48], in_=pp[:, 256:304], func=AF.Sigmoid), r=[pp], w=[gsb])

                def fb():
                    qT = nextq()
                    P.dma(ga_t[ti], ga_t[ti][:, :], gsb, gsb[:, 0:48], q="sp")
                    transpose8(ob, qT, qT[:, 0:1, :], nblk=1, eng="dve", alt=rcnt["q"] % 2 == 1)
                    P.dma(kwT_t[ti], kwT_t[ti][:], qT, qT[:, 0, :], q="sp")
                    P.dma(vw_t[ti], vw_t[ti][:, :], ob, ob[:, 128:256], q="sp")
                return fb
            elif bi in (4, 5):
                ob = wk16[bi % 2]
                rope(pp, pp[:, 0:512], 8, 64, tCr, tCr[:, :], tSr, tSr[:, :], ob, ob[:, 0:512], o32, t2)
                kz = None
                if bi == 5:
                    kz = hb[(ti + 1) % 2] if False else wk16[2]
                    P.op("pool", lambda e: e.tensor_tensor(out=kz[:, 0:512], in0=o32[:, 0:512], in1=tZ[:], op=ALU.mult),
                         r=[o32, tZ], w=[kz])

                def fb():
                    qT = nextq()
                    transpose8(ob, qT, qT[:], nblk=4, eng="dve", alt=rcnt["q"] % 2 == 1)
                    dst = (QrT_t if bi == 4 else KrT_t)[ti]
                    P.dma(dst, dst[:].rearrange("(p r) t -> r p t", r=128), qT, qT[:], q="sp")
                    if bi == 5:
                        P.dma(kz_t[ti], kz_t[ti][:, :], kz, kz[:, 0:512], q="sp")
                return fb
            else:
                ob = wk16[bi % 2]
                P.op("act", lambda e: e.copy(out=ob[:, 0:512], in_=pp[:]), r=[pp], w=[ob])
                hh = bi - 6

                def fb():
                    P.dma(vr_t[ti], vr_t[ti][:, hh * 512:(hh + 1) * 512], ob, ob[:, 0:512], q="sp")
                return fb

        load_tabs_a(0)
        load_tabs_r(0)
        prep_x(0)
        for ti in range(ntl):
            for bi in range(6):
                mm_block(ti, bi)
            f0 = post_a(ti, 0)
            mm_block(ti, 6)
            f1 = post_a(ti, 1)
            mm_block(ti, 7)
            f0()
            f2 = post_a(ti, 2)
            f1()
            f2()
            f3 = post_a(ti, 3)
            if ti + 1 < ntl:
                prep_x(ti + 1)
            f3()
            if ti + 1 < ntl:
                load_tabs_a(ti + 1)
            f4 = post_a(ti, 4)
            f5 = post_a(ti, 5)
            f4()
            if ti + 1 < ntl:
                load_tabs_r(ti + 1)
            f6 = post_a(ti, 6)
            f5()
            f7 = post_a(ti, 7)
            f6()
            f7()
        for parent, subs in subs_c:
            uncarve(parent, subs)

    def proj_pass1(slot):
        w_d = I["w_in"]
        w0 = load_w(slot, 0, w_d, w_d.ap[:, 3888:3888 + 1536].rearrange("(k p) f -> p k f", p=128), [128, 8, 1536])
        w1 = load_w(slot, 8 * 1536, w_d, w_d.ap[:, 3888 + 1536:DIN].rearrange("(k p) f -> p k f", p=128), [128, 8, 1536])
        yield
        for ti in range(NT if lim is None else lim):
            P.dma(hT, hT[:, :, 0:128], hmT_t[ti], hmT_t[ti][:])
            for bi in range(6):
                pp = pb[bi % 7]
                wv = w0 if bi < 3 else w1
                cc = (bi % 3) * 512
                for k in range(8):
                    mm(pp, pp[:], hT, hT[:, k, 0:128], slot, wv[:, k, cc:cc + 512], k == 0, k == 7)
                o32 = wk32[bi % 3]
                fn = AF.Silu if bi < 2 else AF.Sigmoid
                P.op("act", lambda e, pp=pp, o32=o32, fn=fn: e.activation(out=o32[:, 0:512], in_=pp[:], func=fn),
                     r=[pp], w=[o32])
                if bi < 2:
                    P.dma(sgr_t[ti], sgr_t[ti][:, bi * 512:(bi + 1) * 512], o32, o32[:, 0:512], q="pool")
                else:
                    P.dma(gm_t[ti], gm_t[ti][:, (bi - 2) * 512:(bi - 1) * 512], o32, o32[:, 0:512], q="pool")


    kcT2_d = scratch("kcT2_d", [2, 64, 256], BF16)
    selT_d = scratch("selT_d", [2, 64, 128], BF16)
    selT_t = [T(selT_d.ap[i], f"selT{i}") for i in range(2)]

    def cmp_phase(slot):
        subs = []

        def cv(off, n, parts=128, name=""):
            t = carve(slot, slot.ap[0:parts, off:off + n], name)
            subs.append(t)
            return t
        w1 = [cv(0, 8192, 64, "w1k"), cv(8192, 8192, 64, "w1v")]
        w2 = [cv(16384, 128, 128, "w2k"), cv(16512, 128, 128, "w2v")]
        tokT = [cv(16640, 4096, 64, "tokT0"), cv(20736, 4096, 64, "tokT1")]
        GT = cv(24832, 512, 128, "GT")
        peb = cv(25344, 64, 32, "peb")
        peT = cv(25408, 32, 64, "peT")
        kcb = cv(25440, 64, 128, "kcb")
        kcTs = cv(25504, 128, 64, "kcTs")
        tCc = sm[1]; tSc = sm[2]; bias = sm[3]
        srcs = [(I["cmp_k_w1"], I["cmp_k_w2"], kcT_d, kc_d), (I["cmp_v_w1"], I["cmp_v_w2"], vcT_d, vc_d)]
        P.dma(peb, peb[:, :], I["cmp_pos_emb"], I["cmp_pos_emb"][:, :], q="pool")
        tr(pbf, pbf[0:64, 0:32], peb, peb[:, :]) if False else P.op(
            "pe", lambda e: e.transpose(out=pbf[0:64, 0:32], in_=peb[:, :], identity=ident[0:32, 0:32]),
            r=[peb, ident], w=[pbf])
        P.op("act", lambda e: e.copy(out=peT[:, :], in_=pbf[0:64, 0:32]), r=[pbf], w=[peT])
        P.op("pool", lambda e: e.memset(GT[:, :], 0.0), w=[GT])
        GT3 = GT[:, :].rearrange("p (m n) -> p m n", m=2)
        for kv in range(2):
            w1_d, w2_d, tT_d, o_d = srcs[kv]
            w1v = w1[kv][:, :].rearrange("p (l j) -> p l j", l=32)
            P.dma(w1[kv], w1v, w1_d, w1_d.ap.rearrange("(l d) j -> d l j", d=64), q="pool")
            w2v = w2[kv][:, :].rearrange("p (m o) -> p m o", m=2)
            P.dma(w2[kv], w2v, w2_d, w2_d.ap.rearrange("(m p) o -> p m o", p=128), q="pool")
            pbias = pb[6]
            for m in range(2):
                for l in range(32):
                    mm(pbias, pbias[:, m:m + 1], w1[kv], w1v[:, l, m * 128:(m + 1) * 128], peT, peT[:, l:l + 1], l == 0, l == 31)
            P.op("act", lambda e, pbias=pbias: e.copy(out=bias[:, 0:2], in_=pbias[:, 0:2]), r=[pbias], w=[bias])
            for g in range(2):
                tk = tokT[g]
                P.dma(tk, tk[:, :], tT_d, tT_d.ap[g * 64:(g + 1) * 64, :])
                tok3 = tk[:, :].rearrange("p (n r) -> p n r", r=16)
                for m in range(2):
                    ph = pb[m]
                    for l in range(32):
                        a, rr_ = l // 16, l % 16
                        mm(ph, ph[:, 0:255], w1[kv], w1v[:, l, m * 128:(m + 1) * 128], tk, tok3[:, a:a + 255, rr_], l == 0, l == 31)
                    hbx, x2, z = wk32[0], wk32[1], wk32[2]
                    P.op("act", lambda e, ph=ph, m=m: e.activation(out=hbx[:, 0:255], in_=ph[:, 0:255], func=AF.Identity,
                                                                   bias=bias[:, m:m + 1]), r=[ph, bias], w=[hbx])
                    P.op("dve", lambda e: e.tensor_tensor(out=x2[:, 0:255], in0=hbx[:, 0:255], in1=hbx[:, 0:255], op=ALU.mult),
                         r=[hbx], w=[x2])
                    P.op("dve", lambda e: e.tensor_scalar(out=x2[:, 0:255], in0=x2[:, 0:255], scalar1=0.044715, scalar2=1.0,
                                                          op0=ALU.mult, op1=ALU.add), r=[x2], w=[x2])
                    P.op("dve", lambda e: e.tensor_tensor(out=z[:, 0:255], in0=x2[:, 0:255], in1=hbx[:, 0:255], op=ALU.mult),
                         r=[x2, hbx], w=[z])
                    P.op("act", lambda e: e.activation(out=z[:, 0:255], in_=z[:, 0:255], func=AF.Sigmoid, scale=1.5957691216),
                         r=[z], w=[z])
                    P.op("dve", lambda e, m=m: e.tensor_tensor(out=GT3[:, m, 0:255], in0=z[:, 0:255], in1=hbx[:, 0:255],
                                                               op=ALU.mult), r=[z, hbx], w=[GT])
                for j in range(2):
                    po = pb[2 + j]
                    for m in range(2):
                        mm(po, po[:, 0:64], GT, GT3[:, m, j * 128:(j + 1) * 128], w2[kv], w2v[:, m, :], m == 0, m == 1)
                    if kv == 0:
                        P.dma(tCc, tCc[:, 0:64], C["ropeCc"], C["ropeCc"][j * 128:(j + 1) * 128, :])
                        P.dma(tSc, tSc[:, 0:16], C["ropeSc"], C["ropeSc"][j * 128:(j + 1) * 128, :])
                        rope(po, po[:, 0:64], 1, 16, tCc, tCc[:, 0:64], tSc, tSc[:, 0:16], kcb, kcb[:, :], wk32[0], wk32[1])
                        P.op("pe", lambda e: e.transpose(out=pbf[0:64, 0:128], in_=kcb[:, :], identity=ident[:]),
                             r=[kcb, ident], w=[pbf])
                        P.op("act", lambda e: e.copy(out=kcTs[:, :], in_=pbf[0:64, 0:128]), r=[pbf], w=[kcTs])
                        P.dma(kcT2_d, kcT2_d.ap[g, :, j * 128:(j + 1) * 128], kcTs, kcTs[:, :], q="pool")
                    else:
                        P.op("act", lambda e, po=po: e.copy(out=kcb[:, :], in_=po[:, 0:64]), r=[po], w=[kcb])
                    P.dma(o_d, o_d.ap[g, j * 128:(j + 1) * 128, :], kcb, kcb[:, :], q="pool")
        uncarve(slot, subs)

    def nsa_phase(slot):
        subs = []

        def cv(off, n, parts=128, name=""):
            t = carve(slot, slot.ap[0:parts, off:off + n], name)
            subs.append(t)
            return t
        KsT = cv(0, 4096, 128, "KsT"); KwT = cv(4096, 4096, 128, "KwT")
        Vs = cv(8192, 2080, 128, "Vs"); Vw = cv(10272, 2080, 128, "Vw")
        KcT = cv(12352, 256, 128, "KcT"); Vc = cv(12608, 130, 128, "Vc")
        C2S = cv(12738, 128, 128, "C2S")
        mfull = [cv(12866, 4096, 128, "mfull0"), cv(24194, 4096, 128, "mfull1")]
        tri = cv(28290, 128, 128, "tri")
        selTs = cv(28418, 128, 64, "selTs")
        QTc = [cv(16962 + i * 1024, 512, 128, f"QTc{i}") for i in range(2)]
        Et = [cv(19010 + i * 1024, 1024, 128, f"Et{i}") for i in range(3)]
        selb = cv(23106, 64, 128, "selb")
        ab = cv(23170, 512, 128, "ab")
        aTs = cv(23682, 512, 128, "aTs")
        TK = tCa; TA = tCr
        P.dma(TK, TK[:, 0:128], C["TK"], C["TK"][:, :])
        P.dma(TA, TA[:, 0:128], C["TA"], C["TA"][:, :])
        P.dma(C2S, C2S[:, :].rearrange("p (j s) -> p j s", j=2), C["C2S"], C["C2S"].ap.rearrange("(j p) s -> p j s", p=128), q="pool")
        P.op("pool", lambda e: e.memset(tri[:, :], 1.0), w=[tri])
        P.op("pool", lambda e: e.affine_select(out=tri[:, :], in_=tri[:, :], pattern=[[1, 128]], compare_op=ALU.is_ge,
                                               fill=0.0, base=0, channel_multiplier=-1), r=[tri], w=[tri])
        Vs3 = Vs[:, :].rearrange("p (j d) -> p j d", d=65)
        Vw3 = Vw[:, :].rearrange("p (j d) -> p j d", d=65)
        Vc3 = Vc[:, :].rearrange("p (j d) -> p j d", d=65)
        C2S3 = C2S[:, :].rearrange("p (j s) -> p j s", j=2)
        SB = [(pb[0], pb[1]), (pb[2], pb[3])]
        OA, OB, OC = pb[4], pb[5], pb[6]
        OA3 = OA[:, 0:260].rearrange("p (h d) -> p h d", d=65)
        OB3 = OB[:, 0:260].rearrange("p (h d) -> p h d", d=65)
        acc = xin[0]; tmpo = xin[1]
        acc3 = acc[:, 0:512].rearrange("p (h d) -> p h d", d=64)
        tmp3 = tmpo[:, 0:512].rearrange("p (h d) -> p h d", d=64)
        den = sm[2]; coef = sm[3]; imp = sm[4]; imp2 = sm[5]; m8 = sm[6]; imp3 = sm[7]
        sidx = {"s": 0, "e": 0}

        def scores(lhs_t, lhs_ap, qt, mask=None):
            b = SB[sidx["s"] % 2]; sidx["s"] += 1
            for hh in range(2):
                mm(b[hh], b[hh][:], lhs_t, lhs_ap[hh * 64:(hh + 1) * 64], qt, qt[hh * 64:(hh + 1) * 64, :], True, True)
            et = Et[sidx["e"] % 3]; sidx["e"] += 1
            for hh in range(2):
                P.op("act", lambda e, hh=hh, et=et, b=b: e.activation(out=et[:, hh * 512:(hh + 1) * 512], in_=b[hh][:],
                                                                      func=AF.Exp, scale=0.125), r=[b[hh]], w=[et])
            if mask is not None:
                m_t, m_ap = mask
                P.op("dve", lambda e: e.tensor_tensor(out=et[:, :].rearrange("p (h q) -> p h q", h=8),
                                                      in0=et[:, :].rearrange("p (h q) -> p h q", h=8),
                                                      in1=m_ap.unsqueeze(1).broadcast_to([128, 8, 128]), op=ALU.mult),
                     r=[et, m_t], w=[et])
            return et

        def warm(n):
            b = SB[sidx["s"] % 2][0]
            for _w in range(n):
                mm(b, b[:, :], ident, ident[:, :], hb[0], hb[0][:, 0:512], True, True)

        def amask(et, cm, qstep, base):
            P.op("pool", lambda e: e.affine_select(out=et[:, :], in_=et[:, :], pattern=[[0, 8], [qstep, 128]],
                                                   compare_op=ALU.is_ge, fill=0.0, base=base, channel_multiplier=cm),
                 r=[et], w=[et])

        def pv(et, v_t, v_ap, first, last):
            for h in range(8):
                o_t = OA if h < 4 else OB
                o3 = OA3 if h < 4 else OB3
                mm(o_t, o3[:, h % 4, :], et, et[:, h * 128:(h + 1) * 128], v_t, v_ap, first and (h % 4 == 0), last, sgc=True)

        pend = []

        def flush(keep=1):
            while len(pend) > keep:
                work, post = pend.pop(0)
                work()
                if post is not None:
                    post()

        def finish_branch(bidx, first, gat):
            P.op("dve", lambda e: e.tensor_scalar(out=den[:, 0:4], in0=OA3[:, :, 64], scalar1=1e-20, scalar2=None, op0=ALU.max),
                 r=[OA], w=[den])
            P.op("dve", lambda e: e.tensor_scalar(out=den[:, 4:8], in0=OB3[:, :, 64], scalar1=1e-20, scalar2=None, op0=ALU.max),
                 r=[OB], w=[den])
            P.op("dve", lambda e: e.reciprocal(out=den[:, 8:16], in_=den[:, 0:8]), r=[den], w=[den])
            g3 = gat[:, 0:24].rearrange("p (h b) -> p h b", b=3)
            P.op("dve", lambda e: e.tensor_tensor(out=coef[:, 0:8], in0=den[:, 8:16], in1=g3[:, :, bidx], op=ALU.mult),
                 r=[den, gat], w=[coef])
            dst, d3 = (acc, acc3) if first else (tmpo, tmp3)
            P.op("dve", lambda e: e.tensor_tensor(out=d3[:, 0:4, :], in0=OA3[:, :, 0:64],
                                                  in1=coef[:, 0:4].unsqueeze(2).broadcast_to([128, 4, 64]), op=ALU.mult),
                 r=[OA, coef], w=[dst])
            P.op("dve", lambda e: e.tensor_tensor(out=d3[:, 4:8, :], in0=OB3[:, :, 0:64],
                                                  in1=coef[:, 4:8].unsqueeze(2).broadcast_to([128, 4, 64]), op=ALU.mult),
                 r=[OB, coef], w=[dst])
            if not first:
                P.op("pool", lambda e: e.tensor_tensor(out=acc[:, 0:512], in0=acc[:, 0:512], in1=tmpo[:, 0:512], op=ALU.add),
                     r=[acc, tmpo], w=[acc])

        for g in range(2):
            for hh in range(2):
                P.dma(KsT, KsT[hh * 64:(hh + 1) * 64, :], ksT_d, ksT_d.ap[g * 64:(g + 1) * 64, :])
                P.dma(KwT, KwT[hh * 64:(hh + 1) * 64, :], kwT_d, kwT_d.ap[g * 64:(g + 1) * 64, :])
            P.dma(Vs, Vs3[:, :, 0:64], vs_d, vs_d.ap[:, g * 64:(g + 1) * 64].rearrange("(j p) d -> p j d", p=128))
            P.dma(Vw, Vw3[:, :, 0:64], vw_d, vw_d.ap[:, g * 64:(g + 1) * 64].rearrange("(j p) d -> p j d", p=128))
            P.op("pool", lambda e: e.memset(Vs3[:, :, 64:65], 1.0), w=[Vs])
            P.op("pool", lambda e: e.memset(Vw3[:, :, 64:65], 1.0), w=[Vw])
            for hh in range(2):
                P.dma(KcT, KcT[hh * 64:(hh + 1) * 64, :], kcT2_d, kcT2_d.ap[g])
            P.dma(Vc, Vc3[:, :, 0:64], vc_d, vc_d.ap[g].rearrange("(j p) d -> p j d", p=128))
            P.op("pool", lambda e: e.memset(Vc3[:, :, 64:65], 1.0), w=[Vc])
            gats = [sm[1], sm[0]]
            for c in (range(NT) if lim is None else lim_c):
                qt = QTc[c % 2]
                gat = gats[c % 2]
                warm(int(_os.environ.get("NWARM0", "8")))
                for hh in range(2):
                    P.dma(qt, qt[hh * 64:(hh + 1) * 64, :].rearrange("d (h t) -> d h t", h=4), QT_t[g][c],
                          QT_t[g][c][hh * 256:(hh + 1) * 256, :].rearrange("(h d) t -> d h t", d=64))
                P.dma(gat, gat[:, 0:24], ga_t[c], ga_t[c][:, g * 24:(g + 1) * 24])

                def post_cmp(c=c, gat=gat):
                    finish_branch(0, True, gat)
                    for h in range(8):
                        if h == 0:
                            P.op("dve", lambda e: e.tensor_scalar(out=imp[:, 0:64], in0=OC[:, 0:64], scalar1=den[:, 8:9],
                                                                  scalar2=None, op0=ALU.mult), r=[OC, den], w=[imp])
                        else:
                            P.op("dve", lambda e, h=h: e.scalar_tensor_tensor(
                                out=imp[:, 0:64], in0=OC[:, h * 64:(h + 1) * 64], scalar=den[:, 8 + h:9 + h], in1=imp[:, 0:64],
                                op0=ALU.mult, op1=ALU.add), r=[OC, den, imp], w=[imp])
                    off = 64 - 2 * c
                    P.op("dve", lambda e: e.tensor_tensor(out=imp2[:, 0:64], in0=imp[:, 0:64], in1=TK[:, off:off + 64], op=ALU.mult),
                         r=[imp, TK], w=[imp2])
                    P.op("dve", lambda e: e.tensor_tensor(out=imp2[:, 0:64], in0=imp2[:, 0:64], in1=TA[:, off:off + 64], op=ALU.add),
                         r=[imp2, TA], w=[imp2])
                    P.op("dve", lambda e: e.memset(imp2[:, 0:1], 1.0e4), r=[imp2], w=[imp2])
                    P.op("dve", lambda e: e.max(out=m8[:, 0:8], in_=imp2[:, 0:64]), r=[imp2], w=[m8])
                    P.op("dve", lambda e: e.match_replace(out=imp3[:, 0:64], in_to_replace=m8[:, 0:8], in_values=imp2[:, 0:64],
                                                          imm_value=-3.0e4), r=[imp2, m8], w=[imp3])
                    P.op("dve", lambda e: e.max(out=m8[:, 8:16], in_=imp3[:, 0:64]), r=[imp3, m8], w=[m8])
                    P.op("dve", lambda e: e.tensor_scalar(out=selb[:, :].rearrange("p (b j) -> p j b", b=2),
                                                          in0=imp2[:, 0:64].rearrange("p (j b) -> p j b", b=2),
                                                          scalar1=m8[:, 15:16], scalar2=None, op0=ALU.is_ge),
                         r=[imp2, m8], w=[selb])
                    P.op("pe", lambda e: e.transpose(out=pbf[0:64, 0:128], in_=selb[:, :], identity=ident[:]),
                         r=[selb, ident], w=[pbf])
                    P.op("dve", lambda e: e.tensor_copy(out=selTs[:, :], in_=pbf[0:64, 0:128]), r=[pbf], w=[selTs])
                    sd = selT_t[c % 2]
                    P.dma(sd, sd[:, :], selTs, selTs[:, :])
                    mf = mfull[c % 2]
                    for b2 in range(2):
                        src = sd[b2 * 32:b2 * 32 + c + 1, :].rearrange("(o r) q -> o (r q)", o=1).partition_broadcast(64)
                        P.dma(mf, mf[b2 * 64:(b2 + 1) * 64, 0:(c + 1) * 128], sd, src)
                    P.op("pool", lambda e: e.tensor_tensor(out=mf[:, c * 128:(c + 1) * 128], in0=mf[:, c * 128:(c + 1) * 128],
                                                           in1=tri[:, :], op=ALU.mult), r=[mf, tri], w=[mf])

                def post_win(gat=gat):
                    finish_branch(2, False, gat)

                def post_sel(c=c, gat=gat):
                    finish_branch(1, False, gat)
                    P.op("act", lambda e: e.copy(out=ab[:, :], in_=acc[:, 0:512]), r=[acc], w=[ab])
                    transpose8(ab, aTs, aTs[:, :].rearrange("p (k t) -> p k t", k=4), nblk=4, eng="act")
                    dst = aT_t[c]
                    P.dma(dst, dst[g * 512:(g + 1) * 512, :].rearrange("(k r) t -> r k t", r=128), aTs,
                          aTs[:, :].rearrange("p (k t) -> p k t", k=4), q="pool")

                njt = 1 if c <= 15 else 2
                for j in range(njt):
                    et = scores(KcT, KcT[:, j * 128:(j + 1) * 128], qt)
                    amask(et, -16, 1, -16 * (128 * j - 8 * c) - 31)
                    flush()

                    def work(et=et, j=j, njt=njt):
                        pv(et, Vc, Vc3[:, j, :], j == 0, j == njt - 1)
                        for h in range(8):
                            mm(OC, OC[:, h * 64:(h + 1) * 64], et, et[:, h * 128:(h + 1) * 128], C2S, C2S3[:, j, :],
                               j == 0 and h == 0, j == njt - 1, sgc=True)
                    pend.append((work, post_cmp if j == njt - 1 else None))
                j0 = max(0, c - 4)
                for j in range(j0, c + 1):
                    et = scores(KwT, KwT[:, j * 128:(j + 1) * 128], qt)
                    if j == c:
                        amask(et, -1, 1, 0)
                    if j == c - 4:
                        amask(et, 1, -1, -1)
                    flush()
                    pend.append((lambda et=et, j=j, j0=j0, c=c: pv(et, Vw, Vw3[:, j, :], j == j0, j == c),
                                 post_win if j == c else None))
                mf = mfull[c % 2]
                for j in range(c + 1):
                    if j == 0:
                        flush(0)
                        warm(int(_os.environ.get("NWARM1", "16")))
                    et = scores(KsT, KsT[:, j * 128:(j + 1) * 128], qt, mask=(mf, mf[:, j * 128:(j + 1) * 128]))
                    flush()
                    pend.append((lambda et=et, j=j, c=c: pv(et, Vs, Vs3[:, j, :], j == 0, j == c),
                                 post_sel if j == c else None))
            flush(0)
        uncarve(slot, subs)

    def ret_phase():
        subs = []
        bufs = []
        for i in range(2):
            q_ = carve(big, big.ap[0:64, i * 2560:i * 2560 + 1024], f"QrTc{i}")
            k_ = carve(big, big.ap[0:64, i * 2560 + 1024:i * 2560 + 2048], f"KrTc{i}")
            z_ = carve(big, big.ap[:, i * 2560 + 2048:i * 2560 + 2560], f"kzc{i}")
            subs += [q_, k_, z_]
            bufs.append((q_, k_, z_))
        dec = [tCr, tSr]
        gct = [tCa, tZ]
        P.dma(dec[0], dec[0][:, :], C["decayT"], C["decayT"][:, 0:512])
        P.dma(dec[1], dec[1][:, :], C["decayT"], C["decayT"][:, 512:1024])
        P.dma(gct[0], gct[0][0:64, :], C["GC"], C["GC"][:, 0:512])
        P.dma(gct[1], gct[1][0:64, :], C["GC"], C["GC"][:, 512:1024])
        xit = sm[1]
        P.dma(xit, xit[:, 0:8], C["XI"], C["XI"][:, :])
        load_gain(gain2, I["ret_gn_gain"])
        Rst = xin[0]; osbs = [xin[1], gain]; sgr = wk32[0]; sq = wk32[1]; tmp = wk32[2]
        Rbf = wk16[0]; ATs = wk16[1]; rob = wk16[2]
        stt = sm[2]
        P.op("pool", lambda e: e.memset(Rst[0:64, :], 0.0), w=[Rst])
        P.op("pool", lambda e: e.memset(Rbf[0:64, :], 0.0), w=[Rbf])
        sq3 = sq[:, :].rearrange("p (h e) -> p h e", h=8)
        tmp3 = tmp[:, :].rearrange("p (h e) -> p h e", h=8)

        def stage1(n):
            q_, k_, z_ = bufs[n % 2]
            vt = hb[n % 2]
            osb = osbs[n % 2]
            P.dma(q_, q_[:, :].rearrange("d (h t) -> d h t", h=8), QrT_t[n], QrT_t[n][:].rearrange("(h d) t -> d h t", d=64))
            P.dma(k_, k_[:, :].rearrange("d (h t) -> d h t", h=8), KrT_t[n], KrT_t[n][:].rearrange("(h d) t -> d h t", d=64))
            P.dma(z_, z_[:, :], kz_t[n], kz_t[n][:, :])
            P.dma(vt, vt[:, :], vr_t[n], vr_t[n][:, :])
            for h in range(8):
                b = pb[h // 4]
                mm(b, b[:, (h % 4) * 128:(h % 4 + 1) * 128], k_, k_[:, h * 128:(h + 1) * 128], q_, q_[:, h * 128:(h + 1) * 128], True, True)
            if n > 0:
                for h in range(8):
                    b = pb[4 + h // 4]
                    mm(b, b[:, (h % 4) * 128:(h % 4 + 1) * 128], q_, q_[:, h * 128:(h + 1) * 128], Rbf, Rbf[0:64, h * 128:(h + 1) * 128], True, True)
            for hh in range(2):
                P.op("dve", lambda e, hh=hh: e.tensor_tensor(out=ATs[:, hh * 512:(hh + 1) * 512], in0=pb[hh][:], in1=dec[hh][:, :],
                                                             op=ALU.mult), r=[pb[hh], dec[hh]], w=[ATs])
            for h in range(8):
                b = pb[2 + h // 4]
                mm(b, b[:, (h % 4) * 128:(h % 4 + 1) * 128], ATs, ATs[:, h * 128:(h + 1) * 128], vt, vt[:, h * 128:(h + 1) * 128], True, True)
            for hh in range(2):
                b = pb[6]
                for h4 in range(4):
                    h = hh * 4 + h4
                    mm(b, b[0:64, h4 * 128:(h4 + 1) * 128], z_, z_[:, h * 64:(h + 1) * 64], vt, vt[:, h * 128:(h + 1) * 128], True, True)
                P.op("pool", lambda e, hh=hh: e.tensor_tensor(out=Rst[0:64, hh * 512:(hh + 1) * 512], in0=Rst[0:64, hh * 512:(hh + 1) * 512],
                                                              in1=gct[hh][0:64, :], op=ALU.mult), r=[Rst, gct[hh]], w=[Rst])
                P.op("dve", lambda e, hh=hh, b=b: e.tensor_tensor(out=Rst[0:64, hh * 512:(hh + 1) * 512], in0=b[0:64, :],
                                                                  in1=Rst[0:64, hh * 512:(hh + 1) * 512], op=ALU.add), r=[b, Rst], w=[Rst])
            P.op("act", lambda e: e.copy(out=Rbf[0:64, :], in_=Rst[0:64, :]), r=[Rst], w=[Rbf])
            for hh in range(2):
                P.op("act", lambda e, hh=hh: e.copy(out=osb[:, hh * 512:(hh + 1) * 512], in_=pb[2 + hh][:]), r=[pb[2 + hh]], w=[osb])
            if n > 0:
                for hh in range(2):
                    P.op("dve", lambda e, hh=hh: e.tensor_tensor(
                        out=tmp3[:, hh * 4:(hh + 1) * 4, :], in0=pb[4 + hh][:].rearrange("p (h e) -> p h e", h=4),
                        in1=xit[:, hh * 4:(hh + 1) * 4].unsqueeze(2).broadcast_to([128, 4, 128]), op=ALU.mult),
                        r=[pb[4 + hh], xit], w=[tmp])
                P.op("pool", lambda e: e.tensor_tensor(out=osb[:, :], in0=osb[:, :], in1=tmp[:, :], op=ALU.add), r=[osb, tmp], w=[osb])

        def stage2(n):
            osb = osbs[n % 2]
            osb3 = osb[:, :].rearrange("p (h e) -> p h e", h=8)
            P.dma(sgr, sgr[:, :], sgr_t[n], sgr_t[n][:, :])
            P.op("dve", lambda e: e.tensor_reduce(out=stt[:, 0:8], in_=osb3, axis=AX.X, op=ALU.add), r=[osb], w=[stt])
            P.op("act", lambda e: e.activation(out=sq[:, :], in_=osb[:, :], func=AF.Square), r=[osb], w=[sq])
            P.op("dve", lambda e: e.tensor_reduce(out=stt[:, 8:16], in_=sq3, axis=AX.X, op=ALU.add), r=[sq], w=[stt])
            P.op("dve", lambda e: e.tensor_scalar(out=stt[:, 0:8], in0=stt[:, 0:8], scalar1=1.0 / 128, scalar2=None, op0=ALU.mult),
                 r=[stt], w=[stt])
            P.op("dve", lambda e: e.tensor_tensor(out=stt[:, 16:24], in0=stt[:, 0:8], in1=stt[:, 0:8], op=ALU.mult), r=[stt], w=[stt])
            P.op("dve", lambda e: e.scalar_tensor_tensor(out=stt[:, 24:32], in0=stt[:, 8:16], scalar=1.0 / 128, in1=stt[:, 16:24],
                                                         op0=ALU.mult, op1=ALU.subtract), r=[stt], w=[stt])
            P.op("dve", lambda e: e.tensor_scalar(out=stt[:, 24:32], in0=stt[:, 24:32], scalar1=1e-5, scalar2=None, op0=ALU.add),
                 r=[stt], w=[stt])
            P.op("pool", lambda e: e.tensor_tensor(out=stt[:, 40:48], in0=stt[:, 24:32], in1=negh[:, 0:8], op=ALU.pow),
                 r=[stt, negh], w=[stt])
            P.op("dve", lambda e: e.scalar_tensor_tensor(out=stt[:, 48:56], in0=stt[:, 0:8], scalar=-1.0, in1=stt[:, 40:48],
                                                         op0=ALU.mult, op1=ALU.mult), r=[stt], w=[stt])
            for h in range(8):
                P.op("act", lambda e, h=h: e.activation(out=osb3[:, h, :], in_=osb3[:, h, :], func=AF.Identity,
                                                        scale=stt[:, 40 + h:41 + h], bias=stt[:, 48 + h:49 + h]),
                     r=[osb, stt], w=[osb])
            P.op("dve", lambda e: e.tensor_tensor(out=osb[:, :], in0=osb[:, :], in1=gain2[:, :], op=ALU.mult), r=[osb, gain2], w=[osb])
            P.op("pool", lambda e: e.tensor_tensor(out=rob[:, :], in0=osb[:, :], in1=sgr[:, :], op=ALU.mult), r=[osb, sgr], w=[rob])
            transpose8(rob, hT, hT[:, :, 0:128], eng="act")
            dst = rT_t[n]
            P.dma(dst, dst[:].rearrange("(k r) t -> r k t", r=128), hT, hT[:, :, 0:128], q="pool")

        nn = NT if lim is None else lim
        for n in range(nn):
            stage1(n)
            if n > 0:
                stage2(n - 1)
        stage2(nn - 1)
        uncarve(big, subs)

    def merge_phase(slot):
        Wn = load_w(slot, 0, I["w_branch_nsa"], I["w_branch_nsa"].ap.rearrange("(k p) f -> p k f", p=128), [128, 8, 1024])
        Wr = load_w(slot, 8192, I["w_branch_ret"], I["w_branch_ret"].ap.rearrange("(k p) f -> p k f", p=128), [128, 8, 1024])
        Wo = load_w(slot, 16384, I["w_out"], I["w_out"].ap.rearrange("(k p) f -> p k f", p=128), [128, 8, 1024])
        yield
        subs = []
        bufs = []
        for i in range(2):
            a_ = carve(big, big.ap[:, i * 2048:i * 2048 + 1024], f"aTt{i}")
            r_ = carve(big, big.ap[:, i * 2048 + 1024:i * 2048 + 2048], f"rTt{i}")
            subs += [a_, r_]
            bufs.append((a_, r_))
        for ti in range(NT if lim is None else lim):
            a_, r_ = bufs[ti % 2]
            a3 = a_[:, :].rearrange("p (k t) -> p k t", k=8)
            r3 = r_[:, :].rearrange("p (k t) -> p k t", k=8)
            P.dma(a_, a3, aT_t[ti], aT_t[ti][:].rearrange("(k r) t -> r k t", r=128))
            P.dma(r_, r3, rT_t[ti], rT_t[ti][:].rearrange("(k r) t -> r k t", r=128))
            ga, gr = wk32[0], wk32[1]
            P.dma(ga, ga[:, :], gm_t[ti], gm_t[ti][:, 0:1024])
            P.dma(gr, gr[:, :], gm_t[ti], gm_t[ti][:, 1024:2048])
            x1s = wk32[2]
            P.dma(x1s, x1s[:, :], x1_t[ti], x1_t[ti][:, :])
            for half in range(2):
                for k in range(8):
                    mm(pb[half], pb[half][:], a_, a3[:, k, :], slot, Wn[:, k, half * 512:(half + 1) * 512], k == 0, k == 7)
                for k in range(8):
                    mm(pb[2 + half], pb[2 + half][:], r_, r3[:, k, :], slot, Wr[:, k, half * 512:(half + 1) * 512], k == 0, k == 7)
            m1, m2 = xin[0], xin[1]
            mb = hb[ti % 2]
            for half in range(2):
                cs = slice(half * 512, (half + 1) * 512)
                P.op("dve", lambda e, half=half, cs=cs: e.tensor_tensor(out=m1[:, cs], in0=pb[half][:], in1=ga[:, cs], op=ALU.mult),
                     r=[pb[half], ga], w=[m1])
                P.op("dve", lambda e, half=half, cs=cs: e.tensor_tensor(out=m2[:, cs], in0=pb[2 + half][:], in1=gr[:, cs], op=ALU.mult),
                     r=[pb[2 + half], gr], w=[m2])
            P.op("pool", lambda e, mb=mb: e.tensor_tensor(out=mb[:, :], in0=m1[:, :], in1=m2[:, :], op=ALU.add), r=[m1, m2], w=[mb])
            transpose8(mb, hT, hT[:, :, 0:128], eng="act")
            for half in range(2):
                po = pb[4 + half]
                for k in range(8):
                    mm(po, po[:], hT, hT[:, k, 0:128], slot, Wo[:, k, half * 512:(half + 1) * 512], k == 0, k == 7)
                P.op("dve", lambda e, po=po, half=half: e.tensor_tensor(out=x1s[:, half * 512:(half + 1) * 512], in0=po[:],
                                                                       in1=x1s[:, half * 512:(half + 1) * 512], op=ALU.add),
                     r=[po, x1s], w=[x1s])
            P.dma(x2_t[ti], x2_t[ti][:, :], x1s, x1s[:, :], q="pool")
        uncarve(big, subs)

    def begin(gen):
        next(gen)
        return gen

    def finish(gen):
        for _ in gen:
            pass

    allp = set(phases) == {"ffn1", "proj", "cmp", "nsa", "ret", "merge", "ffn2"} and not _os.environ.get("SKIPP1")
    out_t = [T(out_d.ap[i * 128:(i + 1) * 128], f"out{i}") for i in range(NT)]
    if allp:
        f1a = begin(ffn_pass(1, 0, wslot[0], x_t, x1_t, I["ffn1_norm"]))
        f1b = begin(ffn_pass(1, 1, wslot[1], x1_t, x1_t, I["ffn1_norm"]))
        finish(f1a)
        p0 = begin(proj_pass0(wslot[0]))
        finish(f1b)
        p1 = begin(proj_pass1(wslot[1]))
        finish(p0)
        finish(p1)
        cmp_phase(wslot[1])
        mg = begin(merge_phase(wslot[1]))
        nsa_phase(wslot[0])
        f2a = begin(ffn_pass(2, 0, wslot[0], x2_t, x2_t, I["ffn2_norm"]))
        ret_phase()
        finish(mg)
        f2b = begin(ffn_pass(2, 1, wslot[1], x2_t, out_t, I["ffn2_norm"], final=True))
        finish(f2a)
        finish(f2b)
    else:
        if "ffn1" in phases:
            finish(ffn_pass(1, 0, wslot[0], x_t, x1_t, I["ffn1_norm"]))
            finish(ffn_pass(1, 1, wslot[1], x1_t, x1_t, I["ffn1_norm"]))
        if "proj" in phases:
            finish(proj_pass0(wslot[0]))
            if not _os.environ.get("SKIPP1"):
                finish(proj_pass1(wslot[1]))
        if "cmp" in phases:
            cmp_phase(wslot[1])
        if "nsa" in phases:
            nsa_phase(wslot[0])
        if "ret" in phases:
            ret_phase()
        if "merge" in phases:
            finish(merge_phase(wslot[1]))
        if "ffn2" in phases:
            finish(ffn_pass(2, 0, wslot[0], x2_t, x2_t, I["ffn2_norm"]))
            finish(ffn_pass(2, 1, wslot[1], x2_t, out_t, I["ffn2_norm"], final=True))
    P.emit()
    return nc, P


_CACHE = {}


def kernel(**inputs):
    n_cores = 8
    if "nc" not in _CACHE:
        _CACHE["nc"] = build()[0]
        _CACHE["consts"] = {k: np.ascontiguousarray(v.reshape(CONST_SHAPES[k]).astype(np.float32))
                            for k, v in _consts().items()}
    nc = _CACHE["nc"]
    cst = _CACHE["consts"]
    shared = {}
    for k, shp in IN_SHAPES.items():
        if k == "x":
            continue
        shared[k] = np.ascontiguousarray(np.asarray(inputs[k], dtype=np.float32).reshape(shp))
    x = np.asarray(inputs["x"], dtype=np.float32)
    in_maps = []
    for c in range(n_cores):
        m = dict(shared)
        m.update(cst)
        m["x"] = np.ascontiguousarray(x[c % 4])
        in_maps.append(m)
    res = run_bass_kernel_spmd(nc, in_maps, core_ids=list(range(n_cores)))
    out = np.stack([np.asarray(res.results[b]["out"], dtype=np.float32) for b in range(4)], axis=0)
    return out
```

```python
import contextlib
import os as _os
import numpy as np
import concourse.bass as bass
import concourse.mybir as mybir
from concourse.bass_utils import run_bass_kernel_spmd

F32 = mybir.dt.float32
BF16 = mybir.dt.bfloat16
AF = mybir.ActivationFunctionType
ALU = mybir.AluOpType
AX = mybir.AxisListType

S = 4096
D = 1024
DFF = 2816
NT = S // 128
DIN = 6960


class T:
    __slots__ = ("ap", "name", "last_w", "readers", "excl")

    def __init__(self, ap, name="", excl=False):
        self.ap = ap
        self.name = name
        self.last_w = None
        self.readers = []
        self.excl = excl

    def __getitem__(self, k):
        return self.ap[k]


class Prog:
    ENGS = ("pe", "act", "dve", "pool", "sp")

    def __init__(self, nc, n_dma_sems=12, same_engine_sync=True):
        self.nc = nc
        self.ops = []
        self.es = contextlib.ExitStack()
        self.same_engine_sync = same_engine_sync
        self.n_dma_sems = n_dma_sems
        self.ncnt = 0

    def sb(self, shape, dt, name=None):
        self.ncnt += 1
        name = name or f"sb{self.ncnt}"
        t = self.es.enter_context(self.nc.sbuf_tensor(name, list(shape), dt))
        return T(t[:], name)

    def ps(self, shape, dt, name=None):
        self.ncnt += 1
        name = name or f"ps{self.ncnt}"
        t = self.es.enter_context(self.nc.psum_tensor(name, list(shape), dt))
        return T(t[:], name, excl=True)

    def dram(self, name, shape, dt, kind="Internal"):
        t = self.nc.dram_tensor(name, list(shape), dt, kind=kind)
        return T(t.ap(), name)

    def op(self, eng, fn, r=(), w=(), dma=False):
        idx = len(self.ops)
        deps = set()
        raw = set()
        for t in r:
            if t.last_w is not None:
                deps.add(t.last_w)
                raw.add(t.last_w)
            if t.excl:
                for rd in t.readers:
                    if self.ops[rd]["eng"] != eng:
                        deps.add(rd)
        for t in w:
            if t.last_w is not None:
                deps.add(t.last_w)
            deps.update(t.readers)
        for t in r:
            t.readers.append(idx)
        for t in w:
            t.last_w = idx
            t.readers = []
        deps.discard(idx)
        self.ops.append({"eng": eng, "fn": fn, "deps": deps, "dma": dma, "raw": raw, "tag": getattr(self, "tag", "")})
        return idx

    def dma(self, out_t, out_ap, in_t, in_ap, q="sp", **kw):
        return self.op(q, lambda e: e.dma_start(out=out_ap, in_=in_ap, **kw),
                       r=[in_t], w=[out_t], dma=True)

    def emit(self):
        nc = self.nc
        ops = self.ops
        nops = len(ops)
        dma_use = {}
        for i, o in enumerate(ops):
            if o["dma"]:
                dma_use.setdefault(o["eng"], []).append(i)
        dma_sem = {}
        for q, lst in dma_use.items():
            for k, i in enumerate(lst):
                slot = k % self.n_dma_sems
                val = 16 * (k // self.n_dma_sems + 1)
                dma_sem[i] = (q, slot, val)
                if k >= self.n_dma_sems:
                    ops[i]["deps"].add(lst[k - self.n_dma_sems])
        waited = {e: {} for e in self.ENGS}
        waited_dma = {e: set() for e in self.ENGS}
        plan = [None] * nops
        signaling = set()
        for i, o in enumerate(ops):
            e = o["eng"]
            need = {}
            need_dma = []
            for d in sorted(o["deps"]):
                po = ops[d]
                if po["dma"]:
                    if d not in waited_dma[e]:
                        waited_dma[e].add(d)
                        need_dma.append(d)
                    continue
                p = po["eng"]
                if p == e and (e == "pe" or not self.same_engine_sync or d not in o["raw"]):
                    continue
                if waited[e].get(p, -1) >= d:
                    continue
                need[p] = max(need.get(p, -1), d)
            for p, d in need.items():
                waited[e][p] = d
                signaling.add(d)
            plan[i] = (need, need_dma)
        semval = {}
        cnt = {e: 0 for e in self.ENGS}
        for i, o in enumerate(ops):
            if o["dma"]:
                continue
            if i in signaling:
                cnt[o["eng"]] += 1
                semval[i] = cnt[o["eng"]]
        self.stats = {e: sum(1 for o in ops if o["eng"] == e) for e in self.ENGS}
        self.stats["signals"] = dict(cnt)
        es = self.es
        sems = {e: es.enter_context(nc.semaphore(f"sem_{e}")) for e in self.ENGS}
        dsems = {q: [es.enter_context(nc.semaphore(f"dsem_{q}{k}")) for k in range(self.n_dma_sems)]
                 for q in dma_use}
        block = es.enter_context(nc.Block())

        def body(ename):
            def f(eng):
                for i, o in enumerate(ops):
                    if o["eng"] != ename:
                        continue
                    need, need_dma = plan[i]
                    for p, d in need.items():
                        eng.wait_ge(sems[p], semval[d])
                    for d in need_dma:
                        q, slot, val = dma_sem[d]
                        eng.wait_ge(dsems[q][slot], val)
                    ins = o["fn"](eng)
                    if o["dma"]:
                        q, slot, val = dma_sem[i]
                        ins.then_inc(dsems[q][slot], 16)
                    elif i in signaling:
                        ins.then_inc(sems[ename], 1)
                for q, lst in dma_use.items():
                    if q != ename:
                        continue
                    last = {}
                    for i in lst:
                        _, slot, val = dma_sem[i]
                        last[slot] = val
                    for slot, val in last.items():
                        eng.wait_ge(dsems[q][slot], val)
            return f

        block.tensor(body("pe"))
        block.scalar(body("act"))
        block.vector(body("dve"))
        block.gpsimd(body("pool"))
        block.sync(body("sp"))

    def close(self):
        self.es.close()


def _consts():
    c = {}
    pos = np.arange(S, dtype=np.float32)

    def rope_tabs(p, rot, theta, nrep):
        half = rot // 2
        fr = (np.float32(theta) ** (-(np.arange(half, dtype=np.float32) * np.float32(2.0) / np.float32(rot)))).astype(np.float32)
        ang = p.astype(np.float32)[:, None] * fr[None, :]
        cs, sn = np.cos(ang).astype(np.float32), np.sin(ang).astype(np.float32)
        C = np.ones((len(p), 64), np.float32)
        C[:, 0:half] = cs
        C[:, half:rot] = cs
        Sg = np.concatenate([-sn, sn], 1)
        return np.tile(C, (1, nrep)), np.tile(Sg, (1, nrep))

    c["ropeCa"], c["ropeSa"] = rope_tabs(pos, 16, 500000.0, 8)
    pc = np.arange(256, dtype=np.float32) * 16 + 31
    c["ropeCc"], c["ropeSc"] = rope_tabs(pc, 16, 500000.0, 1)
    c["ropeCr"], c["ropeSr"] = rope_tabs(pos, 64, 10000.0, 8)
    lg = np.log(1.0 - 2.0 ** (-5.0 - np.arange(8, dtype=np.float64)))
    i = np.arange(128, dtype=np.float64)
    diff = i[None, :] - i[:, None]
    dT = np.zeros((128, 8, 128), np.float64)
    for h in range(8):
        dT[:, h, :] = np.where(diff >= 0, np.exp(np.maximum(diff, 0) * lg[h]), 0.0) * 0.125
    c["decayT"] = dT.astype(np.float32).reshape(128, 1024)
    zeta = np.exp((127.0 - i)[None, :] * lg[:, None])
    c["ZT"] = np.repeat(zeta.T[:, :, None], 64, axis=2).reshape(128, 512).astype(np.float32)
    xi = np.exp((i + 1.0)[None, :] * lg[:, None]) * 0.125
    c["XI"] = np.ascontiguousarray(xi.T).astype(np.float32)
    gch = np.exp(128.0 * lg)
    c["GC"] = np.broadcast_to(gch[None, :, None], (64, 8, 128)).reshape(64, 1024).astype(np.float32)
    n = np.arange(256)
    cs_ = n * 16
    ss_ = np.arange(64) * 64
    ov = np.clip(np.minimum(cs_[:, None] + 32, ss_[None, :] + 64) - np.maximum(cs_[:, None], ss_[None, :]), 0, None)
    c2s = ov.astype(np.float32) / 32.0
    c2s[255] = 0.0
    c["C2S"] = c2s
    key = np.arange(S)
    c["EM"] = (key[None, :] // 64 == np.arange(64)[:, None]).astype(np.float32)
    sp = np.arange(128) - 64
    curp = (np.arange(128) >= 64).astype(np.int64)[:, None]
    forced = (sp[None, :] == curp) | (sp[None, :] == curp - 1)
    future = sp[None, :] > curp
    c["TK"] = (~(forced | future)).astype(np.float32)
    c["TA"] = np.where(forced, 1.0e4, np.where(future, -1.0e4, 0.0)).astype(np.float32)
    return c


CONST_SHAPES = {"ropeCa": [S, 512], "ropeSa": [S, 128], "ropeCc": [256, 64], "ropeSc": [256, 16],
                "ropeCr": [S, 512], "ropeSr": [S, 512], "decayT": [128, 1024], "ZT": [128, 512],
                "XI": [128, 8], "GC": [64, 1024], "C2S": [256, 64], "EM": [64, S],
                "TK": [128, 128], "TA": [128, 128]}

IN_SHAPES = {"x": [S, D], "ffn1_norm": [1, D], "ffn1_w_gate": [D, DFF], "ffn1_w_up": [D, DFF],
             "ffn1_w_down": [DFF, D], "mix_norm": [1, D], "w_in": [D, DIN], "cmp_pos_emb": [32, 64],
             "cmp_k_w1": [2048, 256], "cmp_k_w2": [256, 64], "cmp_v_w1": [2048, 256], "cmp_v_w2": [256, 64],
             "ret_gn_gain": [1, 1024], "w_branch_nsa": [D, D], "w_branch_ret": [D, D], "w_out": [D, D],
             "ffn2_norm": [1, D], "ffn2_w_gate": [D, DFF], "ffn2_w_up": [D, DFF], "ffn2_w_down": [DFF, D],
             "final_norm": [1, D]}

G_CHUNK = [float(np.exp(128.0 * np.log(1.0 - 2.0 ** (-5.0 - h)))) for h in range(8)]
NEGBIG = -2.0e4


def build(phases=("ffn1", "proj", "cmp", "nsa", "ret", "merge", "ffn2"), dbg=(), lim=None):
    nc = bass.Bass("TRN2", target_bir_lowering=False)
    P = Prog(nc)
    lim_c = [0, 5, 17] if lim else None
    I = {k: P.dram(k, v, F32, kind="ExternalInput") for k, v in IN_SHAPES.items()}
    C = {k: P.dram(k, v, F32, kind="ExternalInput") for k, v in CONST_SHAPES.items()}
    out_d = P.dram("out", [S, D], F32, kind="ExternalOutput")

    def scratch(name, shape, dt):
        return P.dram(name, shape, dt, kind=("ExternalOutput" if name in dbg else "Internal"))

    def tiles(t, n=NT, rows=128):
        return [T(t.ap[i * rows:(i + 1) * rows], f"{t.name}{i}") for i in range(n)]

    def ctiles(t, n=NT, cols=128):
        return [T(t.ap[..., i * cols:(i + 1) * cols], f"{t.name}{i}") for i in range(n)]

    x_t = tiles(I["x"])
    x1_d = scratch("x1_d", [S, D], F32); x1_t = tiles(x1_d)
    x2_d = scratch("x2_d", [S, D], F32); x2_t = tiles(x2_d)
    hT_d = scratch("hT_d", [8, 128, 8, 512], BF16)
    hT_t = [T(hT_d.ap[i], f"hT{i}") for i in range(8)]
    hmT_d = scratch("hmT_d", [NT, 128, 8, 128], BF16)
    hmT_t = [T(hmT_d.ap[i], f"hmT{i}") for i in range(NT)]
    QT_d = scratch("QT_d", [2, 512, S], BF16)
    QT_t = [ctiles(T(QT_d.ap[g], f"QT{g}_")) for g in range(2)]
    kcT_d = scratch("kcT_d", [128, S], BF16); kcT_t = ctiles(kcT_d)
    vcT_d = scratch("vcT_d", [128, S], BF16); vcT_t = ctiles(vcT_d)
    ksT_d = scratch("ksT_d", [128, S], BF16); ksT_t = ctiles(ksT_d)
    kwT_d = scratch("kwT_d", [128, S], BF16); kwT_t = ctiles(kwT_d)
    vs_d = scratch("vs_d", [S, 128], BF16); vs_t = tiles(vs_d)
    vw_d = scratch("vw_d", [S, 128], BF16); vw_t = tiles(vw_d)
    ga_d = scratch("ga_d", [S, 48], F32); ga_t = tiles(ga_d)
    QrT_d = scratch("QrT_d", [512, S], BF16); QrT_t = ctiles(QrT_d)
    KrT_d = scratch("KrT_d", [512, S], BF16); KrT_t = ctiles(KrT_d)
    kz_d = scratch("kz_d", [S, 512], BF16); kz_t = tiles(kz_d)
    vr_d = scratch("vr_d", [S, 1024], BF16); vr_t = tiles(vr_d)
    sgr_d = scratch("sgr_d", [S, 1024], F32); sgr_t = tiles(sgr_d)
    gm_d = scratch("gm_d", [S, 2048], F32); gm_t = tiles(gm_d)
    kc_d = scratch("kc_d", [2, 256, 64], BF16)
    vc_d = scratch("vc_d", [2, 256, 64], BF16)
    aT_d = scratch("aT_d", [1024, S], BF16); aT_t = ctiles(aT_d)
    rT_d = scratch("rT_d", [1024, S], BF16); rT_t = ctiles(rT_d)

    WSLOT = 33792
    wslot = [P.sb([128, WSLOT], BF16, f"wslot{i}") for i in range(2)]
    ident = P.sb([128, 128], BF16, "ident")
    gain = P.sb([128, 1024], F32, "gain")
    gain2 = P.sb([128, 1024], F32, "gain2")
    xin = [P.sb([128, 1024], F32, f"xin{i}") for i in range(2)]
    xres = xin
    hb = [P.sb([128, 1024], BF16, f"hb{i}") for i in range(2)]
    st1 = [P.sb([128, 4], F32, f"st1_{i}") for i in range(2)]
    big = P.sb([128, 5632], BF16, "big")
    hT = P.sb([128, 8, 512], BF16, "hT")
    wk32 = [P.sb([128, 1024], F32, f"wk32_{i}") for i in range(3)]
    wk16 = [P.sb([128, 1024], BF16, f"wk16_{i}") for i in range(3)]
    sm = [P.sb([128, 64], F32, f"sm{i}") for i in range(8)]
    tCa = P.sb([128, 512], F32, "tCa"); tSa = P.sb([128, 128], F32, "tSa")
    tCr = P.sb([128, 512], F32, "tCr"); tSr = P.sb([128, 512], F32, "tSr")
    tZ = P.sb([128, 512], F32, "tZ")
    pb = [P.ps([128, 512], F32, f"pb{i}") for i in range(7)]
    pbf = P.ps([128, 1024], BF16, "pbf")

    negh = P.sb([128, 8], F32, "negh")
    gat3 = P.sb([128, 32], F32, "gat3")
    P.op("pool", lambda e: e.memset(negh[:], -0.5), w=[negh])
    P.op("pool", lambda e: e.memset(ident[:], 0.0), w=[ident])
    P.op("pool", lambda e: e.affine_select(out=ident[:], in_=ident[:], pattern=[[-1, 128]], compare_op=ALU.not_equal,
                                           fill=1.0, base=0, channel_multiplier=1), r=[ident], w=[ident])

    cnt = {"rr": 0}

    def rr(lst):
        cnt["rr"] += 1
        return lst[cnt["rr"] % len(lst)]

    def mm(out_t, out_ap, a_t, a_ap, b_t, b_ap, start, stop, sgc=False):
        P.op("pe", lambda e: e.matmul(out_ap, lhsT=a_ap, rhs=b_ap, start=start, stop=stop, skip_group_check=sgc),
             r=[a_t, b_t], w=[out_t])

    def tr(out_t, out_ap, in_t, in_ap):
        P.op("pe", lambda e: e.transpose(out=out_ap, in_=in_ap, identity=ident[:]), r=[in_t, ident], w=[out_t])

    def load_w(slot_t, off, src_t, src_ap, shape):
        n = int(np.prod(shape[1:]))
        dst = slot_t[:, off:off + n]
        if len(shape) == 3:
            dst = dst.rearrange("p (a b) -> p a b", a=shape[1])
        P.dma(slot_t, dst, src_t, src_ap, q="pool")
        return dst

    def rmsnorm(x_tile, gain_t, hb_t, stt, sq=None):
        sq = hb_t
        P.op("act", lambda e: e.activation(out=sq[:], in_=x_tile[:], func=AF.Square, accum_out=stt[:, 0:1]),
             r=[x_tile], w=[sq, stt])
        P.op("dve", lambda e: e.tensor_scalar(out=stt[:, 1:2], in0=stt[:, 0:1], scalar1=1.0 / D, scalar2=1e-6,
                                              op0=ALU.mult, op1=ALU.add), r=[stt], w=[stt])
        P.op("pool", lambda e: e.tensor_tensor(out=stt[:, 3:4], in0=stt[:, 1:2], in1=negh[:, 0:1], op=ALU.pow),
             r=[stt, negh], w=[stt])
        P.op("dve", lambda e: e.scalar_tensor_tensor(out=hb_t[:], in0=x_tile[:], scalar=stt[:, 3:4], in1=gain_t[:],
                                                     op0=ALU.mult, op1=ALU.mult), r=[x_tile, stt, gain_t], w=[hb_t])

    pb6bf = pb[6].ap.bitcast(BF16)

    def transpose8(src_t, dst_t, dst_ap, nblk=8, eng="act", alt=False):
        bank_t, bank = (pb[6], pb6bf) if alt else (pbf, pbf.ap)
        for k in range(nblk):
            tr(bank_t, bank[:, k * 128:(k + 1) * 128], src_t, src_t[:, k * 128:(k + 1) * 128])
        src = bank[:, 0:nblk * 128].rearrange("p (k t) -> p k t", k=nblk)
        if eng == "act":
            P.op("act", lambda e: e.copy(out=dst_ap, in_=src), r=[bank_t], w=[dst_t])
        else:
            P.op("dve", lambda e: e.tensor_copy(out=dst_ap, in_=src), r=[bank_t], w=[dst_t])

    def load_gain(gt, src):
        P.dma(gt, gt[:], src, src.ap.partition_broadcast(128))

    def carve(parent, ap, name=""):
        t = T(ap, name)
        t.last_w = parent.last_w
        t.readers = list(parent.readers)
        return t

    def uncarve(parent, subs):
        for t in subs:
            parent.readers.extend(t.readers)
            if t.last_w is not None:
                parent.readers.append(t.last_w)

    def ffn_pass(which, hf, slot, xsrc_t, xdst_t, norm_gain, final=False):
        pre = f"ffn{which}_"
        wg_d, wu_d, wd_d = I[pre + "w_gate"], I[pre + "w_up"], I[pre + "w_down"]
        c0 = hf * 1408
        wg = load_w(slot, 0, wg_d, wg_d.ap[:, c0:c0 + 1408].rearrange("(k p) f -> p k f", p=128), [128, 8, 1408])
        wu = load_w(slot, 11264, wu_d, wu_d.ap[:, c0:c0 + 1408].rearrange("(k p) f -> p k f", p=128), [128, 8, 1408])
        wd = load_w(slot, 22528, wd_d, wd_d.ap[c0:c0 + 1408, :].rearrange("(k p) d -> p k d", p=128), [128, 11, 1024])
        yield
        aT = big[:, 0:11 * 512].rearrange("p (f t) -> p f t", f=11)
        if hf == 0:
            load_gain(gain, norm_gain)
        if final:
            load_gain(gain2, I["final_norm"])
        nst = 8 if lim is None else min(lim, 8)
        sgs = [carve(wk32[0], wk32[0].ap[:, 0:512], "sga"), carve(wk32[0], wk32[0].ap[:, 512:1024], "sgb")]
        xs = [xin[0], xin[1], wk32[1], wk32[2]]
        hbs = [hb[0], hb[1], wk16[0], wk16[1]]
        st4 = [st1[0], st1[1], sm[6], sm[7]]
        if hf == 0:
            xrs = [[(tCa, tCa[:, :]), (tCr, tCr[:, :])], [(tSr, tSr[:, :]), (tZ, tZ[:, :])]]
        else:
            xrs = [[(x_, x_[:, 0:512]), (x_, x_[:, 512:1024])] for x_ in xs]

        def norm_part(st):
            for tt in range(4):
                ti = st * 4 + tt
                xt = xs[tt]
                P.dma(xt, xt[:], xsrc_t[ti], xsrc_t[ti][:, :])
                rmsnorm(xt, gain, hbs[tt], st4[tt], sq=wk16[2])

        def trans_part(st):
            for tt in range(4):
                transpose8(hbs[tt], hT, hT[:, :, tt * 128:(tt + 1) * 128], eng="act")
            P.dma(hT_t[st], hT_t[st][:], hT, hT[:], q="pool")

        def gu_part(st):
            for f in range(11):
                pg = pb[(2 * f) % 4]
                pu = pb[(2 * f + 1) % 4]
                for k in range(8):
                    mm(pg, pg[:], slot, wg[:, k, f * 128:(f + 1) * 128], hT, hT[:, k, :], k == 0, k == 7)
                for k in range(8):
                    mm(pu, pu[:], slot, wu[:, k, f * 128:(f + 1) * 128], hT, hT[:, k, :], k == 0, k == 7)
                sg = sgs[f % 2]
                P.op("act", lambda e, sg=sg, pg=pg: e.activation(out=sg[:, :], in_=pg[:], func=AF.Silu), r=[pg], w=[sg])
                P.op("dve", lambda e, sg=sg, pu=pu, f=f: e.tensor_tensor(out=aT[:, f, :], in0=pu[:], in1=sg[:, :], op=ALU.mult),
                     r=[pu, sg], w=[big])

        def down_part(st):
            for tt in range(4):
                ti = st * 4 + tt
                xr = xrs[ti % len(xrs)]
                for half in range(2):
                    P.dma(xr[half][0], xr[half][1], xsrc_t[ti], xsrc_t[ti][:, half * 512:(half + 1) * 512])
                for half in range(2):
                    py = pb[4 + (2 * tt + half) % 3]
                    for f in range(11):
                        mm(py, py[:], big, aT[:, f, tt * 128:(tt + 1) * 128], slot, wd[:, f, half * 512:(half + 1) * 512],
                           f == 0, f == 10)
                    xh_t, xh = xr[half]
                    P.op("dve", lambda e, xh=xh, py=py: e.scalar_tensor_tensor(out=xh, in0=py[:], scalar=0.5, in1=xh,
                                                                              op0=ALU.mult, op1=ALU.add), r=[py, xh_t], w=[xh_t])
                if final:
                    xf = xr[0][0]
                    ob = gain
                    stt = st4[tt]
                    P.op("act", lambda e, xf=xf, stt=stt: e.activation(out=ob[:], in_=xf[:], func=AF.Square,
                                                                       accum_out=stt[:, 0:1]), r=[xf], w=[ob, stt])
                    P.op("dve", lambda e, stt=stt: e.tensor_scalar(out=stt[:, 1:2], in0=stt[:, 0:1], scalar1=1.0 / D,
                                                                   scalar2=1e-6, op0=ALU.mult, op1=ALU.add), r=[stt], w=[stt])
                    P.op("pool", lambda e, stt=stt: e.tensor_tensor(out=stt[:, 3:4], in0=stt[:, 1:2], in1=negh[:, 0:1], op=ALU.pow),
                         r=[stt, negh], w=[stt])
                    P.op("dve", lambda e, xf=xf, stt=stt: e.scalar_tensor_tensor(out=ob[:], in0=xf[:], scalar=stt[:, 3:4],
                                                                                in1=gain2[:], op0=ALU.mult, op1=ALU.mult),
                         r=[xf, stt, gain2], w=[ob])
                    P.dma(xdst_t[ti], xdst_t[ti][:, :], ob, ob[:], q="pool")
                else:
                    for half in range(2):
                        P.dma(xdst_t[ti], xdst_t[ti][:, half * 512:(half + 1) * 512], xr[half][0], xr[half][1], q="pool")

        if hf == 0:
            norm_part(0)
            trans_part(0)
            for st in range(nst):
                gu_part(st)
                if st + 1 < nst:
                    norm_part(st + 1)
                down_part(st)
                if st + 1 < nst:
                    trans_part(st + 1)
        else:
            P.dma(hT, hT[:], hT_t[0], hT_t[0][:])
            for st in range(nst):
                gu_part(st)
                if st + 1 < nst:
                    P.dma(hT, hT[:], hT_t[st + 1], hT_t[st + 1][:])
                down_part(st)
        uncarve(wk32[0], sgs)

    def rope(src_t, src_ap, H, R, Ct, Cap, St, Sap, out_t, out_ap, o32, t2):
        n = H * 64
        half = R // 2
        P.op("dve", lambda e: e.tensor_tensor(out=o32[:, 0:n], in0=src_ap, in1=Cap, op=ALU.mult), r=[Ct, src_t], w=[o32])
        s3 = src_ap.rearrange("p (h d) -> p h d", h=H)
        S3 = Sap.rearrange("p (h r) -> p h r", h=H)
        t3 = t2[:, 0:H * R].rearrange("p (h r) -> p h r", h=H)
        o3 = o32[:, 0:n].rearrange("p (h d) -> p h d", h=H)
        P.op("dve", lambda e: e.tensor_tensor(out=t3[:, :, 0:half], in0=s3[:, :, half:R], in1=S3[:, :, 0:half],
                                              op=ALU.mult), r=[St, src_t], w=[t2])
        P.op("dve", lambda e: e.tensor_tensor(out=t3[:, :, half:R], in0=s3[:, :, 0:half], in1=S3[:, :, half:R],
                                              op=ALU.mult), r=[St, src_t], w=[t2])
        P.op("dve", lambda e: e.tensor_tensor(out=o3[:, :, 0:R], in0=o3[:, :, 0:R], in1=t3, op=ALU.add),
             r=[o32, t2], w=[o32])
        P.op("act", lambda e: e.copy(out=out_ap, in_=o32[:, 0:n]), r=[o32], w=[out_t])

    def proj_pass0(slot):
        w_d = I["w_in"]
        w0 = load_w(slot, 0, w_d, w_d.ap[:, 0:1944].rearrange("(k p) f -> p k f", p=128), [128, 8, 1944])
        w1 = load_w(slot, 8 * 1944, w_d, w_d.ap[:, 1944:3888].rearrange("(k p) f -> p k f", p=128), [128, 8, 1944])
        yield

        def wcol(k, c0, c1):
            if c1 <= 1944:
                return w0[:, k, c0:c1]
            assert c0 >= 1944
            return w1[:, k, c0 - 1944:c1 - 1944]

        load_gain(gain, I["mix_norm"])
        P.dma(tZ, tZ[:], C["ZT"], C["ZT"][:, :])
        qTs = [P.sb([128, 4, 128], BF16, "qTa"), P.sb([128, 4, 128], BF16, "qTb")]
        blocks = [(0, 512), (512, 1024), (1024, 1536), (1536, 1840), (1840, 2352), (2352, 2864), (2864, 3376), (3376, 3888)]
        subs_c = []
        hbuf = [carve(hT, hT.ap[:, :, 0:128], "hTa"), carve(hT, hT.ap[:, :, 128:256], "hTb")]
        o32s = [carve(wk32[0], wk32[0].ap[:, 0:512], "o32a"), carve(wk32[0], wk32[0].ap[:, 512:1024], "o32b")]
        t2s = [carve(wk32[1], wk32[1].ap[:, 0:512], "t2a"), carve(wk32[1], wk32[1].ap[:, 512:1024], "t2b")]
        subs_c = [(hT, hbuf), (wk32[0], o32s), (wk32[1], t2s)]
        ntl = NT if lim is None else lim
        rcnt = {"r": 0, "q": 0}

        def prep_x(ti):
            P.tag = f"p0.t{ti}.prep"
            xt = xin[ti % 2]
            P.dma(xt, xt[:], x1_t[ti], x1_t[ti][:, :])
            hbt = hb[ti % 2]
            rmsnorm(xt, gain, hbt, st1[ti % 2])
            hb_ = hbuf[ti % 2]
            transpose8(hbt, hb_, hb_[:, :, :], eng="act")
            P.dma(hmT_t[ti], hmT_t[ti][:], hb_, hb_[:, :, :], q="sp")

        def load_tabs_a(ti):
            P.dma(tCa, tCa[:], C["ropeCa"], C["ropeCa"][ti * 128:(ti + 1) * 128, :])
            P.dma(tSa, tSa[:], C["ropeSa"], C["ropeSa"][ti * 128:(ti + 1) * 128, :])

        def load_tabs_r(ti):
            P.dma(tCr, tCr[:], C["ropeCr"], C["ropeCr"][ti * 128:(ti + 1) * 128, :])
            P.dma(tSr, tSr[:], C["ropeSr"], C["ropeSr"][ti * 128:(ti + 1) * 128, :])

        def mm_block(ti, bi):
            P.tag = f"p0.t{ti}.mm{bi}"
            c0, c1 = blocks[bi]
            pp = pb[bi % 6]
            hb_ = hbuf[ti % 2]
            subs = [(c0, c1)] if not (c0 < 1944 < c1) else [(c0, 1944), (1944, c1)]
            for (s0, s1) in subs:
                for k in range(8):
                    mm(pp, pp[:, s0 - c0:s1 - c0], hb_, hb_[:, k, :], slot, wcol(k, s0, s1), k == 0, k == 7)

        def post_a(ti, bi):
            P.tag = f"p0.t{ti}.A{bi}"
            return post_a_(ti, bi)

        def post_a_(ti, bi):
            pp = pb[bi % 6]
            rcnt["r"] += 1
            o32, t2 = o32s[rcnt["r"] % 2], t2s[rcnt["r"] % 2]

            def nextq():
                rcnt["q"] += 1
                P.tag = f"p0.t{ti}.B{bi}"
                return qTs[rcnt["q"] % 2]
            if bi in (0, 1):
                ob = wk16[bi % 2]
                rope(pp, pp[:, 0:512], 8, 16, tCa, tCa[:, :], tSa, tSa[:, :], ob, ob[:, 0:512], o32, t2)

                def fb():
                    qT = nextq()
                    transpose8(ob, qT, qT[:], nblk=4, eng="dve", alt=rcnt["q"] % 2 == 1)
                    dst = QT_t[bi][ti]
                    P.dma(dst, dst[:].rearrange("(p r) t -> r p t", r=128), qT, qT[:], q="sp")
                return fb
            elif bi == 2:
                ob = wk16[2]
                P.op("act", lambda e: e.copy(out=ob[:, 0:256], in_=pp[:, 0:256]), r=[pp], w=[ob])
                rope(pp, pp[:, 256:384], 2, 16, tCa, tCa[:, 0:128], tSa, tSa[:, 0:32], ob, ob[:, 256:384], o32, t2)
                P.op("act", lambda e: e.copy(out=ob[:, 384:512], in_=pp[:, 384:512]), r=[pp], w=[ob])

                def fb():
                    qT = nextq()
                    transpose8(ob, qT, qT[:, 0:3, :], nblk=3, eng="dve", alt=rcnt["q"] % 2 == 1)
                    P.dma(kcT_t[ti], kcT_t[ti][:], qT, qT[:, 0, :], q="sp")
                    P.dma(vcT_t[ti], vcT_t[ti][:], qT, qT[:, 1, :], q="sp")
                    P.dma(ksT_t[ti], ksT_t[ti][:], qT, qT[:, 2, :], q="sp")
                    P.dma(vs_t[ti], vs_t[ti][:, :], ob, ob[:, 384:512], q="sp")
                return fb
            elif bi == 3:
                ob = hb[(ti + 1) % 2]
                ob = wk16[2]
                rope(pp, pp[:, 0:128], 2, 16, tCa, tCa[:, 0:128], tSa, tSa[:, 0:32], ob, ob[:, 0:128], o32, t2)
                P.op("act", lambda e: e.copy(out=ob[:, 128:256], in_=pp[:, 128:256]), r=[pp], w=[ob])
                gsb = sm[0]
                P.op("act", lambda e: e.activation(out=gsb[:, 0:48], in_=pp[:, 256:304], func=AF.Sigmoid), r=[pp], w=[gsb])

                def fb():
                    qT = nextq()
                    P.dma(ga_t[ti], ga_t[ti][:, :], gsb, gsb[:, 0:48], q="sp")
                    transpose8(ob, qT, qT[:, 0:1, :], nblk=1, eng="dve", alt=rcnt["q"] % 2 == 1)
                    P.dma(kwT_t[ti], kwT_t[ti][:], qT, qT[:, 0, :], q="sp")
                    P.dma(vw_t[ti], vw_t[ti][:, :], ob, ob[:, 128:256], q="sp")
                return fb
            elif bi in (4, 5):
                ob = wk16[bi % 2]
                rope(pp, pp[:, 0:512], 8, 64, tCr, tCr[:, :], tSr, tSr[:, :], ob, ob[:, 0:512], o32, t2)
                kz = None
                if bi == 5:
                    kz = hb[(ti + 1) % 2] if False else wk16[2]
                    P.op("pool", lambda e: e.tensor_tensor(out=kz[:, 0:512], in0=o32[:, 0:512], in1=tZ[:], op=ALU.mult),
                         r=[o32, tZ], w=[kz])

                def fb():
                    qT = nextq()
                    transpose8(ob, qT, qT[:], nblk=4, eng="dve", alt=rcnt["q"] % 2 == 1)
                    dst = (QrT_t if bi == 4 else KrT_t)[ti]
                    P.dma(dst, dst[:].rearrange("(p r) t -> r p t", r=128), qT, qT[:], q="sp")
                    if bi == 5:
                        P.dma(kz_t[ti], kz_t[ti][:, :], kz, kz[:, 0:512], q="sp")
                return fb
            else:
                ob = wk16[bi % 2]
                P.op("act", lambda e: e.copy(out=ob[:, 0:512], in_=pp[:]), r=[pp], w=[ob])
                hh = bi - 6

                def fb():
                    P.dma(vr_t[ti], vr_t[ti][:, hh * 512:(hh + 1) * 512], ob, ob[:, 0:512], q="sp")
                return fb

        load_tabs_a(0)
        load_tabs_r(0)
        prep_x(0)
        for ti in range(ntl):
            for bi in range(6):
                mm_block(ti, bi)
            f0 = post_a(ti, 0)
            mm_block(ti, 6)
            f1 = post_a(ti, 1)
            mm_block(ti, 7)
            f0()
            f2 = post_a(ti, 2)
            f1()
            f2()
            f3 = post_a(ti, 3)
            if ti + 1 < ntl:
                prep_x(ti + 1)
            f3()
            if ti + 1 < ntl:
                load_tabs_a(ti + 1)
            f4 = post_a(ti, 4)
            f5 = post_a(ti, 5)
            f4()
            if ti + 1 < ntl:
                load_tabs_r(ti + 1)
            f6 = post_a(ti, 6)
            f5()
            f7 = post_a(ti, 7)
            f6()
            f7()
        for parent, subs in subs_c:
            uncarve(parent, subs)

    def proj_pass1(slot):
        w_d = I["w_in"]
        w0 = load_w(slot, 0, w_d, w_d.ap[:, 3888:3888 + 1536].rearrange("(k p) f -> p k f", p=128), [128, 8, 1536])
        w1 = load_w(slot, 8 * 1536, w_d, w_d.ap[:, 3888 + 1536:DIN].rearrange("(k p) f -> p k f", p=128), [128, 8, 1536])
        yield
        for ti in range(NT if lim is None else lim):
            P.dma(hT, hT[:, :, 0:128], hmT_t[ti], hmT_t[ti][:])
            for bi in range(6):
                pp = pb[bi % 7]
                wv = w0 if bi < 3 else w1
                cc = (bi % 3) * 512
                for k in range(8):
                    mm(pp, pp[:], hT, hT[:, k, 0:128], slot, wv[:, k, cc:cc + 512], k == 0, k == 7)
                o32 = wk32[bi % 3]
                fn = AF.Silu if bi < 2 else AF.Sigmoid
                P.op("act", lambda e, pp=pp, o32=o32, fn=fn: e.activation(out=o32[:, 0:512], in_=pp[:], func=fn),
                     r=[pp], w=[o32])
                if bi < 2:
                    P.dma(sgr_t[ti], sgr_t[ti][:, bi * 512:(bi + 1) * 512], o32, o32[:, 0:512], q="pool")
                else:
                    P.dma(gm_t[ti], gm_t[ti][:, (bi - 2) * 512:(bi - 1) * 512], o32, o32[:, 0:512], q="pool")


    kcT2_d = scratch("kcT2_d", [2, 64, 256], BF16)
    selT_d = scratch("selT_d", [2, 64, 128], BF16)
    selT_t = [T(selT_d.ap[i], f"selT{i}") for i in range(2)]

    def cmp_phase(slot):
        subs = []

        def cv(off, n, parts=128, name=""):
            t = carve(slot, slot.ap[0:parts, off:off + n], name)
            subs.append(t)
            return t
        w1 = [cv(0, 8192, 64, "w1k"), cv(8192, 8192, 64, "w1v")]
        w2 = [cv(16384, 128, 128, "w2k"), cv(16512, 128, 128, "w2v")]
        tokT = [cv(16640, 4096, 64, "tokT0"), cv(20736, 4096, 64, "tokT1")]
        GT = cv(24832, 512, 128, "GT")
        peb = cv(25344, 64, 32, "peb")
        peT = cv(25408, 32, 64, "peT")
        kcb = cv(25440, 64, 128, "kcb")
        kcTs = cv(25504, 128, 64, "kcTs")
        tCc = sm[1]; tSc = sm[2]; bias = sm[3]
        srcs = [(I["cmp_k_w1"], I["cmp_k_w2"], kcT_d, kc_d), (I["cmp_v_w1"], I["cmp_v_w2"], vcT_d, vc_d)]
        P.dma(peb, peb[:, :], I["cmp_pos_emb"], I["cmp_pos_emb"][:, :], q="pool")
        tr(pbf, pbf[0:64, 0:32], peb, peb[:, :]) if False else P.op(
            "pe", lambda e: e.transpose(out=pbf[0:64, 0:32], in_=peb[:, :], identity=ident[0:32, 0:32]),
            r=[peb, ident], w=[pbf])
        P.op("act", lambda e: e.copy(out=peT[:, :], in_=pbf[0:64, 0:32]), r=[pbf], w=[peT])
        P.op("pool", lambda e: e.memset(GT[:, :], 0.0), w=[GT])
        GT3 = GT[:, :].rearrange("p (m n) -> p m n", m=2)
        for kv in range(2):
            w1_d, w2_d, tT_d, o_d = srcs[kv]
            w1v = w1[kv][:, :].rearrange("p (l j) -> p l j", l=32)
            P.dma(w1[kv], w1v, w1_d, w1_d.ap.rearrange("(l d) j -> d l j", d=64), q="pool")
            w2v = w2[kv][:, :].rearrange("p (m o) -> p m o", m=2)
            P.dma(w2[kv], w2v, w2_d, w2_d.ap.rearrange("(m p) o -> p m o", p=128), q="pool")
            pbias = pb[6]
            for m in range(2):
                for l in range(32):
                    mm(pbias, pbias[:, m:m + 1], w1[kv], w1v[:, l, m * 128:(m + 1) * 128], peT, peT[:, l:l + 1], l == 0, l == 31)
            P.op("act", lambda e, pbias=pbias: e.copy(out=bias[:, 0:2], in_=pbias[:, 0:2]), r=[pbias], w=[bias])
            for g in range(2):
                tk = tokT[g]
                P.dma(tk, tk[:, :], tT_d, tT_d.ap[g * 64:(g + 1) * 64, :])
                tok3 = tk[:, :].rearrange("p (n r) -> p n r", r=16)
                for m in range(2):
                    ph = pb[m]
                    for l in range(32):
                        a, rr_ = l // 16, l % 16
                        mm(ph, ph[:, 0:255], w1[kv], w1v[:, l, m * 128:(m + 1) * 128], tk, tok3[:, a:a + 255, rr_], l == 0, l == 31)
                    hbx, x2, z = wk32[0], wk32[1], wk32[2]
                    P.op("act", lambda e, ph=ph, m=m: e.activation(out=hbx[:, 0:255], in_=ph[:, 0:255], func=AF.Identity,
                                                                   bias=bias[:, m:m + 1]), r=[ph, bias], w=[hbx])
                    P.op("dve", lambda e: e.tensor_tensor(out=x2[:, 0:255], in0=hbx[:, 0:255], in1=hbx[:, 0:255], op=ALU.mult),
                         r=[hbx], w=[x2])
                    P.op("dve", lambda e: e.tensor_scalar(out=x2[:, 0:255], in0=x2[:, 0:255], scalar1=0.044715, scalar2=1.0,
                                                          op0=ALU.mult, op1=ALU.add), r=[x2], w=[x2])
                    P.op("dve", lambda e: e.tensor_tensor(out=z[:, 0:255], in0=x2[:, 0:255], in1=hbx[:, 0:255], op=ALU.mult),
                         r=[x2, hbx], w=[z])
                    P.op("act", lambda e: e.activation(out=z[:, 0:255], in_=z[:, 0:255], func=AF.Sigmoid, scale=1.5957691216),
                         r=[z], w=[z])
                    P.op("dve", lambda e, m=m: e.tensor_tensor(out=GT3[:, m, 0:255], in0=z[:, 0:255], in1=hbx[:, 0:255],
                                                               op=ALU.mult), r=[z, hbx], w=[GT])
                for j in range(2):
                    po = pb[2 + j]
                    for m in range(2):
                        mm(po, po[:, 0:64], GT, GT3[:, m, j * 128:(j + 1) * 128], w2[kv], w2v[:, m, :], m == 0, m == 1)
                    if kv == 0:
                        P.dma(tCc, tCc[:, 0:64], C["ropeCc"], C["ropeCc"][j * 128:(j + 1) * 128, :])
                        P.dma(tSc, tSc[:, 0:16], C["ropeSc"], C["ropeSc"][j * 128:(j + 1) * 128, :])
                        rope(po, po[:, 0:64], 1, 16, tCc, tCc[:, 0:64], tSc, tSc[:, 0:16], kcb, kcb[:, :], wk32[0], wk32[1])
                        P.op("pe", lambda e: e.transpose(out=pbf[0:64, 0:128], in_=kcb[:, :], identity=ident[:]),
                             r=[kcb, ident], w=[pbf])
                        P.op("act", lambda e: e.copy(out=kcTs[:, :], in_=pbf[0:64, 0:128]), r=[pbf], w=[kcTs])
                        P.dma(kcT2_d, kcT2_d.ap[g, :, j * 128:(j + 1) * 128], kcTs, kcTs[:, :], q="pool")
                    else:
                        P.op("act", lambda e, po=po: e.copy(out=kcb[:, :], in_=po[:, 0:64]), r=[po], w=[kcb])
                    P.dma(o_d, o_d.ap[g, j * 128:(j + 1) * 128, :], kcb, kcb[:, :], q="pool")
        uncarve(slot, subs)

    def nsa_phase(slot):
        subs = []

        def cv(off, n, parts=128, name=""):
            t = carve(slot, slot.ap[0:parts, off:off + n], name)
            subs.append(t)
            return t
        KsT = cv(0, 4096, 128, "KsT"); KwT = cv(4096, 4096, 128, "KwT")
        Vs = cv(8192, 2080, 128, "Vs"); Vw = cv(10272, 2080, 128, "Vw")
        KcT = cv(12352, 256, 128, "KcT"); Vc = cv(12608, 130, 128, "Vc")
        C2S = cv(12738, 128, 128, "C2S")
        mfull = [cv(12866, 4096, 128, "mfull0"), cv(24194, 4096, 128, "mfull1")]
        tri = cv(28290, 128, 128, "tri")
        selTs = cv(28418, 128, 64, "selTs")
        QTc = [cv(16962 + i * 1024, 512, 128, f"QTc{i}") for i in range(2)]
        Et = [cv(19010 + i * 1024, 1024, 128, f"Et{i}") for i in range(3)]
        selb = cv(23106, 64, 128, "selb")
        ab = cv(23170, 512, 128, "ab")
        aTs = cv(23682, 512, 128, "aTs")
        TK = tCa; TA = tCr
        P.dma(TK, TK[:, 0:128], C["TK"], C["TK"][:, :])
        P.dma(TA, TA[:, 0:128], C["TA"], C["TA"][:, :])
        P.dma(C2S, C2S[:, :].rearrange("p (j s) -> p j s", j=2), C["C2S"], C["C2S"].ap.rearrange("(j p) s -> p j s", p=128), q="pool")
        P.op("pool", lambda e: e.memset(tri[:, :], 1.0), w=[tri])
        P.op("pool", lambda e: e.affine_select(out=tri[:, :], in_=tri[:, :], pattern=[[1, 128]], compare_op=ALU.is_ge,
                                               fill=0.0, base=0, channel_multiplier=-1), r=[tri], w=[tri])
        Vs3 = Vs[:, :].rearrange("p (j d) -> p j d", d=65)
        Vw3 = Vw[:, :].rearrange("p (j d) -> p j d", d=65)
        Vc3 = Vc[:, :].rearrange("p (j d) -> p j d", d=65)
        C2S3 = C2S[:, :].rearrange("p (j s) -> p j s", j=2)
        SB = [(pb[0], pb[1]), (pb[2], pb[3])]
        OA, OB, OC = pb[4], pb[5], pb[6]
        OA3 = OA[:, 0:260].rearrange("p (h d) -> p h d", d=65)
        OB3 = OB[:, 0:260].rearrange("p (h d) -> p h d", d=65)
        tmpo = xin[1]
        tmp3 = tmpo[:, 0:512].rearrange("p (h d) -> p h d", d=64)
        den = sm[2]; coef = sm[3]; imp = sm[4]; imp2 = sm[5]; m8 = sm[6]; imp3 = sm[7]
        sidx = {"s": 0, "e": 0}

        def scores(lhs_t, lhs_ap, qt, mask=None):
            b = SB[sidx["s"] % 2]; sidx["s"] += 1
            for hh in range(2):
                mm(b[hh], b[hh][:], lhs_t, lhs_ap[hh * 64:(hh + 1) * 64], qt, qt[hh * 64:(hh + 1) * 64, :], True, True)
            et = Et[sidx["e"] % 3]; sidx["e"] += 1
            for hh in range(2):
                P.op("act", lambda e, hh=hh, et=et, b=b: e.activation(out=et[:, hh * 512:(hh + 1) * 512], in_=b[hh][:],
                                                                      func=AF.Exp, scale=0.125), r=[b[hh]], w=[et])
            if mask is not None:
                m_t, m_ap = mask
                P.op("dve", lambda e: e.tensor_tensor(out=et[:, :].rearrange("p (h q) -> p h q", h=8),
                                                      in0=et[:, :].rearrange("p (h q) -> p h q", h=8),
                                                      in1=m_ap.unsqueeze(1).broadcast_to([128, 8, 128]), op=ALU.mult),
                     r=[et, m_t], w=[et])
            return et

        def warm(n):
            b = SB[sidx["s"] % 2][0]
            for _w in range(n):
                mm(b, b[:, :], ident, ident[:, :], hb[0], hb[0][:, 0:512], True, True)

        def amask(et, cm, qstep, base):
            P.op("pool", lambda e: e.affine_select(out=et[:, :], in_=et[:, :], pattern=[[0, 8], [qstep, 128]],
                                                   compare_op=ALU.is_ge, fill=0.0, base=base, channel_multiplier=cm),
                 r=[et], w=[et])

        def pv(et, v_t, v_ap, first, last):
            for h in range(8):
                o_t = OA if h < 4 else OB
                o3 = OA3 if h < 4 else OB3
                mm(o_t, o3[:, h % 4, :], et, et[:, h * 128:(h + 1) * 128], v_t, v_ap, first and (h % 4 == 0), last, sgc=True)

        pend = []

        def flush(keep=1):
            while len(pend) > keep:
                work, post = pend.pop(0)
                work()
                if post is not None:
                    post()

        def finish_branch(bidx, first, gat, acc):
            acc3 = acc[:, 0:512].rearrange("p (h d) -> p h d", d=64)
            P.op("dve", lambda e: e.tensor_scalar(out=den[:, 0:4], in0=OA3[:, :, 64], scalar1=1e-20, scalar2=None, op0=ALU.max),
                 r=[OA], w=[den])
            P.op("dve", lambda e: e.tensor_scalar(out=den[:, 4:8], in0=OB3[:, :, 64], scalar1=1e-20, scalar2=None, op0=ALU.max),
                 r=[OB], w=[den])
            P.op("dve", lambda e: e.reciprocal(out=den[:, 8:16], in_=den[:, 0:8]), r=[den], w=[den])
            g3 = gat[:, 0:24].rearrange("p (h b) -> p h b", b=3)
            P.op("dve", lambda e: e.tensor_tensor(out=coef[:, 0:8], in0=den[:, 8:16], in1=g3[:, :, bidx], op=ALU.mult),
                 r=[den, gat], w=[coef])
            dst, d3 = (acc, acc3) if first else (tmpo, tmp3)
            P.op("dve", lambda e: e.tensor_tensor(out=d3[:, 0:4, :], in0=OA3[:, :, 0:64],
                                                  in1=coef[:, 0:4].unsqueeze(2).broadcast_to([128, 4, 64]), op=ALU.mult),
                 r=[OA, coef], w=[dst])
            P.op("dve", lambda e: e.tensor_tensor(out=d3[:, 4:8, :], in0=OB3[:, :, 0:64],
                                                  in1=coef[:, 4:8].unsqueeze(2).broadcast_to([128, 4, 64]), op=ALU.mult),
                 r=[OB, coef], w=[dst])
            if not first:
                P.op("pool", lambda e: e.tensor_tensor(out=acc[:, 0:512], in0=acc[:, 0:512], in1=tmpo[:, 0:512], op=ALU.add),
                     r=[acc, tmpo], w=[acc])

        for g in range(2):
            for hh in range(2):
                P.dma(KsT, KsT[hh * 64:(hh + 1) * 64, :], ksT_d, ksT_d.ap[g * 64:(g + 1) * 64, :])
                P.dma(KwT, KwT[hh * 64:(hh + 1) * 64, :], kwT_d, kwT_d.ap[g * 64:(g + 1) * 64, :])
            P.dma(Vs, Vs3[:, :, 0:64], vs_d, vs_d.ap[:, g * 64:(g + 1) * 64].rearrange("(j p) d -> p j d", p=128))
            P.dma(Vw, Vw3[:, :, 0:64], vw_d, vw_d.ap[:, g * 64:(g + 1) * 64].rearrange("(j p) d -> p j d", p=128))
            P.op("pool", lambda e: e.memset(Vs3[:, :, 64:65], 1.0), w=[Vs])
            P.op("pool", lambda e: e.memset(Vw3[:, :, 64:65], 1.0), w=[Vw])
            for hh in range(2):
                P.dma(KcT, KcT[hh * 64:(hh + 1) * 64, :], kcT2_d, kcT2_d.ap[g])
            P.dma(Vc, Vc3[:, :, 0:64], vc_d, vc_d.ap[g].rearrange("(j p) d -> p j d", p=128))
            P.op("pool", lambda e: e.memset(Vc3[:, :, 64:65], 1.0), w=[Vc])
            gats = [sm[1], sm[0], gat3]
            accs = [xin[0], wk32[0]]
            clist = list(range(NT) if lim is None else lim_c)

            def emit_sel(c):
                qt = QTc[c % 2]
                gat = gats[c % 3]
                acc_c = accs[c % 2]
                mf = mfull[c % 2]

                def post_sel():
                    finish_branch(1, False, gat, acc_c)
                    P.op("act", lambda e: e.copy(out=ab[:, :], in_=acc_c[:, 0:512]), r=[acc_c], w=[ab])
                    transpose8(ab, aTs, aTs[:, :].rearrange("p (k t) -> p k t", k=4), nblk=4, eng="act")
                    dst = aT_t[c]
                    P.dma(dst, dst[g * 512:(g + 1) * 512, :].rearrange("(k r) t -> r k t", r=128), aTs,
                          aTs[:, :].rearrange("p (k t) -> p k t", k=4), q="pool")
                for j in range(c + 1):
                    et = scores(KsT, KsT[:, j * 128:(j + 1) * 128], qt, mask=(mf, mf[:, j * 128:(j + 1) * 128]))
                    flush()
                    pend.append((lambda et=et, j=j: pv(et, Vs, Vs3[:, j, :], j == 0, j == c), post_sel if j == c else None))

            for ci, c in enumerate(clist):
                qt = QTc[c % 2]
                gat = gats[c % 3]
                warm(int(_os.environ.get("NWARM0", "0")))
                for hh in range(2):
                    P.dma(qt, qt[hh * 64:(hh + 1) * 64, :].rearrange("d (h t) -> d h t", h=4), QT_t[g][c],
                          QT_t[g][c][hh * 256:(hh + 1) * 256, :].rearrange("(h d) t -> d h t", d=64))
                P.dma(gat, gat[:, 0:24], ga_t[c], ga_t[c][:, g * 24:(g + 1) * 24])

                def post_cmp(c=c, gat=gat):
                    finish_branch(0, True, gat, accs[c % 2])
                    for h in range(8):
                        if h == 0:
                            P.op("dve", lambda e: e.tensor_scalar(out=imp[:, 0:64], in0=OC[:, 0:64], scalar1=den[:, 8:9],
                                                                  scalar2=None, op0=ALU.mult), r=[OC, den], w=[imp])
                        else:
                            P.op("dve", lambda e, h=h: e.scalar_tensor_tensor(
                                out=imp[:, 0:64], in0=OC[:, h * 64:(h + 1) * 64], scalar=den[:, 8 + h:9 + h], in1=imp[:, 0:64],
                                op0=ALU.mult, op1=ALU.add), r=[OC, den, imp], w=[imp])
                    off = 64 - 2 * c
                    P.op("dve", lambda e: e.tensor_tensor(out=imp2[:, 0:64], in0=imp[:, 0:64], in1=TK[:, off:off + 64], op=ALU.mult),
                         r=[imp, TK], w=[imp2])
                    P.op("dve", lambda e: e.tensor_tensor(out=imp2[:, 0:64], in0=imp2[:, 0:64], in1=TA[:, off:off + 64], op=ALU.add),
                         r=[imp2, TA], w=[imp2])
                    P.op("dve", lambda e: e.memset(imp2[:, 0:1], 1.0e4), r=[imp2], w=[imp2])
                    P.op("dve", lambda e: e.max(out=m8[:, 0:8], in_=imp2[:, 0:64]), r=[imp2], w=[m8])
                    P.op("dve", lambda e: e.match_replace(out=imp3[:, 0:64], in_to_replace=m8[:, 0:8], in_values=imp2[:, 0:64],
                                                          imm_value=-3.0e4), r=[imp2, m8], w=[imp3])
                    P.op("dve", lambda e: e.max(out=m8[:, 8:16], in_=imp3[:, 0:64]), r=[imp3, m8], w=[m8])
                    P.op("dve", lambda e: e.tensor_scalar(out=selb[:, :].rearrange("p (b j) -> p j b", b=2),
                                                          in0=imp2[:, 0:64].rearrange("p (j b) -> p j b", b=2),
                                                          scalar1=m8[:, 15:16], scalar2=None, op0=ALU.is_ge),
                         r=[imp2, m8], w=[selb])
                    P.op("pe", lambda e: e.transpose(out=pbf[0:64, 0:128], in_=selb[:, :], identity=ident[:]),
                         r=[selb, ident], w=[pbf])
                    P.op("dve", lambda e: e.tensor_copy(out=selTs[:, :], in_=pbf[0:64, 0:128]), r=[pbf], w=[selTs])
                    sd = selT_t[c % 2]
                    P.dma(sd, sd[:, :], selTs, selTs[:, :])
                    mf = mfull[c % 2]
                    for b2 in range(2):
                        src = sd[b2 * 32:b2 * 32 + c + 1, :].rearrange("(o r) q -> o (r q)", o=1).partition_broadcast(64)
                        P.dma(mf, mf[b2 * 64:(b2 + 1) * 64, 0:(c + 1) * 128], sd, src)
                    P.op("pool", lambda e: e.tensor_tensor(out=mf[:, c * 128:(c + 1) * 128], in0=mf[:, c * 128:(c + 1) * 128],
                                                           in1=tri[:, :], op=ALU.mult), r=[mf, tri], w=[mf])

                def post_win(c=c, gat=gat):
                    finish_branch(2, False, gat, accs[c % 2])

                njt = 1 if c <= 15 else 2
                for j in range(njt):
                    et = scores(KcT, KcT[:, j * 128:(j + 1) * 128], qt)
                    amask(et, -16, 1, -16 * (128 * j - 8 * c) - 31)
                    flush()

                    def work(et=et, j=j, njt=njt):
                        pv(et, Vc, Vc3[:, j, :], j == 0, j == njt - 1)
                        for h in range(8):
                            mm(OC, OC[:, h * 64:(h + 1) * 64], et, et[:, h * 128:(h + 1) * 128], C2S, C2S3[:, j, :],
                               j == 0 and h == 0, j == njt - 1, sgc=True)
                    pend.append((work, post_cmp if j == njt - 1 else None))
                j0 = max(0, c - 4)
                for j in range(j0, c + 1):
                    et = scores(KwT, KwT[:, j * 128:(j + 1) * 128], qt)
                    if j == c:
                        amask(et, -1, 1, 0)
                    if j == c - 4:
                        amask(et, 1, -1, -1)
                    flush()
                    pend.append((lambda et=et, j=j, j0=j0, c=c: pv(et, Vw, Vw3[:, j, :], j == j0, j == c),
                                 post_win if j == c else None))
                if ci > 0:
                    emit_sel(clist[ci - 1])
            flush(0)
            emit_sel(clist[-1])
            flush(0)
        uncarve(slot, subs)

    def ret_phase():
        subs = []
        bufs = []
        for i in range(2):
            q_ = carve(big, big.ap[0:64, i * 2560:i * 2560 + 1024], f"QrTc{i}")
            k_ = carve(big, big.ap[0:64, i * 2560 + 1024:i * 2560 + 2048], f"KrTc{i}")
            z_ = carve(big, big.ap[:, i * 2560 + 2048:i * 2560 + 2560], f"kzc{i}")
            subs += [q_, k_, z_]
            bufs.append((q_, k_, z_))
        dec = [tCr, tSr]
        gct = [tCa, tZ]
        P.dma(dec[0], dec[0][:, :], C["decayT"], C["decayT"][:, 0:512])
        P.dma(dec[1], dec[1][:, :], C["decayT"], C["decayT"][:, 512:1024])
        P.dma(gct[0], gct[0][0:64, :], C["GC"], C["GC"][:, 0:512])
        P.dma(gct[1], gct[1][0:64, :], C["GC"], C["GC"][:, 512:1024])
        xit = sm[1]
        P.dma(xit, xit[:, 0:8], C["XI"], C["XI"][:, :])
        load_gain(gain2, I["ret_gn_gain"])
        Rst = xin[0]; osbs = [xin[1], gain]; sgr = wk32[0]; sq = wk32[1]; tmp = wk32[2]
        Rbf = wk16[0]; ATs = wk16[1]; rob = wk16[2]
        stt = sm[2]
        P.op("pool", lambda e: e.memset(Rst[0:64, :], 0.0), w=[Rst])
        P.op("pool", lambda e: e.memset(Rbf[0:64, :], 0.0), w=[Rbf])
        sq3 = sq[:, :].rearrange("p (h e) -> p h e", h=8)
        tmp3 = tmp[:, :].rearrange("p (h e) -> p h e", h=8)

        def stage1(n):
            q_, k_, z_ = bufs[n % 2]
            vt = hb[n % 2]
            osb = osbs[n % 2]
            P.dma(q_, q_[:, :].rearrange("d (h t) -> d h t", h=8), QrT_t[n], QrT_t[n][:].rearrange("(h d) t -> d h t", d=64))
            P.dma(k_, k_[:, :].rearrange("d (h t) -> d h t", h=8), KrT_t[n], KrT_t[n][:].rearrange("(h d) t -> d h t", d=64))
            P.dma(z_, z_[:, :], kz_t[n], kz_t[n][:, :])
            P.dma(vt, vt[:, :], vr_t[n], vr_t[n][:, :])
            for h in range(8):
                b = pb[h // 4]
                mm(b, b[:, (h % 4) * 128:(h % 4 + 1) * 128], k_, k_[:, h * 128:(h + 1) * 128], q_, q_[:, h * 128:(h + 1) * 128], True, True)
            if n > 0:
                for h in range(8):
                    b = pb[4 + h // 4]
                    mm(b, b[:, (h % 4) * 128:(h % 4 + 1) * 128], q_, q_[:, h * 128:(h + 1) * 128], Rbf, Rbf[0:64, h * 128:(h + 1) * 128], True, True)
            for hh in range(2):
                P.op("dve", lambda e, hh=hh: e.tensor_tensor(out=ATs[:, hh * 512:(hh + 1) * 512], in0=pb[hh][:], in1=dec[hh][:, :],
                                                             op=ALU.mult), r=[pb[hh], dec[hh]], w=[ATs])
            for h in range(8):
                b = pb[2 + h // 4]
                mm(b, b[:, (h % 4) * 128:(h % 4 + 1) * 128], ATs, ATs[:, h * 128:(h + 1) * 128], vt, vt[:, h * 128:(h + 1) * 128], True, True)
            for hh in range(2):
                b = pb[6]
                for h4 in range(4):
                    h = hh * 4 + h4
                    mm(b, b[0:64, h4 * 128:(h4 + 1) * 128], z_, z_[:, h * 64:(h + 1) * 64], vt, vt[:, h * 128:(h + 1) * 128], True, True)
                P.op("pool", lambda e, hh=hh: e.tensor_tensor(out=Rst[0:64, hh * 512:(hh + 1) * 512], in0=Rst[0:64, hh * 512:(hh + 1) * 512],
                                                              in1=gct[hh][0:64, :], op=ALU.mult), r=[Rst, gct[hh]], w=[Rst])
                P.op("dve", lambda e, hh=hh, b=b: e.tensor_tensor(out=Rst[0:64, hh * 512:(hh + 1) * 512], in0=b[0:64, :],
                                                                  in1=Rst[0:64, hh * 512:(hh + 1) * 512], op=ALU.add), r=[b, Rst], w=[Rst])
            P.op("act", lambda e: e.copy(out=Rbf[0:64, :], in_=Rst[0:64, :]), r=[Rst], w=[Rbf])
            for hh in range(2):
                P.op("act", lambda e, hh=hh: e.copy(out=osb[:, hh * 512:(hh + 1) * 512], in_=pb[2 + hh][:]), r=[pb[2 + hh]], w=[osb])
            if n > 0:
                for hh in range(2):
                    P.op("dve", lambda e, hh=hh: e.tensor_tensor(
                        out=tmp3[:, hh * 4:(hh + 1) * 4, :], in0=pb[4 + hh][:].rearrange("p (h e) -> p h e", h=4),
                        in1=xit[:, hh * 4:(hh + 1) * 4].unsqueeze(2).broadcast_to([128, 4, 128]), op=ALU.mult),
                        r=[pb[4 + hh], xit], w=[tmp])
                P.op("pool", lambda e: e.tensor_tensor(out=osb[:, :], in0=osb[:, :], in1=tmp[:, :], op=ALU.add), r=[osb, tmp], w=[osb])

        def stage2(n):
            osb = osbs[n % 2]
            osb3 = osb[:, :].rearrange("p (h e) -> p h e", h=8)
            P.dma(sgr, sgr[:, :], sgr_t[n], sgr_t[n][:, :])
            P.op("dve", lambda e: e.tensor_reduce(out=stt[:, 0:8], in_=osb3, axis=AX.X, op=ALU.add), r=[osb], w=[stt])
            P.op("act", lambda e: e.activation(out=sq[:, :], in_=osb[:, :], func=AF.Square), r=[osb], w=[sq])
            P.op("dve", lambda e: e.tensor_reduce(out=stt[:, 8:16], in_=sq3, axis=AX.X, op=ALU.add), r=[sq], w=[stt])
            P.op("dve", lambda e: e.tensor_scalar(out=stt[:, 0:8], in0=stt[:, 0:8], scalar1=1.0 / 128, scalar2=None, op0=ALU.mult),
                 r=[stt], w=[stt])
            P.op("dve", lambda e: e.tensor_tensor(out=stt[:, 16:24], in0=stt[:, 0:8], in1=stt[:, 0:8], op=ALU.mult), r=[stt], w=[stt])
            P.op("dve", lambda e: e.scalar_tensor_tensor(out=stt[:, 24:32], in0=stt[:, 8:16], scalar=1.0 / 128, in1=stt[:, 16:24],
                                                         op0=ALU.mult, op1=ALU.subtract), r=[stt], w=[stt])
            P.op("dve", lambda e: e.tensor_scalar(out=stt[:, 24:32], in0=stt[:, 24:32], scalar1=1e-5, scalar2=None, op0=ALU.add),
                 r=[stt], w=[stt])
            P.op("pool", lambda e: e.tensor_tensor(out=stt[:, 40:48], in0=stt[:, 24:32], in1=negh[:, 0:8], op=ALU.pow),
                 r=[stt, negh], w=[stt])
            P.op("dve", lambda e: e.scalar_tensor_tensor(out=stt[:, 48:56], in0=stt[:, 0:8], scalar=-1.0, in1=stt[:, 40:48],
                                                         op0=ALU.mult, op1=ALU.mult), r=[stt], w=[stt])
            for h in range(8):
                P.op("act", lambda e, h=h: e.activation(out=osb3[:, h, :], in_=osb3[:, h, :], func=AF.Identity,
                                                        scale=stt[:, 40 + h:41 + h], bias=stt[:, 48 + h:49 + h]),
                     r=[osb, stt], w=[osb])
            P.op("dve", lambda e: e.tensor_tensor(out=osb[:, :], in0=osb[:, :], in1=gain2[:, :], op=ALU.mult), r=[osb, gain2], w=[osb])
            P.op("pool", lambda e: e.tensor_tensor(out=rob[:, :], in0=osb[:, :], in1=sgr[:, :], op=ALU.mult), r=[osb, sgr], w=[rob])
            transpose8(rob, hT, hT[:, :, 0:128], eng="act")
            dst = rT_t[n]
            P.dma(dst, dst[:].rearrange("(k r) t -> r k t", r=128), hT, hT[:, :, 0:128], q="pool")

        nn = NT if lim is None else lim
        for n in range(nn):
            stage1(n)
            if n > 0:
                stage2(n - 1)
        stage2(nn - 1)
        uncarve(big, subs)

    def merge_phase(slot):
        Wn = load_w(slot, 0, I["w_branch_nsa"], I["w_branch_nsa"].ap.rearrange("(k p) f -> p k f", p=128), [128, 8, 1024])
        Wr = load_w(slot, 8192, I["w_branch_ret"], I["w_branch_ret"].ap.rearrange("(k p) f -> p k f", p=128), [128, 8, 1024])
        Wo = load_w(slot, 16384, I["w_out"], I["w_out"].ap.rearrange("(k p) f -> p k f", p=128), [128, 8, 1024])
        yield
        subs = []
        bufs = []
        for i in range(2):
            a_ = carve(big, big.ap[:, i * 2048:i * 2048 + 1024], f"aTt{i}")
            r_ = carve(big, big.ap[:, i * 2048 + 1024:i * 2048 + 2048], f"rTt{i}")
            subs += [a_, r_]
            bufs.append((a_, r_))
        for ti in range(NT if lim is None else lim):
            a_, r_ = bufs[ti % 2]
            a3 = a_[:, :].rearrange("p (k t) -> p k t", k=8)
            r3 = r_[:, :].rearrange("p (k t) -> p k t", k=8)
            P.dma(a_, a3, aT_t[ti], aT_t[ti][:].rearrange("(k r) t -> r k t", r=128))
            P.dma(r_, r3, rT_t[ti], rT_t[ti][:].rearrange("(k r) t -> r k t", r=128))
            ga, gr = wk32[0], wk32[1]
            P.dma(ga, ga[:, :], gm_t[ti], gm_t[ti][:, 0:1024])
            P.dma(gr, gr[:, :], gm_t[ti], gm_t[ti][:, 1024:2048])
            x1s = wk32[2]
            P.dma(x1s, x1s[:, :], x1_t[ti], x1_t[ti][:, :])
            for half in range(2):
                for k in range(8):
                    mm(pb[half], pb[half][:], a_, a3[:, k, :], slot, Wn[:, k, half * 512:(half + 1) * 512], k == 0, k == 7)
                for k in range(8):
                    mm(pb[2 + half], pb[2 + half][:], r_, r3[:, k, :], slot, Wr[:, k, half * 512:(half + 1) * 512], k == 0, k == 7)
            m1, m2 = xin[0], xin[1]
            mb = hb[ti % 2]
            for half in range(2):
                cs = slice(half * 512, (half + 1) * 512)
                P.op("dve", lambda e, half=half, cs=cs: e.tensor_tensor(out=m1[:, cs], in0=pb[half][:], in1=ga[:, cs], op=ALU.mult),
                     r=[pb[half], ga], w=[m1])
                P.op("dve", lambda e, half=half, cs=cs: e.tensor_tensor(out=m2[:, cs], in0=pb[2 + half][:], in1=gr[:, cs], op=ALU.mult),
                     r=[pb[2 + half], gr], w=[m2])
            P.op("pool", lambda e, mb=mb: e.tensor_tensor(out=mb[:, :], in0=m1[:, :], in1=m2[:, :], op=ALU.add), r=[m1, m2], w=[mb])
            transpose8(mb, hT, hT[:, :, 0:128], eng="act")
            for half in range(2):
                po = pb[4 + half]
                for k in range(8):
                    mm(po, po[:], hT, hT[:, k, 0:128], slot, Wo[:, k, half * 512:(half + 1) * 512], k == 0, k == 7)
                P.op("dve", lambda e, po=po, half=half: e.tensor_tensor(out=x1s[:, half * 512:(half + 1) * 512], in0=po[:],
                                                                       in1=x1s[:, half * 512:(half + 1) * 512], op=ALU.add),
                     r=[po, x1s], w=[x1s])
            P.dma(x2_t[ti], x2_t[ti][:, :], x1s, x1s[:, :], q="pool")
        uncarve(big, subs)

    def begin(gen):
        next(gen)
        return gen

    def finish(gen):
        for _ in gen:
            pass

    allp = set(phases) == {"ffn1", "proj", "cmp", "nsa", "ret", "merge", "ffn2"} and not _os.environ.get("SKIPP1")
    out_t = [T(out_d.ap[i * 128:(i + 1) * 128], f"out{i}") for i in range(NT)]
    if allp:
        f1a = begin(ffn_pass(1, 0, wslot[0], x_t, x1_t, I["ffn1_norm"]))
        f1b = begin(ffn_pass(1, 1, wslot[1], x1_t, x1_t, I["ffn1_norm"]))
        finish(f1a)
        p0 = begin(proj_pass0(wslot[0]))
        finish(f1b)
        p1 = begin(proj_pass1(wslot[1]))
        finish(p0)
        finish(p1)
        cmp_phase(wslot[1])
        mg = begin(merge_phase(wslot[1]))
        nsa_phase(wslot[0])
        f2a = begin(ffn_pass(2, 0, wslot[0], x2_t, x2_t, I["ffn2_norm"]))
        ret_phase()
        finish(mg)
        f2b = begin(ffn_pass(2, 1, wslot[1], x2_t, out_t, I["ffn2_norm"], final=True))
        finish(f2a)
        finish(f2b)
    else:
        if "ffn1" in phases:
            finish(ffn_pass(1, 0, wslot[0], x_t, x1_t, I["ffn1_norm"]))
            finish(ffn_pass(1, 1, wslot[1], x1_t, x1_t, I["ffn1_norm"]))
        if "proj" in phases:
            finish(proj_pass0(wslot[0]))
            if not _os.environ.get("SKIPP1"):
                finish(proj_pass1(wslot[1]))
        if "cmp" in phases:
            cmp_phase(wslot[1])
        if "nsa" in phases:
            nsa_phase(wslot[0])
        if "ret" in phases:
            ret_phase()
        if "merge" in phases:
            finish(merge_phase(wslot[1]))
        if "ffn2" in phases:
            finish(ffn_pass(2, 0, wslot[0], x2_t, x2_t, I["ffn2_norm"]))
            finish(ffn_pass(2, 1, wslot[1], x2_t, out_t, I["ffn2_norm"], final=True))
    P.emit()
    return nc, P


_CACHE = {}


def kernel(**inputs):
    n_cores = 8
    if "nc" not in _CACHE:
        _CACHE["nc"] = build()[0]
        _CACHE["consts"] = {k: np.ascontiguousarray(v.reshape(CONST_SHAPES[k]).astype(np.float32))
                            for k, v in _consts().items()}
    nc = _CACHE["nc"]
    cst = _CACHE["consts"]
    shared = {}
    for k, shp in IN_SHAPES.items():
        if k == "x":
            continue
        shared[k] = np.ascontiguousarray(np.asarray(inputs[k], dtype=np.float32).reshape(shp))
    x = np.asarray(inputs["x"], dtype=np.float32)
    in_maps = []
    for c in range(n_cores):
        m = dict(shared)
        m.update(cst)
        m["x"] = np.ascontiguousarray(x[c % 4])
        in_maps.append(m)
    res = run_bass_kernel_spmd(nc, in_maps, core_ids=list(range(n_cores)))
    out = np.stack([np.asarray(res.results[b]["out"], dtype=np.float32) for b in range(4)], axis=0)
    return out
```
